# Optimizing a Trainium2 kernel written in Bass

```python
import math
import jax
import jax.numpy as jnp
from jax import lax
import numpy as np

D_MODEL = 1024
BATCH = 32
SEQ = 256
DEPTH = 2
DEC_BATCH = 4
DEC_SEQ = 4096
PAST_LEN = 256

GRID_W = 64
D_MIX = D_MODEL
SSM_WIDTH = D_MIX // 4
SSM_CH_PER_GROUP = 16
SSM_GROUPS = SSM_WIDTH // SSM_CH_PER_GROUP
SSM_STATE = 64
NA_WIDTH = D_MIX // 2
NA_HEAD_DIM = 64
NA_HEADS = NA_WIDTH // NA_HEAD_DIM
NA_MAX_ROWS = 8
NA_COLS = 16
GM_WIDTH = D_MIX - SSM_WIDTH - NA_WIDTH
GM_GROUPS = 4
GM_CHUNK = 128
D_FF = 2816
N_MOD = 9
IN_COLS = SSM_WIDTH + 3 * NA_WIDTH + 2 * GM_WIDTH
IN_SPLITS = (SSM_WIDTH, SSM_WIDTH + NA_WIDTH, SSM_WIDTH + 2 * NA_WIDTH, SSM_WIDTH + 3 * NA_WIDTH)
ATTN_Q_BLOCK = 128
RMS_EPS = 1e-6
LN_EPS = 1e-5
NEG_INF = -1e30

kernel_name = 'hybrid_diffusion_prefix_step'


def rmsnorm(x, g):
    x32 = x.astype(jnp.float32)
    y = x32 * lax.rsqrt(jnp.mean(x32 * x32, axis=-1, keepdims=True) + RMS_EPS)
    return (y * g.astype(jnp.float32)).astype(x.dtype)


def layernorm(x):
    x32 = x.astype(jnp.float32)
    xc = x32 - jnp.mean(x32, axis=-1, keepdims=True)
    var = jnp.mean(xc * xc, axis=-1, keepdims=True)
    return (xc * lax.rsqrt(var + LN_EPS)).astype(x.dtype)


def modulate(h, shift, scale):
    return h * (1.0 + scale) + shift


def swiglu_ffn(h, w_in, w_out):
    gate, up = jnp.split(h @ w_in, 2, axis=-1)
    return (jax.nn.silu(gate) * up) @ w_out


def adaln(cond, w_ada, b_ada):
    mod = jax.nn.silu(cond) @ w_ada + b_ada
    return mod.reshape(cond.shape[0], N_MOD, D_MODEL)


def _linear_recurrence(e1, e2):
    a1, b1 = e1
    a2, b2 = e2
    return a1 * a2, a2 * b1 + b2


def ssm_mixer(x_ssm, lp, init_state):
    f32 = jnp.float32
    b_, seq_len, _ = x_ssm.shape
    u = x_ssm.astype(f32).reshape(b_, seq_len, SSM_GROUPS, SSM_CH_PER_GROUP)
    uc = u.astype(jnp.complex64)
    y = lp['ssm_d'].astype(f32).reshape(SSM_GROUPS, SSM_CH_PER_GROUP) * u
    finals = []
    for d in range(2):
        reverse = d == 1
        lam = lax.complex(lp['ssm_lambda_re'][d].astype(f32), lp['ssm_lambda_im'][d].astype(f32))
        dt = jnp.exp(lp['ssm_log_dt'][d].astype(f32))[:, None]
        lbar = jnp.exp(lam * dt)
        b_mat = lax.complex(lp['ssm_b_re'][d].astype(f32), lp['ssm_b_im'][d].astype(f32))
        bbar = ((lbar - 1.0) / lam)[..., None] * b_mat
        c_mat = lax.complex(lp['ssm_c_re'][d].astype(f32), lp['ssm_c_im'][d].astype(f32))
        bu = jnp.einsum('gpc,blgc->blgp', bbar, uc)
        if init_state is not None:
            s0 = lax.complex(init_state[:, d, ..., 0].astype(f32), init_state[:, d, ..., 1].astype(f32))
            edge = seq_len - 1 if reverse else 0
            bu = bu.at[:, edge].add(lbar * s0)
        a = jnp.broadcast_to(lbar, bu.shape)
        _, s = lax.associative_scan(_linear_recurrence, (a, bu), axis=1, reverse=reverse)
        y = y + jnp.real(jnp.einsum('gcp,blgp->blgc', c_mat, s))
        if init_state is None:
            fin = s[:, 0] if reverse else s[:, seq_len - 1]
            finals.append(jnp.stack([jnp.real(fin), jnp.imag(fin)], axis=-1))
    y = jax.nn.gelu(y.reshape(b_, seq_len, SSM_WIDTH))
    y = y * jax.nn.sigmoid(y @ lp['ssm_glu_w'].astype(f32) + lp['ssm_glu_b'].astype(f32))
    final_state = jnp.stack(finals, axis=1) if init_state is None else None
    return y.astype(x_ssm.dtype), final_state


def dense_attention(q, k, v):
    b_, s_len, h_, dh = q.shape
    nb = s_len // ATTN_Q_BLOCK
    qb = jnp.moveaxis(q.reshape(b_, nb, ATTN_Q_BLOCK, h_, dh), 1, 0)
    scale = dh ** -0.5

    def block(qi):
        s = jnp.einsum('bqhd,bkhd->bhqk', qi, k).astype(jnp.float32) * scale
        p = jax.nn.softmax(s, axis=-1).astype(v.dtype)
        return jnp.einsum('bhqk,bkhd->bqhd', p, v)

    out = lax.map(block, qb)
    return jnp.moveaxis(out, 0, 1).reshape(b_, s_len, h_ * dh)


def neighbourhood_attention(q, k, v, k_ctx, v_ctx, rpb):
    b_, seq_len, h_, dh = q.shape
    rows = seq_len // GRID_W
    kh = min(NA_MAX_ROWS, rows)
    kw = NA_COLS
    kb = 2 * kw
    nb = GRID_W // kw
    qcol = np.arange(GRID_W).reshape(nb, kw)
    kc0 = np.clip(np.arange(nb) * kw - kw // 2, 0, GRID_W - kb)
    kcol = kc0[:, None] + np.arange(kb)[None, :]
    cs = np.clip(qcol - kw // 2, 0, GRID_W - kw)
    win = (kcol[:, None, :] >= cs[:, :, None]) & (kcol[:, None, :] < cs[:, :, None] + kw)
    dc = np.clip(kcol[:, None, :] - qcol[:, :, None], -(kw - 1), kw - 1) + (kw - 1)
    bias_c = rpb[:, :, dc]
    q_g = q.reshape(b_, rows, GRID_W, h_, dh)
    k_g = k.reshape(b_, rows, GRID_W, h_, dh)
    v_g = v.reshape(b_, rows, GRID_W, h_, dh)
    scale = dh ** -0.5

    def row_block(r):
        rs = jnp.clip(r - kh // 2, 0, rows - kh)
        k_blk = lax.dynamic_slice_in_dim(k_g, rs, kh, axis=1)[:, :, kcol]
        v_blk = lax.dynamic_slice_in_dim(v_g, rs, kh, axis=1)[:, :, kcol]
        q_r = lax.dynamic_index_in_dim(q_g, r, axis=1, keepdims=False).reshape(b_, nb, kw, h_, dh)
        s_win = jnp.einsum('bnqhd,bknchd->bhnqkc', q_r, k_blk).astype(jnp.float32) * scale
        dr = rs + jnp.arange(kh) - r + (NA_MAX_ROWS - 1)
        bias = jnp.transpose(jnp.take(bias_c, dr, axis=1), (0, 2, 3, 1, 4))
        s_win = jnp.where(win[None, None, :, :, None, :], s_win + bias[None].astype(jnp.float32), NEG_INF)
        s_ctx = jnp.einsum('bnqhd,bkhd->bhnqk', q_r, k_ctx).astype(jnp.float32) * scale
        logits = jnp.concatenate([s_win.reshape(b_, h_, nb, kw, kh * kb), s_ctx], axis=-1)
        p = jax.nn.softmax(logits, axis=-1).astype(v.dtype)
        p_win = p[..., :kh * kb].reshape(b_, h_, nb, kw, kh, kb)
        p_ctx = p[..., kh * kb:]
        o = jnp.einsum('bhnqkc,bknchd->bnqhd', p_win, v_blk) + jnp.einsum('bhnqk,bkhd->bnqhd', p_ctx, v_ctx)
        return o.reshape(b_, GRID_W, h_, dh)

    out = lax.map(row_block, jnp.arange(rows))
    return jnp.moveaxis(out, 0, 1).reshape(b_, seq_len, h_ * dh)


def spatial_gating(uv, ws, bs):
    b_, seq_len, _ = uv.shape
    u, v = jnp.split(jax.nn.gelu(uv), 2, axis=-1)
    v = layernorm(v)
    vc = v.reshape(b_, seq_len // GM_CHUNK, GM_CHUNK, GM_GROUPS, GM_WIDTH // GM_GROUPS)
    sp = jnp.einsum('gij,bnjgc->bnigc', ws, vc) + jnp.transpose(bs)[:, :, None]
    return u * sp.reshape(b_, seq_len, GM_WIDTH)


def token_mixing(h, lp, ctx_kv, ssm_init):
    b_, seq_len, _ = h.shape
    x_ssm, q, k, v, uv = jnp.split(h @ lp['w_in'], IN_SPLITS, axis=-1)
    y_ssm, ssm_final = ssm_mixer(x_ssm, lp, ssm_init)
    q = rmsnorm(q.reshape(b_, seq_len, NA_HEADS, NA_HEAD_DIM), lp['na_q_norm'])
    k = rmsnorm(k.reshape(b_, seq_len, NA_HEADS, NA_HEAD_DIM), lp['na_k_norm'])
    v = v.reshape(b_, seq_len, NA_HEADS, NA_HEAD_DIM)
    if ctx_kv is None:
        y_na = dense_attention(q, k, v)
    else:
        y_na = neighbourhood_attention(q, k, v, ctx_kv[0], ctx_kv[1], lp['na_rpb'])
    y_gm = spatial_gating(uv, lp['gm_ws'], lp['gm_bs'])
    out = jnp.concatenate([y_ssm, y_na, y_gm], axis=-1) @ lp['w_out']
    ctx_state = (k, v, ssm_final) if ctx_kv is None else None
    return out, ctx_state


def trunk_layer(x, mod, lp, ctx_kv, ssm_init):
    mod = mod.astype(x.dtype)
    sh1, sc1, g1, sh2, sc2, g2, sh3, sc3, g3 = [mod[:, i][:, None, :] for i in range(N_MOD)]
    h = modulate(rmsnorm(x, lp['norm_ffn1']), sh1, sc1)
    x = x + 0.5 * g1 * swiglu_ffn(h, lp['ffn1_w_in'], lp['ffn1_w_out'])
    h = modulate(rmsnorm(x, lp['norm_mix']), sh2, sc2)
    mix, ctx_state = token_mixing(h, lp, ctx_kv, ssm_init)
    x = x + g2 * mix
    h = modulate(rmsnorm(x, lp['norm_ffn2']), sh3, sc3)
    x = x + 0.5 * g3 * swiglu_ffn(h, lp['ffn2_w_in'], lp['ffn2_w_out'])
    return x, ctx_state


def setup_inputs(seed: int = 0) -> dict:
    key = jax.random.key(seed)
    ks = jax.random.split(key, 40)
    f32 = jnp.float32

    def nrm(k, shape, scale):
        return jax.random.normal(k, shape, f32) * scale

    def gain(k, shape):
        return 1.0 + 0.01 * jax.random.normal(k, shape, f32)

    ssm_shape = (DEPTH, 2, SSM_GROUPS, SSM_STATE)
    n_idx = jnp.arange(SSM_STATE, dtype=f32)
    return {
        'x_prompt': nrm(ks[0], (BATCH, SEQ, D_MODEL), 1.0),
        'x_sample': nrm(ks[1], (DEC_BATCH, DEC_SEQ, D_MODEL), 1.0),
        'c': nrm(ks[2], (DEC_BATCH, D_MODEL), 1.0),
        'cache_k': nrm(ks[3], (DEC_BATCH, DEPTH, PAST_LEN, NA_HEADS, NA_HEAD_DIM), 1.0),
        'cache_v': nrm(ks[4], (DEC_BATCH, DEPTH, PAST_LEN, NA_HEADS, NA_HEAD_DIM), 1.0),
        'state_ssm': nrm(ks[5], (DEC_BATCH, DEPTH, 2, SSM_GROUPS, SSM_STATE, 2), 0.1),
        'c_ctx': nrm(ks[6], (D_MODEL,), 1.0),
        'w_ada': nrm(ks[7], (DEPTH, D_MODEL, N_MOD * D_MODEL), 0.02),
        'b_ada': nrm(ks[8], (DEPTH, N_MOD * D_MODEL), 0.02),
        'norm_ffn1': gain(ks[9], (DEPTH, D_MODEL)),
        'ffn1_w_in': nrm(ks[10], (DEPTH, D_MODEL, 2 * D_FF), D_MODEL ** -0.5),
        'ffn1_w_out': nrm(ks[11], (DEPTH, D_FF, D_MODEL), D_FF ** -0.5),
        'norm_mix': gain(ks[12], (DEPTH, D_MODEL)),
        'w_in': nrm(ks[13], (DEPTH, D_MODEL, IN_COLS), D_MODEL ** -0.5),
        'w_out': nrm(ks[14], (DEPTH, D_MIX, D_MODEL), D_MIX ** -0.5),
        'ssm_lambda_re': -0.5 + nrm(ks[15], ssm_shape, 0.01),
        'ssm_lambda_im': math.pi * n_idx + nrm(ks[16], ssm_shape, 0.01),
        'ssm_log_dt': jax.random.uniform(ks[17], (DEPTH, 2, SSM_GROUPS), f32, math.log(1e-3), math.log(1e-1)),
        'ssm_b_re': nrm(ks[18], (DEPTH, 2, SSM_GROUPS, SSM_STATE, SSM_CH_PER_GROUP), (2 * SSM_CH_PER_GROUP) ** -0.5),
        'ssm_b_im': nrm(ks[19], (DEPTH, 2, SSM_GROUPS, SSM_STATE, SSM_CH_PER_GROUP), (2 * SSM_CH_PER_GROUP) ** -0.5),
        'ssm_c_re': nrm(ks[20], (DEPTH, 2, SSM_GROUPS, SSM_CH_PER_GROUP, SSM_STATE), (2 * SSM_STATE) ** -0.5),
        'ssm_c_im': nrm(ks[21], (DEPTH, 2, SSM_GROUPS, SSM_CH_PER_GROUP, SSM_STATE), (2 * SSM_STATE) ** -0.5),
        'ssm_d': nrm(ks[22], (DEPTH, SSM_WIDTH), 1.0),
        'ssm_glu_w': nrm(ks[23], (DEPTH, SSM_WIDTH, SSM_WIDTH), SSM_WIDTH ** -0.5),
        'ssm_glu_b': nrm(ks[24], (DEPTH, SSM_WIDTH), 0.02),
        'na_q_norm': gain(ks[25], (DEPTH, NA_HEAD_DIM)),
        'na_k_norm': gain(ks[26], (DEPTH, NA_HEAD_DIM)),
        'na_rpb': nrm(ks[27], (DEPTH, NA_HEADS, 2 * NA_MAX_ROWS - 1, 2 * NA_COLS - 1), 0.1),
        'gm_ws': nrm(ks[28], (DEPTH, GM_GROUPS, GM_CHUNK, GM_CHUNK), GM_CHUNK ** -0.5),
        'gm_bs': gain(ks[29], (DEPTH, GM_GROUPS, GM_CHUNK)),
        'norm_ffn2': gain(ks[30], (DEPTH, D_MODEL)),
        'ffn2_w_in': nrm(ks[31], (DEPTH, D_MODEL, 2 * D_FF), D_MODEL ** -0.5),
        'ffn2_w_out': nrm(ks[32], (DEPTH, D_FF, D_MODEL), D_FF ** -0.5),
    }


def reference(x_prompt, x_sample, c, cache_k, cache_v, state_ssm, c_ctx, w_ada, b_ada,
              norm_ffn1, ffn1_w_in, ffn1_w_out, norm_mix, w_in, w_out,
              ssm_lambda_re, ssm_lambda_im, ssm_log_dt, ssm_b_re, ssm_b_im, ssm_c_re, ssm_c_im,
              ssm_d, ssm_glu_w, ssm_glu_b, na_q_norm, na_k_norm, na_rpb, gm_ws, gm_bs,
              norm_ffn2, ffn2_w_in, ffn2_w_out):
    stacked = {
        'norm_ffn1': norm_ffn1, 'ffn1_w_in': ffn1_w_in, 'ffn1_w_out': ffn1_w_out,
        'norm_mix': norm_mix, 'w_in': w_in, 'w_out': w_out,
        'ssm_lambda_re': ssm_lambda_re, 'ssm_lambda_im': ssm_lambda_im, 'ssm_log_dt': ssm_log_dt,
        'ssm_b_re': ssm_b_re, 'ssm_b_im': ssm_b_im, 'ssm_c_re': ssm_c_re, 'ssm_c_im': ssm_c_im,
        'ssm_d': ssm_d, 'ssm_glu_w': ssm_glu_w, 'ssm_glu_b': ssm_glu_b,
        'na_q_norm': na_q_norm, 'na_k_norm': na_k_norm, 'na_rpb': na_rpb,
        'gm_ws': gm_ws, 'gm_bs': gm_bs,
        'norm_ffn2': norm_ffn2, 'ffn2_w_in': ffn2_w_in, 'ffn2_w_out': ffn2_w_out,
    }
    xp = x_prompt
    xs = x_sample
    new_k, new_v, new_s = [], [], []
    for l in range(DEPTH):
        lp = {name: arr[l] for name, arr in stacked.items()}
        mod_ctx = adaln(c_ctx[None, :], w_ada[l], b_ada[l])
        xp, (k_c, v_c, s_c) = trunk_layer(xp, mod_ctx, lp, None, None)
        new_k.append(k_c)
        new_v.append(v_c)
        new_s.append(s_c)
        mod_lat = adaln(c, w_ada[l], b_ada[l])
        xs, _ = trunk_layer(xs, mod_lat, lp, (cache_k[:, l], cache_v[:, l]), state_ssm[:, l])
    new_cache_k = jnp.stack(new_k, axis=1)
    new_cache_v = jnp.stack(new_v, axis=1)
    new_state_ssm = jnp.stack(new_s, axis=1)
    return (xp, xs, new_cache_k, new_cache_v, new_state_ssm)
```

```python
import math
import os
import numpy as np
import concourse.bass as bass
import concourse.mybir as mybir
from concourse.bass_utils import run_bass_kernel_spmd

F32 = mybir.dt.float32
BF16 = mybir.dt.bfloat16
AF = mybir.ActivationFunctionType
ALU = mybir.AluOpType
AX = mybir.AxisListType

D = 1024
DFF = 2816
DEPTH = 2
NP_TOK = 1024
NS_TOK = 4096
NT = NP_TOK + NS_TOK
NB = NT // 512
INC = 2304
TWO_PI = 2.0 * math.pi
NEG = -30000.0


class Buf:
    __slots__ = ("t", "w", "r", "name")

    def __init__(self, t, name=""):
        self.t = t
        self.w = None
        self.r = {}
        self.name = name

    def __getitem__(self, idx):
        return self.t[idx]


class Sched:
    def __init__(self, nc, n_dma_sems=40):
        self.nc = nc
        self.stack = []
        self.eng = {"pe": nc.tensor, "act": nc.scalar, "dve": nc.vector, "pool": nc.gpsimd, "sp": nc.sync}
        self.sems = {}
        self.cnt = {}
        for k in ("pe", "act", "dve", "pool"):
            self.sems[k] = self._sem("s_" + k)
            self.cnt[k] = 0
        self.dma_keys = []
        for i in range(n_dma_sems):
            k = "d%d" % i
            self.sems[k] = self._sem("s_" + k)
            self.cnt[k] = 0
            self.dma_keys.append(k)
        self.dma_rr = 0
        self.seen = {e: {} for e in self.eng}
        self.ninst = 0

    def _sem(self, name):
        cm = self.nc.semaphore(name)
        h = cm.__enter__()
        self.stack.append(cm)
        return h

    def mark(self):
        return len(self.stack)

    def release(self, mark):
        while len(self.stack) > mark:
            self.stack.pop().__exit__(None, None, None)

    def sbuf(self, name, shape, dtype):
        self.uid = getattr(self, "uid", 0) + 1
        name = "%s_u%d" % (name, self.uid)
        cm = self.nc.sbuf_tensor(name, list(shape), dtype)
        t = cm.__enter__()
        self.stack.append(cm)
        return Buf(t, name)

    def psum(self, name, shape, dtype=F32):
        cm = self.nc.psum_tensor(name, list(shape), dtype)
        t = cm.__enter__()
        self.stack.append(cm)
        return Buf(t, name)

    def dram(self, name, shape, dtype, kind="Internal"):
        return Buf(self.nc.dram_tensor(name, list(shape), dtype, kind=kind), name)

    def _need(self, e, deps):
        seen = self.seen[e]
        todo = {}
        for (k, v) in deps:
            if seen.get(k, 0) >= v:
                continue
            if todo.get(k, 0) < v:
                todo[k] = v
        for k, v in todo.items():
            self.eng[e].wait_ge(self.sems[k], v)
            seen[k] = v
            self.ninst += 1

    @staticmethod
    def _deps(reads, writes):
        deps = []
        for b in reads:
            if b.w is not None:
                deps.append(b.w)
        for b in writes:
            if b.w is not None:
                deps.append(b.w)
            deps.extend(b.r.items())
        return deps

    def _commit(self, k, v, reads, writes):
        for b in reads:
            if b.r.get(k, 0) < v:
                b.r[k] = v
        for b in writes:
            b.w = (k, v)
            b.r = {}

    def op(self, e, fn, reads=(), writes=()):
        self._need(e, self._deps(reads, writes))
        inst = fn(self.eng[e])
        self.cnt[e] += 1
        inst.then_inc(self.sems[e], 1)
        self._commit(e, self.cnt[e], reads, writes)
        self.ninst += 1
        return inst

    def group(self, e, fns, reads=(), writes=()):
        self._need(e, self._deps(reads, writes))
        inst = None
        for fn in fns:
            inst = fn(self.eng[e])
            self.ninst += 1
        self.cnt[e] += 1
        inst.then_inc(self.sems[e], 1)
        self._commit(e, self.cnt[e], reads, writes)

    def dma(self, q, out_ap, in_ap, reads=(), writes=(), **kw):
        nsw = 8
        if q == "pool":
            self.sw_rr = (getattr(self, "sw_rr", -1) + 1) % nsw
            k = self.dma_keys[self.sw_rr]
        else:
            k = self.dma_keys[nsw + self.dma_rr]
            self.dma_rr = (self.dma_rr + 1) % (len(self.dma_keys) - nsw)
        deps = self._deps(reads, writes)
        if self.cnt[k] > 0:
            deps.append((k, self.cnt[k]))
        self._need(q, deps)
        inst = self.eng[q].dma_start(out=out_ap, in_=in_ap, **kw)
        self.cnt[k] += 16
        inst.then_inc(self.sems[k], 16)
        self._commit(k, self.cnt[k], reads, writes)
        self.ninst += 1
        return inst

    def barrier(self):
        allk = [(k, v) for k, v in self.cnt.items() if v > 0]
        for e in self.eng:
            self._need(e, allk)


def _ap(b):
    return b.t.ap()


def build_program(debug=False):
    nc = bass.Bass("TRN2", target_bir_lowering=False)
    s = Sched(nc)
    ctx = nc.allow_non_contiguous_dma(reason="small strided parameter loads")
    ctx.__enter__()

    def din(name, shape):
        return s.dram(name, shape, F32, kind="ExternalInput")

    def dout(name, shape):
        return s.dram(name, shape, F32, kind="ExternalOutput")

    xp = din("xp", [NP_TOK, D])
    xs = din("xs", [NS_TOK, D])
    cond = din("cond", [2, D])
    ck = din("ck", [DEPTH, 256, 512])
    cv = din("cv", [DEPTH, 256, 512])
    sst = din("sst", [DEPTH, 2, 1024, 2])
    w_ada = din("w_ada", [DEPTH, D, 9 * D])
    b_ada = din("b_ada", [DEPTH, 9 * D])
    norms = [din("norm_ffn1", [DEPTH, D]), din("norm_mix", [DEPTH, D]), din("norm_ffn2", [DEPTH, D])]
    f1_in = din("ffn1_w_in", [DEPTH, D, 2 * DFF])
    f1_out = din("ffn1_w_out", [DEPTH, DFF, D])
    f2_in = din("ffn2_w_in", [DEPTH, D, 2 * DFF])
    f2_out = din("ffn2_w_out", [DEPTH, DFF, D])
    w_in = din("w_in", [DEPTH, D, INC])
    w_out = din("w_out", [DEPTH, D, D])
    lam_re = din("ssm_lambda_re", [DEPTH, 2, 1024])
    lam_im = din("ssm_lambda_im", [DEPTH, 2, 1024])
    log_dt = din("ssm_log_dt", [DEPTH, 2, 16])
    b_re = din("ssm_b_re", [DEPTH, 2, 1024, 16])
    b_im = din("ssm_b_im", [DEPTH, 2, 1024, 16])
    c_re = din("ssm_c_re", [DEPTH, 2, 256, 64])
    c_im = din("ssm_c_im", [DEPTH, 2, 256, 64])
    ssm_d = din("ssm_d", [DEPTH, 256])
    glu_w = din("ssm_glu_w", [DEPTH, 256, 256])
    glu_b = din("ssm_glu_b", [DEPTH, 256])
    qn_g = din("na_q_norm", [DEPTH, 64])
    kn_g = din("na_k_norm", [DEPTH, 64])
    rpb = din("na_rpb", [DEPTH, 8 * 15, 31])
    gm_ws = din("gm_ws", [DEPTH, 4, 128, 128])
    gm_bs = din("gm_bs", [DEPTH, 4, 128])
    c_ident = din("c_ident", [128, 128])
    c_oh = din("c_oh", [31, 64 * 128])
    c_mask = din("c_mask", [128, 64])
    c_iota = din("c_iota", [128, 1024])
    yp = dout("yp", [NP_TOK, D])
    ys = dout("ys", [NS_TOK, D])
    nk = dout("nk", [4, DEPTH, 256, 512])
    nv = dout("nv", [4, DEPTH, 256, 512])
    nst = dout("nst", [4, DEPTH, 2, 1024, 2])
    skind = "ExternalOutput" if debug else "Internal"
    xT_d = s.dram("xT_d", [D, NT], F32, kind=skind)
    ymix_d = s.dram("ymix_d", [D, NT], BF16, kind=skind)
    us_d = s.dram("us_d", [256, NT], BF16, kind=skind)
    q_d = s.dram("q_d", [512, NT], BF16, kind=skind)
    k_d = s.dram("k_d", [512, NT], BF16, kind=skind)
    v_d = s.dram("v_d", [NT, 512], BF16, kind=skind)
    ug_d = s.dram("ug_d", [256, NT], BF16, kind=skind)
    va_d = s.dram("va_d", [NT, 256], BF16, kind=skind)
    vb_d = s.dram("vb_d", [NT, 256], BF16, kind=skind)

    banks = [s.psum("bank%d" % i, [128, 512], F32) for i in range(8)]
    bank_rr = [0]

    def ps():
        b = banks[bank_rr[0]]
        bank_rr[0] = (bank_rr[0] + 1) % 8
        return b

    ident = s.sbuf("ident", [128, 128], F32)
    s.dma("sp", ident[:], _ap(c_ident), writes=[ident])
    ones_bf = s.sbuf("ones_bf", [128, 128], BF16)
    s.op("dve", lambda e: e.memset(ones_bf[:], 1.0), writes=[ones_bf])
    mean_bf = s.sbuf("mean_bf", [128, 128], BF16)
    s.op("dve", lambda e: e.memset(mean_bf[:], 1.0 / 1024.0), writes=[mean_bf])
    bd_bf = s.sbuf("bd_bf", [128, 128], BF16)
    s.op("dve", lambda e: e.memset(bd_bf[:], 0.0), writes=[bd_bf])
    s.op("dve", lambda e: e.memset(bd_bf[0:64, 0:64], 1.0 / 64.0), reads=[bd_bf], writes=[bd_bf])
    s.op("dve", lambda e: e.memset(bd_bf[64:128, 64:128], 1.0 / 64.0), reads=[bd_bf], writes=[bd_bf])
    pi_c = s.sbuf("pi_c", [128, 1], F32)
    s.op("dve", lambda e: e.memset(pi_c[:], math.pi), writes=[pi_c])
    eps6 = s.sbuf("eps6", [128, 1], F32)
    s.op("dve", lambda e: e.memset(eps6[:], 1e-6), writes=[eps6])
    eps5 = s.sbuf("eps5", [128, 1], F32)
    s.op("dve", lambda e: e.memset(eps5[:], 1e-5), writes=[eps5])

    def rsqrt(dst_b, dst_ap, src_b, src_ap, eps_b, scale=1.0):
        P = dst_ap.shape[0]
        s.op("act", lambda e: e.activation(out=dst_ap, in_=src_ap, func=AF.Sqrt, scale=scale, bias=eps_b[0:P, 0:1]),
             reads=[src_b, eps_b], writes=[dst_b])
        s.op("dve", lambda e: e.reciprocal(out=dst_ap, in_=dst_ap), reads=[dst_b], writes=[dst_b])
    Aco = s.sbuf("Aco", [128, DEPTH, 3, 2, 8], F32)
    Bco = s.sbuf("Bco", [128, DEPTH, 3, 2, 8], F32)
    Gco = s.sbuf("Gco", [128, DEPTH, 3, 2, 8], F32)

    def mm_group(out_buf, out_ap, pairs, reads):
        n = len(pairs)
        fns = []
        for i, (l, r) in enumerate(pairs):
            fns.append(lambda e, l=l, r=r, i=i: e.matmul(out_ap, l, r, start=(i == 0), stop=(i == n - 1)))
        s.group("pe", fns, reads=reads, writes=[out_buf])

    def setup_adaln(after_dma=None):
        m0 = s.mark()
        cs32 = s.sbuf("cs32", [128, 8, 2], F32)
        csb = s.sbuf("csb", [128, 8, 2], BF16)
        for c in range(2):
            s.dma("sp", cs32[:, :, c], _ap(cond)[c, :].rearrange("(kt p) -> p kt", p=128), writes=[cs32])
        s.op("act", lambda e: e.activation(out=csb[:], in_=cs32[:], func=AF.Silu), reads=[cs32], writes=[csb])
        gn = s.sbuf("gn", [128, DEPTH, 3, 8], F32)
        for i in range(3):
            for l in range(DEPTH):
                s.dma("sp", gn[:, l, i, :], _ap(norms[i])[l, :].rearrange("(kt p) -> p kt", p=128), writes=[gn])
        wa = [s.sbuf("wa%d" % i, [128, 8, 1024], BF16) for i in range(2)]
        badaT = s.sbuf("badaT", [128, 72], F32)
        modT = s.sbuf("modT", [128, 72, 2], F32)
        for l in range(DEPTH):
            s.dma("sp", badaT[:], _ap(b_ada)[l, :].rearrange("(ft p) -> p ft", p=128), writes=[badaT])
            pb = ps()
            for ch in range(9):
                w = wa[ch % 2]
                s.dma("pool", w[:], _ap(w_ada)[l, :, ch * 1024:(ch + 1) * 1024].rearrange("(kt p) f -> p kt f", p=128),
                      writes=[w])
                for f8 in range(8):
                    ft = ch * 8 + f8
                    mm_group(pb, pb[:, 2 * ft:2 * ft + 2],
                             [(w[:, kt, f8 * 128:(f8 + 1) * 128], csb[:, kt, :]) for kt in range(8)], [w, csb])
            for c in range(2):
                s.op("dve", lambda e, c=c: e.tensor_tensor(out=modT[:, :, c], in0=pb[:, c:144:2], in1=badaT[:], op=ALU.add),
                     reads=[pb, badaT], writes=[modT])
            for i in range(3):
                for c in range(2):
                    sh = modT[:, (3 * i) * 8:(3 * i) * 8 + 8, c]
                    sc = modT[:, (3 * i + 1) * 8:(3 * i + 1) * 8 + 8, c]
                    gt = modT[:, (3 * i + 2) * 8:(3 * i + 2) * 8 + 8, c]
                    s.op("dve", lambda e, sc=sc, l=l, i=i, c=c: e.scalar_tensor_tensor(
                        out=Aco[:, l, i, c, :], in0=sc, scalar=1.0, in1=gn[:, l, i, :], op0=ALU.add, op1=ALU.mult),
                        reads=[modT, gn], writes=[Aco])
                    s.op("dve", lambda e, sh=sh, l=l, i=i, c=c: e.tensor_copy(out=Bco[:, l, i, c, :], in_=sh),
                         reads=[modT], writes=[Bco])
                    s.op("dve", lambda e, gt=gt, l=l, i=i, c=c: e.tensor_scalar(
                        out=Gco[:, l, i, c, :], in0=gt, scalar1=(1.0 if i == 1 else 0.5), scalar2=None, op0=ALU.mult),
                        reads=[modT], writes=[Gco])
        if after_dma is not None:
            after_dma()
        s.barrier()
        s.release(m0)

    def load_xT(xT, blk):
        s.dma("sp", xT[:], _ap(xT_d)[:, blk * 512:(blk + 1) * 512].rearrange("(kt p) n -> p kt n", p=128),
              reads=[xT_d], writes=[xT])

    def store_xT(xT, blk):
        s.dma("sp", _ap(xT_d)[:, blk * 512:(blk + 1) * 512].rearrange("(kt p) n -> p kt n", p=128), xT[:],
              reads=[xT], writes=[xT_d])

    def load_x_tm(xT, xtm, blk):
        src = _ap(xp)[blk * 512:(blk + 1) * 512, :] if blk < 2 else _ap(xs)[(blk - 2) * 512:(blk - 1) * 512, :]
        for tt in range(4):
            s.dma("sp", xtm[:, 0, :], src[tt * 128:(tt + 1) * 128, :], writes=[xtm])
            for half in range(2):
                pb = ps()
                s.group("pe", [lambda e, k4=k4, half=half, pb=pb: e.transpose(pb[:, k4 * 128:(k4 + 1) * 128],
                                                                               xtm[:, 0, (half * 4 + k4) * 128:(half * 4 + k4 + 1) * 128], ident[:])
                               for k4 in range(4)], reads=[xtm, ident], writes=[pb])
                dst = xT[:, half * 4:(half + 1) * 4, tt * 128:(tt + 1) * 128]
                if half:
                    s.op("act", lambda e, dst=dst, pb=pb: e.copy(out=dst, in_=pb[:].rearrange("p (k n) -> p k n", k=4)),
                         reads=[pb], writes=[xT])
                else:
                    s.op("dve", lambda e, dst=dst, pb=pb: e.tensor_copy(out=dst, in_=pb[:].rearrange("p (k n) -> p k n", k=4)),
                         reads=[pb], writes=[xT])

    def store_y_tm(xT, ytm, blk):
        dst = _ap(yp)[blk * 512:(blk + 1) * 512, :] if blk < 2 else _ap(ys)[(blk - 2) * 512:(blk - 1) * 512, :]
        ydr = yp if blk < 2 else ys
        for tt in range(4):
            for half in range(2):
                pb = ps()
                s.group("pe", [lambda e, k4=k4, tt=tt, half=half, pb=pb: e.transpose(
                    pb[:, k4 * 128:(k4 + 1) * 128], xT[:, half * 4 + k4, tt * 128:(tt + 1) * 128], ident[:])
                    for k4 in range(4)], reads=[xT, ident], writes=[pb])
                if half:
                    s.op("act", lambda e, pb=pb: e.copy(out=ytm[:, 0, 512:1024], in_=pb[:]), reads=[pb], writes=[ytm])
                else:
                    s.op("dve", lambda e, pb=pb: e.tensor_copy(out=ytm[:, 0, 0:512], in_=pb[:]), reads=[pb], writes=[ytm])
            s.dma("sp", dst[tt * 128:(tt + 1) * 128, :], ytm[:, 0, :], reads=[ytm], writes=[ydr])

    def norm_mod(xT, hb, tmp2, rstd, l, i, c):
        for kt in range(8):
            s.op("act", lambda e, kt=kt: e.activation(out=hb[:, kt, :], in_=xT[:, kt, :], func=AF.Square),
                 reads=[xT], writes=[hb])
        pb = ps()
        mm_group(pb, pb[:], [(mean_bf[:], hb[:, kt, :]) for kt in range(8)], [mean_bf, hb])
        rsqrt(rstd, rstd[:], pb, pb[:], eps6)
        for kt in range(8):
            t = tmp2[kt % 2]
            s.op("dve", lambda e, kt=kt, t=t: e.tensor_tensor(out=t[:], in0=xT[:, kt, :], in1=rstd[:], op=ALU.mult),
                 reads=[xT, rstd], writes=[t])
            s.op("act", lambda e, kt=kt, t=t: e.activation(out=hb[:, kt, :], in_=t[:], func=AF.Identity,
                                                           scale=Aco[:, l, i, c, kt:kt + 1], bias=Bco[:, l, i, c, kt:kt + 1]),
                 reads=[t, Aco, Bco], writes=[hb])

    def alloc_ffn_w():
        W1 = [s.sbuf("W1_%d" % j, [128, 8, 512], BF16) for j in range(11)]
        W2 = [s.sbuf("W2_%d" % j, [128, 2, 1024], BF16) for j in range(11)]
        return (W1, W2)

    def issue_ffn_w(W, l, win_d, wout_d):
        W1, W2 = W
        for j in (0, 5, 1, 6, 2, 7, 3, 8, 4, 9, 10):
            s.dma("pool", W1[j][:], _ap(win_d)[l, :, j * 512:(j + 1) * 512].rearrange("(kt p) f -> p kt f", p=128),
                  writes=[W1[j]])
        for j in range(11):
            s.dma("pool", W2[j][:], _ap(wout_d)[l, j * 256:(j + 1) * 256, :].rearrange("(kt p) f -> p kt f", p=128),
                  writes=[W2[j]])

    def ffn_phase(l, i, win_d, wout_d, first, last, W=None):
        m0 = s.mark()
        if W is None:
            W = alloc_ffn_w()
            issue_ffn_w(W, l, win_d, wout_d)
        W1, W2 = W
        xTs = [s.sbuf("xT%d" % j, [128, 8, 512], F32) for j in range(2)]
        hb = s.sbuf("hb", [128, 8, 512], BF16)
        actb = s.sbuf("actb", [128, 22, 512], BF16)
        tmp2 = [s.sbuf("tmp%d" % j, [128, 512], F32) for j in range(2)]
        sg2 = tmp2
        rstd = s.sbuf("rstd", [128, 512], F32)
        if os.environ.get("MK_VERBOSE"):
            print("ffn phase free sbuf bytes/partition:", nc.sbuf_bytes_remaining, "first/last", first, last)
        xtm = s.sbuf("xtm", [128, 1, 1024], F32) if (first or last) else None

        def w1cols(col):
            return W1[col // 512], col % 512

        def fetch(blk):
            if first:
                load_x_tm(xTs[blk % 2], xtm, blk)
            else:
                load_xT(xTs[blk % 2], blk)

        fetch(0)
        for blk in range(NB):
            c = 0 if blk < 2 else 1
            xT = xTs[blk % 2]
            if blk + 1 < NB and not first:
                fetch(blk + 1)
            norm_mod(xT, hb, tmp2, rstd, l, i, c)
            for j in range(22):
                wg, og = w1cols(j * 128)
                wu, ou = w1cols(DFF + j * 128)
                pg = ps()
                mm_group(pg, pg[:], [(wg[:, kt, og:og + 128], hb[:, kt, :]) for kt in range(8)], [wg, hb])
                pu = ps()
                mm_group(pu, pu[:], [(wu[:, kt, ou:ou + 128], hb[:, kt, :]) for kt in range(8)], [wu, hb])
                sg = sg2[j % 2]
                s.op("act", lambda e, sg=sg, pg=pg: e.activation(out=sg[:], in_=pg[:], func=AF.Silu), reads=[pg], writes=[sg])
                s.op("dve", lambda e, sg=sg, pu=pu, j=j: e.tensor_tensor(out=actb[:, j, :], in0=sg[:], in1=pu[:], op=ALU.mult),
                     reads=[sg, pu], writes=[actb])
            if blk + 1 < NB and first:
                fetch(blk + 1)
            for ft in range(8):
                po = ps()
                mm_group(po, po[:], [(W2[j // 2][:, j % 2, ft * 128:(ft + 1) * 128], actb[:, j, :]) for j in range(22)],
                         W2 + [actb])
                s.op("dve", lambda e, po=po, ft=ft, xT=xT: e.scalar_tensor_tensor(
                    out=xT[:, ft, :], in0=po[:], scalar=Gco[:, l, i, c, ft:ft + 1], in1=xT[:, ft, :], op0=ALU.mult, op1=ALU.add),
                    reads=[po, Gco, xT], writes=[xT])
            if last:
                store_y_tm(xT, xtm, blk)
            else:
                store_xT(xT, blk)
        s.barrier()
        s.release(m0)

    def proj_phase(l):
        m0 = s.mark()
        WI = [s.sbuf("WI_%d" % j, [128, 8, 256], BF16) for j in range(9)]
        for j in range(9):
            s.dma("pool", WI[j][:], _ap(w_in)[l, :, j * 256:(j + 1) * 256].rearrange("(kt p) f -> p kt f", p=128),
                  writes=[WI[j]])
        xTs = [s.sbuf("xT%d" % j, [128, 8, 512], F32) for j in range(2)]
        hbs = [s.sbuf("hb%d" % j, [128, 8, 512], BF16) for j in range(2)]
        tmp2 = [s.sbuf("tmp%d" % j, [128, 512], F32) for j in range(2)]
        rstd = s.sbuf("rstd", [128, 512], F32)
        gq = s.sbuf("gq", [128, 2], F32)
        for h in range(2):
            s.dma("sp", gq[h * 64:(h + 1) * 64, 0:1], _ap(qn_g)[l, :].rearrange("(p o) -> p o", o=1), writes=[gq])
            s.dma("sp", gq[h * 64:(h + 1) * 64, 1:2], _ap(kn_g)[l, :].rearrange("(p o) -> p o", o=1), writes=[gq])
        gk_b = s.sbuf("gk_b", [128, 64], F32)
        s.dma("sp", gk_b[:], _ap(kn_g)[l:l + 1, :].to_broadcast([128, 64]), writes=[gk_b])
        st_b = [s.sbuf("st_b%d" % j, [128, 512], BF16) for j in range(4)]
        sqb = [s.sbuf("sqb%d" % j, [128, 512], BF16) for j in range(4)]
        r2 = [s.sbuf("r2_%d" % j, [128, 512], F32) for j in range(4)]
        gvb = [s.sbuf("gvb_%d" % j, [128, 256], F32) for j in range(2)]
        qk32 = [s.sbuf("qk32_%d" % j, [128, 512], F32) for j in range(4)]
        g1 = [s.sbuf("g1_%d" % j, [128, 512], F32) for j in range(3)]
        g2 = [s.sbuf("g2_%d" % j, [128, 512], F32) for j in range(3)]
        g3 = [s.sbuf("g3_%d" % j, [128, 512], F32) for j in range(3)]
        vtm32 = [s.sbuf("vtm32_%d" % j, [128, 512], F32) for j in range(2)]
        ktm32 = [s.sbuf("ktm32_%d" % j, [128, 512], F32) for j in range(2)]
        vtmb = [s.sbuf("vtmb_%d" % j, [128, 512], BF16) for j in range(2)]
        vA = [s.sbuf("vA_%d" % j, [128, 256], BF16) for j in range(2)]
        vB = [s.sbuf("vB_%d" % j, [128, 256], BF16) for j in range(2)]
        for j in range(2):
            s.op("dve", lambda e, j=j: e.memset(vA[j][:], 0.0), writes=[vA[j]])
            s.op("dve", lambda e, j=j: e.memset(vB[j][:], 0.0), writes=[vB[j]])
        stat = [s.sbuf("stat%d" % j, [128, 6], F32) for j in range(2)]
        mv = [s.sbuf("mv%d" % j, [128, 2], F32) for j in range(2)]
        rs1 = [s.sbuf("rs1_%d" % j, [128, 1], F32) for j in range(2)]
        ss8 = [s.sbuf("ss8_%d" % j, [128, 8], F32) for j in range(2)]
        cnt = [0]

        def wcols(col):
            return WI[col // 256], col % 256

        def fm_tile(col):
            w, o = wcols(col)
            pb = ps()
            mm_group(pb, pb[:], [(w[:, kt, o:o + 128], hb[:, kt, :]) for kt in range(8)], [w, hb])
            return pb

        def gelu_ops(src_b, src_ap, dst_b, dst_ap, n, P=128):
            k = cnt[0] % 3
            cnt[0] += 1
            a, b2, c2 = g1[k], g2[k], g3[k]
            s.op("act", lambda e: e.activation(out=a[0:P, 0:n], in_=src_ap, func=AF.Square), reads=[src_b], writes=[a])
            s.op("dve", lambda e: e.tensor_scalar(out=a[0:P, 0:n], in0=a[0:P, 0:n], scalar1=0.044715, scalar2=1.0,
                                                  op0=ALU.mult, op1=ALU.add), reads=[a], writes=[a])
            s.op("dve", lambda e: e.tensor_tensor(out=b2[0:P, 0:n], in0=a[0:P, 0:n], in1=src_ap, op=ALU.mult),
                 reads=[a, src_b], writes=[b2])
            s.op("act", lambda e: e.activation(out=c2[0:P, 0:n], in_=b2[0:P, 0:n], func=AF.Sigmoid, scale=1.5957691216057308),
                 reads=[b2], writes=[c2])
            s.op("dve", lambda e: e.tensor_tensor(out=dst_ap, in0=c2[0:P, 0:n], in1=src_ap, op=ALU.mult),
                 reads=[c2, src_b], writes=[dst_b])

        for blk in range(NB):
            c = 0 if blk < 2 else 1
            tok = slice(blk * 512, (blk + 1) * 512)
            xT = xTs[blk % 2]
            hb = hbs[blk % 2]
            if blk == 0:
                load_xT(xT, 0)
            if blk + 1 < NB:
                load_xT(xTs[(blk + 1) % 2], blk + 1)
            norm_mod(xT, hb, tmp2, rstd, l, 1, c)
            for a in range(2):
                pb = fm_tile(a * 128)
                sb = st_b[cnt[0] % 4]; cnt[0] += 1
                s.op("act", lambda e, sb=sb, pb=pb: e.copy(out=sb[:], in_=pb[:]), reads=[pb], writes=[sb])
                s.dma("sp", _ap(us_d)[a * 128:(a + 1) * 128, tok], sb[:], reads=[sb], writes=[us_d])
            for qk in range(2):
                for t4 in range(4):
                    pb = fm_tile(256 + qk * 512 + t4 * 128)
                    sq = sqb[t4]
                    s.op("act", lambda e, sq=sq, pb=pb: e.activation(out=sq[:], in_=pb[:], func=AF.Square),
                         reads=[pb], writes=[sq])
                    q32 = qk32[t4]
                    s.op("act", lambda e, q32=q32, pb=pb: e.copy(out=q32[:], in_=pb[:]), reads=[pb], writes=[q32])
                    pb = q32
                    pm = ps()
                    mm_group(pm, pm[:], [(bd_bf[:], sq[:])], [bd_bf, sq])
                    r = r2[t4]
                    rsqrt(r, r[:], pm, pm[:], eps6)
                    sb = st_b[cnt[0] % 4]; cnt[0] += 1
                    s.op("dve", lambda e, sb=sb, pb=pb, r=r, qk=qk: e.scalar_tensor_tensor(
                        out=sb[:], in0=pb[:], scalar=gq[:, qk:qk + 1], in1=r[:], op0=ALU.mult, op1=ALU.mult),
                        reads=[pb, gq, r], writes=[sb])
                    dd = q_d if qk == 0 else k_d
                    s.dma("sp", _ap(dd)[t4 * 128:(t4 + 1) * 128, tok], sb[:], reads=[sb], writes=[dd])
            for a in range(2):
                pb = fm_tile(1792 + a * 128)
                sb = st_b[cnt[0] % 4]; cnt[0] += 1
                gelu_ops(pb, pb[:], sb, sb[:], 512)
                s.dma("sp", _ap(ug_d)[a * 128:(a + 1) * 128, tok], sb[:], reads=[sb], writes=[ug_d])
            for tt in range(4):
                trow = slice(blk * 512 + tt * 128, blk * 512 + (tt + 1) * 128)
                hT = [hb[:, kt, tt * 128:(tt + 1) * 128] for kt in range(8)]
                k2 = tt % 2
                pv = ps()
                for half in range(2):
                    w = WI[5 + half]
                    mm_group(pv, pv[:, half * 256:(half + 1) * 256], [(hT[kt], w[:, kt, :]) for kt in range(8)], [w, hb])
                vb = vtmb[k2]
                s.op("act", lambda e, vb=vb, pv=pv: e.copy(out=vb[:], in_=pv[:]), reads=[pv], writes=[vb])
                s.dma("sp", _ap(v_d)[trow, :], vb[:], reads=[vb], writes=[v_d])
                if blk < 2:
                    seq = (blk * 512 + tt * 128) // 256
                    pos = (tt % 2) * 128
                    v32 = vtm32[k2]
                    s.op("dve", lambda e, v32=v32, pv=pv: e.tensor_copy(out=v32[:], in_=pv[:]), reads=[pv], writes=[v32])
                    s.dma("sp", _ap(nv)[seq, l, pos:pos + 128, :], v32[:], reads=[v32], writes=[nv])
                    pk = ps()
                    for half in range(2):
                        w = WI[3 + half]
                        mm_group(pk, pk[:, half * 256:(half + 1) * 256], [(hT[kt], w[:, kt, :]) for kt in range(8)], [w, hb])
                    a1 = g1[cnt[0] % 3]; cnt[0] += 1
                    s.op("act", lambda e, a1=a1, pk=pk: e.activation(out=a1[:], in_=pk[:], func=AF.Square), reads=[pk], writes=[a1])
                    s8 = ss8[k2]
                    s.op("dve", lambda e, a1=a1, s8=s8: e.tensor_reduce(out=s8[:], in_=a1[:].rearrange("p (h d) -> p h d", d=64),
                                                                       axis=AX.X, op=ALU.add), reads=[a1], writes=[s8])
                    rsqrt(s8, s8[:], s8, s8[:], eps6, scale=1.0 / 64.0)
                    k32 = ktm32[k2]
                    for h in range(8):
                        s.op("dve", lambda e, h=h, k32=k32, pk=pk, s8=s8: e.scalar_tensor_tensor(
                            out=k32[:, h * 64:(h + 1) * 64], in0=pk[:, h * 64:(h + 1) * 64], scalar=s8[:, h:h + 1],
                            in1=gk_b[:], op0=ALU.mult, op1=ALU.mult), reads=[pk, s8, gk_b], writes=[k32])
                    s.dma("sp", _ap(nk)[seq, l, pos:pos + 128, :], k32[:], reads=[k32], writes=[nk])
                pg = ps()
                mm_group(pg, pg[:, 0:256], [(hT[kt], WI[8][:, kt, :]) for kt in range(8)], [WI[8], hb])
                gv = gvb[k2]
                gelu_ops(pg, pg[:, 0:256], gv, gv[:, 0:256], 256)
                stt, mvv, rr = stat[k2], mv[k2], rs1[k2]
                s.op("dve", lambda e, stt=stt, gv=gv: e.bn_stats(out=stt[:], in_=gv[:, 0:256]), reads=[gv], writes=[stt])
                s.op("dve", lambda e, stt=stt, mvv=mvv: e.bn_aggr(out=mvv[:], in_=stt[:]), reads=[stt], writes=[mvv])
                rsqrt(rr, rr[:], mvv, mvv[:, 1:2], eps5)
                va, vbb = vA[k2], vB[k2]
                for (dst, off) in ((va, 0), (vbb, 64)):
                    for a in range(2):
                        cs_ = slice(a * 128 + off, a * 128 + off + 64)
                        s.op("dve", lambda e, dst=dst, cs_=cs_, gv=gv, mvv=mvv, rr=rr: e.tensor_scalar(
                            out=dst[:, cs_], in0=gv[:, cs_], scalar1=mvv[:, 0:1], scalar2=rr[:, 0:1],
                            op0=ALU.subtract, op1=ALU.mult), reads=[gv, mvv, rr], writes=[dst])
                s.dma("sp", _ap(va_d)[trow, :], va[:], reads=[va], writes=[va_d])
                s.dma("sp", _ap(vb_d)[trow, :], vbb[:], reads=[vbb], writes=[vb_d])
        s.barrier()
        s.release(m0)

    def outproj_phase(l, WO=None):
        m0 = s.mark()
        if WO is None:
            WO = s.sbuf("WO", [128, 8, 1024], BF16)
            s.dma("pool", WO[:], _ap(w_out)[l].rearrange("(kt p) f -> p kt f", p=128), writes=[WO])
        xTs = [s.sbuf("xT%d" % j, [128, 8, 512], F32) for j in range(2)]
        yms = [s.sbuf("ym%d" % j, [128, 8, 512], BF16) for j in range(2)]
        def fetch(blk):
            load_xT(xTs[blk % 2], blk)
            s.dma("sp", yms[blk % 2][:], _ap(ymix_d)[:, blk * 512:(blk + 1) * 512].rearrange("(kt p) n -> p kt n", p=128),
                  reads=[ymix_d], writes=[yms[blk % 2]])

        fetch(0)
        for blk in range(NB):
            c = 0 if blk < 2 else 1
            xT, ym = xTs[blk % 2], yms[blk % 2]
            if blk + 1 < NB:
                fetch(blk + 1)
            for ft in range(8):
                po = ps()
                mm_group(po, po[:], [(WO[:, kt, ft * 128:(ft + 1) * 128], ym[:, kt, :]) for kt in range(8)], [WO, ym])
                s.op("dve", lambda e, po=po, ft=ft, xT=xT: e.scalar_tensor_tensor(
                    out=xT[:, ft, :], in0=po[:], scalar=Gco[:, l, 1, c, ft:ft + 1], in1=xT[:, ft, :], op0=ALU.mult, op1=ALU.add),
                    reads=[po, Gco, xT], writes=[xT])
            store_xT(xT, blk)
        s.barrier()
        s.release(m0)

    def ssm_part(l):
        m0 = s.mark()
        uS = s.sbuf("uS", [128, 2, NT], BF16)
        for a in range(2):
            s.dma("sp", uS[:, a, :], _ap(us_d)[a * 128:(a + 1) * 128, :], reads=[us_d], writes=[uS])
        Y = s.sbuf("Y", [128, 2, NT], F32)
        dco = s.sbuf("dco", [128, 2], F32)
        s.dma("sp", dco[:], _ap(ssm_d)[l, :].rearrange("(a p) -> p a", p=128), writes=[dco])
        iota = s.sbuf("iota", [128, 1024], F32)
        s.dma("sp", iota[:], _ap(c_iota), writes=[iota])
        P8 = lambda n: s.sbuf(n, [128, 8], F32)
        lr, li, ldt, dtv, ar, ai, rho, ang, sinv, cosv, lbr, lbi = [P8("p8_%d" % i) for i in range(12)]
        nr, den, kr, ki, t8a, t8b, thr, nf8, cN, sN = [P8("q8_%d" % i) for i in range(10)]
        ii8 = s.sbuf("ii8", [128, 8], mybir.dt.int32)
        BT = s.sbuf("BT", [128, 8, 2, 128], BF16)
        CT = s.sbuf("CT", [128, 8, 2, 128], BF16)
        s0 = s.sbuf("s0", [128, 8, 2], F32)
        init = s.sbuf("init", [128, 8, 2], F32)
        FIN = [s.sbuf("FIN%d" % i, [128, 8, 2], F32) for i in range(4)]
        zl = s.sbuf("zl", [128, 2], F32)
        tq = s.sbuf("tq", [128, 2], F32)

        def dv(fn, reads, writes, e="dve"):
            s.op(e, fn, reads=reads, writes=writes)

        C1 = 6.28125
        C2 = TWO_PI - C1
        PI_S = 3.1415925
        hpi = s.sbuf("hpi", [128, 1], F32)
        dv(lambda e: e.memset(hpi[:], 0.5 * math.pi), [], [hpi])

        def sincos(ang_b, ang_ap, r_b, r_ap, sin_b, sin_ap, cos_b, cos_ap, ii_b, nf_b, n):
            dv(lambda e: e.tensor_scalar(out=ii_b[:, 0:n], in0=ang_ap, scalar1=1.0 / TWO_PI, scalar2=None, op0=ALU.mult), [ang_b], [ii_b])
            dv(lambda e: e.tensor_copy(out=nf_b[:, 0:n], in_=ii_b[:, 0:n]), [ii_b], [nf_b])
            dv(lambda e: e.scalar_tensor_tensor(out=r_ap, in0=nf_b[:, 0:n], scalar=-C1, in1=ang_ap, op0=ALU.mult, op1=ALU.add),
               [nf_b, ang_b], [r_b])
            dv(lambda e: e.scalar_tensor_tensor(out=r_ap, in0=nf_b[:, 0:n], scalar=-C2, in1=r_ap, op0=ALU.mult, op1=ALU.add),
               [nf_b, r_b], [r_b])
            dv(lambda e: e.tensor_scalar(out=r_ap, in0=r_ap, scalar1=-PI_S, scalar2=None, op0=ALU.max), [r_b], [r_b])
            dv(lambda e: e.tensor_scalar(out=r_ap, in0=r_ap, scalar1=PI_S, scalar2=None, op0=ALU.min), [r_b], [r_b])
            s.op("act", lambda e: e.activation(out=sin_ap, in_=r_ap, func=AF.Sin), reads=[r_b], writes=[sin_b])
            s.op("act", lambda e: e.activation(out=nf_b[:, 0:n], in_=r_ap, func=AF.Abs), reads=[r_b], writes=[nf_b])
            s.op("act", lambda e: e.activation(out=cos_ap, in_=nf_b[:, 0:n], func=AF.Sin, scale=-1.0, bias=hpi[:, 0:1]),
                 reads=[nf_b, hpi], writes=[cos_b])

        for d in range(2):
            s.dma("sp", lr[:], _ap(lam_re)[l, d, :].rearrange("(j q) -> q j", q=128), writes=[lr])
            s.dma("sp", li[:], _ap(lam_im)[l, d, :].rearrange("(j q) -> q j", q=128), writes=[li])
            for h in range(2):
                s.dma("sp", ldt[h * 64:(h + 1) * 64, :],
                      _ap(log_dt)[l, d, :].rearrange("(j h) -> h j", h=2)[h:h + 1, :].to_broadcast([64, 8]), writes=[ldt])
            s.op("act", lambda e: e.activation(out=dtv[:], in_=ldt[:], func=AF.Exp), reads=[ldt], writes=[dtv])
            dv(lambda e: e.tensor_tensor(out=ar[:], in0=lr[:], in1=dtv[:], op=ALU.mult), [lr, dtv], [ar])
            dv(lambda e: e.tensor_tensor(out=ai[:], in0=li[:], in1=dtv[:], op=ALU.mult), [li, dtv], [ai])
            s.op("act", lambda e: e.activation(out=rho[:], in_=ar[:], func=AF.Exp), reads=[ar], writes=[rho])
            sincos(ai, ai[:], thr, thr[:], sinv, sinv[:], cosv, cosv[:], ii8, nf8, 8)
            dv(lambda e: e.tensor_tensor(out=lbr[:], in0=rho[:], in1=cosv[:], op=ALU.mult), [rho, cosv], [lbr])
            dv(lambda e: e.tensor_tensor(out=lbi[:], in0=rho[:], in1=sinv[:], op=ALU.mult), [rho, sinv], [lbi])
            dv(lambda e: e.tensor_scalar(out=nr[:], in0=lbr[:], scalar1=-1.0, scalar2=None, op0=ALU.add), [lbr], [nr])
            dv(lambda e: e.tensor_tensor(out=den[:], in0=lr[:], in1=lr[:], op=ALU.mult), [lr], [den])
            dv(lambda e: e.tensor_tensor(out=t8a[:], in0=li[:], in1=li[:], op=ALU.mult), [li], [t8a])
            dv(lambda e: e.tensor_tensor(out=den[:], in0=den[:], in1=t8a[:], op=ALU.add), [den, t8a], [den])
            dv(lambda e: e.reciprocal(out=den[:], in_=den[:]), [den], [den])
            dv(lambda e: e.tensor_tensor(out=t8a[:], in0=nr[:], in1=lr[:], op=ALU.mult), [nr, lr], [t8a])
            dv(lambda e: e.tensor_tensor(out=t8b[:], in0=lbi[:], in1=li[:], op=ALU.mult), [lbi, li], [t8b])
            dv(lambda e: e.tensor_tensor(out=t8a[:], in0=t8a[:], in1=t8b[:], op=ALU.add), [t8a, t8b], [t8a])
            dv(lambda e: e.tensor_tensor(out=kr[:], in0=t8a[:], in1=den[:], op=ALU.mult), [t8a, den], [kr])
            dv(lambda e: e.tensor_tensor(out=t8a[:], in0=lbi[:], in1=lr[:], op=ALU.mult), [lbi, lr], [t8a])
            dv(lambda e: e.tensor_tensor(out=t8b[:], in0=nr[:], in1=li[:], op=ALU.mult), [nr, li], [t8b])
            dv(lambda e: e.tensor_tensor(out=t8a[:], in0=t8a[:], in1=t8b[:], op=ALU.subtract), [t8a, t8b], [t8a])
            dv(lambda e: e.tensor_tensor(out=ki[:], in0=t8a[:], in1=den[:], op=ALU.mult), [t8a, den], [ki])
            m1 = s.mark()
            Bn = [s.sbuf("Bn%d" % i, [128, 8, 16], F32) for i in range(2)]
            Zp = [s.sbuf("Zp%d" % i, [128, 8, 128], F32) for i in range(2)]
            tz = s.sbuf("tz", [128, 16], F32)
            INc = [s.sbuf("INc%d" % i, [128, 8, 128], F32) for i in range(2)]
            s.dma("sp", Bn[0][:], _ap(b_re)[l, d].rearrange("(j q) c -> q j c", q=128), writes=[Bn[0]])
            s.dma("sp", Bn[1][:], _ap(b_im)[l, d].rearrange("(j q) c -> q j c", q=128), writes=[Bn[1]])
            for ri in range(2):
                dv(lambda e, ri=ri: e.memset(Zp[ri][:], 0.0), [], [Zp[ri]])
            for j in range(8):
                for h in range(2):
                    pr = slice(h * 64, (h + 1) * 64)
                    cs_ = slice(32 * (j % 4) + 16 * h, 32 * (j % 4) + 16 * h + 16)
                    dv(lambda e, j=j, pr=pr: e.tensor_scalar(out=tz[pr, :], in0=Bn[1][pr, j, :], scalar1=ki[pr, j:j + 1],
                                                             scalar2=None, op0=ALU.mult), [Bn[1], ki], [tz])
                    dv(lambda e, j=j, pr=pr, cs_=cs_: e.scalar_tensor_tensor(
                        out=Zp[0][pr, j, cs_], in0=Bn[0][pr, j, :], scalar=kr[pr, j:j + 1], in1=tz[pr, :],
                        op0=ALU.mult, op1=ALU.subtract), [Bn[0], kr, tz], [Zp[0]])
                    dv(lambda e, j=j, pr=pr: e.tensor_scalar(out=tz[pr, :], in0=Bn[0][pr, j, :], scalar1=ki[pr, j:j + 1],
                                                             scalar2=None, op0=ALU.mult), [Bn[0], ki], [tz])
                    dv(lambda e, j=j, pr=pr, cs_=cs_: e.scalar_tensor_tensor(
                        out=Zp[1][pr, j, cs_], in0=Bn[1][pr, j, :], scalar=kr[pr, j:j + 1], in1=tz[pr, :],
                        op0=ALU.mult, op1=ALU.add), [Bn[1], kr, tz], [Zp[1]])
            for ri, cd in enumerate((c_re, c_im)):
                dv(lambda e, ri=ri: e.memset(INc[ri][:], 0.0), [], [INc[ri]])
                for j in range(8):
                    for h in range(2):
                        g = 2 * j + h
                        r0 = 32 * (j % 4) + 16 * h
                        s.dma("sp", INc[ri][r0:r0 + 16, j, h * 64:(h + 1) * 64], _ap(cd)[l, d, g * 16:(g + 1) * 16, :],
                              reads=[INc[ri]], writes=[INc[ri]])
            for j in range(8):
                for ri in range(2):
                    pb = ps()
                    s.group("pe", [lambda e, pb=pb, j=j, ri=ri: e.transpose(pb[:, 0:128], Zp[ri][:, j, :], ident[:])],
                            reads=[Zp[ri], ident], writes=[pb])
                    s.op("act", lambda e, pb=pb, j=j, ri=ri: e.copy(out=BT[:, j, ri, :], in_=pb[:, 0:128]), reads=[pb], writes=[BT])
                    pc = ps()
                    s.group("pe", [lambda e, pc=pc, j=j, ri=ri: e.transpose(pc[:, 0:128], INc[ri][:, j, :], ident[:])],
                            reads=[INc[ri], ident], writes=[pc])
                    s.op("act", lambda e, pc=pc, j=j, ri=ri: e.activation(out=CT[:, j, ri, :], in_=pc[:, 0:128], func=AF.Copy,
                                                                        scale=(1.0 if ri == 0 else -1.0)),
                         reads=[pc], writes=[CT])
            s.dma("sp", s0[:], _ap(sst)[l, d].rearrange("(j q) r -> q j r", q=128), writes=[s0])
            dv(lambda e: e.tensor_tensor(out=t8a[:], in0=cosv[:], in1=s0[:, :, 0], op=ALU.mult), [cosv, s0], [t8a])
            dv(lambda e: e.tensor_tensor(out=t8b[:], in0=sinv[:], in1=s0[:, :, 1], op=ALU.mult), [sinv, s0], [t8b])
            dv(lambda e: e.tensor_tensor(out=init[:, :, 0], in0=t8a[:], in1=t8b[:], op=ALU.subtract), [t8a, t8b], [init])
            dv(lambda e: e.tensor_tensor(out=t8a[:], in0=sinv[:], in1=s0[:, :, 0], op=ALU.mult), [sinv, s0], [t8a])
            dv(lambda e: e.tensor_tensor(out=t8b[:], in0=cosv[:], in1=s0[:, :, 1], op=ALU.mult), [cosv, s0], [t8b])
            dv(lambda e: e.tensor_tensor(out=init[:, :, 1], in0=t8a[:], in1=t8b[:], op=ALU.add), [t8a, t8b], [init])
            s.barrier()
            s.release(m1)
            m2 = s.mark()
            rhoT = s.sbuf("rhoT", [128, 4, 1024], F32)
            tabc = s.sbuf("tabc", [128, 1024], F32)
            tabs = s.sbuf("tabs", [128, 1024], F32)
            ccol = s.sbuf("ccol", [128, 4, 2], F32)
            scol = s.sbuf("scol", [128, 4, 2], F32)
            angt = s.sbuf("angt", [128, 1024], F32)
            ang2 = s.sbuf("ang2", [128, 1024], F32)
            iiT = s.sbuf("iiT", [128, 1024], mybir.dt.int32)
            nfT = s.sbuf("nfT", [128, 1024], F32)
            tcb = s.sbuf("tcb", [128, 4, 1024], BF16)
            tsb = s.sbuf("tsb", [128, 4, 1024], BF16)
            tD_ = [[s.sbuf("tD%d_%d" % (k, i), [128, 1024], BF16) for i in range(2)] for k in range(2)]
            tP_ = [[s.sbuf("tP%d_%d" % (k, i), [128, 1024], BF16) for i in range(2)] for k in range(2)]
            wb_ = [[s.sbuf("wb%d_%d" % (k, i), [128, 1024], BF16) for i in range(2)] for k in range(2)]
            wri_ = [[s.sbuf("wri%d_%d" % (k, i), [128, 1024], F32) for i in range(2)] for k in range(2)]
            zb_ = [[s.sbuf("zb%d_%d" % (k, i), [128, 1024], BF16) for i in range(2)] for k in range(2)]
            bsb_ = [[s.sbuf("bsb%d_%d" % (k, i), [128, 1024], BF16) for i in range(2)] for k in range(2)]
            ucnt = [0]
            fq = s.sbuf("fq", [128, 4], F32)
            Sb = [[s.sbuf("Sb%d_%d" % (j4, ri), [128, 1024], BF16) for ri in range(2)] for j4 in range(4)]
            if os.environ.get("MK_VERBOSE"):
                print("ssm free sbuf bytes/partition:", nc.sbuf_bytes_remaining)
            units = [("p", sq, sq * 256, 256) for sq in range(4)]
            for kk in range(4):
                knat = kk if d == 0 else 3 - kk
                units.append(("s", kk, NP_TOK + knat * 1024, 1024))
            for a in range(2):
                for j4 in range(4):
                    j = a * 4 + j4
                    dv(lambda e, j=j: e.tensor_scalar(out=angt[:], in0=iota[:], scalar1=thr[:, j:j + 1], scalar2=None, op0=ALU.mult),
                       [iota, thr], [angt])
                    sincos(angt, angt[:], ang2, ang2[:], tabs, tabs[:], tabc, tabc[:], iiT, nfT, 1024)
                    s.op("act", lambda e, j=j, j4=j4: e.activation(out=rhoT[:, j4, :], in_=iota[:], func=AF.Identity, scale=0.0,
                                                                 bias=rho[:, j:j + 1]), reads=[iota, rho], writes=[rhoT])
                    s.op("act", lambda e, j4=j4: e.copy(out=tcb[:, j4, :], in_=tabc[:]), reads=[tabc], writes=[tcb])
                    s.op("act", lambda e, j4=j4: e.copy(out=tsb[:, j4, :], in_=tabs[:]), reads=[tabs], writes=[tsb])
                    for ci, col in enumerate((255, 1023)):
                        dv(lambda e, j4=j4, ci=ci, col=col: e.tensor_copy(out=ccol[:, j4, ci:ci + 1], in_=tabc[:, col:col + 1]), [tabc], [ccol])
                        dv(lambda e, j4=j4, ci=ci, col=col: e.tensor_copy(out=scol[:, j4, ci:ci + 1], in_=tabs[:, col:col + 1]), [tabs], [scol])
                    dv(lambda e, j=j, j4=j4: e.tensor_scalar(out=tq[:, 0:1], in0=scol[:, j4, 1:2], scalar1=sinv[:, j:j + 1], scalar2=None,
                                                             op0=ALU.mult), [scol, sinv], [tq])
                    dv(lambda e, j=j, j4=j4: e.scalar_tensor_tensor(out=cN[:, j:j + 1], in0=ccol[:, j4, 1:2], scalar=cosv[:, j:j + 1],
                                                                    in1=tq[:, 0:1], op0=ALU.mult, op1=ALU.subtract), [ccol, cosv, tq], [cN])
                    dv(lambda e, j=j, j4=j4: e.tensor_scalar(out=tq[:, 1:2], in0=ccol[:, j4, 1:2], scalar1=sinv[:, j:j + 1], scalar2=None,
                                                             op0=ALU.mult), [ccol, sinv], [tq])
                    dv(lambda e, j=j, j4=j4: e.scalar_tensor_tensor(out=sN[:, j:j + 1], in0=scol[:, j4, 1:2], scalar=cosv[:, j:j + 1],
                                                                    in1=tq[:, 1:2], op0=ALU.mult, op1=ALU.add), [scol, cosv, tq], [sN])
                for (kind, idx, tok0, n) in units:
                    for j4 in range(4):
                        j = a * 4 + j4
                        c_f, sn_f = tcb[:, j4, 0:n], tsb[:, j4, 0:n]
                        kb = ucnt[0] % 2
                        ucnt[0] += 1
                        tD, tP, wb, wri, zb, bsb = tD_[kb], tP_[kb], wb_[kb], wri_[kb], zb_[kb], bsb_[kb]
                        for p0 in range(0, n, 512):
                            pn = min(512, n - p0)
                            sl = slice(p0, p0 + pn) if d == 0 else slice(n - p0 - pn, n - p0)
                            for ri in range(2):
                                pb = ps()
                                mm_group(pb, pb[:, 0:pn], [(BT[:, j, ri, :], uS[:, a, tok0 + p0:tok0 + p0 + pn])], [BT, uS])
                                src_ = pb[:, 0:pn] if d == 0 else pb[:, 0:pn][:, ::-1]
                                s.op("act", lambda e, ri=ri, sl=sl, src_=src_: e.copy(out=bsb[ri][:, sl], in_=src_), reads=[pb], writes=[bsb[ri]])
                        br_, bi_ = bsb[0][:, 0:n], bsb[1][:, 0:n]
                        dv(lambda e: e.tensor_tensor(out=tD[0][:, 0:n], in0=c_f, in1=br_, op=ALU.mult), [tcb, bsb[0]], [tD[0]])
                        dv(lambda e: e.tensor_tensor(out=tD[1][:, 0:n], in0=sn_f, in1=bi_, op=ALU.mult), [tsb, bsb[1]], [tD[1]])
                        dv(lambda e: e.tensor_tensor(out=wb[0][:, 0:n], in0=tD[0][:, 0:n], in1=tD[1][:, 0:n], op=ALU.add),
                           [tD[0], tD[1]], [wb[0]])
                        dv(lambda e: e.tensor_tensor(out=tP[0][:, 0:n], in0=c_f, in1=bi_, op=ALU.mult), [tcb, bsb[1]], [tP[0]])
                        dv(lambda e: e.tensor_tensor(out=tP[1][:, 0:n], in0=sn_f, in1=br_, op=ALU.mult), [tsb, bsb[0]], [tP[1]])
                        dv(lambda e: e.tensor_tensor(out=wb[1][:, 0:n], in0=tP[0][:, 0:n], in1=tP[1][:, 0:n], op=ALU.subtract),
                           [tP[0], tP[1]], [wb[1]])
                        for ri in range(2):
                            if kind == "p":
                                ini = 0.0
                                rds = [rhoT, wb[ri]]
                            else:
                                ini = init[:, j, ri:ri + 1]
                                rds = [rhoT, wb[ri], init]
                            s.op("dve", lambda e, ri=ri, ini=ini, j4=j4: e.tensor_tensor_scan(
                                out=wri[ri][:, 0:n], data0=rhoT[:, j4, 0:n], data1=wb[ri][:, 0:n], initial=ini,
                                op0=ALU.mult, op1=ALU.add), reads=rds, writes=[wri[ri]])
                            s.op("act", lambda e, ri=ri: e.copy(out=zb[ri][:, 0:n], in_=wri[ri][:, 0:n]), reads=[wri[ri]], writes=[zb[ri]])
                        if kind == "s" and idx < 3:
                            dv(lambda e, j=j: e.tensor_scalar(out=tq[:, 0:1], in0=wri[1][:, n - 1:n], scalar1=sN[:, j:j + 1], scalar2=None,
                                                              op0=ALU.mult), [wri[1], sN], [tq])
                            dv(lambda e, j=j: e.tensor_scalar(out=tq[:, 1:2], in0=wri[0][:, n - 1:n], scalar1=sN[:, j:j + 1], scalar2=None,
                                                              op0=ALU.mult), [wri[0], sN], [tq])
                            dv(lambda e, j=j: e.scalar_tensor_tensor(out=init[:, j, 0:1], in0=wri[0][:, n - 1:n], scalar=cN[:, j:j + 1],
                                                                     in1=tq[:, 0:1], op0=ALU.mult, op1=ALU.subtract), [wri[0], cN, tq], [init])
                            dv(lambda e, j=j: e.scalar_tensor_tensor(out=init[:, j, 1:2], in0=wri[1][:, n - 1:n], scalar=cN[:, j:j + 1],
                                                                     in1=tq[:, 1:2], op0=ALU.mult, op1=ALU.add), [wri[1], cN, tq], [init])
                        if kind == "p":
                            fb = FIN[idx]
                            cl, sl_ = ccol[:, j4, 0:1], scol[:, j4, 0:1]
                            zrl, zil = wri[0][:, n - 1:n], wri[1][:, n - 1:n]
                            dv(lambda e: e.tensor_tensor(out=fq[:, 0:1], in0=sl_, in1=zil, op=ALU.mult), [scol, wri[1]], [fq])
                            dv(lambda e: e.tensor_tensor(out=fq[:, 1:2], in0=cl, in1=zrl, op=ALU.mult), [ccol, wri[0]], [fq])
                            dv(lambda e, fb=fb, j=j: e.tensor_tensor(out=fb[:, j, 0:1], in0=fq[:, 1:2], in1=fq[:, 0:1], op=ALU.subtract), [fq], [fb])
                            dv(lambda e: e.tensor_tensor(out=fq[:, 2:3], in0=sl_, in1=zrl, op=ALU.mult), [scol, wri[0]], [fq])
                            dv(lambda e: e.tensor_tensor(out=fq[:, 3:4], in0=cl, in1=zil, op=ALU.mult), [ccol, wri[1]], [fq])
                            dv(lambda e, fb=fb, j=j: e.tensor_tensor(out=fb[:, j, 1:2], in0=fq[:, 2:3], in1=fq[:, 3:4], op=ALU.add), [fq], [fb])
                        zr_, zi_ = zb[0][:, 0:n], zb[1][:, 0:n]
                        sr_o = Sb[j4][0][:, 0:n] if d == 0 else Sb[j4][0][:, 0:n][:, ::-1]
                        si_o = Sb[j4][1][:, 0:n] if d == 0 else Sb[j4][1][:, 0:n][:, ::-1]
                        dv(lambda e: e.tensor_tensor(out=tD[0][:, 0:n], in0=c_f, in1=zr_, op=ALU.mult), [tcb, zb[0]], [tD[0]])
                        dv(lambda e: e.tensor_tensor(out=tD[1][:, 0:n], in0=sn_f, in1=zi_, op=ALU.mult), [tsb, zb[1]], [tD[1]])
                        dv(lambda e, sr_o=sr_o: e.tensor_tensor(out=sr_o, in0=tD[0][:, 0:n], in1=tD[1][:, 0:n], op=ALU.subtract),
                           [tD[0], tD[1]], [Sb[j4][0]])
                        dv(lambda e: e.tensor_tensor(out=tP[0][:, 0:n], in0=sn_f, in1=zr_, op=ALU.mult), [tsb, zb[0]], [tP[0]])
                        dv(lambda e: e.tensor_tensor(out=tP[1][:, 0:n], in0=c_f, in1=zi_, op=ALU.mult), [tcb, zb[1]], [tP[1]])
                        dv(lambda e, si_o=si_o: e.tensor_tensor(out=si_o, in0=tP[0][:, 0:n], in1=tP[1][:, 0:n], op=ALU.add),
                           [tP[0], tP[1]], [Sb[j4][1]])
                    for p0 in range(0, n, 512):
                        pn = min(512, n - p0)
                        pb = ps()
                        pairs = []
                        rd = [CT]
                        for j4 in range(4):
                            for ri in range(2):
                                pairs.append((CT[:, a * 4 + j4, ri, :], Sb[j4][ri][:, p0:p0 + pn]))
                                rd.append(Sb[j4][ri])
                        mm_group(pb, pb[:, 0:pn], pairs, rd)
                        ysl = Y[:, a, tok0 + p0:tok0 + p0 + pn]
                        if d == 0:
                            dv(lambda e, pb=pb, ysl=ysl, pn=pn, a=a, p0=p0, tok0=tok0: e.scalar_tensor_tensor(
                                out=ysl, in0=uS[:, a, tok0 + p0:tok0 + p0 + pn], scalar=dco[:, a:a + 1], in1=pb[:, 0:pn],
                                op0=ALU.mult, op1=ALU.add), [uS, dco, pb], [Y])
                        else:
                            dv(lambda e, pb=pb, ysl=ysl, pn=pn: e.tensor_tensor(out=ysl, in0=ysl, in1=pb[:, 0:pn], op=ALU.add),
                               [Y, pb], [Y])
            for sq in range(4):
                s.dma("sp", _ap(nst)[sq, l, d].rearrange("(j q) r -> q j r", q=128), FIN[sq][:], reads=[FIN[sq]], writes=[nst])
            s.barrier()
            s.release(m2)
        m3 = s.mark()
        tD = [s.sbuf("tDg%d" % i, [128, 512], F32) for i in range(2)]
        tP = [s.sbuf("tPg%d" % i, [128, 512], F32) for i in range(2)]
        Wg = s.sbuf("Wg", [128, 2, 256], BF16)
        s.dma("pool", Wg[:], _ap(glu_w)[l].rearrange("(a p) f -> p a f", p=128), writes=[Wg])
        gb = s.sbuf("gb", [128, 2], F32)
        s.dma("sp", gb[:], _ap(glu_b)[l, :].rearrange("(a p) -> p a", p=128), writes=[gb])
        gel = [s.sbuf("gel%d" % a, [128, 512], F32) for a in range(2)]
        gelb = [s.sbuf("gelb%d" % a, [128, 512], BF16) for a in range(2)]
        sig = s.sbuf("sig", [128, 512], F32)
        yo = [s.sbuf("yo%d" % a, [128, 512], BF16) for a in range(2)]
        for blk in range(NB):
            tok = slice(blk * 512, (blk + 1) * 512)
            for a in range(2):
                ysl = Y[:, a, tok]
                s.op("act", lambda e, ysl=ysl: e.activation(out=tD[0][:, 0:512], in_=ysl, func=AF.Square), reads=[Y], writes=[tD[0]])
                dv(lambda e: e.tensor_scalar(out=tD[0][:, 0:512], in0=tD[0][:, 0:512], scalar1=0.044715, scalar2=1.0,
                                             op0=ALU.mult, op1=ALU.add), [tD[0]], [tD[0]])
                dv(lambda e, ysl=ysl: e.tensor_tensor(out=tD[1][:, 0:512], in0=tD[0][:, 0:512], in1=ysl, op=ALU.mult), [tD[0], Y], [tD[1]])
                s.op("act", lambda e: e.activation(out=tP[0][:, 0:512], in_=tD[1][:, 0:512], func=AF.Sigmoid, scale=1.5957691216057308),
                     reads=[tD[1]], writes=[tP[0]])
                dv(lambda e, a=a, ysl=ysl: e.tensor_tensor(out=gel[a][:], in0=tP[0][:, 0:512], in1=ysl, op=ALU.mult), [tP[0], Y], [gel[a]])
                s.op("act", lambda e, a=a: e.copy(out=gelb[a][:], in_=gel[a][:]), reads=[gel[a]], writes=[gelb[a]])
            for a in range(2):
                pb = ps()
                mm_group(pb, pb[:], [(Wg[:, a2, a * 128:(a + 1) * 128], gelb[a2][:]) for a2 in range(2)], [Wg, gelb[0], gelb[1]])
                s.op("act", lambda e, pb=pb, a=a: e.activation(out=sig[:], in_=pb[:], func=AF.Sigmoid, bias=gb[:, a:a + 1]),
                     reads=[pb, gb], writes=[sig])
                dv(lambda e, a=a: e.tensor_tensor(out=yo[a][:], in0=sig[:], in1=gel[a][:], op=ALU.mult), [sig, gel[a]], [yo[a]])
                s.dma("sp", _ap(ymix_d)[a * 128:(a + 1) * 128, tok], yo[a][:], reads=[yo[a]], writes=[ymix_d])
        s.barrier()
        s.release(m0)

    def attn_part(l):
        m0 = s.mark()
        qP = s.sbuf("qP", [128, 4, NP_TOK], BF16)
        kP = s.sbuf("kP", [128, 4, NP_TOK], BF16)
        for t4 in range(4):
            s.dma("sp", qP[:, t4, :], _ap(q_d)[t4 * 128:(t4 + 1) * 128, 0:NP_TOK], reads=[q_d], writes=[qP])
            s.dma("sp", kP[:, t4, :], _ap(k_d)[t4 * 128:(t4 + 1) * 128, 0:NP_TOK], reads=[k_d], writes=[kP])
        vP = s.sbuf("vP", [128, 8, 512], BF16)
        s.dma("sp", vP[:], _ap(v_d)[0:NP_TOK, :].rearrange("(t p) f -> p t f", p=128), reads=[v_d], writes=[vP])
        Eb = [s.sbuf("Eb%d" % i, [128, 2, 256], BF16) for i in range(2)]
        rdn = [s.sbuf("rdn%d" % i, [128, 256], F32) for i in range(2)]
        yat = [s.sbuf("yat%d" % i, [128, 256], BF16) for i in range(2)]
        n_it = 0
        for sq in range(4):
            for hp in range(4):
                ya = yat[(sq * 4 + hp) % 2]
                for hh in range(2):
                    pr = slice(hh * 64, (hh + 1) * 64)
                    E = Eb[n_it % 2]
                    rd_ = rdn[n_it % 2]
                    n_it += 1
                    for kt2 in range(2):
                        pb = ps()
                        mm_group(pb, pb[:, 0:256],
                                 [(kP[pr, hp, sq * 256 + kt2 * 128: sq * 256 + (kt2 + 1) * 128], qP[pr, hp, sq * 256:(sq + 1) * 256])],
                                 [kP, qP])
                        s.op("act", lambda e, E=E, pb=pb, kt2=kt2: e.activation(out=E[:, kt2, :], in_=pb[:, 0:256], func=AF.Exp, scale=0.125),
                             reads=[pb], writes=[E])
                    pn_ = ps()
                    mm_group(pn_, pn_[:, 0:256], [(vP[:, sq * 2 + kt2, hp * 128:(hp + 1) * 128], E[:, kt2, :]) for kt2 in range(2)], [vP, E])
                    pd_ = ps()
                    mm_group(pd_, pd_[:, 0:256], [(ones_bf[:], E[:, kt2, :]) for kt2 in range(2)], [ones_bf, E])
                    s.op("dve", lambda e, rd_=rd_, pd_=pd_, pr=pr: e.reciprocal(out=rd_[pr, :], in_=pd_[pr, 0:256]), reads=[pd_], writes=[rd_])
                    s.op("dve", lambda e, ya=ya, pn_=pn_, rd_=rd_, pr=pr: e.tensor_tensor(out=ya[pr, :], in0=pn_[pr, 0:256], in1=rd_[pr, :],
                                                                                        op=ALU.mult), reads=[pn_, rd_], writes=[ya])
                s.dma("sp", _ap(ymix_d)[256 + hp * 128:256 + (hp + 1) * 128, sq * 256:(sq + 1) * 256], ya[:], reads=[ya], writes=[ymix_d])
        s.barrier()
        s.release(m0)
        m0 = s.mark()
        BTt = s.sbuf("BTt", [128, 8, 15, 64], BF16)
        identb = s.sbuf("identb", [128, 128], BF16)
        s.op("dve", lambda e: e.tensor_copy(out=identb[:], in_=ident[:]), reads=[ident], writes=[identb])
        m1 = s.mark()
        oh = s.sbuf("oh", [31, 64, 128], F32)
        s.dma("sp", oh[:], _ap(c_oh).rearrange("d (k q) -> d k q", q=128), writes=[oh])
        msk = s.sbuf("msk", [128, 64], F32)
        s.dma("sp", msk[:], _ap(c_mask), writes=[msk])
        rpT = s.sbuf("rpT", [31, 128], F32)
        s.op("dve", lambda e: e.memset(rpT[:], 0.0), writes=[rpT])
        s.dma("sp", rpT[:, 0:120], _ap(rpb)[l].rearrange("x d -> d x"), reads=[rpT], writes=[rpT])
        BTf = BTt[:].rearrange("p h d k -> p (h d) k")
        for k4 in range(16):
            pb = ps()
            for ki in range(4):
                kk = k4 * 4 + ki
                mm_group(pb, pb[:, ki * 128:ki * 128 + 128], [(oh[:, kk, :], rpT[:])], [oh, rpT])
            for ki in range(4):
                kk = k4 * 4 + ki
                s.op("dve", lambda e, pb=pb, ki=ki, kk=kk: e.tensor_scalar(out=BTf[:, :, kk], in0=pb[:, ki * 128:ki * 128 + 120],
                                                                           scalar1=msk[:, kk:kk + 1], scalar2=8.0, op0=ALU.add, op1=ALU.mult),
                     reads=[pb, msk], writes=[BTt])
        s.barrier()
        s.release(m1)
        if int(os.environ.get("MK_ATT_STOP", "9")) <= 2:
            return
        qS = s.sbuf("qS", [128, 4, NS_TOK], BF16)
        kS = s.sbuf("kS", [128, 4, NS_TOK], BF16)
        for t4 in range(4):
            s.dma("sp", qS[:, t4, :], _ap(q_d)[t4 * 128:(t4 + 1) * 128, NP_TOK:NT], reads=[q_d], writes=[qS])
            s.dma("sp", kS[:, t4, :], _ap(k_d)[t4 * 128:(t4 + 1) * 128, NP_TOK:NT], reads=[k_d], writes=[kS])
        vS = s.sbuf("vS", [128, 64, 512], BF16)
        s.dma("sp", vS[0:64, :, :], _ap(v_d)[NP_TOK:NT, :].rearrange("(r c) f -> c r f", c=64), reads=[v_d], writes=[vS])
        s.dma("sp", vS[64:128, 0:63, :], _ap(v_d)[NP_TOK + 64:NT, :].rearrange("(r c) f -> c r f", c=64), reads=[v_d], writes=[vS])
        s.op("dve", lambda e: e.memset(vS[64:128, 63:64, :], 0.0), reads=[vS], writes=[vS])
        ck32 = s.sbuf("ck32", [128, 2, 512], F32)
        s.dma("sp", ck32[:], _ap(ck)[l].rearrange("(t p) f -> p t f", p=128), writes=[ck32])
        kC = s.sbuf("kC", [128, 4, 256], BF16)
        for hp in range(4):
            pb = ps()
            s.group("pe", [lambda e, pb=pb, t=t, hp=hp: e.transpose(pb[:, t * 128:(t + 1) * 128], ck32[:, t, hp * 128:(hp + 1) * 128], ident[:])
                           for t in range(2)], reads=[ck32, ident], writes=[pb])
            s.op("act", lambda e, pb=pb, hp=hp: e.copy(out=kC[:, hp, :], in_=pb[:, 0:256]), reads=[pb], writes=[kC])
        vC = s.sbuf("vC", [128, 2, 512], BF16)
        s.dma("pool", vC[:], _ap(cv)[l].rearrange("(t p) f -> p t f", p=128), writes=[vC])
        NBUF = 3
        EC = [s.sbuf("EC%d" % i, [128, 6, 2, 64], BF16) for i in range(NBUF)]
        rdw = [s.sbuf("rdw%d" % i, [128, 128], F32) for i in range(NBUF)]
        YN = [s.sbuf("YN%d" % i, [128, 4, 512], BF16) for i in range(2)]
        it = 0
        for r in range(int(os.environ.get("MK_NA_ROWS", "64"))):
            rs = min(max(r - 4, 0), 56)
            dr0 = rs - r + 7
            yn = YN[(r // 8) % 2]
            for hp in range(4):
                k2 = it % NBUF
                it += 1
                E_ = EC[k2]
                for hh in range(2):
                    pr = slice(hh * 64, (hh + 1) * 64)
                    h = 2 * hp + hh
                    qrow = qS[pr, hp, r * 64:(r + 1) * 64]
                    idb = identb[pr, hh * 64:(hh + 1) * 64]
                    pw = ps()
                    fw = []
                    for m in range(4):
                        fw.append(lambda e, pw=pw, m=m, pr=pr, qrow=qrow: e.matmul(
                            pw[:, m * 64:(m + 1) * 64], kS[pr, hp, (rs + 2 * m) * 64:(rs + 2 * m + 2) * 64], qrow, start=True, stop=False))
                        fw.append(lambda e, pw=pw, m=m, pr=pr, h=h, idb=idb: e.matmul(
                            pw[:, m * 64:(m + 1) * 64], BTt[pr, h, dr0 + 2 * m:dr0 + 2 * m + 2, :].rearrange("p d k -> p (d k)"), idb,
                            start=False, stop=True))
                    for t in range(2):
                        fw.append(lambda e, pw=pw, t=t, pr=pr, qrow=qrow: e.matmul(
                            pw[:, 256 + t * 64:256 + (t + 1) * 64], kC[pr, hp, t * 128:(t + 1) * 128], qrow, start=True, stop=True))
                    s.group("pe", fw, reads=[kS, kC, qS, BTt, identb], writes=[pw])
                    s.op("act", lambda e, E_=E_, pw=pw, hh=hh: e.activation(
                        out=E_[:, :, hh, :], in_=pw[:, 0:384].rearrange("p (m q) -> p m q", m=6), func=AF.Exp, scale=0.125),
                        reads=[pw], writes=[E_])
                pnd = ps()
                rhs6 = [E_[:, m, :, :].rearrange("p h q -> p (h q)") for m in range(6)]
                lv = [vS[:, rs + 2 * m, hp * 128:(hp + 1) * 128] for m in range(4)] + [vC[:, t, hp * 128:(hp + 1) * 128] for t in range(2)]
                mm_group(pnd, pnd[:, 0:128], [(lv[m], rhs6[m]) for m in range(6)], [vS, vC, E_])
                mm_group(pnd, pnd[:, 128:256], [(ones_bf[:], rhs6[m]) for m in range(6)], [ones_bf, E_])
                rd_ = rdw[k2]
                s.op("dve", lambda e, rd_=rd_, pnd=pnd: e.reciprocal(out=rd_[:], in_=pnd[:, 128:256]), reads=[pnd], writes=[rd_])
                for hh in range(2):
                    pr = slice(hh * 64, (hh + 1) * 64)
                    s.op("dve", lambda e, yn=yn, pnd=pnd, rd_=rd_, pr=pr, hp=hp, r=r, hh=hh: e.tensor_tensor(
                        out=yn[pr, hp, (r % 8) * 64:(r % 8 + 1) * 64], in0=pnd[pr, hh * 64:(hh + 1) * 64], in1=rd_[pr, hh * 64:(hh + 1) * 64],
                        op=ALU.mult), reads=[pnd, rd_], writes=[yn])
            if r % 8 == 7:
                tok0 = NP_TOK + (r // 8) * 512
                for hp in range(4):
                    s.dma("sp", _ap(ymix_d)[256 + hp * 128:256 + (hp + 1) * 128, tok0:tok0 + 512], yn[:, hp, :], reads=[yn], writes=[ymix_d])
        s.barrier()
        s.release(m0)

    def gate_part(l):
        m0 = s.mark()
        ws32 = s.sbuf("ws32", [128, 4, 128], F32)
        s.dma("sp", ws32[:], _ap(gm_ws)[l].rearrange("g i j -> i g j"), writes=[ws32])
        wsT = s.sbuf("wsT", [128, 4, 128], BF16)
        pb = ps()
        s.group("pe", [lambda e, g=g: e.transpose(pb[:, g * 128:(g + 1) * 128], ws32[:, g, :], ident[:]) for g in range(4)],
                reads=[ws32, ident], writes=[pb])
        s.op("act", lambda e: e.copy(out=wsT[:].rearrange("p g i -> p (g i)"), in_=pb[:]), reads=[pb], writes=[wsT])
        BS = s.sbuf("BS", [128, 2, 128], F32)
        for g in range(4):
            s.dma("sp", BS[(g % 2) * 64:(g % 2 + 1) * 64, g // 2, :], _ap(gm_bs)[l, g:g + 1, :].to_broadcast([64, 128]), writes=[BS])
        ug = [s.sbuf("ug%d" % i, [128, 2, 512], BF16) for i in range(2)]
        vAl = [s.sbuf("vAl%d" % i, [128, 4, 256], BF16) for i in range(2)]
        vBl = [s.sbuf("vBl%d" % i, [128, 4, 256], BF16) for i in range(2)]
        tg = [s.sbuf("tg%d" % i, [128, 128], F32) for i in range(2)]
        yg = [s.sbuf("yg%d" % i, [128, 2, 512], BF16) for i in range(2)]
        it = 0
        def fetch(blk):
            k2 = blk % 2
            tok = slice(blk * 512, (blk + 1) * 512)
            for a in range(2):
                s.dma("sp", ug[k2][:, a, :], _ap(ug_d)[a * 128:(a + 1) * 128, tok], reads=[ug_d], writes=[ug[k2]])
            s.dma("sp", vAl[k2][:], _ap(va_d)[tok, :].rearrange("(t p) f -> p t f", p=128), reads=[va_d], writes=[vAl[k2]])
            s.dma("sp", vBl[k2][:], _ap(vb_d)[tok, :].rearrange("(t p) f -> p t f", p=128), reads=[vb_d], writes=[vBl[k2]])

        fetch(0)
        for blk in range(NB):
            k2 = blk % 2
            tok = slice(blk * 512, (blk + 1) * 512)
            if blk + 1 < NB:
                fetch(blk + 1)
            for t in range(4):
                for a in range(2):
                    pq = ps()
                    mm_group(pq, pq[:, 0:128], [(vAl[k2][:, t, a * 128:(a + 1) * 128], wsT[:, 2 * a, :]),
                                                (vBl[k2][:, t, a * 128:(a + 1) * 128], wsT[:, 2 * a + 1, :])], [vAl[k2], vBl[k2], wsT])
                    tt_ = tg[it % 2]
                    it += 1
                    s.op("dve", lambda e, tt_=tt_, pq=pq, a=a: e.tensor_tensor(out=tt_[:], in0=pq[:, 0:128], in1=BS[:, a, :], op=ALU.add),
                         reads=[pq, BS], writes=[tt_])
                    s.op("dve", lambda e, tt_=tt_, a=a, t=t, k2=k2: e.tensor_tensor(
                        out=yg[k2][:, a, t * 128:(t + 1) * 128], in0=tt_[:], in1=ug[k2][:, a, t * 128:(t + 1) * 128], op=ALU.mult),
                        reads=[tt_, ug[k2]], writes=[yg[k2]])
            for a in range(2):
                s.dma("sp", _ap(ymix_d)[768 + a * 128:768 + (a + 1) * 128, tok], yg[k2][:, a, :], reads=[yg[k2]], writes=[ymix_d])
        s.barrier()
        s.release(m0)

    def mix_phase(l):
        ssm_part(l)
        attn_part(l)
        gate_part(l)

    PH = os.environ.get("MK_PHASES", "all")
    if PH != "all":
        for name in PH.split(","):
            {"adaln": setup_adaln, "ffn": lambda: ffn_phase(0, 0, f1_in, f1_out, True, False), "proj": lambda: proj_phase(0),
             "ssm": lambda: ssm_part(0), "attn": lambda: attn_part(0), "gate": lambda: gate_part(0),
             "outproj": lambda: outproj_phase(0)}[name]()
        s.barrier()
        s.release(0)
        ctx.__exit__(None, None, None)
        return nc
    mW = s.mark()
    W = alloc_ffn_w()
    setup_adaln(after_dma=lambda: issue_ffn_w(W, 0, f1_in, f1_out))
    for l in range(DEPTH):
        if l == 0:
            ffn_phase(l, 0, f1_in, f1_out, first=True, last=False, W=W)
            s.release(mW)
        else:
            ffn_phase(l, 0, f1_in, f1_out, first=False, last=False)
        proj_phase(l)
        ssm_part(l)
        attn_part(l)
        mW = s.mark()
        W = alloc_ffn_w()
        mWO = s.mark()
        WO = s.sbuf("WO", [128, 8, 1024], BF16)
        s.dma("pool", WO[:], _ap(w_out)[l].rearrange("(kt p) f -> p kt f", p=128), writes=[WO])
        issue_ffn_w(W, l, f2_in, f2_out)
        gate_part(l)
        outproj_phase(l, WO=WO)
        s.release(mWO)
        ffn_phase(l, 2, f2_in, f2_out, first=False, last=(l == DEPTH - 1), W=W)
        s.release(mW)
    outs = [yp, ys, nk, nv, nst]
    s.barrier()
    s.release(0)
    ctx.__exit__(None, None, None)
    return nc


_NC_CACHE = {}


def _consts():
    ident = np.eye(128, dtype=np.float32)
    kc = np.arange(64)[:, None]
    qc = np.arange(64)[None, :]
    dc = kc - qc + 15
    oh = np.zeros((31, 64, 128), np.float32)
    for q in range(64):
        for k in range(64):
            if 0 <= dc[k, q] <= 30:
                oh[dc[k, q], k, q] = 1.0
                oh[dc[k, q], k, 64 + q] = 1.0
    cs = np.clip(np.arange(64) - 8, 0, 48)
    win = (kc >= cs[None, :]) & (kc < cs[None, :] + 16)
    mask = np.where(win, 0.0, NEG).astype(np.float32)
    iota = np.tile(np.arange(1024, dtype=np.float32)[None, :], (128, 1))
    return {"c_ident": ident, "c_oh": oh.reshape(31, 8192), "c_mask": np.ascontiguousarray(np.concatenate([mask.T, mask.T], axis=0)), "c_iota": iota}


def kernel(**inputs):
    debug = bool(int(os.environ.get("MK_DEBUG", "0")))
    key = ("nc", debug)
    if key not in _NC_CACHE:
        _NC_CACHE[key] = build_program(debug=debug)
    nc = _NC_CACHE[key]
    f = lambda a: np.ascontiguousarray(np.asarray(a, dtype=np.float32))
    x_prompt, x_sample = f(inputs["x_prompt"]), f(inputs["x_sample"])
    c, c_ctx = f(inputs["c"]), f(inputs["c_ctx"])
    cache_k, cache_v, state_ssm = f(inputs["cache_k"]), f(inputs["cache_v"]), f(inputs["state_ssm"])
    shared = {}
    for name in ("w_ada", "b_ada", "norm_ffn1", "norm_mix", "norm_ffn2", "ffn1_w_in", "ffn1_w_out", "ffn2_w_in",
                 "ffn2_w_out", "w_in", "w_out", "ssm_d", "ssm_glu_w", "ssm_glu_b", "na_q_norm", "na_k_norm",
                 "gm_ws", "gm_bs"):
        shared[name] = f(inputs[name])
    shared["ssm_lambda_re"] = f(inputs["ssm_lambda_re"]).reshape(DEPTH, 2, 1024)
    shared["ssm_lambda_im"] = f(inputs["ssm_lambda_im"]).reshape(DEPTH, 2, 1024)
    shared["ssm_log_dt"] = f(inputs["ssm_log_dt"])
    shared["ssm_b_re"] = f(inputs["ssm_b_re"]).reshape(DEPTH, 2, 1024, 16)
    shared["ssm_b_im"] = f(inputs["ssm_b_im"]).reshape(DEPTH, 2, 1024, 16)
    shared["ssm_c_re"] = f(inputs["ssm_c_re"]).reshape(DEPTH, 2, 256, 64)
    shared["ssm_c_im"] = f(inputs["ssm_c_im"]).reshape(DEPTH, 2, 256, 64)
    shared["na_rpb"] = f(inputs["na_rpb"]).reshape(DEPTH, 120, 31)
    shared.update(_consts())
    in_maps = []
    for core in range(8):
        b = core // 2
        m = dict(shared)
        m["xp"] = x_prompt[4 * core:4 * core + 4].reshape(NP_TOK, D)
        m["xs"] = x_sample[b]
        m["cond"] = np.stack([c_ctx, c[b]], axis=0)
        m["ck"] = cache_k[b].reshape(DEPTH, 256, 512)
        m["cv"] = cache_v[b].reshape(DEPTH, 256, 512)
        m["sst"] = state_ssm[b].reshape(DEPTH, 2, 1024, 2)
        in_maps.append(m)
    res = run_bass_kernel_spmd(nc, in_maps, core_ids=list(range(8)))
    R = res.results
    if debug:
        kernel.last_results = R
    y_prompt = np.concatenate([R[i]["yp"].reshape(4, 256, D) for i in range(8)], axis=0)
    y_sample = np.stack([np.concatenate([R[2 * b]["ys"][:2048], R[2 * b + 1]["ys"][2048:]], axis=0) for b in range(4)], axis=0)
    new_k = np.concatenate([R[i]["nk"].reshape(4, DEPTH, 256, 8, 64) for i in range(8)], axis=0)
    new_v = np.concatenate([R[i]["nv"].reshape(4, DEPTH, 256, 8, 64) for i in range(8)], axis=0)
    new_s = np.concatenate([R[i]["nst"].reshape(4, DEPTH, 2, 16, 64, 2) for i in range(8)], axis=0)
    return (y_prompt.astype(np.float32), y_sample.astype(np.float32), new_k.astype(np.float32),
            new_v.astype(np.float32), new_s.astype(np.float32))
```

```python
import math
import os
import numpy as np
import concourse.bass as bass
import concourse.mybir as mybir
from concourse.bass_utils import run_bass_kernel_spmd

F32 = mybir.dt.float32
BF16 = mybir.dt.bfloat16
AF = mybir.ActivationFunctionType
ALU = mybir.AluOpType
AX = mybir.AxisListType

D = 1024
DFF = 2816
DEPTH = 2
NP_TOK = 1024
NS_TOK = 4096
NT = NP_TOK + NS_TOK
NB = NT // 512
INC = 2304
TWO_PI = 2.0 * math.pi
NEG = -30000.0


class Buf:
    __slots__ = ("t", "w", "r", "name")

    def __init__(self, t, name=""):
        self.t = t
        self.w = None
        self.r = {}
        self.name = name

    def __getitem__(self, idx):
        return self.t[idx]


class Sched:
    def __init__(self, nc, n_dma_sems=40):
        self.nc = nc
        self.stack = []
        self.eng = {"pe": nc.tensor, "act": nc.scalar, "dve": nc.vector, "pool": nc.gpsimd, "sp": nc.sync}
        self.sems = {}
        self.cnt = {}
        for k in ("pe", "act", "dve", "pool"):
            self.sems[k] = self._sem("s_" + k)
            self.cnt[k] = 0
        self.dma_keys = []
        for i in range(n_dma_sems):
            k = "d%d" % i
            self.sems[k] = self._sem("s_" + k)
            self.cnt[k] = 0
            self.dma_keys.append(k)
        self.dma_rr = 0
        self.seen = {e: {} for e in self.eng}
        self.ninst = 0

    def _sem(self, name):
        cm = self.nc.semaphore(name)
        h = cm.__enter__()
        self.stack.append(cm)
        return h

    def mark(self):
        return len(self.stack)

    def release(self, mark):
        while len(self.stack) > mark:
            self.stack.pop().__exit__(None, None, None)

    def sbuf(self, name, shape, dtype):
        self.uid = getattr(self, "uid", 0) + 1
        name = "%s_u%d" % (name, self.uid)
        cm = self.nc.sbuf_tensor(name, list(shape), dtype)
        t = cm.__enter__()
        self.stack.append(cm)
        return Buf(t, name)

    def psum(self, name, shape, dtype=F32):
        cm = self.nc.psum_tensor(name, list(shape), dtype)
        t = cm.__enter__()
        self.stack.append(cm)
        return Buf(t, name)

    def dram(self, name, shape, dtype, kind="Internal"):
        return Buf(self.nc.dram_tensor(name, list(shape), dtype, kind=kind), name)

    def _need(self, e, deps):
        seen = self.seen[e]
        todo = {}
        for (k, v) in deps:
            if seen.get(k, 0) >= v:
                continue
            if todo.get(k, 0) < v:
                todo[k] = v
        for k, v in todo.items():
            self.eng[e].wait_ge(self.sems[k], v)
            seen[k] = v
            self.ninst += 1

    @staticmethod
    def _deps(reads, writes):
        deps = []
        for b in reads:
            if b.w is not None:
                deps.append(b.w)
        for b in writes:
            if b.w is not None:
                deps.append(b.w)
            deps.extend(b.r.items())
        return deps

    def _commit(self, k, v, reads, writes):
        for b in reads:
            if b.r.get(k, 0) < v:
                b.r[k] = v
        for b in writes:
            b.w = (k, v)
            b.r = {}

    def op(self, e, fn, reads=(), writes=()):
        self._need(e, self._deps(reads, writes))
        inst = fn(self.eng[e])
        self.cnt[e] += 1
        inst.then_inc(self.sems[e], 1)
        self._commit(e, self.cnt[e], reads, writes)
        self.ninst += 1
        return inst

    def group(self, e, fns, reads=(), writes=()):
        self._need(e, self._deps(reads, writes))
        inst = None
        for fn in fns:
            inst = fn(self.eng[e])
            self.ninst += 1
        self.cnt[e] += 1
        inst.then_inc(self.sems[e], 1)
        self._commit(e, self.cnt[e], reads, writes)

    def dma(self, q, out_ap, in_ap, reads=(), writes=(), **kw):
        nsw = 8
        if q == "pool":
            self.sw_rr = (getattr(self, "sw_rr", -1) + 1) % nsw
            k = self.dma_keys[self.sw_rr]
        else:
            k = self.dma_keys[nsw + self.dma_rr]
            self.dma_rr = (self.dma_rr + 1) % (len(self.dma_keys) - nsw)
        deps = self._deps(reads, writes)
        if self.cnt[k] > 0:
            deps.append((k, self.cnt[k]))
        self._need(q, deps)
        inst = self.eng[q].dma_start(out=out_ap, in_=in_ap, **kw)
        self.cnt[k] += 16
        inst.then_inc(self.sems[k], 16)
        self._commit(k, self.cnt[k], reads, writes)
        self.ninst += 1
        return inst

    def barrier(self):
        allk = [(k, v) for k, v in self.cnt.items() if v > 0]
        for e in self.eng:
            self._need(e, allk)


def _ap(b):
    return b.t.ap()


def build_program(debug=False):
    nc = bass.Bass("TRN2", target_bir_lowering=False)
    s = Sched(nc)
    ctx = nc.allow_non_contiguous_dma(reason="small strided parameter loads")
    ctx.__enter__()

    def din(name, shape):
        return s.dram(name, shape, F32, kind="ExternalInput")

    def dout(name, shape):
        return s.dram(name, shape, F32, kind="ExternalOutput")

    xp = din("xp", [NP_TOK, D])
    xs = din("xs", [NS_TOK, D])
    cond = din("cond", [2, D])
    ck = din("ck", [DEPTH, 256, 512])
    cv = din("cv", [DEPTH, 256, 512])
    sst = din("sst", [DEPTH, 2, 1024, 2])
    w_ada = din("w_ada", [DEPTH, D, 9 * D])
    b_ada = din("b_ada", [DEPTH, 9 * D])
    norms = [din("norm_ffn1", [DEPTH, D]), din("norm_mix", [DEPTH, D]), din("norm_ffn2", [DEPTH, D])]
    f1_in = din("ffn1_w_in", [DEPTH, D, 2 * DFF])
    f1_out = din("ffn1_w_out", [DEPTH, DFF, D])
    f2_in = din("ffn2_w_in", [DEPTH, D, 2 * DFF])
    f2_out = din("ffn2_w_out", [DEPTH, DFF, D])
    w_in = din("w_in", [DEPTH, D, INC])
    w_out = din("w_out", [DEPTH, D, D])
    lam_re = din("ssm_lambda_re", [DEPTH, 2, 1024])
    lam_im = din("ssm_lambda_im", [DEPTH, 2, 1024])
    log_dt = din("ssm_log_dt", [DEPTH, 2, 16])
    b_re = din("ssm_b_re", [DEPTH, 2, 1024, 16])
    b_im = din("ssm_b_im", [DEPTH, 2, 1024, 16])
    c_re = din("ssm_c_re", [DEPTH, 2, 256, 64])
    c_im = din("ssm_c_im", [DEPTH, 2, 256, 64])
    ssm_d = din("ssm_d", [DEPTH, 256])
    glu_w = din("ssm_glu_w", [DEPTH, 256, 256])
    glu_b = din("ssm_glu_b", [DEPTH, 256])
    qn_g = din("na_q_norm", [DEPTH, 64])
    kn_g = din("na_k_norm", [DEPTH, 64])
    rpb = din("na_rpb", [DEPTH, 8 * 15, 31])
    gm_ws = din("gm_ws", [DEPTH, 4, 128, 128])
    gm_bs = din("gm_bs", [DEPTH, 4, 128])
    c_ident = din("c_ident", [128, 128])
    c_oh = din("c_oh", [31, 64 * 128])
    c_mask = din("c_mask", [128, 64])
    c_iota = din("c_iota", [128, 1024])
    yp = dout("yp", [NP_TOK, D])
    ys = dout("ys", [NS_TOK, D])
    nk = dout("nk", [4, DEPTH, 256, 512])
    nv = dout("nv", [4, DEPTH, 256, 512])
    nst = dout("nst", [4, DEPTH, 2, 1024, 2])
    skind = "ExternalOutput" if debug else "Internal"
    xT_d = s.dram("xT_d", [D, NT], F32, kind=skind)
    ymix_d = s.dram("ymix_d", [D, NT], BF16, kind=skind)
    us_d = s.dram("us_d", [256, NT], BF16, kind=skind)
    q_d = s.dram("q_d", [512, NT], BF16, kind=skind)
    k_d = s.dram("k_d", [512, NT], BF16, kind=skind)
    v_d = s.dram("v_d", [NT, 512], BF16, kind=skind)
    ug_d = s.dram("ug_d", [256, NT], BF16, kind=skind)
    va_d = s.dram("va_d", [NT, 256], BF16, kind=skind)
    vb_d = s.dram("vb_d", [NT, 256], BF16, kind=skind)

    banks = [s.psum("bank%d" % i, [128, 512], F32) for i in range(8)]
    bank_rr = [0]

    def ps():
        b = banks[bank_rr[0]]
        bank_rr[0] = (bank_rr[0] + 1) % 8
        return b

    ident = s.sbuf("ident", [128, 128], F32)
    s.dma("sp", ident[:], _ap(c_ident), writes=[ident])
    ones_bf = s.sbuf("ones_bf", [128, 128], BF16)
    s.op("dve", lambda e: e.memset(ones_bf[:], 1.0), writes=[ones_bf])
    mean_bf = s.sbuf("mean_bf", [128, 128], BF16)
    s.op("dve", lambda e: e.memset(mean_bf[:], 1.0 / 1024.0), writes=[mean_bf])
    bd_bf = s.sbuf("bd_bf", [128, 128], BF16)
    s.op("dve", lambda e: e.memset(bd_bf[:], 0.0), writes=[bd_bf])
    s.op("dve", lambda e: e.memset(bd_bf[0:64, 0:64], 1.0 / 64.0), reads=[bd_bf], writes=[bd_bf])
    s.op("dve", lambda e: e.memset(bd_bf[64:128, 64:128], 1.0 / 64.0), reads=[bd_bf], writes=[bd_bf])
    pi_c = s.sbuf("pi_c", [128, 1], F32)
    s.op("dve", lambda e: e.memset(pi_c[:], math.pi), writes=[pi_c])
    eps6 = s.sbuf("eps6", [128, 1], F32)
    s.op("dve", lambda e: e.memset(eps6[:], 1e-6), writes=[eps6])
    eps5 = s.sbuf("eps5", [128, 1], F32)
    s.op("dve", lambda e: e.memset(eps5[:], 1e-5), writes=[eps5])

    def rsqrt(dst_b, dst_ap, src_b, src_ap, eps_b, scale=1.0):
        P = dst_ap.shape[0]
        s.op("act", lambda e: e.activation(out=dst_ap, in_=src_ap, func=AF.Sqrt, scale=scale, bias=eps_b[0:P, 0:1]),
             reads=[src_b, eps_b], writes=[dst_b])
        s.op("dve", lambda e: e.reciprocal(out=dst_ap, in_=dst_ap), reads=[dst_b], writes=[dst_b])
    Aco = s.sbuf("Aco", [128, DEPTH, 3, 2, 8], F32)
    Bco = s.sbuf("Bco", [128, DEPTH, 3, 2, 8], F32)
    Gco = s.sbuf("Gco", [128, DEPTH, 3, 2, 8], F32)

    def mm_group(out_buf, out_ap, pairs, reads):
        n = len(pairs)
        fns = []
        for i, (l, r) in enumerate(pairs):
            fns.append(lambda e, l=l, r=r, i=i: e.matmul(out_ap, l, r, start=(i == 0), stop=(i == n - 1)))
        s.group("pe", fns, reads=reads, writes=[out_buf])

    def setup_adaln(after_dma=None):
        m0 = s.mark()
        cs32 = s.sbuf("cs32", [128, 8, 2], F32)
        csb = s.sbuf("csb", [128, 8, 2], BF16)
        for c in range(2):
            s.dma("sp", cs32[:, :, c], _ap(cond)[c, :].rearrange("(kt p) -> p kt", p=128), writes=[cs32])
        s.op("act", lambda e: e.activation(out=csb[:], in_=cs32[:], func=AF.Silu), reads=[cs32], writes=[csb])
        gn = s.sbuf("gn", [128, DEPTH, 3, 8], F32)
        for i in range(3):
            for l in range(DEPTH):
                s.dma("sp", gn[:, l, i, :], _ap(norms[i])[l, :].rearrange("(kt p) -> p kt", p=128), writes=[gn])
        wa = [s.sbuf("wa%d" % i, [128, 8, 1024], BF16) for i in range(2)]
        badaT = s.sbuf("badaT", [128, 72], F32)
        modT = s.sbuf("modT", [128, 72, 2], F32)
        for l in range(DEPTH):
            s.dma("sp", badaT[:], _ap(b_ada)[l, :].rearrange("(ft p) -> p ft", p=128), writes=[badaT])
            pb = ps()
            for ch in range(9):
                w = wa[ch % 2]
                s.dma("pool", w[:], _ap(w_ada)[l, :, ch * 1024:(ch + 1) * 1024].rearrange("(kt p) f -> p kt f", p=128),
                      writes=[w])
                for f8 in range(8):
                    ft = ch * 8 + f8
                    mm_group(pb, pb[:, 2 * ft:2 * ft + 2],
                             [(w[:, kt, f8 * 128:(f8 + 1) * 128], csb[:, kt, :]) for kt in range(8)], [w, csb])
            for c in range(2):
                s.op("dve", lambda e, c=c: e.tensor_tensor(out=modT[:, :, c], in0=pb[:, c:144:2], in1=badaT[:], op=ALU.add),
                     reads=[pb, badaT], writes=[modT])
            for i in range(3):
                for c in range(2):
                    sh = modT[:, (3 * i) * 8:(3 * i) * 8 + 8, c]
                    sc = modT[:, (3 * i + 1) * 8:(3 * i + 1) * 8 + 8, c]
                    gt = modT[:, (3 * i + 2) * 8:(3 * i + 2) * 8 + 8, c]
                    s.op("dve", lambda e, sc=sc, l=l, i=i, c=c: e.scalar_tensor_tensor(
                        out=Aco[:, l, i, c, :], in0=sc, scalar=1.0, in1=gn[:, l, i, :], op0=ALU.add, op1=ALU.mult),
                        reads=[modT, gn], writes=[Aco])
                    s.op("dve", lambda e, sh=sh, l=l, i=i, c=c: e.tensor_copy(out=Bco[:, l, i, c, :], in_=sh),
                         reads=[modT], writes=[Bco])
                    s.op("dve", lambda e, gt=gt, l=l, i=i, c=c: e.tensor_scalar(
                        out=Gco[:, l, i, c, :], in0=gt, scalar1=(1.0 if i == 1 else 0.5), scalar2=None, op0=ALU.mult),
                        reads=[modT], writes=[Gco])
        if after_dma is not None:
            after_dma()
        s.barrier()
        s.release(m0)

    def load_xT(xT, blk):
        s.dma("sp", xT[:], _ap(xT_d)[:, blk * 512:(blk + 1) * 512].rearrange("(kt p) n -> p kt n", p=128),
              reads=[xT_d], writes=[xT])

    def store_xT(xT, blk):
        s.dma("sp", _ap(xT_d)[:, blk * 512:(blk + 1) * 512].rearrange("(kt p) n -> p kt n", p=128), xT[:],
              reads=[xT], writes=[xT_d])

    def load_x_tm(xT, xtm, blk):
        src = _ap(xp)[blk * 512:(blk + 1) * 512, :] if blk < 2 else _ap(xs)[(blk - 2) * 512:(blk - 1) * 512, :]
        for tt in range(4):
            s.dma("sp", xtm[:, 0, :], src[tt * 128:(tt + 1) * 128, :], writes=[xtm])
            for half in range(2):
                pb = ps()
                s.group("pe", [lambda e, k4=k4, half=half, pb=pb: e.transpose(pb[:, k4 * 128:(k4 + 1) * 128],
                                                                               xtm[:, 0, (half * 4 + k4) * 128:(half * 4 + k4 + 1) * 128], ident[:])
                               for k4 in range(4)], reads=[xtm, ident], writes=[pb])
                dst = xT[:, half * 4:(half + 1) * 4, tt * 128:(tt + 1) * 128]
                if half:
                    s.op("act", lambda e, dst=dst, pb=pb: e.copy(out=dst, in_=pb[:].rearrange("p (k n) -> p k n", k=4)),
                         reads=[pb], writes=[xT])
                else:
                    s.op("dve", lambda e, dst=dst, pb=pb: e.tensor_copy(out=dst, in_=pb[:].rearrange("p (k n) -> p k n", k=4)),
                         reads=[pb], writes=[xT])

    def store_y_tm(xT, ytm, blk):
        dst = _ap(yp)[blk * 512:(blk + 1) * 512, :] if blk < 2 else _ap(ys)[(blk - 2) * 512:(blk - 1) * 512, :]
        ydr = yp if blk < 2 else ys
        for tt in range(4):
            for half in range(2):
                pb = ps()
                s.group("pe", [lambda e, k4=k4, tt=tt, half=half, pb=pb: e.transpose(
                    pb[:, k4 * 128:(k4 + 1) * 128], xT[:, half * 4 + k4, tt * 128:(tt + 1) * 128], ident[:])
                    for k4 in range(4)], reads=[xT, ident], writes=[pb])
                if half:
                    s.op("act", lambda e, pb=pb: e.copy(out=ytm[:, 0, 512:1024], in_=pb[:]), reads=[pb], writes=[ytm])
                else:
                    s.op("dve", lambda e, pb=pb: e.tensor_copy(out=ytm[:, 0, 0:512], in_=pb[:]), reads=[pb], writes=[ytm])
            s.dma("sp", dst[tt * 128:(tt + 1) * 128, :], ytm[:, 0, :], reads=[ytm], writes=[ydr])

    def norm_mod(xT, hb, tmp2, rstd, l, i, c):
        for kt in range(8):
            s.op("act", lambda e, kt=kt: e.activation(out=hb[:, kt, :], in_=xT[:, kt, :], func=AF.Square),
                 reads=[xT], writes=[hb])
        pb = ps()
        mm_group(pb, pb[:], [(mean_bf[:], hb[:, kt, :]) for kt in range(8)], [mean_bf, hb])
        rsqrt(rstd, rstd[:], pb, pb[:], eps6)
        for kt in range(8):
            t = tmp2[kt % 2]
            s.op("dve", lambda e, kt=kt, t=t: e.tensor_tensor(out=t[:], in0=xT[:, kt, :], in1=rstd[:], op=ALU.mult),
                 reads=[xT, rstd], writes=[t])
            s.op("act", lambda e, kt=kt, t=t: e.activation(out=hb[:, kt, :], in_=t[:], func=AF.Identity,
                                                           scale=Aco[:, l, i, c, kt:kt + 1], bias=Bco[:, l, i, c, kt:kt + 1]),
                 reads=[t, Aco, Bco], writes=[hb])

    def alloc_ffn_w():
        W1 = [s.sbuf("W1_%d" % j, [128, 8, 512], BF16) for j in range(11)]
        W2 = [s.sbuf("W2_%d" % j, [128, 2, 1024], BF16) for j in range(11)]
        return (W1, W2)

    def issue_ffn_w(W, l, win_d, wout_d):
        W1, W2 = W
        for j in (0, 5, 1, 6, 2, 7, 3, 8, 4, 9, 10):
            s.dma("pool", W1[j][:], _ap(win_d)[l, :, j * 512:(j + 1) * 512].rearrange("(kt p) f -> p kt f", p=128),
                  writes=[W1[j]])
        for j in range(11):
            s.dma("pool", W2[j][:], _ap(wout_d)[l, j * 256:(j + 1) * 256, :].rearrange("(kt p) f -> p kt f", p=128),
                  writes=[W2[j]])

    def ffn_phase(l, i, win_d, wout_d, first, last, W=None):
        m0 = s.mark()
        if W is None:
            W = alloc_ffn_w()
            issue_ffn_w(W, l, win_d, wout_d)
        W1, W2 = W
        xTs = [s.sbuf("xT%d" % j, [128, 8, 512], F32) for j in range(2)]
        hb = s.sbuf("hb", [128, 8, 512], BF16)
        actb = s.sbuf("actb", [128, 22, 512], BF16)
        tmp2 = [s.sbuf("tmp%d" % j, [128, 512], F32) for j in range(2)]
        sg2 = tmp2
        rstd = s.sbuf("rstd", [128, 512], F32)
        if os.environ.get("MK_VERBOSE"):
            print("ffn phase free sbuf bytes/partition:", nc.sbuf_bytes_remaining, "first/last", first, last)
        xtm = s.sbuf("xtm", [128, 1, 1024], F32) if (first or last) else None

        def w1cols(col):
            return W1[col // 512], col % 512

        def fetch(blk):
            if first:
                load_x_tm(xTs[blk % 2], xtm, blk)
            else:
                load_xT(xTs[blk % 2], blk)

        fetch(0)
        for blk in range(NB):
            c = 0 if blk < 2 else 1
            xT = xTs[blk % 2]
            if blk + 1 < NB and not first:
                fetch(blk + 1)
            norm_mod(xT, hb, tmp2, rstd, l, i, c)
            for j in range(22):
                wg, og = w1cols(j * 128)
                wu, ou = w1cols(DFF + j * 128)
                pg = ps()
                mm_group(pg, pg[:], [(wg[:, kt, og:og + 128], hb[:, kt, :]) for kt in range(8)], [wg, hb])
                pu = ps()
                mm_group(pu, pu[:], [(wu[:, kt, ou:ou + 128], hb[:, kt, :]) for kt in range(8)], [wu, hb])
                sg = sg2[j % 2]
                s.op("act", lambda e, sg=sg, pg=pg: e.activation(out=sg[:], in_=pg[:], func=AF.Silu), reads=[pg], writes=[sg])
                s.op("dve", lambda e, sg=sg, pu=pu, j=j: e.tensor_tensor(out=actb[:, j, :], in0=sg[:], in1=pu[:], op=ALU.mult),
                     reads=[sg, pu], writes=[actb])
            if blk + 1 < NB and first:
                fetch(blk + 1)
            for ft in range(8):
                po = ps()
                mm_group(po, po[:], [(W2[j // 2][:, j % 2, ft * 128:(ft + 1) * 128], actb[:, j, :]) for j in range(22)],
                         W2 + [actb])
                s.op("dve", lambda e, po=po, ft=ft, xT=xT: e.scalar_tensor_tensor(
                    out=xT[:, ft, :], in0=po[:], scalar=Gco[:, l, i, c, ft:ft + 1], in1=xT[:, ft, :], op0=ALU.mult, op1=ALU.add),
                    reads=[po, Gco, xT], writes=[xT])
            if last:
                store_y_tm(xT, xtm, blk)
            else:
                store_xT(xT, blk)
        s.barrier()
        s.release(m0)

    def proj_phase(l):
        m0 = s.mark()
        WI = [s.sbuf("WI_%d" % j, [128, 8, 256], BF16) for j in range(9)]
        for j in range(9):
            s.dma("pool", WI[j][:], _ap(w_in)[l, :, j * 256:(j + 1) * 256].rearrange("(kt p) f -> p kt f", p=128),
                  writes=[WI[j]])
        xTs = [s.sbuf("xT%d" % j, [128, 8, 512], F32) for j in range(2)]
        hbs = [s.sbuf("hb%d" % j, [128, 8, 512], BF16) for j in range(2)]
        tmp2 = [s.sbuf("tmp%d" % j, [128, 512], F32) for j in range(2)]
        rstd = s.sbuf("rstd", [128, 512], F32)
        gq = s.sbuf("gq", [128, 2], F32)
        for h in range(2):
            s.dma("sp", gq[h * 64:(h + 1) * 64, 0:1], _ap(qn_g)[l, :].rearrange("(p o) -> p o", o=1), writes=[gq])
            s.dma("sp", gq[h * 64:(h + 1) * 64, 1:2], _ap(kn_g)[l, :].rearrange("(p o) -> p o", o=1), writes=[gq])
        gk_b = s.sbuf("gk_b", [128, 64], F32)
        s.dma("sp", gk_b[:], _ap(kn_g)[l:l + 1, :].to_broadcast([128, 64]), writes=[gk_b])
        st_b = [s.sbuf("st_b%d" % j, [128, 512], BF16) for j in range(4)]
        sqb = [s.sbuf("sqb%d" % j, [128, 512], BF16) for j in range(4)]
        r2 = [s.sbuf("r2_%d" % j, [128, 512], F32) for j in range(4)]
        gvb = [s.sbuf("gvb_%d" % j, [128, 256], F32) for j in range(2)]
        qk32 = [s.sbuf("qk32_%d" % j, [128, 512], F32) for j in range(4)]
        g0 = [s.sbuf("g0_%d" % j, [128, 512], F32) for j in range(3)]
        g1 = [s.sbuf("g1_%d" % j, [128, 512], F32) for j in range(3)]
        g2 = [s.sbuf("g2_%d" % j, [128, 512], F32) for j in range(3)]
        g3 = [s.sbuf("g3_%d" % j, [128, 512], F32) for j in range(3)]
        vtm32 = [s.sbuf("vtm32_%d" % j, [128, 512], F32) for j in range(2)]
        ktm32 = [s.sbuf("ktm32_%d" % j, [128, 512], F32) for j in range(2)]
        vtmb = [s.sbuf("vtmb_%d" % j, [128, 512], BF16) for j in range(2)]
        vA = [s.sbuf("vA_%d" % j, [128, 256], BF16) for j in range(2)]
        vB = [s.sbuf("vB_%d" % j, [128, 256], BF16) for j in range(2)]
        for j in range(2):
            s.op("dve", lambda e, j=j: e.memset(vA[j][:], 0.0), writes=[vA[j]])
            s.op("dve", lambda e, j=j: e.memset(vB[j][:], 0.0), writes=[vB[j]])
        stat = [s.sbuf("stat%d" % j, [128, 6], F32) for j in range(2)]
        mv = [s.sbuf("mv%d" % j, [128, 2], F32) for j in range(2)]
        rs1 = [s.sbuf("rs1_%d" % j, [128, 1], F32) for j in range(2)]
        ss8 = [s.sbuf("ss8_%d" % j, [128, 8], F32) for j in range(2)]
        cnt = [0]

        def wcols(col):
            return WI[col // 256], col % 256

        def fm_tile(col):
            w, o = wcols(col)
            pb = ps()
            mm_group(pb, pb[:], [(w[:, kt, o:o + 128], hb[:, kt, :]) for kt in range(8)], [w, hb])
            return pb

        def gelu_ops(src_b, src_ap, dst_b, dst_ap, n, P=128):
            k = cnt[0] % 3
            cnt[0] += 1
            a, b2, c2 = g1[k], g2[k], g3[k]
            g0_ = g0[k]
            s.op("act", lambda e: e.copy(out=g0_[0:P, 0:n], in_=src_ap), reads=[src_b], writes=[g0_])
            src_b, src_ap = g0_, g0_[0:P, 0:n]
            s.op("act", lambda e: e.activation(out=a[0:P, 0:n], in_=src_ap, func=AF.Square), reads=[src_b], writes=[a])
            s.op("dve", lambda e: e.tensor_scalar(out=a[0:P, 0:n], in0=a[0:P, 0:n], scalar1=0.044715, scalar2=1.0,
                                                  op0=ALU.mult, op1=ALU.add), reads=[a], writes=[a])
            s.op("dve", lambda e: e.tensor_tensor(out=b2[0:P, 0:n], in0=a[0:P, 0:n], in1=src_ap, op=ALU.mult),
                 reads=[a, src_b], writes=[b2])
            s.op("act", lambda e: e.activation(out=c2[0:P, 0:n], in_=b2[0:P, 0:n], func=AF.Sigmoid, scale=1.5957691216057308),
                 reads=[b2], writes=[c2])
            s.op("dve", lambda e: e.tensor_tensor(out=dst_ap, in0=c2[0:P, 0:n], in1=src_ap, op=ALU.mult),
                 reads=[c2, src_b], writes=[dst_b])

        for blk in range(NB):
            c = 0 if blk < 2 else 1
            tok = slice(blk * 512, (blk + 1) * 512)
            xT = xTs[blk % 2]
            hb = hbs[blk % 2]
            if blk == 0:
                load_xT(xT, 0)
            if blk + 1 < NB:
                load_xT(xTs[(blk + 1) % 2], blk + 1)
            norm_mod(xT, hb, tmp2, rstd, l, 1, c)
            for a in range(2):
                pb = fm_tile(a * 128)
                sb = st_b[cnt[0] % 4]; cnt[0] += 1
                s.op("act", lambda e, sb=sb, pb=pb: e.copy(out=sb[:], in_=pb[:]), reads=[pb], writes=[sb])
                s.dma("sp", _ap(us_d)[a * 128:(a + 1) * 128, tok], sb[:], reads=[sb], writes=[us_d])
            for qk in range(2):
                for t4 in range(4):
                    pb = fm_tile(256 + qk * 512 + t4 * 128)
                    sq = sqb[t4]
                    s.op("act", lambda e, sq=sq, pb=pb: e.activation(out=sq[:], in_=pb[:], func=AF.Square),
                         reads=[pb], writes=[sq])
                    q32 = qk32[t4]
                    s.op("act", lambda e, q32=q32, pb=pb: e.copy(out=q32[:], in_=pb[:]), reads=[pb], writes=[q32])
                    pb = q32
                    pm = ps()
                    mm_group(pm, pm[:], [(bd_bf[:], sq[:])], [bd_bf, sq])
                    r = r2[t4]
                    rsqrt(r, r[:], pm, pm[:], eps6)
                    sb = st_b[cnt[0] % 4]; cnt[0] += 1
                    s.op("dve", lambda e, sb=sb, pb=pb, r=r, qk=qk: e.scalar_tensor_tensor(
                        out=sb[:], in0=pb[:], scalar=gq[:, qk:qk + 1], in1=r[:], op0=ALU.mult, op1=ALU.mult),
                        reads=[pb, gq, r], writes=[sb])
                    dd = q_d if qk == 0 else k_d
                    s.dma("sp", _ap(dd)[t4 * 128:(t4 + 1) * 128, tok], sb[:], reads=[sb], writes=[dd])
            for a in range(2):
                pb = fm_tile(1792 + a * 128)
                sb = st_b[cnt[0] % 4]; cnt[0] += 1
                gelu_ops(pb, pb[:], sb, sb[:], 512)
                s.dma("sp", _ap(ug_d)[a * 128:(a + 1) * 128, tok], sb[:], reads=[sb], writes=[ug_d])
            for tt in range(4):
                trow = slice(blk * 512 + tt * 128, blk * 512 + (tt + 1) * 128)
                hT = [hb[:, kt, tt * 128:(tt + 1) * 128] for kt in range(8)]
                k2 = tt % 2
                pv = ps()
                for half in range(2):
                    w = WI[5 + half]
                    mm_group(pv, pv[:, half * 256:(half + 1) * 256], [(hT[kt], w[:, kt, :]) for kt in range(8)], [w, hb])
                vb = vtmb[k2]
                s.op("act", lambda e, vb=vb, pv=pv: e.copy(out=vb[:], in_=pv[:]), reads=[pv], writes=[vb])
                s.dma("sp", _ap(v_d)[trow, :], vb[:], reads=[vb], writes=[v_d])
                if blk < 2:
                    seq = (blk * 512 + tt * 128) // 256
                    pos = (tt % 2) * 128
                    v32 = vtm32[k2]
                    s.op("dve", lambda e, v32=v32, pv=pv: e.tensor_copy(out=v32[:], in_=pv[:]), reads=[pv], writes=[v32])
                    s.dma("sp", _ap(nv)[seq, l, pos:pos + 128, :], v32[:], reads=[v32], writes=[nv])
                    pk = ps()
                    for half in range(2):
                        w = WI[3 + half]
                        mm_group(pk, pk[:, half * 256:(half + 1) * 256], [(hT[kt], w[:, kt, :]) for kt in range(8)], [w, hb])
                    a1 = g1[cnt[0] % 3]; cnt[0] += 1
                    s.op("act", lambda e, a1=a1, pk=pk: e.activation(out=a1[:], in_=pk[:], func=AF.Square), reads=[pk], writes=[a1])
                    s8 = ss8[k2]
                    s.op("dve", lambda e, a1=a1, s8=s8: e.tensor_reduce(out=s8[:], in_=a1[:].rearrange("p (h d) -> p h d", d=64),
                                                                       axis=AX.X, op=ALU.add), reads=[a1], writes=[s8])
                    rsqrt(s8, s8[:], s8, s8[:], eps6, scale=1.0 / 64.0)
                    k32 = ktm32[k2]
                    for h in range(8):
                        s.op("dve", lambda e, h=h, k32=k32, pk=pk, s8=s8: e.scalar_tensor_tensor(
                            out=k32[:, h * 64:(h + 1) * 64], in0=pk[:, h * 64:(h + 1) * 64], scalar=s8[:, h:h + 1],
                            in1=gk_b[:], op0=ALU.mult, op1=ALU.mult), reads=[pk, s8, gk_b], writes=[k32])
                    s.dma("sp", _ap(nk)[seq, l, pos:pos + 128, :], k32[:], reads=[k32], writes=[nk])
                pg = ps()
                mm_group(pg, pg[:, 0:256], [(hT[kt], WI[8][:, kt, :]) for kt in range(8)], [WI[8], hb])
                gv = gvb[k2]
                gelu_ops(pg, pg[:, 0:256], gv, gv[:, 0:256], 256)
                stt, mvv, rr = stat[k2], mv[k2], rs1[k2]
                s.op("dve", lambda e, stt=stt, gv=gv: e.bn_stats(out=stt[:], in_=gv[:, 0:256]), reads=[gv], writes=[stt])
                s.op("dve", lambda e, stt=stt, mvv=mvv: e.bn_aggr(out=mvv[:], in_=stt[:]), reads=[stt], writes=[mvv])
                rsqrt(rr, rr[:], mvv, mvv[:, 1:2], eps5)
                va, vbb = vA[k2], vB[k2]
                for (dst, off) in ((va, 0), (vbb, 64)):
                    for a in range(2):
                        cs_ = slice(a * 128 + off, a * 128 + off + 64)
                        s.op("dve", lambda e, dst=dst, cs_=cs_, gv=gv, mvv=mvv, rr=rr: e.tensor_scalar(
                            out=dst[:, cs_], in0=gv[:, cs_], scalar1=mvv[:, 0:1], scalar2=rr[:, 0:1],
                            op0=ALU.subtract, op1=ALU.mult), reads=[gv, mvv, rr], writes=[dst])
                s.dma("sp", _ap(va_d)[trow, :], va[:], reads=[va], writes=[va_d])
                s.dma("sp", _ap(vb_d)[trow, :], vbb[:], reads=[vbb], writes=[vb_d])
        s.barrier()
        s.release(m0)

    def outproj_phase(l, WO=None):
        m0 = s.mark()
        if WO is None:
            WO = s.sbuf("WO", [128, 8, 1024], BF16)
            s.dma("pool", WO[:], _ap(w_out)[l].rearrange("(kt p) f -> p kt f", p=128), writes=[WO])
        xTs = [s.sbuf("xT%d" % j, [128, 8, 512], F32) for j in range(2)]
        yms = [s.sbuf("ym%d" % j, [128, 8, 512], BF16) for j in range(2)]
        def fetch(blk):
            load_xT(xTs[blk % 2], blk)
            s.dma("sp", yms[blk % 2][:], _ap(ymix_d)[:, blk * 512:(blk + 1) * 512].rearrange("(kt p) n -> p kt n", p=128),
                  reads=[ymix_d], writes=[yms[blk % 2]])

        fetch(0)
        for blk in range(NB):
            c = 0 if blk < 2 else 1
            xT, ym = xTs[blk % 2], yms[blk % 2]
            if blk + 1 < NB:
                fetch(blk + 1)
            for ft in range(8):
                po = ps()
                mm_group(po, po[:], [(WO[:, kt, ft * 128:(ft + 1) * 128], ym[:, kt, :]) for kt in range(8)], [WO, ym])
                s.op("dve", lambda e, po=po, ft=ft, xT=xT: e.scalar_tensor_tensor(
                    out=xT[:, ft, :], in0=po[:], scalar=Gco[:, l, 1, c, ft:ft + 1], in1=xT[:, ft, :], op0=ALU.mult, op1=ALU.add),
                    reads=[po, Gco, xT], writes=[xT])
            store_xT(xT, blk)
        s.barrier()
        s.release(m0)

    def ssm_part(l):
        m0 = s.mark()
        uS = s.sbuf("uS", [128, 2, NT], BF16)
        for a in range(2):
            s.dma("sp", uS[:, a, :], _ap(us_d)[a * 128:(a + 1) * 128, :], reads=[us_d], writes=[uS])
        Y = s.sbuf("Y", [128, 2, NT], F32)
        dco = s.sbuf("dco", [128, 2], F32)
        s.dma("sp", dco[:], _ap(ssm_d)[l, :].rearrange("(a p) -> p a", p=128), writes=[dco])
        iota = s.sbuf("iota", [128, 1024], F32)
        s.dma("sp", iota[:], _ap(c_iota), writes=[iota])
        P8 = lambda n: s.sbuf(n, [128, 8], F32)
        lr, li, ldt, dtv, ar, ai, rho, ang, sinv, cosv, lbr, lbi = [P8("p8_%d" % i) for i in range(12)]
        nr, den, kr, ki, t8a, t8b, thr, nf8, cN, sN = [P8("q8_%d" % i) for i in range(10)]
        ii8 = s.sbuf("ii8", [128, 8], mybir.dt.int32)
        BT = s.sbuf("BT", [128, 8, 2, 128], BF16)
        CT = s.sbuf("CT", [128, 8, 2, 128], BF16)
        s0 = s.sbuf("s0", [128, 8, 2], F32)
        init = s.sbuf("init", [128, 8, 2], F32)
        FIN = [s.sbuf("FIN%d" % i, [128, 8, 2], F32) for i in range(4)]
        zl = s.sbuf("zl", [128, 2], F32)
        tq = s.sbuf("tq", [128, 2], F32)

        def dv(fn, reads, writes, e="dve"):
            s.op(e, fn, reads=reads, writes=writes)

        C1 = 6.28125
        C2 = TWO_PI - C1
        PI_S = 3.1415925
        hpi = s.sbuf("hpi", [128, 1], F32)
        dv(lambda e: e.memset(hpi[:], 0.5 * math.pi), [], [hpi])

        def sincos(ang_b, ang_ap, r_b, r_ap, sin_b, sin_ap, cos_b, cos_ap, ii_b, nf_b, n):
            dv(lambda e: e.tensor_scalar(out=ii_b[:, 0:n], in0=ang_ap, scalar1=1.0 / TWO_PI, scalar2=None, op0=ALU.mult), [ang_b], [ii_b])
            dv(lambda e: e.tensor_copy(out=nf_b[:, 0:n], in_=ii_b[:, 0:n]), [ii_b], [nf_b])
            dv(lambda e: e.scalar_tensor_tensor(out=r_ap, in0=nf_b[:, 0:n], scalar=-C1, in1=ang_ap, op0=ALU.mult, op1=ALU.add),
               [nf_b, ang_b], [r_b])
            dv(lambda e: e.scalar_tensor_tensor(out=r_ap, in0=nf_b[:, 0:n], scalar=-C2, in1=r_ap, op0=ALU.mult, op1=ALU.add),
               [nf_b, r_b], [r_b])
            dv(lambda e: e.tensor_scalar(out=r_ap, in0=r_ap, scalar1=-PI_S, scalar2=None, op0=ALU.max), [r_b], [r_b])
            dv(lambda e: e.tensor_scalar(out=r_ap, in0=r_ap, scalar1=PI_S, scalar2=None, op0=ALU.min), [r_b], [r_b])
            s.op("act", lambda e: e.activation(out=sin_ap, in_=r_ap, func=AF.Sin), reads=[r_b], writes=[sin_b])
            s.op("act", lambda e: e.activation(out=nf_b[:, 0:n], in_=r_ap, func=AF.Abs), reads=[r_b], writes=[nf_b])
            s.op("act", lambda e: e.activation(out=cos_ap, in_=nf_b[:, 0:n], func=AF.Sin, scale=-1.0, bias=hpi[:, 0:1]),
                 reads=[nf_b, hpi], writes=[cos_b])

        for d in range(2):
            s.dma("sp", lr[:], _ap(lam_re)[l, d, :].rearrange("(j q) -> q j", q=128), writes=[lr])
            s.dma("sp", li[:], _ap(lam_im)[l, d, :].rearrange("(j q) -> q j", q=128), writes=[li])
            for h in range(2):
                s.dma("sp", ldt[h * 64:(h + 1) * 64, :],
                      _ap(log_dt)[l, d, :].rearrange("(j h) -> h j", h=2)[h:h + 1, :].to_broadcast([64, 8]), writes=[ldt])
            s.op("act", lambda e: e.activation(out=dtv[:], in_=ldt[:], func=AF.Exp), reads=[ldt], writes=[dtv])
            dv(lambda e: e.tensor_tensor(out=ar[:], in0=lr[:], in1=dtv[:], op=ALU.mult), [lr, dtv], [ar])
            dv(lambda e: e.tensor_tensor(out=ai[:], in0=li[:], in1=dtv[:], op=ALU.mult), [li, dtv], [ai])
            s.op("act", lambda e: e.activation(out=rho[:], in_=ar[:], func=AF.Exp), reads=[ar], writes=[rho])
            sincos(ai, ai[:], thr, thr[:], sinv, sinv[:], cosv, cosv[:], ii8, nf8, 8)
            dv(lambda e: e.tensor_tensor(out=lbr[:], in0=rho[:], in1=cosv[:], op=ALU.mult), [rho, cosv], [lbr])
            dv(lambda e: e.tensor_tensor(out=lbi[:], in0=rho[:], in1=sinv[:], op=ALU.mult), [rho, sinv], [lbi])
            dv(lambda e: e.tensor_scalar(out=nr[:], in0=lbr[:], scalar1=-1.0, scalar2=None, op0=ALU.add), [lbr], [nr])
            dv(lambda e: e.tensor_tensor(out=den[:], in0=lr[:], in1=lr[:], op=ALU.mult), [lr], [den])
            dv(lambda e: e.tensor_tensor(out=t8a[:], in0=li[:], in1=li[:], op=ALU.mult), [li], [t8a])
            dv(lambda e: e.tensor_tensor(out=den[:], in0=den[:], in1=t8a[:], op=ALU.add), [den, t8a], [den])
            dv(lambda e: e.reciprocal(out=den[:], in_=den[:]), [den], [den])
            dv(lambda e: e.tensor_tensor(out=t8a[:], in0=nr[:], in1=lr[:], op=ALU.mult), [nr, lr], [t8a])
            dv(lambda e: e.tensor_tensor(out=t8b[:], in0=lbi[:], in1=li[:], op=ALU.mult), [lbi, li], [t8b])
            dv(lambda e: e.tensor_tensor(out=t8a[:], in0=t8a[:], in1=t8b[:], op=ALU.add), [t8a, t8b], [t8a])
            dv(lambda e: e.tensor_tensor(out=kr[:], in0=t8a[:], in1=den[:], op=ALU.mult), [t8a, den], [kr])
            dv(lambda e: e.tensor_tensor(out=t8a[:], in0=lbi[:], in1=lr[:], op=ALU.mult), [lbi, lr], [t8a])
            dv(lambda e: e.tensor_tensor(out=t8b[:], in0=nr[:], in1=li[:], op=ALU.mult), [nr, li], [t8b])
            dv(lambda e: e.tensor_tensor(out=t8a[:], in0=t8a[:], in1=t8b[:], op=ALU.subtract), [t8a, t8b], [t8a])
            dv(lambda e: e.tensor_tensor(out=ki[:], in0=t8a[:], in1=den[:], op=ALU.mult), [t8a, den], [ki])
            m1 = s.mark()
            Bn = [s.sbuf("Bn%d" % i, [128, 8, 16], F32) for i in range(2)]
            Zp = [s.sbuf("Zp%d" % i, [128, 8, 128], F32) for i in range(2)]
            tz = s.sbuf("tz", [128, 16], F32)
            INc = [s.sbuf("INc%d" % i, [128, 8, 128], F32) for i in range(2)]
            s.dma("sp", Bn[0][:], _ap(b_re)[l, d].rearrange("(j q) c -> q j c", q=128), writes=[Bn[0]])
            s.dma("sp", Bn[1][:], _ap(b_im)[l, d].rearrange("(j q) c -> q j c", q=128), writes=[Bn[1]])
            for ri in range(2):
                dv(lambda e, ri=ri: e.memset(Zp[ri][:], 0.0), [], [Zp[ri]])
            for j in range(8):
                for h in range(2):
                    pr = slice(h * 64, (h + 1) * 64)
                    cs_ = slice(32 * (j % 4) + 16 * h, 32 * (j % 4) + 16 * h + 16)
                    dv(lambda e, j=j, pr=pr: e.tensor_scalar(out=tz[pr, :], in0=Bn[1][pr, j, :], scalar1=ki[pr, j:j + 1],
                                                             scalar2=None, op0=ALU.mult), [Bn[1], ki], [tz])
                    dv(lambda e, j=j, pr=pr, cs_=cs_: e.scalar_tensor_tensor(
                        out=Zp[0][pr, j, cs_], in0=Bn[0][pr, j, :], scalar=kr[pr, j:j + 1], in1=tz[pr, :],
                        op0=ALU.mult, op1=ALU.subtract), [Bn[0], kr, tz], [Zp[0]])
                    dv(lambda e, j=j, pr=pr: e.tensor_scalar(out=tz[pr, :], in0=Bn[0][pr, j, :], scalar1=ki[pr, j:j + 1],
                                                             scalar2=None, op0=ALU.mult), [Bn[0], ki], [tz])
                    dv(lambda e, j=j, pr=pr, cs_=cs_: e.scalar_tensor_tensor(
                        out=Zp[1][pr, j, cs_], in0=Bn[1][pr, j, :], scalar=kr[pr, j:j + 1], in1=tz[pr, :],
                        op0=ALU.mult, op1=ALU.add), [Bn[1], kr, tz], [Zp[1]])
            for ri, cd in enumerate((c_re, c_im)):
                dv(lambda e, ri=ri: e.memset(INc[ri][:], 0.0), [], [INc[ri]])
                for j in range(8):
                    for h in range(2):
                        g = 2 * j + h
                        r0 = 32 * (j % 4) + 16 * h
                        s.dma("sp", INc[ri][r0:r0 + 16, j, h * 64:(h + 1) * 64], _ap(cd)[l, d, g * 16:(g + 1) * 16, :],
                              reads=[INc[ri]], writes=[INc[ri]])
            for j in range(8):
                for ri in range(2):
                    pb = ps()
                    s.group("pe", [lambda e, pb=pb, j=j, ri=ri: e.transpose(pb[:, 0:128], Zp[ri][:, j, :], ident[:])],
                            reads=[Zp[ri], ident], writes=[pb])
                    s.op("act", lambda e, pb=pb, j=j, ri=ri: e.copy(out=BT[:, j, ri, :], in_=pb[:, 0:128]), reads=[pb], writes=[BT])
                    pc = ps()
                    s.group("pe", [lambda e, pc=pc, j=j, ri=ri: e.transpose(pc[:, 0:128], INc[ri][:, j, :], ident[:])],
                            reads=[INc[ri], ident], writes=[pc])
                    s.op("act", lambda e, pc=pc, j=j, ri=ri: e.activation(out=CT[:, j, ri, :], in_=pc[:, 0:128], func=AF.Copy,
                                                                        scale=(1.0 if ri == 0 else -1.0)),
                         reads=[pc], writes=[CT])
            s.dma("sp", s0[:], _ap(sst)[l, d].rearrange("(j q) r -> q j r", q=128), writes=[s0])
            dv(lambda e: e.tensor_tensor(out=t8a[:], in0=cosv[:], in1=s0[:, :, 0], op=ALU.mult), [cosv, s0], [t8a])
            dv(lambda e: e.tensor_tensor(out=t8b[:], in0=sinv[:], in1=s0[:, :, 1], op=ALU.mult), [sinv, s0], [t8b])
            dv(lambda e: e.tensor_tensor(out=init[:, :, 0], in0=t8a[:], in1=t8b[:], op=ALU.subtract), [t8a, t8b], [init])
            dv(lambda e: e.tensor_tensor(out=t8a[:], in0=sinv[:], in1=s0[:, :, 0], op=ALU.mult), [sinv, s0], [t8a])
            dv(lambda e: e.tensor_tensor(out=t8b[:], in0=cosv[:], in1=s0[:, :, 1], op=ALU.mult), [cosv, s0], [t8b])
            dv(lambda e: e.tensor_tensor(out=init[:, :, 1], in0=t8a[:], in1=t8b[:], op=ALU.add), [t8a, t8b], [init])
            s.barrier()
            s.release(m1)
            m2 = s.mark()
            rhoT = s.sbuf("rhoT", [128, 4, 1024], F32)
            tabc = s.sbuf("tabc", [128, 1024], F32)
            tabs = s.sbuf("tabs", [128, 1024], F32)
            ccol = s.sbuf("ccol", [128, 4, 2], F32)
            scol = s.sbuf("scol", [128, 4, 2], F32)
            angt = s.sbuf("angt", [128, 1024], F32)
            ang2 = s.sbuf("ang2", [128, 1024], F32)
            iiT = s.sbuf("iiT", [128, 1024], mybir.dt.int32)
            nfT = s.sbuf("nfT", [128, 1024], F32)
            tcb = s.sbuf("tcb", [128, 4, 1024], BF16)
            tsb = s.sbuf("tsb", [128, 4, 1024], BF16)
            tD_ = [[s.sbuf("tD%d_%d" % (k, i), [128, 1024], BF16) for i in range(2)] for k in range(2)]
            tP_ = [[s.sbuf("tP%d_%d" % (k, i), [128, 1024], BF16) for i in range(2)] for k in range(2)]
            wb_ = [[s.sbuf("wb%d_%d" % (k, i), [128, 1024], BF16) for i in range(2)] for k in range(2)]
            wri_ = [[s.sbuf("wri%d_%d" % (k, i), [128, 1024], F32) for i in range(2)] for k in range(2)]
            zb_ = [[s.sbuf("zb%d_%d" % (k, i), [128, 1024], BF16) for i in range(2)] for k in range(2)]
            bsb_ = [[s.sbuf("bsb%d_%d" % (k, i), [128, 1024], BF16) for i in range(2)] for k in range(2)]
            ucnt = [0]
            fq = s.sbuf("fq", [128, 4], F32)
            Sb = [[s.sbuf("Sb%d_%d" % (j4, ri), [128, 1024], BF16) for ri in range(2)] for j4 in range(4)]
            if os.environ.get("MK_VERBOSE"):
                print("ssm free sbuf bytes/partition:", nc.sbuf_bytes_remaining)
            units = [("p", sq, sq * 256, 256) for sq in range(4)]
            for kk in range(4):
                knat = kk if d == 0 else 3 - kk
                units.append(("s", kk, NP_TOK + knat * 1024, 1024))
            for a in range(2):
                for j4 in range(4):
                    j = a * 4 + j4
                    dv(lambda e, j=j: e.tensor_scalar(out=angt[:], in0=iota[:], scalar1=thr[:, j:j + 1], scalar2=None, op0=ALU.mult),
                       [iota, thr], [angt])
                    sincos(angt, angt[:], ang2, ang2[:], tabs, tabs[:], tabc, tabc[:], iiT, nfT, 1024)
                    s.op("act", lambda e, j=j, j4=j4: e.activation(out=rhoT[:, j4, :], in_=iota[:], func=AF.Identity, scale=0.0,
                                                                 bias=rho[:, j:j + 1]), reads=[iota, rho], writes=[rhoT])
                    s.op("act", lambda e, j4=j4: e.copy(out=tcb[:, j4, :], in_=tabc[:]), reads=[tabc], writes=[tcb])
                    s.op("act", lambda e, j4=j4: e.copy(out=tsb[:, j4, :], in_=tabs[:]), reads=[tabs], writes=[tsb])
                    for ci, col in enumerate((255, 1023)):
                        dv(lambda e, j4=j4, ci=ci, col=col: e.tensor_copy(out=ccol[:, j4, ci:ci + 1], in_=tabc[:, col:col + 1]), [tabc], [ccol])
                        dv(lambda e, j4=j4, ci=ci, col=col: e.tensor_copy(out=scol[:, j4, ci:ci + 1], in_=tabs[:, col:col + 1]), [tabs], [scol])
                    dv(lambda e, j=j, j4=j4: e.tensor_scalar(out=tq[:, 0:1], in0=scol[:, j4, 1:2], scalar1=sinv[:, j:j + 1], scalar2=None,
                                                             op0=ALU.mult), [scol, sinv], [tq])
                    dv(lambda e, j=j, j4=j4: e.scalar_tensor_tensor(out=cN[:, j:j + 1], in0=ccol[:, j4, 1:2], scalar=cosv[:, j:j + 1],
                                                                    in1=tq[:, 0:1], op0=ALU.mult, op1=ALU.subtract), [ccol, cosv, tq], [cN])
                    dv(lambda e, j=j, j4=j4: e.tensor_scalar(out=tq[:, 1:2], in0=ccol[:, j4, 1:2], scalar1=sinv[:, j:j + 1], scalar2=None,
                                                             op0=ALU.mult), [ccol, sinv], [tq])
                    dv(lambda e, j=j, j4=j4: e.scalar_tensor_tensor(out=sN[:, j:j + 1], in0=scol[:, j4, 1:2], scalar=cosv[:, j:j + 1],
                                                                    in1=tq[:, 1:2], op0=ALU.mult, op1=ALU.add), [scol, cosv, tq], [sN])
                for (kind, idx, tok0, n) in units:
                    for j4 in range(4):
                        j = a * 4 + j4
                        c_f, sn_f = tcb[:, j4, 0:n], tsb[:, j4, 0:n]
                        kb = ucnt[0] % 2
                        ucnt[0] += 1
                        tD, tP, wb, wri, zb, bsb = tD_[kb], tP_[kb], wb_[kb], wri_[kb], zb_[kb], bsb_[kb]
                        for p0 in range(0, n, 512):
                            pn = min(512, n - p0)
                            sl = slice(p0, p0 + pn) if d == 0 else slice(n - p0 - pn, n - p0)
                            for ri in range(2):
                                pb = ps()
                                mm_group(pb, pb[:, 0:pn], [(BT[:, j, ri, :], uS[:, a, tok0 + p0:tok0 + p0 + pn])], [BT, uS])
                                src_ = pb[:, 0:pn] if d == 0 else pb[:, 0:pn][:, ::-1]
                                s.op("act", lambda e, ri=ri, sl=sl, src_=src_: e.copy(out=bsb[ri][:, sl], in_=src_), reads=[pb], writes=[bsb[ri]])
                        br_, bi_ = bsb[0][:, 0:n], bsb[1][:, 0:n]
                        dv(lambda e: e.tensor_tensor(out=tD[0][:, 0:n], in0=c_f, in1=br_, op=ALU.mult), [tcb, bsb[0]], [tD[0]])
                        dv(lambda e: e.tensor_tensor(out=tD[1][:, 0:n], in0=sn_f, in1=bi_, op=ALU.mult), [tsb, bsb[1]], [tD[1]])
                        dv(lambda e: e.tensor_tensor(out=wb[0][:, 0:n], in0=tD[0][:, 0:n], in1=tD[1][:, 0:n], op=ALU.add),
                           [tD[0], tD[1]], [wb[0]])
                        dv(lambda e: e.tensor_tensor(out=tP[0][:, 0:n], in0=c_f, in1=bi_, op=ALU.mult), [tcb, bsb[1]], [tP[0]])
                        dv(lambda e: e.tensor_tensor(out=tP[1][:, 0:n], in0=sn_f, in1=br_, op=ALU.mult), [tsb, bsb[0]], [tP[1]])
                        dv(lambda e: e.tensor_tensor(out=wb[1][:, 0:n], in0=tP[0][:, 0:n], in1=tP[1][:, 0:n], op=ALU.subtract),
                           [tP[0], tP[1]], [wb[1]])
                        for ri in range(2):
                            if kind == "p":
                                ini = 0.0
                                rds = [rhoT, wb[ri]]
                            else:
                                ini = init[:, j, ri:ri + 1]
                                rds = [rhoT, wb[ri], init]
                            s.op("dve", lambda e, ri=ri, ini=ini, j4=j4: e.tensor_tensor_scan(
                                out=wri[ri][:, 0:n], data0=rhoT[:, j4, 0:n], data1=wb[ri][:, 0:n], initial=ini,
                                op0=ALU.mult, op1=ALU.add), reads=rds, writes=[wri[ri]])
                            s.op("act", lambda e, ri=ri: e.copy(out=zb[ri][:, 0:n], in_=wri[ri][:, 0:n]), reads=[wri[ri]], writes=[zb[ri]])
                        if kind == "s" and idx < 3:
                            dv(lambda e, j=j: e.tensor_scalar(out=tq[:, 0:1], in0=wri[1][:, n - 1:n], scalar1=sN[:, j:j + 1], scalar2=None,
                                                              op0=ALU.mult), [wri[1], sN], [tq])
                            dv(lambda e, j=j: e.tensor_scalar(out=tq[:, 1:2], in0=wri[0][:, n - 1:n], scalar1=sN[:, j:j + 1], scalar2=None,
                                                              op0=ALU.mult), [wri[0], sN], [tq])
                            dv(lambda e, j=j: e.scalar_tensor_tensor(out=init[:, j, 0:1], in0=wri[0][:, n - 1:n], scalar=cN[:, j:j + 1],
                                                                     in1=tq[:, 0:1], op0=ALU.mult, op1=ALU.subtract), [wri[0], cN, tq], [init])
                            dv(lambda e, j=j: e.scalar_tensor_tensor(out=init[:, j, 1:2], in0=wri[1][:, n - 1:n], scalar=cN[:, j:j + 1],
                                                                     in1=tq[:, 1:2], op0=ALU.mult, op1=ALU.add), [wri[1], cN, tq], [init])
                        if kind == "p":
                            fb = FIN[idx]
                            cl, sl_ = ccol[:, j4, 0:1], scol[:, j4, 0:1]
                            zrl, zil = wri[0][:, n - 1:n], wri[1][:, n - 1:n]
                            dv(lambda e: e.tensor_tensor(out=fq[:, 0:1], in0=sl_, in1=zil, op=ALU.mult), [scol, wri[1]], [fq])
                            dv(lambda e: e.tensor_tensor(out=fq[:, 1:2], in0=cl, in1=zrl, op=ALU.mult), [ccol, wri[0]], [fq])
                            dv(lambda e, fb=fb, j=j: e.tensor_tensor(out=fb[:, j, 0:1], in0=fq[:, 1:2], in1=fq[:, 0:1], op=ALU.subtract), [fq], [fb])
                            dv(lambda e: e.tensor_tensor(out=fq[:, 2:3], in0=sl_, in1=zrl, op=ALU.mult), [scol, wri[0]], [fq])
                            dv(lambda e: e.tensor_tensor(out=fq[:, 3:4], in0=cl, in1=zil, op=ALU.mult), [ccol, wri[1]], [fq])
                            dv(lambda e, fb=fb, j=j: e.tensor_tensor(out=fb[:, j, 1:2], in0=fq[:, 2:3], in1=fq[:, 3:4], op=ALU.add), [fq], [fb])
                        zr_, zi_ = zb[0][:, 0:n], zb[1][:, 0:n]
                        sr_o = Sb[j4][0][:, 0:n] if d == 0 else Sb[j4][0][:, 0:n][:, ::-1]
                        si_o = Sb[j4][1][:, 0:n] if d == 0 else Sb[j4][1][:, 0:n][:, ::-1]
                        dv(lambda e: e.tensor_tensor(out=tD[0][:, 0:n], in0=c_f, in1=zr_, op=ALU.mult), [tcb, zb[0]], [tD[0]])
                        dv(lambda e: e.tensor_tensor(out=tD[1][:, 0:n], in0=sn_f, in1=zi_, op=ALU.mult), [tsb, zb[1]], [tD[1]])
                        dv(lambda e, sr_o=sr_o: e.tensor_tensor(out=sr_o, in0=tD[0][:, 0:n], in1=tD[1][:, 0:n], op=ALU.subtract),
                           [tD[0], tD[1]], [Sb[j4][0]])
                        dv(lambda e: e.tensor_tensor(out=tP[0][:, 0:n], in0=sn_f, in1=zr_, op=ALU.mult), [tsb, zb[0]], [tP[0]])
                        dv(lambda e: e.tensor_tensor(out=tP[1][:, 0:n], in0=c_f, in1=zi_, op=ALU.mult), [tcb, zb[1]], [tP[1]])
                        dv(lambda e, si_o=si_o: e.tensor_tensor(out=si_o, in0=tP[0][:, 0:n], in1=tP[1][:, 0:n], op=ALU.add),
                           [tP[0], tP[1]], [Sb[j4][1]])
                    for p0 in range(0, n, 512):
                        pn = min(512, n - p0)
                        pb = ps()
                        pairs = []
                        rd = [CT]
                        for j4 in range(4):
                            for ri in range(2):
                                pairs.append((CT[:, a * 4 + j4, ri, :], Sb[j4][ri][:, p0:p0 + pn]))
                                rd.append(Sb[j4][ri])
                        mm_group(pb, pb[:, 0:pn], pairs, rd)
                        ysl = Y[:, a, tok0 + p0:tok0 + p0 + pn]
                        if d == 0:
                            dv(lambda e, pb=pb, ysl=ysl, pn=pn, a=a, p0=p0, tok0=tok0: e.scalar_tensor_tensor(
                                out=ysl, in0=uS[:, a, tok0 + p0:tok0 + p0 + pn], scalar=dco[:, a:a + 1], in1=pb[:, 0:pn],
                                op0=ALU.mult, op1=ALU.add), [uS, dco, pb], [Y])
                        else:
                            dv(lambda e, pb=pb, ysl=ysl, pn=pn: e.tensor_tensor(out=ysl, in0=ysl, in1=pb[:, 0:pn], op=ALU.add),
                               [Y, pb], [Y])
            for sq in range(4):
                s.dma("sp", _ap(nst)[sq, l, d].rearrange("(j q) r -> q j r", q=128), FIN[sq][:], reads=[FIN[sq]], writes=[nst])
            s.barrier()
            s.release(m2)
        m3 = s.mark()
        tD = [s.sbuf("tDg%d" % i, [128, 512], F32) for i in range(2)]
        tP = [s.sbuf("tPg%d" % i, [128, 512], F32) for i in range(2)]
        Wg = s.sbuf("Wg", [128, 2, 256], BF16)
        s.dma("pool", Wg[:], _ap(glu_w)[l].rearrange("(a p) f -> p a f", p=128), writes=[Wg])
        gb = s.sbuf("gb", [128, 2], F32)
        s.dma("sp", gb[:], _ap(glu_b)[l, :].rearrange("(a p) -> p a", p=128), writes=[gb])
        gel = [s.sbuf("gel%d" % a, [128, 512], F32) for a in range(2)]
        gelb = [s.sbuf("gelb%d" % a, [128, 512], BF16) for a in range(2)]
        sig = s.sbuf("sig", [128, 512], F32)
        yo = [s.sbuf("yo%d" % a, [128, 512], BF16) for a in range(2)]
        for blk in range(NB):
            tok = slice(blk * 512, (blk + 1) * 512)
            for a in range(2):
                ysl = Y[:, a, tok]
                s.op("act", lambda e, ysl=ysl: e.activation(out=tD[0][:, 0:512], in_=ysl, func=AF.Square), reads=[Y], writes=[tD[0]])
                dv(lambda e: e.tensor_scalar(out=tD[0][:, 0:512], in0=tD[0][:, 0:512], scalar1=0.044715, scalar2=1.0,
                                             op0=ALU.mult, op1=ALU.add), [tD[0]], [tD[0]])
                dv(lambda e, ysl=ysl: e.tensor_tensor(out=tD[1][:, 0:512], in0=tD[0][:, 0:512], in1=ysl, op=ALU.mult), [tD[0], Y], [tD[1]])
                s.op("act", lambda e: e.activation(out=tP[0][:, 0:512], in_=tD[1][:, 0:512], func=AF.Sigmoid, scale=1.5957691216057308),
                     reads=[tD[1]], writes=[tP[0]])
                dv(lambda e, a=a, ysl=ysl: e.tensor_tensor(out=gel[a][:], in0=tP[0][:, 0:512], in1=ysl, op=ALU.mult), [tP[0], Y], [gel[a]])
                s.op("act", lambda e, a=a: e.copy(out=gelb[a][:], in_=gel[a][:]), reads=[gel[a]], writes=[gelb[a]])
            for a in range(2):
                pb = ps()
                mm_group(pb, pb[:], [(Wg[:, a2, a * 128:(a + 1) * 128], gelb[a2][:]) for a2 in range(2)], [Wg, gelb[0], gelb[1]])
                s.op("act", lambda e, pb=pb, a=a: e.activation(out=sig[:], in_=pb[:], func=AF.Sigmoid, bias=gb[:, a:a + 1]),
                     reads=[pb, gb], writes=[sig])
                dv(lambda e, a=a: e.tensor_tensor(out=yo[a][:], in0=sig[:], in1=gel[a][:], op=ALU.mult), [sig, gel[a]], [yo[a]])
                s.dma("sp", _ap(ymix_d)[a * 128:(a + 1) * 128, tok], yo[a][:], reads=[yo[a]], writes=[ymix_d])
        s.barrier()
        s.release(m0)

    def attn_part(l):
        m0 = s.mark()
        qP = s.sbuf("qP", [128, 4, NP_TOK], BF16)
        kP = s.sbuf("kP", [128, 4, NP_TOK], BF16)
        for t4 in range(4):
            s.dma("sp", qP[:, t4, :], _ap(q_d)[t4 * 128:(t4 + 1) * 128, 0:NP_TOK], reads=[q_d], writes=[qP])
            s.dma("sp", kP[:, t4, :], _ap(k_d)[t4 * 128:(t4 + 1) * 128, 0:NP_TOK], reads=[k_d], writes=[kP])
        vP = s.sbuf("vP", [128, 8, 512], BF16)
        s.dma("sp", vP[:], _ap(v_d)[0:NP_TOK, :].rearrange("(t p) f -> p t f", p=128), reads=[v_d], writes=[vP])
        Eb = [s.sbuf("Eb%d" % i, [128, 2, 256], BF16) for i in range(2)]
        rdn = [s.sbuf("rdn%d" % i, [128, 256], F32) for i in range(2)]
        yat = [s.sbuf("yat%d" % i, [128, 256], BF16) for i in range(2)]
        n_it = 0
        for sq in range(4):
            for hp in range(4):
                ya = yat[(sq * 4 + hp) % 2]
                for hh in range(2):
                    pr = slice(hh * 64, (hh + 1) * 64)
                    E = Eb[n_it % 2]
                    rd_ = rdn[n_it % 2]
                    n_it += 1
                    for kt2 in range(2):
                        pb = ps()
                        mm_group(pb, pb[:, 0:256],
                                 [(kP[pr, hp, sq * 256 + kt2 * 128: sq * 256 + (kt2 + 1) * 128], qP[pr, hp, sq * 256:(sq + 1) * 256])],
                                 [kP, qP])
                        s.op("act", lambda e, E=E, pb=pb, kt2=kt2: e.activation(out=E[:, kt2, :], in_=pb[:, 0:256], func=AF.Exp, scale=0.125),
                             reads=[pb], writes=[E])
                    pn_ = ps()
                    mm_group(pn_, pn_[:, 0:256], [(vP[:, sq * 2 + kt2, hp * 128:(hp + 1) * 128], E[:, kt2, :]) for kt2 in range(2)], [vP, E])
                    pd_ = ps()
                    mm_group(pd_, pd_[:, 0:256], [(ones_bf[:], E[:, kt2, :]) for kt2 in range(2)], [ones_bf, E])
                    s.op("dve", lambda e, rd_=rd_, pd_=pd_, pr=pr: e.reciprocal(out=rd_[pr, :], in_=pd_[pr, 0:256]), reads=[pd_], writes=[rd_])
                    s.op("dve", lambda e, ya=ya, pn_=pn_, rd_=rd_, pr=pr: e.tensor_tensor(out=ya[pr, :], in0=pn_[pr, 0:256], in1=rd_[pr, :],
                                                                                        op=ALU.mult), reads=[pn_, rd_], writes=[ya])
                s.dma("sp", _ap(ymix_d)[256 + hp * 128:256 + (hp + 1) * 128, sq * 256:(sq + 1) * 256], ya[:], reads=[ya], writes=[ymix_d])
        s.barrier()
        s.release(m0)
        m0 = s.mark()
        BTt = s.sbuf("BTt", [128, 8, 15, 64], BF16)
        identb = s.sbuf("identb", [128, 128], BF16)
        s.op("dve", lambda e: e.tensor_copy(out=identb[:], in_=ident[:]), reads=[ident], writes=[identb])
        m1 = s.mark()
        oh = s.sbuf("oh", [31, 64, 128], F32)
        s.dma("sp", oh[:], _ap(c_oh).rearrange("d (k q) -> d k q", q=128), writes=[oh])
        msk = s.sbuf("msk", [128, 64], F32)
        s.dma("sp", msk[:], _ap(c_mask), writes=[msk])
        rpT = s.sbuf("rpT", [31, 128], F32)
        s.op("dve", lambda e: e.memset(rpT[:], 0.0), writes=[rpT])
        s.dma("sp", rpT[:, 0:120], _ap(rpb)[l].rearrange("x d -> d x"), reads=[rpT], writes=[rpT])
        BTf = BTt[:].rearrange("p h d k -> p (h d) k")
        for k4 in range(16):
            pb = ps()
            for ki in range(4):
                kk = k4 * 4 + ki
                mm_group(pb, pb[:, ki * 128:ki * 128 + 128], [(oh[:, kk, :], rpT[:])], [oh, rpT])
            for ki in range(4):
                kk = k4 * 4 + ki
                s.op("dve", lambda e, pb=pb, ki=ki, kk=kk: e.tensor_scalar(out=BTf[:, :, kk], in0=pb[:, ki * 128:ki * 128 + 120],
                                                                           scalar1=msk[:, kk:kk + 1], scalar2=8.0, op0=ALU.add, op1=ALU.mult),
                     reads=[pb, msk], writes=[BTt])
        s.barrier()
        s.release(m1)
        if int(os.environ.get("MK_ATT_STOP", "9")) <= 2:
            return
        qS = s.sbuf("qS", [128, 4, NS_TOK], BF16)
        kS = s.sbuf("kS", [128, 4, NS_TOK], BF16)
        for t4 in range(4):
            s.dma("sp", qS[:, t4, :], _ap(q_d)[t4 * 128:(t4 + 1) * 128, NP_TOK:NT], reads=[q_d], writes=[qS])
            s.dma("sp", kS[:, t4, :], _ap(k_d)[t4 * 128:(t4 + 1) * 128, NP_TOK:NT], reads=[k_d], writes=[kS])
        vS = s.sbuf("vS", [128, 64, 512], BF16)
        s.dma("sp", vS[0:64, :, :], _ap(v_d)[NP_TOK:NT, :].rearrange("(r c) f -> c r f", c=64), reads=[v_d], writes=[vS])
        s.dma("sp", vS[64:128, 0:63, :], _ap(v_d)[NP_TOK + 64:NT, :].rearrange("(r c) f -> c r f", c=64), reads=[v_d], writes=[vS])
        s.op("dve", lambda e: e.memset(vS[64:128, 63:64, :], 0.0), reads=[vS], writes=[vS])
        ck32 = s.sbuf("ck32", [128, 2, 512], F32)
        s.dma("sp", ck32[:], _ap(ck)[l].rearrange("(t p) f -> p t f", p=128), writes=[ck32])
        kC = s.sbuf("kC", [128, 4, 256], BF16)
        for hp in range(4):
            pb = ps()
            s.group("pe", [lambda e, pb=pb, t=t, hp=hp: e.transpose(pb[:, t * 128:(t + 1) * 128], ck32[:, t, hp * 128:(hp + 1) * 128], ident[:])
                           for t in range(2)], reads=[ck32, ident], writes=[pb])
            s.op("act", lambda e, pb=pb, hp=hp: e.copy(out=kC[:, hp, :], in_=pb[:, 0:256]), reads=[pb], writes=[kC])
        vC = s.sbuf("vC", [128, 2, 512], BF16)
        s.dma("pool", vC[:], _ap(cv)[l].rearrange("(t p) f -> p t f", p=128), writes=[vC])
        NBUF = 3
        EC = [s.sbuf("EC%d" % i, [128, 6, 2, 64], BF16) for i in range(NBUF)]
        rdw = [s.sbuf("rdw%d" % i, [128, 128], F32) for i in range(NBUF)]
        YN = [s.sbuf("YN%d" % i, [128, 4, 512], BF16) for i in range(2)]
        it = 0
        for r in range(int(os.environ.get("MK_NA_ROWS", "64"))):
            rs = min(max(r - 4, 0), 56)
            dr0 = rs - r + 7
            yn = YN[(r // 8) % 2]
            for hp in range(4):
                k2 = it % NBUF
                it += 1
                E_ = EC[k2]
                for hh in range(2):
                    pr = slice(hh * 64, (hh + 1) * 64)
                    h = 2 * hp + hh
                    qrow = qS[pr, hp, r * 64:(r + 1) * 64]
                    idb = identb[pr, hh * 64:(hh + 1) * 64]
                    pw = ps()
                    fw = []
                    for m in range(4):
                        fw.append(lambda e, pw=pw, m=m, pr=pr, qrow=qrow: e.matmul(
                            pw[:, m * 64:(m + 1) * 64], kS[pr, hp, (rs + 2 * m) * 64:(rs + 2 * m + 2) * 64], qrow, start=True, stop=False))
                        fw.append(lambda e, pw=pw, m=m, pr=pr, h=h, idb=idb: e.matmul(
                            pw[:, m * 64:(m + 1) * 64], BTt[pr, h, dr0 + 2 * m:dr0 + 2 * m + 2, :].rearrange("p d k -> p (d k)"), idb,
                            start=False, stop=True))
                    for t in range(2):
                        fw.append(lambda e, pw=pw, t=t, pr=pr, qrow=qrow: e.matmul(
                            pw[:, 256 + t * 64:256 + (t + 1) * 64], kC[pr, hp, t * 128:(t + 1) * 128], qrow, start=True, stop=True))
                    s.group("pe", fw, reads=[kS, kC, qS, BTt, identb], writes=[pw])
                    s.op("act", lambda e, E_=E_, pw=pw, hh=hh: e.activation(
                        out=E_[:, :, hh, :], in_=pw[:, 0:384].rearrange("p (m q) -> p m q", m=6), func=AF.Exp, scale=0.125),
                        reads=[pw], writes=[E_])
                pnd = ps()
                rhs6 = [E_[:, m, :, :].rearrange("p h q -> p (h q)") for m in range(6)]
                lv = [vS[:, rs + 2 * m, hp * 128:(hp + 1) * 128] for m in range(4)] + [vC[:, t, hp * 128:(hp + 1) * 128] for t in range(2)]
                mm_group(pnd, pnd[:, 0:128], [(lv[m], rhs6[m]) for m in range(6)], [vS, vC, E_])
                mm_group(pnd, pnd[:, 128:256], [(ones_bf[:], rhs6[m]) for m in range(6)], [ones_bf, E_])
                rd_ = rdw[k2]
                s.op("dve", lambda e, rd_=rd_, pnd=pnd: e.reciprocal(out=rd_[:], in_=pnd[:, 128:256]), reads=[pnd], writes=[rd_])
                for hh in range(2):
                    pr = slice(hh * 64, (hh + 1) * 64)
                    s.op("dve", lambda e, yn=yn, pnd=pnd, rd_=rd_, pr=pr, hp=hp, r=r, hh=hh: e.tensor_tensor(
                        out=yn[pr, hp, (r % 8) * 64:(r % 8 + 1) * 64], in0=pnd[pr, hh * 64:(hh + 1) * 64], in1=rd_[pr, hh * 64:(hh + 1) * 64],
                        op=ALU.mult), reads=[pnd, rd_], writes=[yn])
            if r % 8 == 7:
                tok0 = NP_TOK + (r // 8) * 512
                for hp in range(4):
                    s.dma("sp", _ap(ymix_d)[256 + hp * 128:256 + (hp + 1) * 128, tok0:tok0 + 512], yn[:, hp, :], reads=[yn], writes=[ymix_d])
        s.barrier()
        s.release(m0)

    def gate_part(l):
        m0 = s.mark()
        ws32 = s.sbuf("ws32", [128, 4, 128], F32)
        s.dma("sp", ws32[:], _ap(gm_ws)[l].rearrange("g i j -> i g j"), writes=[ws32])
        wsT = s.sbuf("wsT", [128, 4, 128], BF16)
        pb = ps()
        s.group("pe", [lambda e, g=g: e.transpose(pb[:, g * 128:(g + 1) * 128], ws32[:, g, :], ident[:]) for g in range(4)],
                reads=[ws32, ident], writes=[pb])
        s.op("act", lambda e: e.copy(out=wsT[:].rearrange("p g i -> p (g i)"), in_=pb[:]), reads=[pb], writes=[wsT])
        BS = s.sbuf("BS", [128, 2, 128], F32)
        for g in range(4):
            s.dma("sp", BS[(g % 2) * 64:(g % 2 + 1) * 64, g // 2, :], _ap(gm_bs)[l, g:g + 1, :].to_broadcast([64, 128]), writes=[BS])
        ug = [s.sbuf("ug%d" % i, [128, 2, 512], BF16) for i in range(2)]
        vAl = [s.sbuf("vAl%d" % i, [128, 4, 256], BF16) for i in range(2)]
        vBl = [s.sbuf("vBl%d" % i, [128, 4, 256], BF16) for i in range(2)]
        tg = [s.sbuf("tg%d" % i, [128, 128], F32) for i in range(2)]
        yg = [s.sbuf("yg%d" % i, [128, 2, 512], BF16) for i in range(2)]
        it = 0
        def fetch(blk):
            k2 = blk % 2
            tok = slice(blk * 512, (blk + 1) * 512)
            for a in range(2):
                s.dma("sp", ug[k2][:, a, :], _ap(ug_d)[a * 128:(a + 1) * 128, tok], reads=[ug_d], writes=[ug[k2]])
            s.dma("sp", vAl[k2][:], _ap(va_d)[tok, :].rearrange("(t p) f -> p t f", p=128), reads=[va_d], writes=[vAl[k2]])
            s.dma("sp", vBl[k2][:], _ap(vb_d)[tok, :].rearrange("(t p) f -> p t f", p=128), reads=[vb_d], writes=[vBl[k2]])

        fetch(0)
        for blk in range(NB):
            k2 = blk % 2
            tok = slice(blk * 512, (blk + 1) * 512)
            if blk + 1 < NB:
                fetch(blk + 1)
            for t in range(4):
                for a in range(2):
                    pq = ps()
                    mm_group(pq, pq[:, 0:128], [(vAl[k2][:, t, a * 128:(a + 1) * 128], wsT[:, 2 * a, :]),
                                                (vBl[k2][:, t, a * 128:(a + 1) * 128], wsT[:, 2 * a + 1, :])], [vAl[k2], vBl[k2], wsT])
                    tt_ = tg[it % 2]
                    it += 1
                    s.op("dve", lambda e, tt_=tt_, pq=pq, a=a: e.tensor_tensor(out=tt_[:], in0=pq[:, 0:128], in1=BS[:, a, :], op=ALU.add),
                         reads=[pq, BS], writes=[tt_])
                    s.op("dve", lambda e, tt_=tt_, a=a, t=t, k2=k2: e.tensor_tensor(
                        out=yg[k2][:, a, t * 128:(t + 1) * 128], in0=tt_[:], in1=ug[k2][:, a, t * 128:(t + 1) * 128], op=ALU.mult),
                        reads=[tt_, ug[k2]], writes=[yg[k2]])
            for a in range(2):
                s.dma("sp", _ap(ymix_d)[768 + a * 128:768 + (a + 1) * 128, tok], yg[k2][:, a, :], reads=[yg[k2]], writes=[ymix_d])
        s.barrier()
        s.release(m0)

    def mix_phase(l):
        ssm_part(l)
        attn_part(l)
        gate_part(l)

    PH = os.environ.get("MK_PHASES", "all")
    if PH != "all":
        for name in PH.split(","):
            {"adaln": setup_adaln, "ffn": lambda: ffn_phase(0, 0, f1_in, f1_out, True, False), "proj": lambda: proj_phase(0),
             "ssm": lambda: ssm_part(0), "attn": lambda: attn_part(0), "gate": lambda: gate_part(0),
             "outproj": lambda: outproj_phase(0)}[name]()
        s.barrier()
        s.release(0)
        ctx.__exit__(None, None, None)
        return nc
    mW = s.mark()
    W = alloc_ffn_w()
    setup_adaln(after_dma=lambda: issue_ffn_w(W, 0, f1_in, f1_out))
    for l in range(DEPTH):
        if l == 0:
            ffn_phase(l, 0, f1_in, f1_out, first=True, last=False, W=W)
            s.release(mW)
        else:
            ffn_phase(l, 0, f1_in, f1_out, first=False, last=False)
        proj_phase(l)
        ssm_part(l)
        attn_part(l)
        mW = s.mark()
        W = alloc_ffn_w()
        mWO = s.mark()
        WO = s.sbuf("WO", [128, 8, 1024], BF16)
        s.dma("pool", WO[:], _ap(w_out)[l].rearrange("(kt p) f -> p kt f", p=128), writes=[WO])
        issue_ffn_w(W, l, f2_in, f2_out)
        gate_part(l)
        outproj_phase(l, WO=WO)
        s.release(mWO)
        ffn_phase(l, 2, f2_in, f2_out, first=False, last=(l == DEPTH - 1), W=W)
        s.release(mW)
    outs = [yp, ys, nk, nv, nst]
    s.barrier()
    s.release(0)
    ctx.__exit__(None, None, None)
    return nc


_NC_CACHE = {}


def _consts():
    ident = np.eye(128, dtype=np.float32)
    kc = np.arange(64)[:, None]
    qc = np.arange(64)[None, :]
    dc = kc - qc + 15
    oh = np.zeros((31, 64, 128), np.float32)
    for q in range(64):
        for k in range(64):
            if 0 <= dc[k, q] <= 30:
                oh[dc[k, q], k, q] = 1.0
                oh[dc[k, q], k, 64 + q] = 1.0
    cs = np.clip(np.arange(64) - 8, 0, 48)
    win = (kc >= cs[None, :]) & (kc < cs[None, :] + 16)
    mask = np.where(win, 0.0, NEG).astype(np.float32)
    iota = np.tile(np.arange(1024, dtype=np.float32)[None, :], (128, 1))
    return {"c_ident": ident, "c_oh": oh.reshape(31, 8192), "c_mask": np.ascontiguousarray(np.concatenate([mask.T, mask.T], axis=0)), "c_iota": iota}


def kernel(**inputs):
    debug = bool(int(os.environ.get("MK_DEBUG", "0")))
    key = ("nc", debug)
    if key not in _NC_CACHE:
        _NC_CACHE[key] = build_program(debug=debug)
    nc = _NC_CACHE[key]
    f = lambda a: np.ascontiguousarray(np.asarray(a, dtype=np.float32))
    x_prompt, x_sample = f(inputs["x_prompt"]), f(inputs["x_sample"])
    c, c_ctx = f(inputs["c"]), f(inputs["c_ctx"])
    cache_k, cache_v, state_ssm = f(inputs["cache_k"]), f(inputs["cache_v"]), f(inputs["state_ssm"])
    shared = {}
    for name in ("w_ada", "b_ada", "norm_ffn1", "norm_mix", "norm_ffn2", "ffn1_w_in", "ffn1_w_out", "ffn2_w_in",
                 "ffn2_w_out", "w_in", "w_out", "ssm_d", "ssm_glu_w", "ssm_glu_b", "na_q_norm", "na_k_norm",
                 "gm_ws", "gm_bs"):
        shared[name] = f(inputs[name])
    shared["ssm_lambda_re"] = f(inputs["ssm_lambda_re"]).reshape(DEPTH, 2, 1024)
    shared["ssm_lambda_im"] = f(inputs["ssm_lambda_im"]).reshape(DEPTH, 2, 1024)
    shared["ssm_log_dt"] = f(inputs["ssm_log_dt"])
    shared["ssm_b_re"] = f(inputs["ssm_b_re"]).reshape(DEPTH, 2, 1024, 16)
    shared["ssm_b_im"] = f(inputs["ssm_b_im"]).reshape(DEPTH, 2, 1024, 16)
    shared["ssm_c_re"] = f(inputs["ssm_c_re"]).reshape(DEPTH, 2, 256, 64)
    shared["ssm_c_im"] = f(inputs["ssm_c_im"]).reshape(DEPTH, 2, 256, 64)
    shared["na_rpb"] = f(inputs["na_rpb"]).reshape(DEPTH, 120, 31)
    shared.update(_consts())
    in_maps = []
    for core in range(8):
        b = core // 2
        m = dict(shared)
        m["xp"] = x_prompt[4 * core:4 * core + 4].reshape(NP_TOK, D)
        m["xs"] = x_sample[b]
        m["cond"] = np.stack([c_ctx, c[b]], axis=0)
        m["ck"] = cache_k[b].reshape(DEPTH, 256, 512)
        m["cv"] = cache_v[b].reshape(DEPTH, 256, 512)
        m["sst"] = state_ssm[b].reshape(DEPTH, 2, 1024, 2)
        in_maps.append(m)
    res = run_bass_kernel_spmd(nc, in_maps, core_ids=list(range(8)))
    R = res.results
    if debug:
        kernel.last_results = R
    y_prompt = np.concatenate([R[i]["yp"].reshape(4, 256, D) for i in range(8)], axis=0)
    y_sample = np.stack([np.concatenate([R[2 * b]["ys"][:2048], R[2 * b + 1]["ys"][2048:]], axis=0) for b in range(4)], axis=0)
    new_k = np.concatenate([R[i]["nk"].reshape(4, DEPTH, 256, 8, 64) for i in range(8)], axis=0)
    new_v = np.concatenate([R[i]["nv"].reshape(4, DEPTH, 256, 8, 64) for i in range(8)], axis=0)
    new_s = np.concatenate([R[i]["nst"].reshape(4, DEPTH, 2, 16, 64, 2) for i in range(8)], axis=0)
    return (y_prompt.astype(np.float32), y_sample.astype(np.float32), new_k.astype(np.float32),
            new_v.astype(np.float32), new_s.astype(np.float32))
```

```python
import math
import os
import numpy as np
import concourse.bass as bass
import concourse.mybir as mybir
from concourse.bass_utils import run_bass_kernel_spmd

F32 = mybir.dt.float32
BF16 = mybir.dt.bfloat16
AF = mybir.ActivationFunctionType
ALU = mybir.AluOpType
AX = mybir.AxisListType

D = 1024
DFF = 2816
DEPTH = 2
NP_TOK = 1024
NS_TOK = 4096
NT = NP_TOK + NS_TOK
NB = NT // 512
INC = 2304
TWO_PI = 2.0 * math.pi
NEG = -30000.0


class Buf:
    __slots__ = ("t", "w", "r", "name")

    def __init__(self, t, name=""):
        self.t = t
        self.w = None
        self.r = {}
        self.name = name

    def __getitem__(self, idx):
        return self.t[idx]


class Sched:
    def __init__(self, nc, n_dma_sems=40):
        self.nc = nc
        self.stack = []
        self.eng = {"pe": nc.tensor, "act": nc.scalar, "dve": nc.vector, "pool": nc.gpsimd, "sp": nc.sync}
        self.sems = {}
        self.cnt = {}
        for k in ("pe", "act", "dve", "pool"):
            self.sems[k] = self._sem("s_" + k)
            self.cnt[k] = 0
        self.dma_keys = []
        for i in range(n_dma_sems):
            k = "d%d" % i
            self.sems[k] = self._sem("s_" + k)
            self.cnt[k] = 0
            self.dma_keys.append(k)
        self.dma_rr = 0
        self.seen = {e: {} for e in self.eng}
        self.ninst = 0

    def _sem(self, name):
        cm = self.nc.semaphore(name)
        h = cm.__enter__()
        self.stack.append(cm)
        return h

    def mark(self):
        return len(self.stack)

    def release(self, mark):
        while len(self.stack) > mark:
            self.stack.pop().__exit__(None, None, None)

    def sbuf(self, name, shape, dtype):
        self.uid = getattr(self, "uid", 0) + 1
        name = "%s_u%d" % (name, self.uid)
        cm = self.nc.sbuf_tensor(name, list(shape), dtype)
        t = cm.__enter__()
        self.stack.append(cm)
        return Buf(t, name)

    def psum(self, name, shape, dtype=F32):
        cm = self.nc.psum_tensor(name, list(shape), dtype)
        t = cm.__enter__()
        self.stack.append(cm)
        return Buf(t, name)

    def dram(self, name, shape, dtype, kind="Internal"):
        return Buf(self.nc.dram_tensor(name, list(shape), dtype, kind=kind), name)

    def _need(self, e, deps):
        seen = self.seen[e]
        todo = {}
        for (k, v) in deps:
            if seen.get(k, 0) >= v:
                continue
            if todo.get(k, 0) < v:
                todo[k] = v
        for k, v in todo.items():
            self.eng[e].wait_ge(self.sems[k], v)
            seen[k] = v
            self.ninst += 1

    @staticmethod
    def _deps(reads, writes):
        deps = []
        for b in reads:
            if b.w is not None:
                deps.append(b.w)
        for b in writes:
            if b.w is not None:
                deps.append(b.w)
            deps.extend(b.r.items())
        return deps

    def _commit(self, k, v, reads, writes):
        for b in reads:
            if b.r.get(k, 0) < v:
                b.r[k] = v
        for b in writes:
            b.w = (k, v)
            b.r = {}

    def op(self, e, fn, reads=(), writes=()):
        self._need(e, self._deps(reads, writes))
        inst = fn(self.eng[e])
        self.cnt[e] += 1
        inst.then_inc(self.sems[e], 1)
        self._commit(e, self.cnt[e], reads, writes)
        self.ninst += 1
        return inst

    def group(self, e, fns, reads=(), writes=()):
        self._need(e, self._deps(reads, writes))
        inst = None
        for fn in fns:
            inst = fn(self.eng[e])
            self.ninst += 1
        self.cnt[e] += 1
        inst.then_inc(self.sems[e], 1)
        self._commit(e, self.cnt[e], reads, writes)

    def dma(self, q, out_ap, in_ap, reads=(), writes=(), **kw):
        nsw = 8
        if q == "pool":
            self.sw_rr = (getattr(self, "sw_rr", -1) + 1) % nsw
            k = self.dma_keys[self.sw_rr]
        else:
            k = self.dma_keys[nsw + self.dma_rr]
            self.dma_rr = (self.dma_rr + 1) % (len(self.dma_keys) - nsw)
        deps = self._deps(reads, writes)
        if self.cnt[k] > 0:
            deps.append((k, self.cnt[k]))
        self._need(q, deps)
        inst = self.eng[q].dma_start(out=out_ap, in_=in_ap, **kw)
        self.cnt[k] += 16
        inst.then_inc(self.sems[k], 16)
        self._commit(k, self.cnt[k], reads, writes)
        self.ninst += 1
        return inst

    def barrier(self):
        allk = [(k, v) for k, v in self.cnt.items() if v > 0]
        for e in self.eng:
            self._need(e, allk)


def _ap(b):
    return b.t.ap()


def build_program(debug=False):
    nc = bass.Bass("TRN2", target_bir_lowering=False)
    s = Sched(nc)
    ctx = nc.allow_non_contiguous_dma(reason="small strided parameter loads")
    ctx.__enter__()

    def din(name, shape):
        return s.dram(name, shape, F32, kind="ExternalInput")

    def dout(name, shape):
        return s.dram(name, shape, F32, kind="ExternalOutput")

    xp = din("xp", [NP_TOK, D])
    xs = din("xs", [NS_TOK, D])
    cond = din("cond", [2, D])
    ck = din("ck", [DEPTH, 256, 512])
    cv = din("cv", [DEPTH, 256, 512])
    sst = din("sst", [DEPTH, 2, 1024, 2])
    w_ada = din("w_ada", [DEPTH, D, 9 * D])
    b_ada = din("b_ada", [DEPTH, 9 * D])
    norms = [din("norm_ffn1", [DEPTH, D]), din("norm_mix", [DEPTH, D]), din("norm_ffn2", [DEPTH, D])]
    f1_in = din("ffn1_w_in", [DEPTH, D, 2 * DFF])
    f1_out = din("ffn1_w_out", [DEPTH, DFF, D])
    f2_in = din("ffn2_w_in", [DEPTH, D, 2 * DFF])
    f2_out = din("ffn2_w_out", [DEPTH, DFF, D])
    w_in = din("w_in", [DEPTH, D, INC])
    w_out = din("w_out", [DEPTH, D, D])
    lam_re = din("ssm_lambda_re", [DEPTH, 2, 1024])
    lam_im = din("ssm_lambda_im", [DEPTH, 2, 1024])
    log_dt = din("ssm_log_dt", [DEPTH, 2, 16])
    b_re = din("ssm_b_re", [DEPTH, 2, 1024, 16])
    b_im = din("ssm_b_im", [DEPTH, 2, 1024, 16])
    c_re = din("ssm_c_re", [DEPTH, 2, 256, 64])
    c_im = din("ssm_c_im", [DEPTH, 2, 256, 64])
    ssm_d = din("ssm_d", [DEPTH, 256])
    glu_w = din("ssm_glu_w", [DEPTH, 256, 256])
    glu_b = din("ssm_glu_b", [DEPTH, 256])
    qn_g = din("na_q_norm", [DEPTH, 64])
    kn_g = din("na_k_norm", [DEPTH, 64])
    rpb = din("na_rpb", [DEPTH, 8 * 15, 31])
    gm_ws = din("gm_ws", [DEPTH, 4, 128, 128])
    gm_bs = din("gm_bs", [DEPTH, 4, 128])
    c_ident = din("c_ident", [128, 128])
    c_oh = din("c_oh", [31, 64 * 128])
    c_mask = din("c_mask", [128, 64])
    c_iota = din("c_iota", [128, 1024])
    c_hsel = din("c_hsel", [128, 2])
    yp = dout("yp", [NP_TOK, D])
    ys = dout("ys", [NS_TOK // 2, D])
    nk = dout("nk", [4, DEPTH, 256, 512])
    nv = dout("nv", [4, DEPTH, 256, 512])
    nst = dout("nst", [4, DEPTH, 2, 1024, 2])
    skind = "ExternalOutput" if debug else "Internal"
    xT_d = s.dram("xT_d", [D, NT], F32, kind=skind)
    ymix_d = s.dram("ymix_d", [D, NT], BF16, kind=skind)
    us_d = s.dram("us_d", [256, NT], BF16, kind=skind)
    q_d = s.dram("q_d", [512, NT], BF16, kind=skind)
    k_d = s.dram("k_d", [512, NT], BF16, kind=skind)
    v_d = s.dram("v_d", [NT, 512], BF16, kind=skind)
    ug_d = s.dram("ug_d", [256, NT], BF16, kind=skind)
    va_d = s.dram("va_d", [NT, 256], BF16, kind=skind)
    vb_d = s.dram("vb_d", [NT, 256], BF16, kind=skind)

    banks = [s.psum("bank%d" % i, [128, 512], F32) for i in range(8)]
    bank_rr = [0]

    def ps():
        b = banks[bank_rr[0]]
        bank_rr[0] = (bank_rr[0] + 1) % 8
        return b

    ident = s.sbuf("ident", [128, 128], F32)
    s.dma("sp", ident[:], _ap(c_ident), writes=[ident])
    ones_bf = s.sbuf("ones_bf", [128, 128], BF16)
    s.op("dve", lambda e: e.memset(ones_bf[:], 1.0), writes=[ones_bf])
    mean_bf = s.sbuf("mean_bf", [128, 128], BF16)
    s.op("dve", lambda e: e.memset(mean_bf[:], 1.0 / 1024.0), writes=[mean_bf])
    bd_bf = s.sbuf("bd_bf", [128, 128], BF16)
    s.op("dve", lambda e: e.memset(bd_bf[:], 0.0), writes=[bd_bf])
    s.op("dve", lambda e: e.memset(bd_bf[0:64, 0:64], 1.0 / 64.0), reads=[bd_bf], writes=[bd_bf])
    s.op("dve", lambda e: e.memset(bd_bf[64:128, 64:128], 1.0 / 64.0), reads=[bd_bf], writes=[bd_bf])
    pi_c = s.sbuf("pi_c", [128, 1], F32)
    s.op("dve", lambda e: e.memset(pi_c[:], math.pi), writes=[pi_c])
    hsel = s.sbuf("hsel", [128, 2], F32)
    s.dma("sp", hsel[:], _ap(c_hsel), writes=[hsel])
    eps6 = s.sbuf("eps6", [128, 1], F32)
    s.op("dve", lambda e: e.memset(eps6[:], 1e-6), writes=[eps6])
    eps5 = s.sbuf("eps5", [128, 1], F32)
    s.op("dve", lambda e: e.memset(eps5[:], 1e-5), writes=[eps5])

    def rsqrt(dst_b, dst_ap, src_b, src_ap, eps_b, scale=1.0):
        P = dst_ap.shape[0]
        s.op("act", lambda e: e.activation(out=dst_ap, in_=src_ap, func=AF.Sqrt, scale=scale, bias=eps_b[0:P, 0:1]),
             reads=[src_b, eps_b], writes=[dst_b])
        s.op("dve", lambda e: e.reciprocal(out=dst_ap, in_=dst_ap), reads=[dst_b], writes=[dst_b])
    Aco = s.sbuf("Aco", [128, DEPTH, 3, 2, 8], F32)
    Bco = s.sbuf("Bco", [128, DEPTH, 3, 2, 8], F32)
    Gco = s.sbuf("Gco", [128, DEPTH, 3, 2, 8], F32)

    def mm_group(out_buf, out_ap, pairs, reads):
        n = len(pairs)
        fns = []
        for i, (l, r) in enumerate(pairs):
            fns.append(lambda e, l=l, r=r, i=i: e.matmul(out_ap, l, r, start=(i == 0), stop=(i == n - 1)))
        s.group("pe", fns, reads=reads, writes=[out_buf])

    def setup_adaln(after_dma=None):
        m0 = s.mark()
        cs32 = s.sbuf("cs32", [128, 8, 2], F32)
        csb = s.sbuf("csb", [128, 8, 2], BF16)
        for c in range(2):
            s.dma("sp", cs32[:, :, c], _ap(cond)[c, :].rearrange("(kt p) -> p kt", p=128), writes=[cs32])
        s.op("act", lambda e: e.activation(out=csb[:], in_=cs32[:], func=AF.Silu), reads=[cs32], writes=[csb])
        gn = s.sbuf("gn", [128, DEPTH, 3, 8], F32)
        for i in range(3):
            for l in range(DEPTH):
                s.dma("sp", gn[:, l, i, :], _ap(norms[i])[l, :].rearrange("(kt p) -> p kt", p=128), writes=[gn])
        wa = [s.sbuf("wa%d" % i, [128, 8, 1024], BF16) for i in range(2)]
        badaT = s.sbuf("badaT", [128, 72], F32)
        modT = s.sbuf("modT", [128, 72, 2], F32)
        for l in range(DEPTH):
            s.dma("sp", badaT[:], _ap(b_ada)[l, :].rearrange("(ft p) -> p ft", p=128), writes=[badaT])
            pb = ps()
            for ch in range(9):
                w = wa[ch % 2]
                s.dma("pool", w[:], _ap(w_ada)[l, :, ch * 1024:(ch + 1) * 1024].rearrange("(kt p) f -> p kt f", p=128),
                      writes=[w])
                for f8 in range(8):
                    ft = ch * 8 + f8
                    mm_group(pb, pb[:, 2 * ft:2 * ft + 2],
                             [(w[:, kt, f8 * 128:(f8 + 1) * 128], csb[:, kt, :]) for kt in range(8)], [w, csb])
            for c in range(2):
                s.op("dve", lambda e, c=c: e.tensor_tensor(out=modT[:, :, c], in0=pb[:, c:144:2], in1=badaT[:], op=ALU.add),
                     reads=[pb, badaT], writes=[modT])
            for i in range(3):
                for c in range(2):
                    sh = modT[:, (3 * i) * 8:(3 * i) * 8 + 8, c]
                    sc = modT[:, (3 * i + 1) * 8:(3 * i + 1) * 8 + 8, c]
                    gt = modT[:, (3 * i + 2) * 8:(3 * i + 2) * 8 + 8, c]
                    s.op("dve", lambda e, sc=sc, l=l, i=i, c=c: e.scalar_tensor_tensor(
                        out=Aco[:, l, i, c, :], in0=sc, scalar=1.0, in1=gn[:, l, i, :], op0=ALU.add, op1=ALU.mult),
                        reads=[modT, gn], writes=[Aco])
                    s.op("dve", lambda e, sh=sh, l=l, i=i, c=c: e.tensor_copy(out=Bco[:, l, i, c, :], in_=sh),
                         reads=[modT], writes=[Bco])
                    s.op("dve", lambda e, gt=gt, l=l, i=i, c=c: e.tensor_scalar(
                        out=Gco[:, l, i, c, :], in0=gt, scalar1=(1.0 if i == 1 else 0.5), scalar2=None, op0=ALU.mult),
                        reads=[modT], writes=[Gco])
        if after_dma is not None:
            after_dma()
        s.barrier()
        s.release(m0)

    def load_xT(xT, blk):
        s.dma("sp", xT[:], _ap(xT_d)[:, blk * 512:(blk + 1) * 512].rearrange("(kt p) n -> p kt n", p=128),
              reads=[xT_d], writes=[xT])

    def store_xT(xT, blk):
        s.dma("sp", _ap(xT_d)[:, blk * 512:(blk + 1) * 512].rearrange("(kt p) n -> p kt n", p=128), xT[:],
              reads=[xT], writes=[xT_d])

    def load_x_tm(xT, xtm, blk):
        src = _ap(xp)[blk * 512:(blk + 1) * 512, :] if blk < 2 else _ap(xs)[(blk - 2) * 512:(blk - 1) * 512, :]
        for tt in range(4):
            s.dma("sp", xtm[:, 0, :], src[tt * 128:(tt + 1) * 128, :], writes=[xtm])
            for half in range(2):
                pb = ps()
                s.group("pe", [lambda e, k4=k4, half=half, pb=pb: e.transpose(pb[:, k4 * 128:(k4 + 1) * 128],
                                                                               xtm[:, 0, (half * 4 + k4) * 128:(half * 4 + k4 + 1) * 128], ident[:])
                               for k4 in range(4)], reads=[xtm, ident], writes=[pb])
                dst = xT[:, half * 4:(half + 1) * 4, tt * 128:(tt + 1) * 128]
                if half:
                    s.op("act", lambda e, dst=dst, pb=pb: e.copy(out=dst, in_=pb[:].rearrange("p (k n) -> p k n", k=4)),
                         reads=[pb], writes=[xT])
                else:
                    s.op("dve", lambda e, dst=dst, pb=pb: e.tensor_copy(out=dst, in_=pb[:].rearrange("p (k n) -> p k n", k=4)),
                         reads=[pb], writes=[xT])

    def store_y_tm(xT, ytm, blk):
        dst = _ap(yp)[blk * 512:(blk + 1) * 512, :] if blk < 2 else _ap(ys)[(blk - 2) * 512:(blk - 1) * 512, :]
        ydr = yp if blk < 2 else ys
        for tt in range(4):
            for half in range(2):
                pb = ps()
                s.group("pe", [lambda e, k4=k4, tt=tt, half=half, pb=pb: e.transpose(
                    pb[:, k4 * 128:(k4 + 1) * 128], xT[:, half * 4 + k4, tt * 128:(tt + 1) * 128], ident[:])
                    for k4 in range(4)], reads=[xT, ident], writes=[pb])
                if half:
                    s.op("act", lambda e, pb=pb: e.copy(out=ytm[:, 0, 512:1024], in_=pb[:]), reads=[pb], writes=[ytm])
                else:
                    s.op("dve", lambda e, pb=pb: e.tensor_copy(out=ytm[:, 0, 0:512], in_=pb[:]), reads=[pb], writes=[ytm])
            s.dma("sp", dst[tt * 128:(tt + 1) * 128, :], ytm[:, 0, :], reads=[ytm], writes=[ydr])

    def norm_mod(xT, hb, tmp2, rstd, l, i, c):
        for kt in range(8):
            s.op("act", lambda e, kt=kt: e.activation(out=hb[:, kt, :], in_=xT[:, kt, :], func=AF.Square),
                 reads=[xT], writes=[hb])
        pb = ps()
        mm_group(pb, pb[:], [(mean_bf[:], hb[:, kt, :]) for kt in range(8)], [mean_bf, hb])
        rsqrt(rstd, rstd[:], pb, pb[:], eps6)
        for kt in range(8):
            t = tmp2[kt % 2]
            s.op("dve", lambda e, kt=kt, t=t: e.tensor_tensor(out=t[:], in0=xT[:, kt, :], in1=rstd[:], op=ALU.mult),
                 reads=[xT, rstd], writes=[t])
            s.op("act", lambda e, kt=kt, t=t: e.activation(out=hb[:, kt, :], in_=t[:], func=AF.Identity,
                                                           scale=Aco[:, l, i, c, kt:kt + 1], bias=Bco[:, l, i, c, kt:kt + 1]),
                 reads=[t, Aco, Bco], writes=[hb])

    def alloc_ffn_w():
        W1 = [s.sbuf("W1_%d" % j, [128, 8, 512], BF16) for j in range(11)]
        W2 = [s.sbuf("W2_%d" % j, [128, 2, 1024], BF16) for j in range(11)]
        return (W1, W2)

    def issue_ffn_w(W, l, win_d, wout_d):
        W1, W2 = W
        for j in (0, 5, 1, 6, 2, 7, 3, 8, 4, 9, 10):
            s.dma("pool", W1[j][:], _ap(win_d)[l, :, j * 512:(j + 1) * 512].rearrange("(kt p) f -> p kt f", p=128),
                  writes=[W1[j]])
        for j in range(11):
            s.dma("pool", W2[j][:], _ap(wout_d)[l, j * 256:(j + 1) * 256, :].rearrange("(kt p) f -> p kt f", p=128),
                  writes=[W2[j]])

    def ffn_phase(l, i, win_d, wout_d, first, last, W=None):
        m0 = s.mark()
        if W is None:
            W = alloc_ffn_w()
            issue_ffn_w(W, l, win_d, wout_d)
        W1, W2 = W
        xTs = [s.sbuf("xT%d" % j, [128, 8, 512], F32) for j in range(2)]
        hb = s.sbuf("hb", [128, 8, 512], BF16)
        actb = s.sbuf("actb", [128, 22, 512], BF16)
        tmp2 = [s.sbuf("tmp%d" % j, [128, 512], F32) for j in range(2)]
        sg2 = tmp2
        rstd = s.sbuf("rstd", [128, 512], F32)
        if os.environ.get("MK_VERBOSE"):
            print("ffn phase free sbuf bytes/partition:", nc.sbuf_bytes_remaining, "first/last", first, last)
        xtm = s.sbuf("xtm", [128, 1, 1024], F32) if (first or last) else None

        def w1cols(col):
            return W1[col // 512], col % 512

        if last:
            items = [("std", 0), ("std", 1)] + [("own", i) for i in range(4)]
        else:
            items = [("std", blk) for blk in range(NB)]

        def fetch(it):
            kind, b_ = items[it]
            xT_ = xTs[it % 2]
            if kind == "own":
                load_xT(xT_, 2 + b_)
                for kt in range(8):
                    t = tmp2[kt % 2]
                    s.dma("sp", t[:], _ap(xT_d)[kt * 128:(kt + 1) * 128, (6 + b_) * 512:(7 + b_) * 512], reads=[xT_d], writes=[t])
                    s.op("dve", lambda e, kt=kt, xT_=xT_: e.tensor_scalar(out=xT_[:, kt, :], in0=xT_[:, kt, :], scalar1=hsel[:, 0:1],
                                                                          scalar2=None, op0=ALU.mult), reads=[xT_, hsel], writes=[xT_])
                    s.op("dve", lambda e, kt=kt, xT_=xT_, t=t: e.scalar_tensor_tensor(
                        out=xT_[:, kt, :], in0=t[:], scalar=hsel[:, 1:2], in1=xT_[:, kt, :], op0=ALU.mult, op1=ALU.add),
                        reads=[t, hsel, xT_], writes=[xT_])
            elif first:
                load_x_tm(xT_, xtm, b_)
            else:
                load_xT(xT_, b_)

        fetch(0)
        for it, (kind, blk) in enumerate(items):
            c = 0 if (kind == "std" and blk < 2) else 1
            xT = xTs[it % 2]
            nxt_own = it + 1 < len(items) and items[it + 1][0] == "own"
            if it + 1 < len(items) and not first and not nxt_own:
                fetch(it + 1)
            norm_mod(xT, hb, tmp2, rstd, l, i, c)
            for j in range(22):
                wg, og = w1cols(j * 128)
                wu, ou = w1cols(DFF + j * 128)
                pg = ps()
                mm_group(pg, pg[:], [(wg[:, kt, og:og + 128], hb[:, kt, :]) for kt in range(8)], [wg, hb])
                pu = ps()
                mm_group(pu, pu[:], [(wu[:, kt, ou:ou + 128], hb[:, kt, :]) for kt in range(8)], [wu, hb])
                sg = sg2[j % 2]
                s.op("act", lambda e, sg=sg, pg=pg: e.activation(out=sg[:], in_=pg[:], func=AF.Silu), reads=[pg], writes=[sg])
                s.op("dve", lambda e, sg=sg, pu=pu, j=j: e.tensor_tensor(out=actb[:, j, :], in0=sg[:], in1=pu[:], op=ALU.mult),
                     reads=[sg, pu], writes=[actb])
            if it + 1 < len(items) and first:
                fetch(it + 1)
            for ft in range(8):
                po = ps()
                mm_group(po, po[:], [(W2[j // 2][:, j % 2, ft * 128:(ft + 1) * 128], actb[:, j, :]) for j in range(22)],
                         W2 + [actb])
                s.op("dve", lambda e, po=po, ft=ft, xT=xT: e.scalar_tensor_tensor(
                    out=xT[:, ft, :], in0=po[:], scalar=Gco[:, l, i, c, ft:ft + 1], in1=xT[:, ft, :], op0=ALU.mult, op1=ALU.add),
                    reads=[po, Gco, xT], writes=[xT])
            if last:
                store_y_tm(xT, xtm, blk if kind == "std" else 2 + blk)
            else:
                store_xT(xT, blk)
            if nxt_own:
                fetch(it + 1)
        s.barrier()
        s.release(m0)

    def proj_phase(l):
        m0 = s.mark()
        WI = [s.sbuf("WI_%d" % j, [128, 8, 256], BF16) for j in range(9)]
        for j in range(9):
            s.dma("pool", WI[j][:], _ap(w_in)[l, :, j * 256:(j + 1) * 256].rearrange("(kt p) f -> p kt f", p=128),
                  writes=[WI[j]])
        xTs = [s.sbuf("xT%d" % j, [128, 8, 512], F32) for j in range(2)]
        hbs = [s.sbuf("hb%d" % j, [128, 8, 512], BF16) for j in range(2)]
        tmp2 = [s.sbuf("tmp%d" % j, [128, 512], F32) for j in range(2)]
        rstd = s.sbuf("rstd", [128, 512], F32)
        gq = s.sbuf("gq", [128, 2], F32)
        for h in range(2):
            s.dma("sp", gq[h * 64:(h + 1) * 64, 0:1], _ap(qn_g)[l, :].rearrange("(p o) -> p o", o=1), writes=[gq])
            s.dma("sp", gq[h * 64:(h + 1) * 64, 1:2], _ap(kn_g)[l, :].rearrange("(p o) -> p o", o=1), writes=[gq])
        gk_b = s.sbuf("gk_b", [128, 64], F32)
        s.dma("sp", gk_b[:], _ap(kn_g)[l:l + 1, :].to_broadcast([128, 64]), writes=[gk_b])
        st_b = [s.sbuf("st_b%d" % j, [128, 512], BF16) for j in range(4)]
        sqb = [s.sbuf("sqb%d" % j, [128, 512], BF16) for j in range(4)]
        r2 = [s.sbuf("r2_%d" % j, [128, 512], F32) for j in range(4)]
        gvb = [s.sbuf("gvb_%d" % j, [128, 256], F32) for j in range(2)]
        qk32 = [s.sbuf("qk32_%d" % j, [128, 512], F32) for j in range(4)]
        g1 = [s.sbuf("g1_%d" % j, [128, 512], F32) for j in range(3)]
        g2 = [s.sbuf("g2_%d" % j, [128, 512], F32) for j in range(3)]
        g3 = [s.sbuf("g3_%d" % j, [128, 512], F32) for j in range(3)]
        vtm32 = [s.sbuf("vtm32_%d" % j, [128, 512], F32) for j in range(2)]
        ktm32 = [s.sbuf("ktm32_%d" % j, [128, 512], F32) for j in range(2)]
        vtmb = [s.sbuf("vtmb_%d" % j, [128, 512], BF16) for j in range(2)]
        vA = [s.sbuf("vA_%d" % j, [128, 256], BF16) for j in range(2)]
        vB = [s.sbuf("vB_%d" % j, [128, 256], BF16) for j in range(2)]
        for j in range(2):
            s.op("dve", lambda e, j=j: e.memset(vA[j][:], 0.0), writes=[vA[j]])
            s.op("dve", lambda e, j=j: e.memset(vB[j][:], 0.0), writes=[vB[j]])
        stat = [s.sbuf("stat%d" % j, [128, 6], F32) for j in range(2)]
        mv = [s.sbuf("mv%d" % j, [128, 2], F32) for j in range(2)]
        rs1 = [s.sbuf("rs1_%d" % j, [128, 1], F32) for j in range(2)]
        ss8 = [s.sbuf("ss8_%d" % j, [128, 8], F32) for j in range(2)]
        cnt = [0]

        def wcols(col):
            return WI[col // 256], col % 256

        def fm_tile(col):
            w, o = wcols(col)
            pb = ps()
            mm_group(pb, pb[:], [(w[:, kt, o:o + 128], hb[:, kt, :]) for kt in range(8)], [w, hb])
            return pb

        def gelu_ops(src_b, src_ap, dst_b, dst_ap, n, P=128):
            k = cnt[0] % 3
            cnt[0] += 1
            a, b2, c2 = g1[k], g2[k], g3[k]
            s.op("act", lambda e: e.activation(out=a[0:P, 0:n], in_=src_ap, func=AF.Square), reads=[src_b], writes=[a])
            s.op("dve", lambda e: e.tensor_scalar(out=a[0:P, 0:n], in0=a[0:P, 0:n], scalar1=0.044715, scalar2=1.0,
                                                  op0=ALU.mult, op1=ALU.add), reads=[a], writes=[a])
            s.op("dve", lambda e: e.tensor_tensor(out=b2[0:P, 0:n], in0=a[0:P, 0:n], in1=src_ap, op=ALU.mult),
                 reads=[a, src_b], writes=[b2])
            s.op("act", lambda e: e.activation(out=c2[0:P, 0:n], in_=b2[0:P, 0:n], func=AF.Sigmoid, scale=1.5957691216057308),
                 reads=[b2], writes=[c2])
            s.op("dve", lambda e: e.tensor_tensor(out=dst_ap, in0=c2[0:P, 0:n], in1=src_ap, op=ALU.mult),
                 reads=[c2, src_b], writes=[dst_b])

        for blk in range(NB):
            c = 0 if blk < 2 else 1
            tok = slice(blk * 512, (blk + 1) * 512)
            xT = xTs[blk % 2]
            hb = hbs[blk % 2]
            if blk == 0:
                load_xT(xT, 0)
            if blk + 1 < NB:
                load_xT(xTs[(blk + 1) % 2], blk + 1)
            norm_mod(xT, hb, tmp2, rstd, l, 1, c)
            for a in range(2):
                pb = fm_tile(a * 128)
                sb = st_b[cnt[0] % 4]; cnt[0] += 1
                s.op("act", lambda e, sb=sb, pb=pb: e.copy(out=sb[:], in_=pb[:]), reads=[pb], writes=[sb])
                s.dma("sp", _ap(us_d)[a * 128:(a + 1) * 128, tok], sb[:], reads=[sb], writes=[us_d])
            for qk in range(2):
                for t4 in range(4):
                    pb = fm_tile(256 + qk * 512 + t4 * 128)
                    sq = sqb[t4]
                    s.op("act", lambda e, sq=sq, pb=pb: e.activation(out=sq[:], in_=pb[:], func=AF.Square),
                         reads=[pb], writes=[sq])
                    q32 = qk32[t4]
                    s.op("act", lambda e, q32=q32, pb=pb: e.copy(out=q32[:], in_=pb[:]), reads=[pb], writes=[q32])
                    pb = q32
                    pm = ps()
                    mm_group(pm, pm[:], [(bd_bf[:], sq[:])], [bd_bf, sq])
                    r = r2[t4]
                    rsqrt(r, r[:], pm, pm[:], eps6)
                    sb = st_b[cnt[0] % 4]; cnt[0] += 1
                    s.op("dve", lambda e, sb=sb, pb=pb, r=r, qk=qk: e.scalar_tensor_tensor(
                        out=sb[:], in0=pb[:], scalar=gq[:, qk:qk + 1], in1=r[:], op0=ALU.mult, op1=ALU.mult),
                        reads=[pb, gq, r], writes=[sb])
                    dd = q_d if qk == 0 else k_d
                    s.dma("sp", _ap(dd)[t4 * 128:(t4 + 1) * 128, tok], sb[:], reads=[sb], writes=[dd])
            for a in range(2):
                pb = fm_tile(1792 + a * 128)
                sb = st_b[cnt[0] % 4]; cnt[0] += 1
                gelu_ops(pb, pb[:], sb, sb[:], 512)
                s.dma("sp", _ap(ug_d)[a * 128:(a + 1) * 128, tok], sb[:], reads=[sb], writes=[ug_d])
            for tt in range(4):
                trow = slice(blk * 512 + tt * 128, blk * 512 + (tt + 1) * 128)
                hT = [hb[:, kt, tt * 128:(tt + 1) * 128] for kt in range(8)]
                k2 = tt % 2
                pv = ps()
                for half in range(2):
                    w = WI[5 + half]
                    mm_group(pv, pv[:, half * 256:(half + 1) * 256], [(hT[kt], w[:, kt, :]) for kt in range(8)], [w, hb])
                vb = vtmb[k2]
                s.op("act", lambda e, vb=vb, pv=pv: e.copy(out=vb[:], in_=pv[:]), reads=[pv], writes=[vb])
                s.dma("sp", _ap(v_d)[trow, :], vb[:], reads=[vb], writes=[v_d])
                if blk < 2:
                    seq = (blk * 512 + tt * 128) // 256
                    pos = (tt % 2) * 128
                    v32 = vtm32[k2]
                    s.op("dve", lambda e, v32=v32, pv=pv: e.tensor_copy(out=v32[:], in_=pv[:]), reads=[pv], writes=[v32])
                    s.dma("sp", _ap(nv)[seq, l, pos:pos + 128, :], v32[:], reads=[v32], writes=[nv])
                    pk = ps()
                    for half in range(2):
                        w = WI[3 + half]
                        mm_group(pk, pk[:, half * 256:(half + 1) * 256], [(hT[kt], w[:, kt, :]) for kt in range(8)], [w, hb])
                    a1 = g1[cnt[0] % 3]; cnt[0] += 1
                    s.op("act", lambda e, a1=a1, pk=pk: e.activation(out=a1[:], in_=pk[:], func=AF.Square), reads=[pk], writes=[a1])
                    s8 = ss8[k2]
                    s.op("dve", lambda e, a1=a1, s8=s8: e.tensor_reduce(out=s8[:], in_=a1[:].rearrange("p (h d) -> p h d", d=64),
                                                                       axis=AX.X, op=ALU.add), reads=[a1], writes=[s8])
                    rsqrt(s8, s8[:], s8, s8[:], eps6, scale=1.0 / 64.0)
                    k32 = ktm32[k2]
                    for h in range(8):
                        s.op("dve", lambda e, h=h, k32=k32, pk=pk, s8=s8: e.scalar_tensor_tensor(
                            out=k32[:, h * 64:(h + 1) * 64], in0=pk[:, h * 64:(h + 1) * 64], scalar=s8[:, h:h + 1],
                            in1=gk_b[:], op0=ALU.mult, op1=ALU.mult), reads=[pk, s8, gk_b], writes=[k32])
                    s.dma("sp", _ap(nk)[seq, l, pos:pos + 128, :], k32[:], reads=[k32], writes=[nk])
                pg = ps()
                mm_group(pg, pg[:, 0:256], [(hT[kt], WI[8][:, kt, :]) for kt in range(8)], [WI[8], hb])
                gv = gvb[k2]
                gelu_ops(pg, pg[:, 0:256], gv, gv[:, 0:256], 256)
                stt, mvv, rr = stat[k2], mv[k2], rs1[k2]
                s.op("dve", lambda e, stt=stt, gv=gv: e.bn_stats(out=stt[:], in_=gv[:, 0:256]), reads=[gv], writes=[stt])
                s.op("dve", lambda e, stt=stt, mvv=mvv: e.bn_aggr(out=mvv[:], in_=stt[:]), reads=[stt], writes=[mvv])
                rsqrt(rr, rr[:], mvv, mvv[:, 1:2], eps5)
                va, vbb = vA[k2], vB[k2]
                for (dst, off) in ((va, 0), (vbb, 64)):
                    for a in range(2):
                        cs_ = slice(a * 128 + off, a * 128 + off + 64)
                        s.op("dve", lambda e, dst=dst, cs_=cs_, gv=gv, mvv=mvv, rr=rr: e.tensor_scalar(
                            out=dst[:, cs_], in0=gv[:, cs_], scalar1=mvv[:, 0:1], scalar2=rr[:, 0:1],
                            op0=ALU.subtract, op1=ALU.mult), reads=[gv, mvv, rr], writes=[dst])
                s.dma("sp", _ap(va_d)[trow, :], va[:], reads=[va], writes=[va_d])
                s.dma("sp", _ap(vb_d)[trow, :], vbb[:], reads=[vbb], writes=[vb_d])
        s.barrier()
        s.release(m0)

    def outproj_phase(l, WO=None):
        m0 = s.mark()
        if WO is None:
            WO = s.sbuf("WO", [128, 8, 1024], BF16)
            s.dma("pool", WO[:], _ap(w_out)[l].rearrange("(kt p) f -> p kt f", p=128), writes=[WO])
        xTs = [s.sbuf("xT%d" % j, [128, 8, 512], F32) for j in range(2)]
        yms = [s.sbuf("ym%d" % j, [128, 8, 512], BF16) for j in range(2)]
        def fetch(blk):
            load_xT(xTs[blk % 2], blk)
            s.dma("sp", yms[blk % 2][:], _ap(ymix_d)[:, blk * 512:(blk + 1) * 512].rearrange("(kt p) n -> p kt n", p=128),
                  reads=[ymix_d], writes=[yms[blk % 2]])

        fetch(0)
        for blk in range(NB):
            c = 0 if blk < 2 else 1
            xT, ym = xTs[blk % 2], yms[blk % 2]
            if blk + 1 < NB:
                fetch(blk + 1)
            for ft in range(8):
                po = ps()
                mm_group(po, po[:], [(WO[:, kt, ft * 128:(ft + 1) * 128], ym[:, kt, :]) for kt in range(8)], [WO, ym])
                s.op("dve", lambda e, po=po, ft=ft, xT=xT: e.scalar_tensor_tensor(
                    out=xT[:, ft, :], in0=po[:], scalar=Gco[:, l, 1, c, ft:ft + 1], in1=xT[:, ft, :], op0=ALU.mult, op1=ALU.add),
                    reads=[po, Gco, xT], writes=[xT])
            store_xT(xT, blk)
        s.barrier()
        s.release(m0)

    def ssm_part(l):
        m0 = s.mark()
        uS = s.sbuf("uS", [128, 2, NT], BF16)
        for a in range(2):
            s.dma("sp", uS[:, a, :], _ap(us_d)[a * 128:(a + 1) * 128, :], reads=[us_d], writes=[uS])
        Y = s.sbuf("Y", [128, 2, NT], F32)
        dco = s.sbuf("dco", [128, 2], F32)
        s.dma("sp", dco[:], _ap(ssm_d)[l, :].rearrange("(a p) -> p a", p=128), writes=[dco])
        iota = s.sbuf("iota", [128, 1024], F32)
        s.dma("sp", iota[:], _ap(c_iota), writes=[iota])
        P8 = lambda n: s.sbuf(n, [128, 8], F32)
        lr, li, ldt, dtv, ar, ai, rho, ang, sinv, cosv, lbr, lbi = [P8("p8_%d" % i) for i in range(12)]
        nr, den, kr, ki, t8a, t8b, thr, nf8, cN, sN = [P8("q8_%d" % i) for i in range(10)]
        ii8 = s.sbuf("ii8", [128, 8], mybir.dt.int32)
        BT = s.sbuf("BT", [128, 8, 2, 128], BF16)
        CT = s.sbuf("CT", [128, 8, 2, 128], BF16)
        s0 = s.sbuf("s0", [128, 8, 2], F32)
        init = s.sbuf("init", [128, 8, 2], F32)
        FIN = [s.sbuf("FIN%d" % i, [128, 8, 2], F32) for i in range(4)]
        zl = s.sbuf("zl", [128, 2], F32)
        tq = s.sbuf("tq", [128, 2], F32)

        def dv(fn, reads, writes, e="dve"):
            s.op(e, fn, reads=reads, writes=writes)

        C1 = 6.28125
        C2 = TWO_PI - C1
        PI_S = 3.1415925
        hpi = s.sbuf("hpi", [128, 1], F32)
        dv(lambda e: e.memset(hpi[:], 0.5 * math.pi), [], [hpi])

        def sincos(ang_b, ang_ap, r_b, r_ap, sin_b, sin_ap, cos_b, cos_ap, ii_b, nf_b, n):
            dv(lambda e: e.tensor_scalar(out=ii_b[:, 0:n], in0=ang_ap, scalar1=1.0 / TWO_PI, scalar2=None, op0=ALU.mult), [ang_b], [ii_b])
            dv(lambda e: e.tensor_copy(out=nf_b[:, 0:n], in_=ii_b[:, 0:n]), [ii_b], [nf_b])
            dv(lambda e: e.scalar_tensor_tensor(out=r_ap, in0=nf_b[:, 0:n], scalar=-C1, in1=ang_ap, op0=ALU.mult, op1=ALU.add),
               [nf_b, ang_b], [r_b])
            dv(lambda e: e.scalar_tensor_tensor(out=r_ap, in0=nf_b[:, 0:n], scalar=-C2, in1=r_ap, op0=ALU.mult, op1=ALU.add),
               [nf_b, r_b], [r_b])
            dv(lambda e: e.tensor_scalar(out=r_ap, in0=r_ap, scalar1=-PI_S, scalar2=None, op0=ALU.max), [r_b], [r_b])
            dv(lambda e: e.tensor_scalar(out=r_ap, in0=r_ap, scalar1=PI_S, scalar2=None, op0=ALU.min), [r_b], [r_b])
            s.op("act", lambda e: e.activation(out=sin_ap, in_=r_ap, func=AF.Sin), reads=[r_b], writes=[sin_b])
            s.op("act", lambda e: e.activation(out=nf_b[:, 0:n], in_=r_ap, func=AF.Abs), reads=[r_b], writes=[nf_b])
            s.op("act", lambda e: e.activation(out=cos_ap, in_=nf_b[:, 0:n], func=AF.Sin, scale=-1.0, bias=hpi[:, 0:1]),
                 reads=[nf_b, hpi], writes=[cos_b])

        for d in range(2):
            s.dma("sp", lr[:], _ap(lam_re)[l, d, :].rearrange("(j q) -> q j", q=128), writes=[lr])
            s.dma("sp", li[:], _ap(lam_im)[l, d, :].rearrange("(j q) -> q j", q=128), writes=[li])
            for h in range(2):
                s.dma("sp", ldt[h * 64:(h + 1) * 64, :],
                      _ap(log_dt)[l, d, :].rearrange("(j h) -> h j", h=2)[h:h + 1, :].to_broadcast([64, 8]), writes=[ldt])
            s.op("act", lambda e: e.activation(out=dtv[:], in_=ldt[:], func=AF.Exp), reads=[ldt], writes=[dtv])
            dv(lambda e: e.tensor_tensor(out=ar[:], in0=lr[:], in1=dtv[:], op=ALU.mult), [lr, dtv], [ar])
            dv(lambda e: e.tensor_tensor(out=ai[:], in0=li[:], in1=dtv[:], op=ALU.mult), [li, dtv], [ai])
            s.op("act", lambda e: e.activation(out=rho[:], in_=ar[:], func=AF.Exp), reads=[ar], writes=[rho])
            sincos(ai, ai[:], thr, thr[:], sinv, sinv[:], cosv, cosv[:], ii8, nf8, 8)
            dv(lambda e: e.tensor_tensor(out=lbr[:], in0=rho[:], in1=cosv[:], op=ALU.mult), [rho, cosv], [lbr])
            dv(lambda e: e.tensor_tensor(out=lbi[:], in0=rho[:], in1=sinv[:], op=ALU.mult), [rho, sinv], [lbi])
            dv(lambda e: e.tensor_scalar(out=nr[:], in0=lbr[:], scalar1=-1.0, scalar2=None, op0=ALU.add), [lbr], [nr])
            dv(lambda e: e.tensor_tensor(out=den[:], in0=lr[:], in1=lr[:], op=ALU.mult), [lr], [den])
            dv(lambda e: e.tensor_tensor(out=t8a[:], in0=li[:], in1=li[:], op=ALU.mult), [li], [t8a])
            dv(lambda e: e.tensor_tensor(out=den[:], in0=den[:], in1=t8a[:], op=ALU.add), [den, t8a], [den])
            dv(lambda e: e.reciprocal(out=den[:], in_=den[:]), [den], [den])
            dv(lambda e: e.tensor_tensor(out=t8a[:], in0=nr[:], in1=lr[:], op=ALU.mult), [nr, lr], [t8a])
            dv(lambda e: e.tensor_tensor(out=t8b[:], in0=lbi[:], in1=li[:], op=ALU.mult), [lbi, li], [t8b])
            dv(lambda e: e.tensor_tensor(out=t8a[:], in0=t8a[:], in1=t8b[:], op=ALU.add), [t8a, t8b], [t8a])
            dv(lambda e: e.tensor_tensor(out=kr[:], in0=t8a[:], in1=den[:], op=ALU.mult), [t8a, den], [kr])
            dv(lambda e: e.tensor_tensor(out=t8a[:], in0=lbi[:], in1=lr[:], op=ALU.mult), [lbi, lr], [t8a])
            dv(lambda e: e.tensor_tensor(out=t8b[:], in0=nr[:], in1=li[:], op=ALU.mult), [nr, li], [t8b])
            dv(lambda e: e.tensor_tensor(out=t8a[:], in0=t8a[:], in1=t8b[:], op=ALU.subtract), [t8a, t8b], [t8a])
            dv(lambda e: e.tensor_tensor(out=ki[:], in0=t8a[:], in1=den[:], op=ALU.mult), [t8a, den], [ki])
            m1 = s.mark()
            Bn = [s.sbuf("Bn%d" % i, [128, 8, 16], F32) for i in range(2)]
            Zp = [s.sbuf("Zp%d" % i, [128, 8, 128], F32) for i in range(2)]
            tz = s.sbuf("tz", [128, 16], F32)
            INc = [s.sbuf("INc%d" % i, [128, 8, 128], F32) for i in range(2)]
            s.dma("sp", Bn[0][:], _ap(b_re)[l, d].rearrange("(j q) c -> q j c", q=128), writes=[Bn[0]])
            s.dma("sp", Bn[1][:], _ap(b_im)[l, d].rearrange("(j q) c -> q j c", q=128), writes=[Bn[1]])
            for ri in range(2):
                dv(lambda e, ri=ri: e.memset(Zp[ri][:], 0.0), [], [Zp[ri]])
            for j in range(8):
                for h in range(2):
                    pr = slice(h * 64, (h + 1) * 64)
                    cs_ = slice(32 * (j % 4) + 16 * h, 32 * (j % 4) + 16 * h + 16)
                    dv(lambda e, j=j, pr=pr: e.tensor_scalar(out=tz[pr, :], in0=Bn[1][pr, j, :], scalar1=ki[pr, j:j + 1],
                                                             scalar2=None, op0=ALU.mult), [Bn[1], ki], [tz])
                    dv(lambda e, j=j, pr=pr, cs_=cs_: e.scalar_tensor_tensor(
                        out=Zp[0][pr, j, cs_], in0=Bn[0][pr, j, :], scalar=kr[pr, j:j + 1], in1=tz[pr, :],
                        op0=ALU.mult, op1=ALU.subtract), [Bn[0], kr, tz], [Zp[0]])
                    dv(lambda e, j=j, pr=pr: e.tensor_scalar(out=tz[pr, :], in0=Bn[0][pr, j, :], scalar1=ki[pr, j:j + 1],
                                                             scalar2=None, op0=ALU.mult), [Bn[0], ki], [tz])
                    dv(lambda e, j=j, pr=pr, cs_=cs_: e.scalar_tensor_tensor(
                        out=Zp[1][pr, j, cs_], in0=Bn[1][pr, j, :], scalar=kr[pr, j:j + 1], in1=tz[pr, :],
                        op0=ALU.mult, op1=ALU.add), [Bn[1], kr, tz], [Zp[1]])
            for ri, cd in enumerate((c_re, c_im)):
                dv(lambda e, ri=ri: e.memset(INc[ri][:], 0.0), [], [INc[ri]])
                for j in range(8):
                    for h in range(2):
                        g = 2 * j + h
                        r0 = 32 * (j % 4) + 16 * h
                        s.dma("sp", INc[ri][r0:r0 + 16, j, h * 64:(h + 1) * 64], _ap(cd)[l, d, g * 16:(g + 1) * 16, :],
                              reads=[INc[ri]], writes=[INc[ri]])
            for j in range(8):
                for ri in range(2):
                    pb = ps()
                    s.group("pe", [lambda e, pb=pb, j=j, ri=ri: e.transpose(pb[:, 0:128], Zp[ri][:, j, :], ident[:])],
                            reads=[Zp[ri], ident], writes=[pb])
                    s.op("act", lambda e, pb=pb, j=j, ri=ri: e.copy(out=BT[:, j, ri, :], in_=pb[:, 0:128]), reads=[pb], writes=[BT])
                    pc = ps()
                    s.group("pe", [lambda e, pc=pc, j=j, ri=ri: e.transpose(pc[:, 0:128], INc[ri][:, j, :], ident[:])],
                            reads=[INc[ri], ident], writes=[pc])
                    s.op("act", lambda e, pc=pc, j=j, ri=ri: e.activation(out=CT[:, j, ri, :], in_=pc[:, 0:128], func=AF.Copy,
                                                                        scale=(1.0 if ri == 0 else -1.0)),
                         reads=[pc], writes=[CT])
            s.dma("sp", s0[:], _ap(sst)[l, d].rearrange("(j q) r -> q j r", q=128), writes=[s0])
            dv(lambda e: e.tensor_tensor(out=t8a[:], in0=cosv[:], in1=s0[:, :, 0], op=ALU.mult), [cosv, s0], [t8a])
            dv(lambda e: e.tensor_tensor(out=t8b[:], in0=sinv[:], in1=s0[:, :, 1], op=ALU.mult), [sinv, s0], [t8b])
            dv(lambda e: e.tensor_tensor(out=init[:, :, 0], in0=t8a[:], in1=t8b[:], op=ALU.subtract), [t8a, t8b], [init])
            dv(lambda e: e.tensor_tensor(out=t8a[:], in0=sinv[:], in1=s0[:, :, 0], op=ALU.mult), [sinv, s0], [t8a])
            dv(lambda e: e.tensor_tensor(out=t8b[:], in0=cosv[:], in1=s0[:, :, 1], op=ALU.mult), [cosv, s0], [t8b])
            dv(lambda e: e.tensor_tensor(out=init[:, :, 1], in0=t8a[:], in1=t8b[:], op=ALU.add), [t8a, t8b], [init])
            s.barrier()
            s.release(m1)
            m2 = s.mark()
            rhoT = s.sbuf("rhoT", [128, 4, 1024], F32)
            tabc = s.sbuf("tabc", [128, 1024], F32)
            tabs = s.sbuf("tabs", [128, 1024], F32)
            ccol = s.sbuf("ccol", [128, 4, 2], F32)
            scol = s.sbuf("scol", [128, 4, 2], F32)
            angt = s.sbuf("angt", [128, 1024], F32)
            ang2 = s.sbuf("ang2", [128, 1024], F32)
            iiT = s.sbuf("iiT", [128, 1024], mybir.dt.int32)
            nfT = s.sbuf("nfT", [128, 1024], F32)
            tcb = s.sbuf("tcb", [128, 4, 1024], BF16)
            tsb = s.sbuf("tsb", [128, 4, 1024], BF16)
            tD_ = [[s.sbuf("tD%d_%d" % (k, i), [128, 1024], BF16) for i in range(2)] for k in range(2)]
            tP_ = [[s.sbuf("tP%d_%d" % (k, i), [128, 1024], BF16) for i in range(2)] for k in range(2)]
            wb_ = [[s.sbuf("wb%d_%d" % (k, i), [128, 1024], BF16) for i in range(2)] for k in range(2)]
            wri_ = [[s.sbuf("wri%d_%d" % (k, i), [128, 1024], F32) for i in range(2)] for k in range(2)]
            zb_ = [[s.sbuf("zb%d_%d" % (k, i), [128, 1024], BF16) for i in range(2)] for k in range(2)]
            bsb_ = [[s.sbuf("bsb%d_%d" % (k, i), [128, 1024], BF16) for i in range(2)] for k in range(2)]
            ucnt = [0]
            fq = s.sbuf("fq", [128, 4], F32)
            Sb = [[s.sbuf("Sb%d_%d" % (j4, ri), [128, 1024], BF16) for ri in range(2)] for j4 in range(4)]
            if os.environ.get("MK_VERBOSE"):
                print("ssm free sbuf bytes/partition:", nc.sbuf_bytes_remaining)
            units = [("p", sq, sq * 256, 256) for sq in range(4)]
            for kk in range(4):
                knat = kk if d == 0 else 3 - kk
                units.append(("s", kk, NP_TOK + knat * 1024, 1024))
            for a in range(2):
                for j4 in range(4):
                    j = a * 4 + j4
                    dv(lambda e, j=j: e.tensor_scalar(out=angt[:], in0=iota[:], scalar1=thr[:, j:j + 1], scalar2=None, op0=ALU.mult),
                       [iota, thr], [angt])
                    sincos(angt, angt[:], ang2, ang2[:], tabs, tabs[:], tabc, tabc[:], iiT, nfT, 1024)
                    s.op("act", lambda e, j=j, j4=j4: e.activation(out=rhoT[:, j4, :], in_=iota[:], func=AF.Identity, scale=0.0,
                                                                 bias=rho[:, j:j + 1]), reads=[iota, rho], writes=[rhoT])
                    s.op("act", lambda e, j4=j4: e.copy(out=tcb[:, j4, :], in_=tabc[:]), reads=[tabc], writes=[tcb])
                    s.op("act", lambda e, j4=j4: e.copy(out=tsb[:, j4, :], in_=tabs[:]), reads=[tabs], writes=[tsb])
                    for ci, col in enumerate((255, 1023)):
                        dv(lambda e, j4=j4, ci=ci, col=col: e.tensor_copy(out=ccol[:, j4, ci:ci + 1], in_=tabc[:, col:col + 1]), [tabc], [ccol])
                        dv(lambda e, j4=j4, ci=ci, col=col: e.tensor_copy(out=scol[:, j4, ci:ci + 1], in_=tabs[:, col:col + 1]), [tabs], [scol])
                    dv(lambda e, j=j, j4=j4: e.tensor_scalar(out=tq[:, 0:1], in0=scol[:, j4, 1:2], scalar1=sinv[:, j:j + 1], scalar2=None,
                                                             op0=ALU.mult), [scol, sinv], [tq])
                    dv(lambda e, j=j, j4=j4: e.scalar_tensor_tensor(out=cN[:, j:j + 1], in0=ccol[:, j4, 1:2], scalar=cosv[:, j:j + 1],
                                                                    in1=tq[:, 0:1], op0=ALU.mult, op1=ALU.subtract), [ccol, cosv, tq], [cN])
                    dv(lambda e, j=j, j4=j4: e.tensor_scalar(out=tq[:, 1:2], in0=ccol[:, j4, 1:2], scalar1=sinv[:, j:j + 1], scalar2=None,
                                                             op0=ALU.mult), [ccol, sinv], [tq])
                    dv(lambda e, j=j, j4=j4: e.scalar_tensor_tensor(out=sN[:, j:j + 1], in0=scol[:, j4, 1:2], scalar=cosv[:, j:j + 1],
                                                                    in1=tq[:, 1:2], op0=ALU.mult, op1=ALU.add), [scol, cosv, tq], [sN])
                for (kind, idx, tok0, n) in units:
                    for j4 in range(4):
                        j = a * 4 + j4
                        c_f, sn_f = tcb[:, j4, 0:n], tsb[:, j4, 0:n]
                        kb = ucnt[0] % 2
                        ucnt[0] += 1
                        tD, tP, wb, wri, zb, bsb = tD_[kb], tP_[kb], wb_[kb], wri_[kb], zb_[kb], bsb_[kb]
                        for p0 in range(0, n, 512):
                            pn = min(512, n - p0)
                            sl = slice(p0, p0 + pn) if d == 0 else slice(n - p0 - pn, n - p0)
                            for ri in range(2):
                                pb = ps()
                                mm_group(pb, pb[:, 0:pn], [(BT[:, j, ri, :], uS[:, a, tok0 + p0:tok0 + p0 + pn])], [BT, uS])
                                src_ = pb[:, 0:pn] if d == 0 else pb[:, 0:pn][:, ::-1]
                                s.op("act", lambda e, ri=ri, sl=sl, src_=src_: e.copy(out=bsb[ri][:, sl], in_=src_), reads=[pb], writes=[bsb[ri]])
                        br_, bi_ = bsb[0][:, 0:n], bsb[1][:, 0:n]
                        dv(lambda e: e.tensor_tensor(out=tD[0][:, 0:n], in0=c_f, in1=br_, op=ALU.mult), [tcb, bsb[0]], [tD[0]])
                        dv(lambda e: e.tensor_tensor(out=tD[1][:, 0:n], in0=sn_f, in1=bi_, op=ALU.mult), [tsb, bsb[1]], [tD[1]])
                        dv(lambda e: e.tensor_tensor(out=wb[0][:, 0:n], in0=tD[0][:, 0:n], in1=tD[1][:, 0:n], op=ALU.add),
                           [tD[0], tD[1]], [wb[0]])
                        dv(lambda e: e.tensor_tensor(out=tP[0][:, 0:n], in0=c_f, in1=bi_, op=ALU.mult), [tcb, bsb[1]], [tP[0]])
                        dv(lambda e: e.tensor_tensor(out=tP[1][:, 0:n], in0=sn_f, in1=br_, op=ALU.mult), [tsb, bsb[0]], [tP[1]])
                        dv(lambda e: e.tensor_tensor(out=wb[1][:, 0:n], in0=tP[0][:, 0:n], in1=tP[1][:, 0:n], op=ALU.subtract),
                           [tP[0], tP[1]], [wb[1]])
                        for ri in range(2):
                            if kind == "p":
                                ini = 0.0
                                rds = [rhoT, wb[ri]]
                            else:
                                ini = init[:, j, ri:ri + 1]
                                rds = [rhoT, wb[ri], init]
                            s.op("dve", lambda e, ri=ri, ini=ini, j4=j4: e.tensor_tensor_scan(
                                out=wri[ri][:, 0:n], data0=rhoT[:, j4, 0:n], data1=wb[ri][:, 0:n], initial=ini,
                                op0=ALU.mult, op1=ALU.add), reads=rds, writes=[wri[ri]])
                            s.op("act", lambda e, ri=ri: e.copy(out=zb[ri][:, 0:n], in_=wri[ri][:, 0:n]), reads=[wri[ri]], writes=[zb[ri]])
                        if kind == "s" and idx < 3:
                            dv(lambda e, j=j: e.tensor_scalar(out=tq[:, 0:1], in0=wri[1][:, n - 1:n], scalar1=sN[:, j:j + 1], scalar2=None,
                                                              op0=ALU.mult), [wri[1], sN], [tq])
                            dv(lambda e, j=j: e.tensor_scalar(out=tq[:, 1:2], in0=wri[0][:, n - 1:n], scalar1=sN[:, j:j + 1], scalar2=None,
                                                              op0=ALU.mult), [wri[0], sN], [tq])
                            dv(lambda e, j=j: e.scalar_tensor_tensor(out=init[:, j, 0:1], in0=wri[0][:, n - 1:n], scalar=cN[:, j:j + 1],
                                                                     in1=tq[:, 0:1], op0=ALU.mult, op1=ALU.subtract), [wri[0], cN, tq], [init])
                            dv(lambda e, j=j: e.scalar_tensor_tensor(out=init[:, j, 1:2], in0=wri[1][:, n - 1:n], scalar=cN[:, j:j + 1],
                                                                     in1=tq[:, 1:2], op0=ALU.mult, op1=ALU.add), [wri[1], cN, tq], [init])
                        if kind == "p":
                            fb = FIN[idx]
                            cl, sl_ = ccol[:, j4, 0:1], scol[:, j4, 0:1]
                            zrl, zil = wri[0][:, n - 1:n], wri[1][:, n - 1:n]
                            dv(lambda e: e.tensor_tensor(out=fq[:, 0:1], in0=sl_, in1=zil, op=ALU.mult), [scol, wri[1]], [fq])
                            dv(lambda e: e.tensor_tensor(out=fq[:, 1:2], in0=cl, in1=zrl, op=ALU.mult), [ccol, wri[0]], [fq])
                            dv(lambda e, fb=fb, j=j: e.tensor_tensor(out=fb[:, j, 0:1], in0=fq[:, 1:2], in1=fq[:, 0:1], op=ALU.subtract), [fq], [fb])
                            dv(lambda e: e.tensor_tensor(out=fq[:, 2:3], in0=sl_, in1=zrl, op=ALU.mult), [scol, wri[0]], [fq])
                            dv(lambda e: e.tensor_tensor(out=fq[:, 3:4], in0=cl, in1=zil, op=ALU.mult), [ccol, wri[1]], [fq])
                            dv(lambda e, fb=fb, j=j: e.tensor_tensor(out=fb[:, j, 1:2], in0=fq[:, 2:3], in1=fq[:, 3:4], op=ALU.add), [fq], [fb])
                        zr_, zi_ = zb[0][:, 0:n], zb[1][:, 0:n]
                        sr_o = Sb[j4][0][:, 0:n] if d == 0 else Sb[j4][0][:, 0:n][:, ::-1]
                        si_o = Sb[j4][1][:, 0:n] if d == 0 else Sb[j4][1][:, 0:n][:, ::-1]
                        dv(lambda e: e.tensor_tensor(out=tD[0][:, 0:n], in0=c_f, in1=zr_, op=ALU.mult), [tcb, zb[0]], [tD[0]])
                        dv(lambda e: e.tensor_tensor(out=tD[1][:, 0:n], in0=sn_f, in1=zi_, op=ALU.mult), [tsb, zb[1]], [tD[1]])
                        dv(lambda e, sr_o=sr_o: e.tensor_tensor(out=sr_o, in0=tD[0][:, 0:n], in1=tD[1][:, 0:n], op=ALU.subtract),
                           [tD[0], tD[1]], [Sb[j4][0]])
                        dv(lambda e: e.tensor_tensor(out=tP[0][:, 0:n], in0=sn_f, in1=zr_, op=ALU.mult), [tsb, zb[0]], [tP[0]])
                        dv(lambda e: e.tensor_tensor(out=tP[1][:, 0:n], in0=c_f, in1=zi_, op=ALU.mult), [tcb, zb[1]], [tP[1]])
                        dv(lambda e, si_o=si_o: e.tensor_tensor(out=si_o, in0=tP[0][:, 0:n], in1=tP[1][:, 0:n], op=ALU.add),
                           [tP[0], tP[1]], [Sb[j4][1]])
                    for p0 in range(0, n, 512):
                        pn = min(512, n - p0)
                        pb = ps()
                        pairs = []
                        rd = [CT]
                        for j4 in range(4):
                            for ri in range(2):
                                pairs.append((CT[:, a * 4 + j4, ri, :], Sb[j4][ri][:, p0:p0 + pn]))
                                rd.append(Sb[j4][ri])
                        mm_group(pb, pb[:, 0:pn], pairs, rd)
                        ysl = Y[:, a, tok0 + p0:tok0 + p0 + pn]
                        if d == 0:
                            dv(lambda e, pb=pb, ysl=ysl, pn=pn, a=a, p0=p0, tok0=tok0: e.scalar_tensor_tensor(
                                out=ysl, in0=uS[:, a, tok0 + p0:tok0 + p0 + pn], scalar=dco[:, a:a + 1], in1=pb[:, 0:pn],
                                op0=ALU.mult, op1=ALU.add), [uS, dco, pb], [Y])
                        else:
                            dv(lambda e, pb=pb, ysl=ysl, pn=pn: e.tensor_tensor(out=ysl, in0=ysl, in1=pb[:, 0:pn], op=ALU.add),
                               [Y, pb], [Y])
            for sq in range(4):
                s.dma("sp", _ap(nst)[sq, l, d].rearrange("(j q) r -> q j r", q=128), FIN[sq][:], reads=[FIN[sq]], writes=[nst])
            s.barrier()
            s.release(m2)
        m3 = s.mark()
        tD = [s.sbuf("tDg%d" % i, [128, 512], F32) for i in range(2)]
        tP = [s.sbuf("tPg%d" % i, [128, 512], F32) for i in range(2)]
        Wg = s.sbuf("Wg", [128, 2, 256], BF16)
        s.dma("pool", Wg[:], _ap(glu_w)[l].rearrange("(a p) f -> p a f", p=128), writes=[Wg])
        gb = s.sbuf("gb", [128, 2], F32)
        s.dma("sp", gb[:], _ap(glu_b)[l, :].rearrange("(a p) -> p a", p=128), writes=[gb])
        gel = [s.sbuf("gel%d" % a, [128, 512], F32) for a in range(2)]
        gelb = [s.sbuf("gelb%d" % a, [128, 512], BF16) for a in range(2)]
        sig = s.sbuf("sig", [128, 512], F32)
        yo = [s.sbuf("yo%d" % a, [128, 512], BF16) for a in range(2)]
        for blk in range(NB):
            tok = slice(blk * 512, (blk + 1) * 512)
            for a in range(2):
                ysl = Y[:, a, tok]
                s.op("act", lambda e, ysl=ysl: e.activation(out=tD[0][:, 0:512], in_=ysl, func=AF.Square), reads=[Y], writes=[tD[0]])
                dv(lambda e: e.tensor_scalar(out=tD[0][:, 0:512], in0=tD[0][:, 0:512], scalar1=0.044715, scalar2=1.0,
                                             op0=ALU.mult, op1=ALU.add), [tD[0]], [tD[0]])
                dv(lambda e, ysl=ysl: e.tensor_tensor(out=tD[1][:, 0:512], in0=tD[0][:, 0:512], in1=ysl, op=ALU.mult), [tD[0], Y], [tD[1]])
                s.op("act", lambda e: e.activation(out=tP[0][:, 0:512], in_=tD[1][:, 0:512], func=AF.Sigmoid, scale=1.5957691216057308),
                     reads=[tD[1]], writes=[tP[0]])
                dv(lambda e, a=a, ysl=ysl: e.tensor_tensor(out=gel[a][:], in0=tP[0][:, 0:512], in1=ysl, op=ALU.mult), [tP[0], Y], [gel[a]])
                s.op("act", lambda e, a=a: e.copy(out=gelb[a][:], in_=gel[a][:]), reads=[gel[a]], writes=[gelb[a]])
            for a in range(2):
                pb = ps()
                mm_group(pb, pb[:], [(Wg[:, a2, a * 128:(a + 1) * 128], gelb[a2][:]) for a2 in range(2)], [Wg, gelb[0], gelb[1]])
                s.op("act", lambda e, pb=pb, a=a: e.activation(out=sig[:], in_=pb[:], func=AF.Sigmoid, bias=gb[:, a:a + 1]),
                     reads=[pb, gb], writes=[sig])
                dv(lambda e, a=a: e.tensor_tensor(out=yo[a][:], in0=sig[:], in1=gel[a][:], op=ALU.mult), [sig, gel[a]], [yo[a]])
                s.dma("sp", _ap(ymix_d)[a * 128:(a + 1) * 128, tok], yo[a][:], reads=[yo[a]], writes=[ymix_d])
        s.barrier()
        s.release(m0)

    def attn_part(l):
        m0 = s.mark()
        qP = s.sbuf("qP", [128, 4, NP_TOK], BF16)
        kP = s.sbuf("kP", [128, 4, NP_TOK], BF16)
        for t4 in range(4):
            s.dma("sp", qP[:, t4, :], _ap(q_d)[t4 * 128:(t4 + 1) * 128, 0:NP_TOK], reads=[q_d], writes=[qP])
            s.dma("sp", kP[:, t4, :], _ap(k_d)[t4 * 128:(t4 + 1) * 128, 0:NP_TOK], reads=[k_d], writes=[kP])
        vP = s.sbuf("vP", [128, 8, 512], BF16)
        s.dma("sp", vP[:], _ap(v_d)[0:NP_TOK, :].rearrange("(t p) f -> p t f", p=128), reads=[v_d], writes=[vP])
        Eb = [s.sbuf("Eb%d" % i, [128, 2, 256], BF16) for i in range(2)]
        rdn = [s.sbuf("rdn%d" % i, [128, 256], F32) for i in range(2)]
        yat = [s.sbuf("yat%d" % i, [128, 256], BF16) for i in range(2)]
        n_it = 0
        for sq in range(4):
            for hp in range(4):
                ya = yat[(sq * 4 + hp) % 2]
                for hh in range(2):
                    pr = slice(hh * 64, (hh + 1) * 64)
                    E = Eb[n_it % 2]
                    rd_ = rdn[n_it % 2]
                    n_it += 1
                    for kt2 in range(2):
                        pb = ps()
                        mm_group(pb, pb[:, 0:256],
                                 [(kP[pr, hp, sq * 256 + kt2 * 128: sq * 256 + (kt2 + 1) * 128], qP[pr, hp, sq * 256:(sq + 1) * 256])],
                                 [kP, qP])
                        s.op("act", lambda e, E=E, pb=pb, kt2=kt2: e.activation(out=E[:, kt2, :], in_=pb[:, 0:256], func=AF.Exp, scale=0.125),
                             reads=[pb], writes=[E])
                    pn_ = ps()
                    mm_group(pn_, pn_[:, 0:256], [(vP[:, sq * 2 + kt2, hp * 128:(hp + 1) * 128], E[:, kt2, :]) for kt2 in range(2)], [vP, E])
                    pd_ = ps()
                    mm_group(pd_, pd_[:, 0:256], [(ones_bf[:], E[:, kt2, :]) for kt2 in range(2)], [ones_bf, E])
                    s.op("dve", lambda e, rd_=rd_, pd_=pd_, pr=pr: e.reciprocal(out=rd_[pr, :], in_=pd_[pr, 0:256]), reads=[pd_], writes=[rd_])
                    s.op("dve", lambda e, ya=ya, pn_=pn_, rd_=rd_, pr=pr: e.tensor_tensor(out=ya[pr, :], in0=pn_[pr, 0:256], in1=rd_[pr, :],
                                                                                        op=ALU.mult), reads=[pn_, rd_], writes=[ya])
                s.dma("sp", _ap(ymix_d)[256 + hp * 128:256 + (hp + 1) * 128, sq * 256:(sq + 1) * 256], ya[:], reads=[ya], writes=[ymix_d])
        s.barrier()
        s.release(m0)
        m0 = s.mark()
        BTt = s.sbuf("BTt", [128, 8, 15, 64], BF16)
        identb = s.sbuf("identb", [128, 128], BF16)
        s.op("dve", lambda e: e.tensor_copy(out=identb[:], in_=ident[:]), reads=[ident], writes=[identb])
        m1 = s.mark()
        oh = s.sbuf("oh", [31, 64, 128], F32)
        s.dma("sp", oh[:], _ap(c_oh).rearrange("d (k q) -> d k q", q=128), writes=[oh])
        msk = s.sbuf("msk", [128, 64], F32)
        s.dma("sp", msk[:], _ap(c_mask), writes=[msk])
        rpT = s.sbuf("rpT", [31, 128], F32)
        s.op("dve", lambda e: e.memset(rpT[:], 0.0), writes=[rpT])
        s.dma("sp", rpT[:, 0:120], _ap(rpb)[l].rearrange("x d -> d x"), reads=[rpT], writes=[rpT])
        BTf = BTt[:].rearrange("p h d k -> p (h d) k")
        for k4 in range(16):
            pb = ps()
            for ki in range(4):
                kk = k4 * 4 + ki
                mm_group(pb, pb[:, ki * 128:ki * 128 + 128], [(oh[:, kk, :], rpT[:])], [oh, rpT])
            for ki in range(4):
                kk = k4 * 4 + ki
                s.op("dve", lambda e, pb=pb, ki=ki, kk=kk: e.tensor_scalar(out=BTf[:, :, kk], in0=pb[:, ki * 128:ki * 128 + 120],
                                                                           scalar1=msk[:, kk:kk + 1], scalar2=8.0, op0=ALU.add, op1=ALU.mult),
                     reads=[pb, msk], writes=[BTt])
        s.barrier()
        s.release(m1)
        if int(os.environ.get("MK_ATT_STOP", "9")) <= 2:
            return
        qS = s.sbuf("qS", [128, 4, NS_TOK], BF16)
        kS = s.sbuf("kS", [128, 4, NS_TOK], BF16)
        for t4 in range(4):
            s.dma("sp", qS[:, t4, :], _ap(q_d)[t4 * 128:(t4 + 1) * 128, NP_TOK:NT], reads=[q_d], writes=[qS])
            s.dma("sp", kS[:, t4, :], _ap(k_d)[t4 * 128:(t4 + 1) * 128, NP_TOK:NT], reads=[k_d], writes=[kS])
        vS = s.sbuf("vS", [128, 64, 512], BF16)
        s.dma("sp", vS[0:64, :, :], _ap(v_d)[NP_TOK:NT, :].rearrange("(r c) f -> c r f", c=64), reads=[v_d], writes=[vS])
        s.dma("sp", vS[64:128, 0:63, :], _ap(v_d)[NP_TOK + 64:NT, :].rearrange("(r c) f -> c r f", c=64), reads=[v_d], writes=[vS])
        s.op("dve", lambda e: e.memset(vS[64:128, 63:64, :], 0.0), reads=[vS], writes=[vS])
        ck32 = s.sbuf("ck32", [128, 2, 512], F32)
        s.dma("sp", ck32[:], _ap(ck)[l].rearrange("(t p) f -> p t f", p=128), writes=[ck32])
        kC = s.sbuf("kC", [128, 4, 256], BF16)
        for hp in range(4):
            pb = ps()
            s.group("pe", [lambda e, pb=pb, t=t, hp=hp: e.transpose(pb[:, t * 128:(t + 1) * 128], ck32[:, t, hp * 128:(hp + 1) * 128], ident[:])
                           for t in range(2)], reads=[ck32, ident], writes=[pb])
            s.op("act", lambda e, pb=pb, hp=hp: e.copy(out=kC[:, hp, :], in_=pb[:, 0:256]), reads=[pb], writes=[kC])
        vC = s.sbuf("vC", [128, 2, 512], BF16)
        s.dma("pool", vC[:], _ap(cv)[l].rearrange("(t p) f -> p t f", p=128), writes=[vC])
        NBUF = 3
        EC = [s.sbuf("EC%d" % i, [128, 6, 2, 64], BF16) for i in range(NBUF)]
        rdw = [s.sbuf("rdw%d" % i, [128, 128], F32) for i in range(NBUF)]
        YN = [s.sbuf("YN%d" % i, [128, 4, 512], BF16) for i in range(2)]
        it = 0
        for r in range(int(os.environ.get("MK_NA_ROWS", "64"))):
            rs = min(max(r - 4, 0), 56)
            dr0 = rs - r + 7
            yn = YN[(r // 8) % 2]
            for hp in range(4):
                k2 = it % NBUF
                it += 1
                E_ = EC[k2]
                for hh in range(2):
                    pr = slice(hh * 64, (hh + 1) * 64)
                    h = 2 * hp + hh
                    qrow = qS[pr, hp, r * 64:(r + 1) * 64]
                    idb = identb[pr, hh * 64:(hh + 1) * 64]
                    pw = ps()
                    fw = []
                    for m in range(4):
                        fw.append(lambda e, pw=pw, m=m, pr=pr, qrow=qrow: e.matmul(
                            pw[:, m * 64:(m + 1) * 64], kS[pr, hp, (rs + 2 * m) * 64:(rs + 2 * m + 2) * 64], qrow, start=True, stop=False))
                        fw.append(lambda e, pw=pw, m=m, pr=pr, h=h, idb=idb: e.matmul(
                            pw[:, m * 64:(m + 1) * 64], BTt[pr, h, dr0 + 2 * m:dr0 + 2 * m + 2, :].rearrange("p d k -> p (d k)"), idb,
                            start=False, stop=True))
                    for t in range(2):
                        fw.append(lambda e, pw=pw, t=t, pr=pr, qrow=qrow: e.matmul(
                            pw[:, 256 + t * 64:256 + (t + 1) * 64], kC[pr, hp, t * 128:(t + 1) * 128], qrow, start=True, stop=True))
                    s.group("pe", fw, reads=[kS, kC, qS, BTt, identb], writes=[pw])
                    s.op("act", lambda e, E_=E_, pw=pw, hh=hh: e.activation(
                        out=E_[:, :, hh, :], in_=pw[:, 0:384].rearrange("p (m q) -> p m q", m=6), func=AF.Exp, scale=0.125),
                        reads=[pw], writes=[E_])
                pnd = ps()
                rhs6 = [E_[:, m, :, :].rearrange("p h q -> p (h q)") for m in range(6)]
                lv = [vS[:, rs + 2 * m, hp * 128:(hp + 1) * 128] for m in range(4)] + [vC[:, t, hp * 128:(hp + 1) * 128] for t in range(2)]
                mm_group(pnd, pnd[:, 0:128], [(lv[m], rhs6[m]) for m in range(6)], [vS, vC, E_])
                mm_group(pnd, pnd[:, 128:256], [(ones_bf[:], rhs6[m]) for m in range(6)], [ones_bf, E_])
                rd_ = rdw[k2]
                s.op("dve", lambda e, rd_=rd_, pnd=pnd: e.reciprocal(out=rd_[:], in_=pnd[:, 128:256]), reads=[pnd], writes=[rd_])
                for hh in range(2):
                    pr = slice(hh * 64, (hh + 1) * 64)
                    s.op("dve", lambda e, yn=yn, pnd=pnd, rd_=rd_, pr=pr, hp=hp, r=r, hh=hh: e.tensor_tensor(
                        out=yn[pr, hp, (r % 8) * 64:(r % 8 + 1) * 64], in0=pnd[pr, hh * 64:(hh + 1) * 64], in1=rd_[pr, hh * 64:(hh + 1) * 64],
                        op=ALU.mult), reads=[pnd, rd_], writes=[yn])
            if r % 8 == 7:
                tok0 = NP_TOK + (r // 8) * 512
                for hp in range(4):
                    s.dma("sp", _ap(ymix_d)[256 + hp * 128:256 + (hp + 1) * 128, tok0:tok0 + 512], yn[:, hp, :], reads=[yn], writes=[ymix_d])
        s.barrier()
        s.release(m0)

    def gate_part(l):
        m0 = s.mark()
        ws32 = s.sbuf("ws32", [128, 4, 128], F32)
        s.dma("sp", ws32[:], _ap(gm_ws)[l].rearrange("g i j -> i g j"), writes=[ws32])
        wsT = s.sbuf("wsT", [128, 4, 128], BF16)
        pb = ps()
        s.group("pe", [lambda e, g=g: e.transpose(pb[:, g * 128:(g + 1) * 128], ws32[:, g, :], ident[:]) for g in range(4)],
                reads=[ws32, ident], writes=[pb])
        s.op("act", lambda e: e.copy(out=wsT[:].rearrange("p g i -> p (g i)"), in_=pb[:]), reads=[pb], writes=[wsT])
        BS = s.sbuf("BS", [128, 2, 128], F32)
        for g in range(4):
            s.dma("sp", BS[(g % 2) * 64:(g % 2 + 1) * 64, g // 2, :], _ap(gm_bs)[l, g:g + 1, :].to_broadcast([64, 128]), writes=[BS])
        ug = [s.sbuf("ug%d" % i, [128, 2, 512], BF16) for i in range(2)]
        vAl = [s.sbuf("vAl%d" % i, [128, 4, 256], BF16) for i in range(2)]
        vBl = [s.sbuf("vBl%d" % i, [128, 4, 256], BF16) for i in range(2)]
        tg = [s.sbuf("tg%d" % i, [128, 128], F32) for i in range(2)]
        yg = [s.sbuf("yg%d" % i, [128, 2, 512], BF16) for i in range(2)]
        it = 0
        def fetch(blk):
            k2 = blk % 2
            tok = slice(blk * 512, (blk + 1) * 512)
            for a in range(2):
                s.dma("sp", ug[k2][:, a, :], _ap(ug_d)[a * 128:(a + 1) * 128, tok], reads=[ug_d], writes=[ug[k2]])
            s.dma("sp", vAl[k2][:], _ap(va_d)[tok, :].rearrange("(t p) f -> p t f", p=128), reads=[va_d], writes=[vAl[k2]])
            s.dma("sp", vBl[k2][:], _ap(vb_d)[tok, :].rearrange("(t p) f -> p t f", p=128), reads=[vb_d], writes=[vBl[k2]])

        fetch(0)
        for blk in range(NB):
            k2 = blk % 2
            tok = slice(blk * 512, (blk + 1) * 512)
            if blk + 1 < NB:
                fetch(blk + 1)
            for t in range(4):
                for a in range(2):
                    pq = ps()
                    mm_group(pq, pq[:, 0:128], [(vAl[k2][:, t, a * 128:(a + 1) * 128], wsT[:, 2 * a, :]),
                                                (vBl[k2][:, t, a * 128:(a + 1) * 128], wsT[:, 2 * a + 1, :])], [vAl[k2], vBl[k2], wsT])
                    tt_ = tg[it % 2]
                    it += 1
                    s.op("dve", lambda e, tt_=tt_, pq=pq, a=a: e.tensor_tensor(out=tt_[:], in0=pq[:, 0:128], in1=BS[:, a, :], op=ALU.add),
                         reads=[pq, BS], writes=[tt_])
                    s.op("dve", lambda e, tt_=tt_, a=a, t=t, k2=k2: e.tensor_tensor(
                        out=yg[k2][:, a, t * 128:(t + 1) * 128], in0=tt_[:], in1=ug[k2][:, a, t * 128:(t + 1) * 128], op=ALU.mult),
                        reads=[tt_, ug[k2]], writes=[yg[k2]])
            for a in range(2):
                s.dma("sp", _ap(ymix_d)[768 + a * 128:768 + (a + 1) * 128, tok], yg[k2][:, a, :], reads=[yg[k2]], writes=[ymix_d])
        s.barrier()
        s.release(m0)

    def mix_phase(l):
        ssm_part(l)
        attn_part(l)
        gate_part(l)

    PH = os.environ.get("MK_PHASES", "all")
    if PH != "all":
        for name in PH.split(","):
            {"adaln": setup_adaln, "ffn": lambda: ffn_phase(0, 0, f1_in, f1_out, True, False), "proj": lambda: proj_phase(0),
             "ssm": lambda: ssm_part(0), "attn": lambda: attn_part(0), "gate": lambda: gate_part(0),
             "outproj": lambda: outproj_phase(0)}[name]()
        s.barrier()
        s.release(0)
        ctx.__exit__(None, None, None)
        return nc
    mW = s.mark()
    W = alloc_ffn_w()
    setup_adaln(after_dma=lambda: issue_ffn_w(W, 0, f1_in, f1_out))
    for l in range(DEPTH):
        if l == 0:
            ffn_phase(l, 0, f1_in, f1_out, first=True, last=False, W=W)
            s.release(mW)
        else:
            ffn_phase(l, 0, f1_in, f1_out, first=False, last=False)
        proj_phase(l)
        ssm_part(l)
        attn_part(l)
        mW = s.mark()
        W = alloc_ffn_w()
        mWO = s.mark()
        WO = s.sbuf("WO", [128, 8, 1024], BF16)
        s.dma("pool", WO[:], _ap(w_out)[l].rearrange("(kt p) f -> p kt f", p=128), writes=[WO])
        issue_ffn_w(W, l, f2_in, f2_out)
        gate_part(l)
        outproj_phase(l, WO=WO)
        s.release(mWO)
        ffn_phase(l, 2, f2_in, f2_out, first=False, last=(l == DEPTH - 1), W=W)
        s.release(mW)
    outs = [yp, ys, nk, nv, nst]
    s.barrier()
    s.release(0)
    ctx.__exit__(None, None, None)
    return nc


_NC_CACHE = {}


def _consts():
    ident = np.eye(128, dtype=np.float32)
    kc = np.arange(64)[:, None]
    qc = np.arange(64)[None, :]
    dc = kc - qc + 15
    oh = np.zeros((31, 64, 128), np.float32)
    for q in range(64):
        for k in range(64):
            if 0 <= dc[k, q] <= 30:
                oh[dc[k, q], k, q] = 1.0
                oh[dc[k, q], k, 64 + q] = 1.0
    cs = np.clip(np.arange(64) - 8, 0, 48)
    win = (kc >= cs[None, :]) & (kc < cs[None, :] + 16)
    mask = np.where(win, 0.0, NEG).astype(np.float32)
    iota = np.tile(np.arange(1024, dtype=np.float32)[None, :], (128, 1))
    return {"c_ident": ident, "c_oh": oh.reshape(31, 8192), "c_mask": np.ascontiguousarray(np.concatenate([mask.T, mask.T], axis=0)), "c_iota": iota}


def kernel(**inputs):
    debug = bool(int(os.environ.get("MK_DEBUG", "0")))
    key = ("nc", debug)
    if key not in _NC_CACHE:
        _NC_CACHE[key] = build_program(debug=debug)
    nc = _NC_CACHE[key]
    f = lambda a: np.ascontiguousarray(np.asarray(a, dtype=np.float32))
    x_prompt, x_sample = f(inputs["x_prompt"]), f(inputs["x_sample"])
    c, c_ctx = f(inputs["c"]), f(inputs["c_ctx"])
    cache_k, cache_v, state_ssm = f(inputs["cache_k"]), f(inputs["cache_v"]), f(inputs["state_ssm"])
    shared = {}
    for name in ("w_ada", "b_ada", "norm_ffn1", "norm_mix", "norm_ffn2", "ffn1_w_in", "ffn1_w_out", "ffn2_w_in",
                 "ffn2_w_out", "w_in", "w_out", "ssm_d", "ssm_glu_w", "ssm_glu_b", "na_q_norm", "na_k_norm",
                 "gm_ws", "gm_bs"):
        shared[name] = f(inputs[name])
    shared["ssm_lambda_re"] = f(inputs["ssm_lambda_re"]).reshape(DEPTH, 2, 1024)
    shared["ssm_lambda_im"] = f(inputs["ssm_lambda_im"]).reshape(DEPTH, 2, 1024)
    shared["ssm_log_dt"] = f(inputs["ssm_log_dt"])
    shared["ssm_b_re"] = f(inputs["ssm_b_re"]).reshape(DEPTH, 2, 1024, 16)
    shared["ssm_b_im"] = f(inputs["ssm_b_im"]).reshape(DEPTH, 2, 1024, 16)
    shared["ssm_c_re"] = f(inputs["ssm_c_re"]).reshape(DEPTH, 2, 256, 64)
    shared["ssm_c_im"] = f(inputs["ssm_c_im"]).reshape(DEPTH, 2, 256, 64)
    shared["na_rpb"] = f(inputs["na_rpb"]).reshape(DEPTH, 120, 31)
    shared.update(_consts())
    in_maps = []
    for core in range(8):
        b = core // 2
        m = dict(shared)
        m["xp"] = x_prompt[4 * core:4 * core + 4].reshape(NP_TOK, D)
        m["xs"] = x_sample[b]
        m["cond"] = np.stack([c_ctx, c[b]], axis=0)
        m["ck"] = cache_k[b].reshape(DEPTH, 256, 512)
        m["cv"] = cache_v[b].reshape(DEPTH, 256, 512)
        m["sst"] = state_ssm[b].reshape(DEPTH, 2, 1024, 2)
        hs = np.zeros((128, 2), np.float32)
        hs[:, core % 2] = 1.0
        m["c_hsel"] = hs
        in_maps.append(m)
    res = run_bass_kernel_spmd(nc, in_maps, core_ids=list(range(8)))
    R = res.results
    if debug:
        kernel.last_results = R
    y_prompt = np.concatenate([R[i]["yp"].reshape(4, 256, D) for i in range(8)], axis=0)
    y_sample = np.stack([np.concatenate([R[2 * b]["ys"], R[2 * b + 1]["ys"]], axis=0) for b in range(4)], axis=0)
    new_k = np.concatenate([R[i]["nk"].reshape(4, DEPTH, 256, 8, 64) for i in range(8)], axis=0)
    new_v = np.concatenate([R[i]["nv"].reshape(4, DEPTH, 256, 8, 64) for i in range(8)], axis=0)
    new_s = np.concatenate([R[i]["nst"].reshape(4, DEPTH, 2, 16, 64, 2) for i in range(8)], axis=0)
    return (y_prompt.astype(np.float32), y_sample.astype(np.float32), new_k.astype(np.float32),
            new_v.astype(np.float32), new_s.astype(np.float32))
```

```python
import math
import os
import numpy as np
import concourse.bass as bass
import concourse.mybir as mybir
from concourse.bass_utils import run_bass_kernel_spmd

F32 = mybir.dt.float32
BF16 = mybir.dt.bfloat16
AF = mybir.ActivationFunctionType
ALU = mybir.AluOpType
AX = mybir.AxisListType

D = 1024
DFF = 2816
DEPTH = 2
NP_TOK = 1024
NS_TOK = 4096
NT = NP_TOK + NS_TOK
NB = NT // 512
INC = 2304
TWO_PI = 2.0 * math.pi
NEG = -30000.0


class Buf:
    __slots__ = ("t", "w", "r", "name")

    def __init__(self, t, name=""):
        self.t = t
        self.w = None
        self.r = {}
        self.name = name

    def __getitem__(self, idx):
        return self.t[idx]


class Sched:
    def __init__(self, nc, n_dma_sems=40):
        self.nc = nc
        self.stack = []
        self.eng = {"pe": nc.tensor, "act": nc.scalar, "dve": nc.vector, "pool": nc.gpsimd, "sp": nc.sync}
        self.sems = {}
        self.cnt = {}
        for k in ("pe", "act", "dve", "pool"):
            self.sems[k] = self._sem("s_" + k)
            self.cnt[k] = 0
        self.dma_keys = []
        for i in range(n_dma_sems):
            k = "d%d" % i
            self.sems[k] = self._sem("s_" + k)
            self.cnt[k] = 0
            self.dma_keys.append(k)
        self.dma_rr = 0
        self.seen = {e: {} for e in self.eng}
        self.ninst = 0

    def _sem(self, name):
        cm = self.nc.semaphore(name)
        h = cm.__enter__()
        self.stack.append(cm)
        return h

    def mark(self):
        return len(self.stack)

    def release(self, mark):
        while len(self.stack) > mark:
            self.stack.pop().__exit__(None, None, None)

    def sbuf(self, name, shape, dtype):
        self.uid = getattr(self, "uid", 0) + 1
        name = "%s_u%d" % (name, self.uid)
        cm = self.nc.sbuf_tensor(name, list(shape), dtype)
        t = cm.__enter__()
        self.stack.append(cm)
        return Buf(t, name)

    def psum(self, name, shape, dtype=F32):
        cm = self.nc.psum_tensor(name, list(shape), dtype)
        t = cm.__enter__()
        self.stack.append(cm)
        return Buf(t, name)

    def dram(self, name, shape, dtype, kind="Internal"):
        return Buf(self.nc.dram_tensor(name, list(shape), dtype, kind=kind), name)

    def _need(self, e, deps):
        seen = self.seen[e]
        todo = {}
        for (k, v) in deps:
            if seen.get(k, 0) >= v:
                continue
            if todo.get(k, 0) < v:
                todo[k] = v
        for k, v in todo.items():
            self.eng[e].wait_ge(self.sems[k], v)
            seen[k] = v
            self.ninst += 1

    @staticmethod
    def _deps(reads, writes):
        deps = []
        for b in reads:
            if b.w is not None:
                deps.append(b.w)
        for b in writes:
            if b.w is not None:
                deps.append(b.w)
            deps.extend(b.r.items())
        return deps

    def _commit(self, k, v, reads, writes):
        for b in reads:
            if b.r.get(k, 0) < v:
                b.r[k] = v
        for b in writes:
            b.w = (k, v)
            b.r = {}

    def op(self, e, fn, reads=(), writes=()):
        self._need(e, self._deps(reads, writes))
        inst = fn(self.eng[e])
        self.cnt[e] += 1
        inst.then_inc(self.sems[e], 1)
        self._commit(e, self.cnt[e], reads, writes)
        self.ninst += 1
        return inst

    def group(self, e, fns, reads=(), writes=()):
        self._need(e, self._deps(reads, writes))
        inst = None
        for fn in fns:
            inst = fn(self.eng[e])
            self.ninst += 1
        self.cnt[e] += 1
        inst.then_inc(self.sems[e], 1)
        self._commit(e, self.cnt[e], reads, writes)

    def dma(self, q, out_ap, in_ap, reads=(), writes=(), **kw):
        nsw = 8
        if q == "pool":
            self.sw_rr = (getattr(self, "sw_rr", -1) + 1) % nsw
            k = self.dma_keys[self.sw_rr]
        else:
            k = self.dma_keys[nsw + self.dma_rr]
            self.dma_rr = (self.dma_rr + 1) % (len(self.dma_keys) - nsw)
        deps = self._deps(reads, writes)
        if self.cnt[k] > 0:
            deps.append((k, self.cnt[k]))
        self._need(q, deps)
        inst = self.eng[q].dma_start(out=out_ap, in_=in_ap, **kw)
        self.cnt[k] += 16
        inst.then_inc(self.sems[k], 16)
        self._commit(k, self.cnt[k], reads, writes)
        self.ninst += 1
        return inst

    def barrier(self):
        allk = [(k, v) for k, v in self.cnt.items() if v > 0]
        for e in self.eng:
            self._need(e, allk)


def _ap(b):
    return b.t.ap()


def build_program(debug=False):
    nc = bass.Bass("TRN2", target_bir_lowering=False)
    s = Sched(nc)
    ctx = nc.allow_non_contiguous_dma(reason="small strided parameter loads")
    ctx.__enter__()

    def din(name, shape):
        return s.dram(name, shape, F32, kind="ExternalInput")

    def dout(name, shape):
        return s.dram(name, shape, F32, kind="ExternalOutput")

    xp = din("xp", [NP_TOK, D])
    xs = din("xs", [NS_TOK, D])
    cond = din("cond", [2, D])
    ck = din("ck", [DEPTH, 256, 512])
    cv = din("cv", [DEPTH, 256, 512])
    sst = din("sst", [DEPTH, 2, 1024, 2])
    w_ada = din("w_ada", [DEPTH, D, 9 * D])
    b_ada = din("b_ada", [DEPTH, 9 * D])
    norms = [din("norm_ffn1", [DEPTH, D]), din("norm_mix", [DEPTH, D]), din("norm_ffn2", [DEPTH, D])]
    f1_in = din("ffn1_w_in", [DEPTH, D, 2 * DFF])
    f1_out = din("ffn1_w_out", [DEPTH, DFF, D])
    f2_in = din("ffn2_w_in", [DEPTH, D, 2 * DFF])
    f2_out = din("ffn2_w_out", [DEPTH, DFF, D])
    w_in = din("w_in", [DEPTH, D, INC])
    w_out = din("w_out", [DEPTH, D, D])
    lam_re = din("ssm_lambda_re", [DEPTH, 2, 1024])
    lam_im = din("ssm_lambda_im", [DEPTH, 2, 1024])
    log_dt = din("ssm_log_dt", [DEPTH, 2, 16])
    b_re = din("ssm_b_re", [DEPTH, 2, 1024, 16])
    b_im = din("ssm_b_im", [DEPTH, 2, 1024, 16])
    c_re = din("ssm_c_re", [DEPTH, 2, 256, 64])
    c_im = din("ssm_c_im", [DEPTH, 2, 256, 64])
    ssm_d = din("ssm_d", [DEPTH, 256])
    glu_w = din("ssm_glu_w", [DEPTH, 256, 256])
    glu_b = din("ssm_glu_b", [DEPTH, 256])
    qn_g = din("na_q_norm", [DEPTH, 64])
    kn_g = din("na_k_norm", [DEPTH, 64])
    rpb = din("na_rpb", [DEPTH, 8 * 15, 31])
    gm_ws = din("gm_ws", [DEPTH, 4, 128, 128])
    gm_bs = din("gm_bs", [DEPTH, 4, 128])
    c_ident = din("c_ident", [128, 128])
    c_oh = din("c_oh", [31, 64 * 128])
    c_mask = din("c_mask", [128, 64])
    c_iota = din("c_iota", [128, 1024])
    c_hsel = din("c_hsel", [128, 2])
    yp = dout("yp", [NP_TOK, D])
    ys = dout("ys", [NS_TOK // 2, D])
    nk = dout("nk", [4, DEPTH, 256, 512])
    nv = dout("nv", [4, DEPTH, 256, 512])
    nst = dout("nst", [4, DEPTH, 2, 1024, 2])
    skind = "ExternalOutput" if debug else "Internal"
    xT_d = s.dram("xT_d", [D, NT], F32, kind=skind)
    ymix_d = s.dram("ymix_d", [D, NT], BF16, kind=skind)
    us_d = s.dram("us_d", [256, NT], BF16, kind=skind)
    q_d = s.dram("q_d", [512, NT], BF16, kind=skind)
    k_d = s.dram("k_d", [512, NT], BF16, kind=skind)
    v_d = s.dram("v_d", [NT, 512], BF16, kind=skind)
    ug_d = s.dram("ug_d", [256, NT], BF16, kind=skind)
    va_d = s.dram("va_d", [NT, 256], BF16, kind=skind)
    vb_d = s.dram("vb_d", [NT, 256], BF16, kind=skind)

    banks = [s.psum("bank%d" % i, [128, 512], F32) for i in range(8)]
    bank_rr = [0]

    def ps():
        b = banks[bank_rr[0]]
        bank_rr[0] = (bank_rr[0] + 1) % 8
        return b

    ident = s.sbuf("ident", [128, 128], F32)
    s.dma("sp", ident[:], _ap(c_ident), writes=[ident])
    ones_bf = s.sbuf("ones_bf", [128, 128], BF16)
    s.op("dve", lambda e: e.memset(ones_bf[:], 1.0), writes=[ones_bf])
    mean_bf = s.sbuf("mean_bf", [128, 128], BF16)
    s.op("dve", lambda e: e.memset(mean_bf[:], 1.0 / 1024.0), writes=[mean_bf])
    bd_bf = s.sbuf("bd_bf", [128, 128], BF16)
    s.op("dve", lambda e: e.memset(bd_bf[:], 0.0), writes=[bd_bf])
    s.op("dve", lambda e: e.memset(bd_bf[0:64, 0:64], 1.0 / 64.0), reads=[bd_bf], writes=[bd_bf])
    s.op("dve", lambda e: e.memset(bd_bf[64:128, 64:128], 1.0 / 64.0), reads=[bd_bf], writes=[bd_bf])
    pi_c = s.sbuf("pi_c", [128, 1], F32)
    s.op("dve", lambda e: e.memset(pi_c[:], math.pi), writes=[pi_c])
    hsel = s.sbuf("hsel", [128, 2], F32)
    s.dma("sp", hsel[:], _ap(c_hsel), writes=[hsel])
    eps6 = s.sbuf("eps6", [128, 1], F32)
    s.op("dve", lambda e: e.memset(eps6[:], 1e-6), writes=[eps6])
    eps5 = s.sbuf("eps5", [128, 1], F32)
    s.op("dve", lambda e: e.memset(eps5[:], 1e-5), writes=[eps5])

    def rsqrt(dst_b, dst_ap, src_b, src_ap, eps_b, scale=1.0):
        P = dst_ap.shape[0]
        s.op("act", lambda e: e.activation(out=dst_ap, in_=src_ap, func=AF.Sqrt, scale=scale, bias=eps_b[0:P, 0:1]),
             reads=[src_b, eps_b], writes=[dst_b])
        s.op("dve", lambda e: e.reciprocal(out=dst_ap, in_=dst_ap), reads=[dst_b], writes=[dst_b])
    Aco = s.sbuf("Aco", [128, DEPTH, 3, 2, 8], F32)
    Bco = s.sbuf("Bco", [128, DEPTH, 3, 2, 8], F32)
    Gco = s.sbuf("Gco", [128, DEPTH, 3, 2, 8], F32)

    def mm_group(out_buf, out_ap, pairs, reads):
        n = len(pairs)
        fns = []
        for i, (l, r) in enumerate(pairs):
            fns.append(lambda e, l=l, r=r, i=i: e.matmul(out_ap, l, r, start=(i == 0), stop=(i == n - 1)))
        s.group("pe", fns, reads=reads, writes=[out_buf])

    def setup_adaln(after_dma=None):
        m0 = s.mark()
        cs32 = s.sbuf("cs32", [128, 8, 2], F32)
        csb = s.sbuf("csb", [128, 8, 2], BF16)
        for c in range(2):
            s.dma("sp", cs32[:, :, c], _ap(cond)[c, :].rearrange("(kt p) -> p kt", p=128), writes=[cs32])
        s.op("act", lambda e: e.activation(out=csb[:], in_=cs32[:], func=AF.Silu), reads=[cs32], writes=[csb])
        gn = s.sbuf("gn", [128, DEPTH, 3, 8], F32)
        for i in range(3):
            for l in range(DEPTH):
                s.dma("sp", gn[:, l, i, :], _ap(norms[i])[l, :].rearrange("(kt p) -> p kt", p=128), writes=[gn])
        wa = [s.sbuf("wa%d" % i, [128, 8, 1024], BF16) for i in range(2)]
        badaT = s.sbuf("badaT", [128, 72], F32)
        modT = s.sbuf("modT", [128, 72, 2], F32)
        for l in range(DEPTH):
            s.dma("sp", badaT[:], _ap(b_ada)[l, :].rearrange("(ft p) -> p ft", p=128), writes=[badaT])
            pb = ps()
            for ch in range(9):
                w = wa[ch % 2]
                s.dma("pool", w[:], _ap(w_ada)[l, :, ch * 1024:(ch + 1) * 1024].rearrange("(kt p) f -> p kt f", p=128),
                      writes=[w])
                for f8 in range(8):
                    ft = ch * 8 + f8
                    mm_group(pb, pb[:, 2 * ft:2 * ft + 2],
                             [(w[:, kt, f8 * 128:(f8 + 1) * 128], csb[:, kt, :]) for kt in range(8)], [w, csb])
            for c in range(2):
                s.op("dve", lambda e, c=c: e.tensor_tensor(out=modT[:, :, c], in0=pb[:, c:144:2], in1=badaT[:], op=ALU.add),
                     reads=[pb, badaT], writes=[modT])
            for i in range(3):
                for c in range(2):
                    sh = modT[:, (3 * i) * 8:(3 * i) * 8 + 8, c]
                    sc = modT[:, (3 * i + 1) * 8:(3 * i + 1) * 8 + 8, c]
                    gt = modT[:, (3 * i + 2) * 8:(3 * i + 2) * 8 + 8, c]
                    s.op("dve", lambda e, sc=sc, l=l, i=i, c=c: e.scalar_tensor_tensor(
                        out=Aco[:, l, i, c, :], in0=sc, scalar=1.0, in1=gn[:, l, i, :], op0=ALU.add, op1=ALU.mult),
                        reads=[modT, gn], writes=[Aco])
                    s.op("dve", lambda e, sh=sh, l=l, i=i, c=c: e.tensor_copy(out=Bco[:, l, i, c, :], in_=sh),
                         reads=[modT], writes=[Bco])
                    s.op("dve", lambda e, gt=gt, l=l, i=i, c=c: e.tensor_scalar(
                        out=Gco[:, l, i, c, :], in0=gt, scalar1=(1.0 if i == 1 else 0.5), scalar2=None, op0=ALU.mult),
                        reads=[modT], writes=[Gco])
        if after_dma is not None:
            after_dma()
        s.barrier()
        s.release(m0)

    def load_xT(xT, blk):
        s.dma("sp", xT[:], _ap(xT_d)[:, blk * 512:(blk + 1) * 512].rearrange("(kt p) n -> p kt n", p=128),
              reads=[xT_d], writes=[xT])

    def store_xT(xT, blk):
        s.dma("sp", _ap(xT_d)[:, blk * 512:(blk + 1) * 512].rearrange("(kt p) n -> p kt n", p=128), xT[:],
              reads=[xT], writes=[xT_d])

    def load_x_tm(xT, xtm, blk):
        src = _ap(xp)[blk * 512:(blk + 1) * 512, :] if blk < 2 else _ap(xs)[(blk - 2) * 512:(blk - 1) * 512, :]
        for tt in range(4):
            s.dma("sp", xtm[:, 0, :], src[tt * 128:(tt + 1) * 128, :], writes=[xtm])
            for half in range(2):
                pb = ps()
                s.group("pe", [lambda e, k4=k4, half=half, pb=pb: e.transpose(pb[:, k4 * 128:(k4 + 1) * 128],
                                                                               xtm[:, 0, (half * 4 + k4) * 128:(half * 4 + k4 + 1) * 128], ident[:])
                               for k4 in range(4)], reads=[xtm, ident], writes=[pb])
                dst = xT[:, half * 4:(half + 1) * 4, tt * 128:(tt + 1) * 128]
                if half:
                    s.op("act", lambda e, dst=dst, pb=pb: e.copy(out=dst, in_=pb[:].rearrange("p (k n) -> p k n", k=4)),
                         reads=[pb], writes=[xT])
                else:
                    s.op("dve", lambda e, dst=dst, pb=pb: e.tensor_copy(out=dst, in_=pb[:].rearrange("p (k n) -> p k n", k=4)),
                         reads=[pb], writes=[xT])

    def store_y_tm(xT, ytm, blk):
        dst = _ap(yp)[blk * 512:(blk + 1) * 512, :] if blk < 2 else _ap(ys)[(blk - 2) * 512:(blk - 1) * 512, :]
        ydr = yp if blk < 2 else ys
        for tt in range(4):
            for half in range(2):
                pb = ps()
                s.group("pe", [lambda e, k4=k4, tt=tt, half=half, pb=pb: e.transpose(
                    pb[:, k4 * 128:(k4 + 1) * 128], xT[:, half * 4 + k4, tt * 128:(tt + 1) * 128], ident[:])
                    for k4 in range(4)], reads=[xT, ident], writes=[pb])
                if half:
                    s.op("act", lambda e, pb=pb: e.copy(out=ytm[:, 0, 512:1024], in_=pb[:]), reads=[pb], writes=[ytm])
                else:
                    s.op("dve", lambda e, pb=pb: e.tensor_copy(out=ytm[:, 0, 0:512], in_=pb[:]), reads=[pb], writes=[ytm])
            s.dma("sp", dst[tt * 128:(tt + 1) * 128, :], ytm[:, 0, :], reads=[ytm], writes=[ydr])

    def norm_mod(xT, hb, tmp2, rstd, l, i, c):
        for kt in range(8):
            s.op("act", lambda e, kt=kt: e.activation(out=hb[:, kt, :], in_=xT[:, kt, :], func=AF.Square),
                 reads=[xT], writes=[hb])
        pb = ps()
        mm_group(pb, pb[:], [(mean_bf[:], hb[:, kt, :]) for kt in range(8)], [mean_bf, hb])
        rsqrt(rstd, rstd[:], pb, pb[:], eps6)
        for kt in range(8):
            t = tmp2[kt % 2]
            s.op("dve", lambda e, kt=kt, t=t: e.tensor_tensor(out=t[:], in0=xT[:, kt, :], in1=rstd[:], op=ALU.mult),
                 reads=[xT, rstd], writes=[t])
            s.op("act", lambda e, kt=kt, t=t: e.activation(out=hb[:, kt, :], in_=t[:], func=AF.Identity,
                                                           scale=Aco[:, l, i, c, kt:kt + 1], bias=Bco[:, l, i, c, kt:kt + 1]),
                 reads=[t, Aco, Bco], writes=[hb])

    def alloc_ffn_w():
        W1 = [s.sbuf("W1_%d" % j, [128, 8, 512], BF16) for j in range(11)]
        W2 = [s.sbuf("W2_%d" % j, [128, 2, 1024], BF16) for j in range(11)]
        return (W1, W2)

    def issue_ffn_w(W, l, win_d, wout_d):
        W1, W2 = W
        for j in (0, 5, 1, 6, 2, 7, 3, 8, 4, 9, 10):
            s.dma("pool", W1[j][:], _ap(win_d)[l, :, j * 512:(j + 1) * 512].rearrange("(kt p) f -> p kt f", p=128),
                  writes=[W1[j]])
        for j in range(11):
            s.dma("pool", W2[j][:], _ap(wout_d)[l, j * 256:(j + 1) * 256, :].rearrange("(kt p) f -> p kt f", p=128),
                  writes=[W2[j]])

    def ffn_phase(l, i, win_d, wout_d, first, last, W=None):
        m0 = s.mark()
        if W is None:
            W = alloc_ffn_w()
            issue_ffn_w(W, l, win_d, wout_d)
        W1, W2 = W
        xTs = [s.sbuf("xT%d" % j, [128, 8, 512], F32) for j in range(2)]
        hb = s.sbuf("hb", [128, 8, 512], BF16)
        actb = s.sbuf("actb", [128, 22, 512], BF16)
        tmp2 = [s.sbuf("tmp%d" % j, [128, 512], F32) for j in range(2)]
        sg2 = tmp2
        rstd = s.sbuf("rstd", [128, 512], F32)
        if os.environ.get("MK_VERBOSE"):
            print("ffn phase free sbuf bytes/partition:", nc.sbuf_bytes_remaining, "first/last", first, last)
        xtm = s.sbuf("xtm", [128, 1, 1024], F32) if (first or last) else None

        def w1cols(col):
            return W1[col // 512], col % 512

        if last:
            items = [("std", 0), ("std", 1)] + [("own", i) for i in range(4)]
        else:
            items = [("std", blk) for blk in range(NB)]

        def fetch(it):
            kind, b_ = items[it]
            xT_ = xTs[it % 2]
            if kind == "own":
                load_xT(xT_, 2 + b_)
                for kt in range(8):
                    t = tmp2[kt % 2]
                    s.dma("sp", t[:], _ap(xT_d)[kt * 128:(kt + 1) * 128, (6 + b_) * 512:(7 + b_) * 512], reads=[xT_d], writes=[t])
                    s.op("dve", lambda e, kt=kt, xT_=xT_: e.tensor_scalar(out=xT_[:, kt, :], in0=xT_[:, kt, :], scalar1=hsel[:, 0:1],
                                                                          scalar2=None, op0=ALU.mult), reads=[xT_, hsel], writes=[xT_])
                    s.op("dve", lambda e, kt=kt, xT_=xT_, t=t: e.scalar_tensor_tensor(
                        out=xT_[:, kt, :], in0=t[:], scalar=hsel[:, 1:2], in1=xT_[:, kt, :], op0=ALU.mult, op1=ALU.add),
                        reads=[t, hsel, xT_], writes=[xT_])
            elif first:
                load_x_tm(xT_, xtm, b_)
            else:
                load_xT(xT_, b_)

        fetch(0)
        for it, (kind, blk) in enumerate(items):
            c = 0 if (kind == "std" and blk < 2) else 1
            xT = xTs[it % 2]
            nxt_own = it + 1 < len(items) and items[it + 1][0] == "own"
            if it + 1 < len(items) and not first and not nxt_own:
                fetch(it + 1)
            norm_mod(xT, hb, tmp2, rstd, l, i, c)
            for j in range(22):
                wg, og = w1cols(j * 128)
                wu, ou = w1cols(DFF + j * 128)
                pg = ps()
                mm_group(pg, pg[:], [(wg[:, kt, og:og + 128], hb[:, kt, :]) for kt in range(8)], [wg, hb])
                pu = ps()
                mm_group(pu, pu[:], [(wu[:, kt, ou:ou + 128], hb[:, kt, :]) for kt in range(8)], [wu, hb])
                sg = sg2[j % 2]
                s.op("act", lambda e, sg=sg, pg=pg: e.activation(out=sg[:], in_=pg[:], func=AF.Silu), reads=[pg], writes=[sg])
                s.op("dve", lambda e, sg=sg, pu=pu, j=j: e.tensor_tensor(out=actb[:, j, :], in0=sg[:], in1=pu[:], op=ALU.mult),
                     reads=[sg, pu], writes=[actb])
            if it + 1 < len(items) and first:
                fetch(it + 1)
            for ft in range(8):
                po = ps()
                mm_group(po, po[:], [(W2[j // 2][:, j % 2, ft * 128:(ft + 1) * 128], actb[:, j, :]) for j in range(22)],
                         W2 + [actb])
                s.op("dve", lambda e, po=po, ft=ft, xT=xT: e.scalar_tensor_tensor(
                    out=xT[:, ft, :], in0=po[:], scalar=Gco[:, l, i, c, ft:ft + 1], in1=xT[:, ft, :], op0=ALU.mult, op1=ALU.add),
                    reads=[po, Gco, xT], writes=[xT])
            if last:
                store_y_tm(xT, xtm, blk if kind == "std" else 2 + blk)
            else:
                store_xT(xT, blk)
            if nxt_own:
                fetch(it + 1)
        s.barrier()
        s.release(m0)

    def proj_phase(l):
        m0 = s.mark()
        WI = [s.sbuf("WI_%d" % j, [128, 8, 256], BF16) for j in range(9)]
        for j in range(9):
            s.dma("pool", WI[j][:], _ap(w_in)[l, :, j * 256:(j + 1) * 256].rearrange("(kt p) f -> p kt f", p=128),
                  writes=[WI[j]])
        xTs = [s.sbuf("xT%d" % j, [128, 8, 512], F32) for j in range(2)]
        hbs = [s.sbuf("hb%d" % j, [128, 8, 512], BF16) for j in range(2)]
        tmp2 = [s.sbuf("tmp%d" % j, [128, 512], F32) for j in range(2)]
        rstd = s.sbuf("rstd", [128, 512], F32)
        gq = s.sbuf("gq", [128, 2], F32)
        for h in range(2):
            s.dma("sp", gq[h * 64:(h + 1) * 64, 0:1], _ap(qn_g)[l, :].rearrange("(p o) -> p o", o=1), writes=[gq])
            s.dma("sp", gq[h * 64:(h + 1) * 64, 1:2], _ap(kn_g)[l, :].rearrange("(p o) -> p o", o=1), writes=[gq])
        gk_b = s.sbuf("gk_b", [128, 64], F32)
        s.dma("sp", gk_b[:], _ap(kn_g)[l:l + 1, :].to_broadcast([128, 64]), writes=[gk_b])
        st_b = [s.sbuf("st_b%d" % j, [128, 512], BF16) for j in range(4)]
        sqb = [s.sbuf("sqb%d" % j, [128, 512], BF16) for j in range(4)]
        r2 = [s.sbuf("r2_%d" % j, [128, 512], F32) for j in range(4)]
        gvb = [s.sbuf("gvb_%d" % j, [128, 256], F32) for j in range(2)]
        qk32 = [s.sbuf("qk32_%d" % j, [128, 512], F32) for j in range(4)]
        g1 = [s.sbuf("g1_%d" % j, [128, 512], F32) for j in range(3)]
        g2 = [s.sbuf("g2_%d" % j, [128, 512], F32) for j in range(3)]
        g3 = [s.sbuf("g3_%d" % j, [128, 512], F32) for j in range(3)]
        vtm32 = [s.sbuf("vtm32_%d" % j, [128, 512], F32) for j in range(2)]
        ktm32 = [s.sbuf("ktm32_%d" % j, [128, 512], F32) for j in range(2)]
        vtmb = [s.sbuf("vtmb_%d" % j, [128, 512], BF16) for j in range(2)]
        vA = [s.sbuf("vA_%d" % j, [128, 256], BF16) for j in range(2)]
        vB = [s.sbuf("vB_%d" % j, [128, 256], BF16) for j in range(2)]
        for j in range(2):
            s.op("dve", lambda e, j=j: e.memset(vA[j][:], 0.0), writes=[vA[j]])
            s.op("dve", lambda e, j=j: e.memset(vB[j][:], 0.0), writes=[vB[j]])
        stat = [s.sbuf("stat%d" % j, [128, 6], F32) for j in range(2)]
        mv = [s.sbuf("mv%d" % j, [128, 2], F32) for j in range(2)]
        rs1 = [s.sbuf("rs1_%d" % j, [128, 1], F32) for j in range(2)]
        ss8 = [s.sbuf("ss8_%d" % j, [128, 8], F32) for j in range(2)]
        cnt = [0]

        def wcols(col):
            return WI[col // 256], col % 256

        def fm_tile(col):
            w, o = wcols(col)
            pb = ps()
            mm_group(pb, pb[:], [(w[:, kt, o:o + 128], hb[:, kt, :]) for kt in range(8)], [w, hb])
            return pb

        def gelu_ops(src_b, src_ap, dst_b, dst_ap, n, P=128):
            k = cnt[0] % 3
            cnt[0] += 1
            a, b2, c2 = g1[k], g2[k], g3[k]
            s.op("act", lambda e: e.activation(out=a[0:P, 0:n], in_=src_ap, func=AF.Square), reads=[src_b], writes=[a])
            s.op("dve", lambda e: e.tensor_scalar(out=a[0:P, 0:n], in0=a[0:P, 0:n], scalar1=0.044715, scalar2=1.0,
                                                  op0=ALU.mult, op1=ALU.add), reads=[a], writes=[a])
            s.op("dve", lambda e: e.tensor_tensor(out=b2[0:P, 0:n], in0=a[0:P, 0:n], in1=src_ap, op=ALU.mult),
                 reads=[a, src_b], writes=[b2])
            s.op("act", lambda e: e.activation(out=c2[0:P, 0:n], in_=b2[0:P, 0:n], func=AF.Sigmoid, scale=1.5957691216057308),
                 reads=[b2], writes=[c2])
            s.op("dve", lambda e: e.tensor_tensor(out=dst_ap, in0=c2[0:P, 0:n], in1=src_ap, op=ALU.mult),
                 reads=[c2, src_b], writes=[dst_b])

        for blk in range(NB):
            c = 0 if blk < 2 else 1
            tok = slice(blk * 512, (blk + 1) * 512)
            xT = xTs[blk % 2]
            hb = hbs[blk % 2]
            if blk == 0:
                load_xT(xT, 0)
            if blk + 1 < NB:
                load_xT(xTs[(blk + 1) % 2], blk + 1)
            norm_mod(xT, hb, tmp2, rstd, l, 1, c)
            for a in range(2):
                pb = fm_tile(a * 128)
                sb = st_b[cnt[0] % 4]; cnt[0] += 1
                s.op("act", lambda e, sb=sb, pb=pb: e.copy(out=sb[:], in_=pb[:]), reads=[pb], writes=[sb])
                s.dma("sp", _ap(us_d)[a * 128:(a + 1) * 128, tok], sb[:], reads=[sb], writes=[us_d])
            for qk in range(2):
                for t4 in range(4):
                    pb = fm_tile(256 + qk * 512 + t4 * 128)
                    sq = sqb[t4]
                    s.op("act", lambda e, sq=sq, pb=pb: e.activation(out=sq[:], in_=pb[:], func=AF.Square),
                         reads=[pb], writes=[sq])
                    q32 = qk32[t4]
                    s.op("act", lambda e, q32=q32, pb=pb: e.copy(out=q32[:], in_=pb[:]), reads=[pb], writes=[q32])
                    pb = q32
                    pm = ps()
                    mm_group(pm, pm[:], [(bd_bf[:], sq[:])], [bd_bf, sq])
                    r = r2[t4]
                    rsqrt(r, r[:], pm, pm[:], eps6)
                    sb = st_b[cnt[0] % 4]; cnt[0] += 1
                    s.op("dve", lambda e, sb=sb, pb=pb, r=r, qk=qk: e.scalar_tensor_tensor(
                        out=sb[:], in0=pb[:], scalar=gq[:, qk:qk + 1], in1=r[:], op0=ALU.mult, op1=ALU.mult),
                        reads=[pb, gq, r], writes=[sb])
                    dd = q_d if qk == 0 else k_d
                    s.dma("sp", _ap(dd)[t4 * 128:(t4 + 1) * 128, tok], sb[:], reads=[sb], writes=[dd])
            for a in range(2):
                pb = fm_tile(1792 + a * 128)
                sb = st_b[cnt[0] % 4]; cnt[0] += 1
                gelu_ops(pb, pb[:], sb, sb[:], 512)
                s.dma("sp", _ap(ug_d)[a * 128:(a + 1) * 128, tok], sb[:], reads=[sb], writes=[ug_d])
            for tt in range(4):
                trow = slice(blk * 512 + tt * 128, blk * 512 + (tt + 1) * 128)
                hT = [hb[:, kt, tt * 128:(tt + 1) * 128] for kt in range(8)]
                k2 = tt % 2
                pv = ps()
                for half in range(2):
                    w = WI[5 + half]
                    mm_group(pv, pv[:, half * 256:(half + 1) * 256], [(hT[kt], w[:, kt, :]) for kt in range(8)], [w, hb])
                vb = vtmb[k2]
                s.op("act", lambda e, vb=vb, pv=pv: e.copy(out=vb[:], in_=pv[:]), reads=[pv], writes=[vb])
                s.dma("sp", _ap(v_d)[trow, :], vb[:], reads=[vb], writes=[v_d])
                if blk < 2:
                    seq = (blk * 512 + tt * 128) // 256
                    pos = (tt % 2) * 128
                    v32 = vtm32[k2]
                    s.op("dve", lambda e, v32=v32, pv=pv: e.tensor_copy(out=v32[:], in_=pv[:]), reads=[pv], writes=[v32])
                    s.dma("sp", _ap(nv)[seq, l, pos:pos + 128, :], v32[:], reads=[v32], writes=[nv])
                    pk = ps()
                    for half in range(2):
                        w = WI[3 + half]
                        mm_group(pk, pk[:, half * 256:(half + 1) * 256], [(hT[kt], w[:, kt, :]) for kt in range(8)], [w, hb])
                    a1 = g1[cnt[0] % 3]; cnt[0] += 1
                    s.op("act", lambda e, a1=a1, pk=pk: e.activation(out=a1[:], in_=pk[:], func=AF.Square), reads=[pk], writes=[a1])
                    s8 = ss8[k2]
                    s.op("dve", lambda e, a1=a1, s8=s8: e.tensor_reduce(out=s8[:], in_=a1[:].rearrange("p (h d) -> p h d", d=64),
                                                                       axis=AX.X, op=ALU.add), reads=[a1], writes=[s8])
                    rsqrt(s8, s8[:], s8, s8[:], eps6, scale=1.0 / 64.0)
                    k32 = ktm32[k2]
                    for h in range(8):
                        s.op("dve", lambda e, h=h, k32=k32, pk=pk, s8=s8: e.scalar_tensor_tensor(
                            out=k32[:, h * 64:(h + 1) * 64], in0=pk[:, h * 64:(h + 1) * 64], scalar=s8[:, h:h + 1],
                            in1=gk_b[:], op0=ALU.mult, op1=ALU.mult), reads=[pk, s8, gk_b], writes=[k32])
                    s.dma("sp", _ap(nk)[seq, l, pos:pos + 128, :], k32[:], reads=[k32], writes=[nk])
                pg = ps()
                mm_group(pg, pg[:, 0:256], [(hT[kt], WI[8][:, kt, :]) for kt in range(8)], [WI[8], hb])
                gv = gvb[k2]
                gelu_ops(pg, pg[:, 0:256], gv, gv[:, 0:256], 256)
                stt, mvv, rr = stat[k2], mv[k2], rs1[k2]
                s.op("dve", lambda e, stt=stt, gv=gv: e.bn_stats(out=stt[:], in_=gv[:, 0:256]), reads=[gv], writes=[stt])
                s.op("dve", lambda e, stt=stt, mvv=mvv: e.bn_aggr(out=mvv[:], in_=stt[:]), reads=[stt], writes=[mvv])
                rsqrt(rr, rr[:], mvv, mvv[:, 1:2], eps5)
                va, vbb = vA[k2], vB[k2]
                for (dst, off) in ((va, 0), (vbb, 64)):
                    for a in range(2):
                        cs_ = slice(a * 128 + off, a * 128 + off + 64)
                        s.op("dve", lambda e, dst=dst, cs_=cs_, gv=gv, mvv=mvv, rr=rr: e.tensor_scalar(
                            out=dst[:, cs_], in0=gv[:, cs_], scalar1=mvv[:, 0:1], scalar2=rr[:, 0:1],
                            op0=ALU.subtract, op1=ALU.mult), reads=[gv, mvv, rr], writes=[dst])
                s.dma("sp", _ap(va_d)[trow, :], va[:], reads=[va], writes=[va_d])
                s.dma("sp", _ap(vb_d)[trow, :], vbb[:], reads=[vbb], writes=[vb_d])
        s.barrier()
        s.release(m0)

    def outproj_phase(l, WO=None):
        m0 = s.mark()
        if WO is None:
            WO = s.sbuf("WO", [128, 8, 1024], BF16)
            s.dma("pool", WO[:], _ap(w_out)[l].rearrange("(kt p) f -> p kt f", p=128), writes=[WO])
        xTs = [s.sbuf("xT%d" % j, [128, 8, 512], F32) for j in range(2)]
        yms = [s.sbuf("ym%d" % j, [128, 8, 512], BF16) for j in range(2)]
        def fetch(blk):
            load_xT(xTs[blk % 2], blk)
            s.dma("sp", yms[blk % 2][:], _ap(ymix_d)[:, blk * 512:(blk + 1) * 512].rearrange("(kt p) n -> p kt n", p=128),
                  reads=[ymix_d], writes=[yms[blk % 2]])

        fetch(0)
        for blk in range(NB):
            c = 0 if blk < 2 else 1
            xT, ym = xTs[blk % 2], yms[blk % 2]
            if blk + 1 < NB:
                fetch(blk + 1)
            for ft in range(8):
                po = ps()
                mm_group(po, po[:], [(WO[:, kt, ft * 128:(ft + 1) * 128], ym[:, kt, :]) for kt in range(8)], [WO, ym])
                s.op("dve", lambda e, po=po, ft=ft, xT=xT: e.scalar_tensor_tensor(
                    out=xT[:, ft, :], in0=po[:], scalar=Gco[:, l, 1, c, ft:ft + 1], in1=xT[:, ft, :], op0=ALU.mult, op1=ALU.add),
                    reads=[po, Gco, xT], writes=[xT])
            store_xT(xT, blk)
        s.barrier()
        s.release(m0)

    def ssm_part(l, pump=None):
        m0 = s.mark()
        uS = s.sbuf("uS", [128, NT], BF16)
        Y = s.sbuf("Y", [128, 2, NT], F32)
        dco = s.sbuf("dco", [128, 2], F32)
        s.dma("sp", dco[:], _ap(ssm_d)[l, :].rearrange("(a p) -> p a", p=128), writes=[dco])
        iota = s.sbuf("iota", [128, 1024], F32)
        s.dma("sp", iota[:], _ap(c_iota), writes=[iota])
        P8 = lambda n: s.sbuf(n, [128, 8], F32)
        lr, li, ldt, dtv, ar, ai, rho, ang, sinv, cosv, lbr, lbi = [P8("p8_%d" % i) for i in range(12)]
        nr, den, kr, ki, t8a, t8b, thr, nf8, cN, sN = [P8("q8_%d" % i) for i in range(10)]
        ii8 = s.sbuf("ii8", [128, 8], mybir.dt.int32)
        BT = s.sbuf("BT", [128, 8, 2, 128], BF16)
        CT = s.sbuf("CT", [128, 8, 2, 128], BF16)
        s0 = s.sbuf("s0", [128, 8, 2], F32)
        init = s.sbuf("init", [128, 8, 2], F32)
        FIN = [s.sbuf("FIN%d" % i, [128, 8, 2], F32) for i in range(4)]
        zl = s.sbuf("zl", [128, 2], F32)
        tq = s.sbuf("tq", [128, 2], F32)

        def dv(fn, reads, writes, e="dve"):
            s.op(e, fn, reads=reads, writes=writes)

        C1 = 6.28125
        C2 = TWO_PI - C1
        PI_S = 3.1415925
        hpi = s.sbuf("hpi", [128, 1], F32)
        dv(lambda e: e.memset(hpi[:], 0.5 * math.pi), [], [hpi])

        def sincos(ang_b, ang_ap, r_b, r_ap, sin_b, sin_ap, cos_b, cos_ap, ii_b, nf_b, n):
            dv(lambda e: e.tensor_scalar(out=ii_b[:, 0:n], in0=ang_ap, scalar1=1.0 / TWO_PI, scalar2=None, op0=ALU.mult), [ang_b], [ii_b])
            dv(lambda e: e.tensor_copy(out=nf_b[:, 0:n], in_=ii_b[:, 0:n]), [ii_b], [nf_b])
            dv(lambda e: e.scalar_tensor_tensor(out=r_ap, in0=nf_b[:, 0:n], scalar=-C1, in1=ang_ap, op0=ALU.mult, op1=ALU.add),
               [nf_b, ang_b], [r_b])
            dv(lambda e: e.scalar_tensor_tensor(out=r_ap, in0=nf_b[:, 0:n], scalar=-C2, in1=r_ap, op0=ALU.mult, op1=ALU.add),
               [nf_b, r_b], [r_b])
            dv(lambda e: e.tensor_scalar(out=r_ap, in0=r_ap, scalar1=-PI_S, scalar2=None, op0=ALU.max), [r_b], [r_b])
            dv(lambda e: e.tensor_scalar(out=r_ap, in0=r_ap, scalar1=PI_S, scalar2=None, op0=ALU.min), [r_b], [r_b])
            s.op("act", lambda e: e.activation(out=sin_ap, in_=r_ap, func=AF.Sin), reads=[r_b], writes=[sin_b])
            s.op("act", lambda e: e.activation(out=nf_b[:, 0:n], in_=r_ap, func=AF.Abs), reads=[r_b], writes=[nf_b])
            s.op("act", lambda e: e.activation(out=cos_ap, in_=nf_b[:, 0:n], func=AF.Sin, scale=-1.0, bias=hpi[:, 0:1]),
                 reads=[nf_b, hpi], writes=[cos_b])

        for d in range(2):
            s.dma("sp", lr[:], _ap(lam_re)[l, d, :].rearrange("(j q) -> q j", q=128), writes=[lr])
            s.dma("sp", li[:], _ap(lam_im)[l, d, :].rearrange("(j q) -> q j", q=128), writes=[li])
            for h in range(2):
                s.dma("sp", ldt[h * 64:(h + 1) * 64, :],
                      _ap(log_dt)[l, d, :].rearrange("(j h) -> h j", h=2)[h:h + 1, :].to_broadcast([64, 8]), writes=[ldt])
            s.op("act", lambda e: e.activation(out=dtv[:], in_=ldt[:], func=AF.Exp), reads=[ldt], writes=[dtv])
            dv(lambda e: e.tensor_tensor(out=ar[:], in0=lr[:], in1=dtv[:], op=ALU.mult), [lr, dtv], [ar])
            dv(lambda e: e.tensor_tensor(out=ai[:], in0=li[:], in1=dtv[:], op=ALU.mult), [li, dtv], [ai])
            s.op("act", lambda e: e.activation(out=rho[:], in_=ar[:], func=AF.Exp), reads=[ar], writes=[rho])
            sincos(ai, ai[:], thr, thr[:], sinv, sinv[:], cosv, cosv[:], ii8, nf8, 8)
            dv(lambda e: e.tensor_tensor(out=lbr[:], in0=rho[:], in1=cosv[:], op=ALU.mult), [rho, cosv], [lbr])
            dv(lambda e: e.tensor_tensor(out=lbi[:], in0=rho[:], in1=sinv[:], op=ALU.mult), [rho, sinv], [lbi])
            dv(lambda e: e.tensor_scalar(out=nr[:], in0=lbr[:], scalar1=-1.0, scalar2=None, op0=ALU.add), [lbr], [nr])
            dv(lambda e: e.tensor_tensor(out=den[:], in0=lr[:], in1=lr[:], op=ALU.mult), [lr], [den])
            dv(lambda e: e.tensor_tensor(out=t8a[:], in0=li[:], in1=li[:], op=ALU.mult), [li], [t8a])
            dv(lambda e: e.tensor_tensor(out=den[:], in0=den[:], in1=t8a[:], op=ALU.add), [den, t8a], [den])
            dv(lambda e: e.reciprocal(out=den[:], in_=den[:]), [den], [den])
            dv(lambda e: e.tensor_tensor(out=t8a[:], in0=nr[:], in1=lr[:], op=ALU.mult), [nr, lr], [t8a])
            dv(lambda e: e.tensor_tensor(out=t8b[:], in0=lbi[:], in1=li[:], op=ALU.mult), [lbi, li], [t8b])
            dv(lambda e: e.tensor_tensor(out=t8a[:], in0=t8a[:], in1=t8b[:], op=ALU.add), [t8a, t8b], [t8a])
            dv(lambda e: e.tensor_tensor(out=kr[:], in0=t8a[:], in1=den[:], op=ALU.mult), [t8a, den], [kr])
            dv(lambda e: e.tensor_tensor(out=t8a[:], in0=lbi[:], in1=lr[:], op=ALU.mult), [lbi, lr], [t8a])
            dv(lambda e: e.tensor_tensor(out=t8b[:], in0=nr[:], in1=li[:], op=ALU.mult), [nr, li], [t8b])
            dv(lambda e: e.tensor_tensor(out=t8a[:], in0=t8a[:], in1=t8b[:], op=ALU.subtract), [t8a, t8b], [t8a])
            dv(lambda e: e.tensor_tensor(out=ki[:], in0=t8a[:], in1=den[:], op=ALU.mult), [t8a, den], [ki])
            m1 = s.mark()
            Bn = [s.sbuf("Bn%d" % i, [128, 8, 16], F32) for i in range(2)]
            Zp = [s.sbuf("Zp%d" % i, [128, 8, 128], F32) for i in range(2)]
            tz = s.sbuf("tz", [128, 16], F32)
            INc = [s.sbuf("INc%d" % i, [128, 8, 128], F32) for i in range(2)]
            s.dma("sp", Bn[0][:], _ap(b_re)[l, d].rearrange("(j q) c -> q j c", q=128), writes=[Bn[0]])
            s.dma("sp", Bn[1][:], _ap(b_im)[l, d].rearrange("(j q) c -> q j c", q=128), writes=[Bn[1]])
            for ri in range(2):
                dv(lambda e, ri=ri: e.memset(Zp[ri][:], 0.0), [], [Zp[ri]])
            for j in range(8):
                for h in range(2):
                    pr = slice(h * 64, (h + 1) * 64)
                    cs_ = slice(32 * (j % 4) + 16 * h, 32 * (j % 4) + 16 * h + 16)
                    dv(lambda e, j=j, pr=pr: e.tensor_scalar(out=tz[pr, :], in0=Bn[1][pr, j, :], scalar1=ki[pr, j:j + 1],
                                                             scalar2=None, op0=ALU.mult), [Bn[1], ki], [tz])
                    dv(lambda e, j=j, pr=pr, cs_=cs_: e.scalar_tensor_tensor(
                        out=Zp[0][pr, j, cs_], in0=Bn[0][pr, j, :], scalar=kr[pr, j:j + 1], in1=tz[pr, :],
                        op0=ALU.mult, op1=ALU.subtract), [Bn[0], kr, tz], [Zp[0]])
                    dv(lambda e, j=j, pr=pr: e.tensor_scalar(out=tz[pr, :], in0=Bn[0][pr, j, :], scalar1=ki[pr, j:j + 1],
                                                             scalar2=None, op0=ALU.mult), [Bn[0], ki], [tz])
                    dv(lambda e, j=j, pr=pr, cs_=cs_: e.scalar_tensor_tensor(
                        out=Zp[1][pr, j, cs_], in0=Bn[1][pr, j, :], scalar=kr[pr, j:j + 1], in1=tz[pr, :],
                        op0=ALU.mult, op1=ALU.add), [Bn[1], kr, tz], [Zp[1]])
            for ri, cd in enumerate((c_re, c_im)):
                dv(lambda e, ri=ri: e.memset(INc[ri][:], 0.0), [], [INc[ri]])
                for j in range(8):
                    for h in range(2):
                        g = 2 * j + h
                        r0 = 32 * (j % 4) + 16 * h
                        s.dma("sp", INc[ri][r0:r0 + 16, j, h * 64:(h + 1) * 64], _ap(cd)[l, d, g * 16:(g + 1) * 16, :],
                              reads=[INc[ri]], writes=[INc[ri]])
            for j in range(8):
                for ri in range(2):
                    pb = ps()
                    s.group("pe", [lambda e, pb=pb, j=j, ri=ri: e.transpose(pb[:, 0:128], Zp[ri][:, j, :], ident[:])],
                            reads=[Zp[ri], ident], writes=[pb])
                    s.op("act", lambda e, pb=pb, j=j, ri=ri: e.copy(out=BT[:, j, ri, :], in_=pb[:, 0:128]), reads=[pb], writes=[BT])
                    pc = ps()
                    s.group("pe", [lambda e, pc=pc, j=j, ri=ri: e.transpose(pc[:, 0:128], INc[ri][:, j, :], ident[:])],
                            reads=[INc[ri], ident], writes=[pc])
                    s.op("act", lambda e, pc=pc, j=j, ri=ri: e.activation(out=CT[:, j, ri, :], in_=pc[:, 0:128], func=AF.Copy,
                                                                        scale=(1.0 if ri == 0 else -1.0)),
                         reads=[pc], writes=[CT])
            s.dma("sp", s0[:], _ap(sst)[l, d].rearrange("(j q) r -> q j r", q=128), writes=[s0])
            dv(lambda e: e.tensor_tensor(out=t8a[:], in0=cosv[:], in1=s0[:, :, 0], op=ALU.mult), [cosv, s0], [t8a])
            dv(lambda e: e.tensor_tensor(out=t8b[:], in0=sinv[:], in1=s0[:, :, 1], op=ALU.mult), [sinv, s0], [t8b])
            dv(lambda e: e.tensor_tensor(out=init[:, :, 0], in0=t8a[:], in1=t8b[:], op=ALU.subtract), [t8a, t8b], [init])
            dv(lambda e: e.tensor_tensor(out=t8a[:], in0=sinv[:], in1=s0[:, :, 0], op=ALU.mult), [sinv, s0], [t8a])
            dv(lambda e: e.tensor_tensor(out=t8b[:], in0=cosv[:], in1=s0[:, :, 1], op=ALU.mult), [cosv, s0], [t8b])
            dv(lambda e: e.tensor_tensor(out=init[:, :, 1], in0=t8a[:], in1=t8b[:], op=ALU.add), [t8a, t8b], [init])
            s.barrier()
            s.release(m1)
            m2 = s.mark()
            rhoT = s.sbuf("rhoT", [128, 4, 1024], F32)
            tabc = s.sbuf("tabc", [128, 1024], F32)
            tabs = s.sbuf("tabs", [128, 1024], F32)
            ccol = s.sbuf("ccol", [128, 4, 2], F32)
            scol = s.sbuf("scol", [128, 4, 2], F32)
            angt = s.sbuf("angt", [128, 1024], F32)
            ang2 = s.sbuf("ang2", [128, 1024], F32)
            iiT = s.sbuf("iiT", [128, 1024], mybir.dt.int32)
            nfT = s.sbuf("nfT", [128, 1024], F32)
            tcb = s.sbuf("tcb", [128, 4, 1024], BF16)
            tsb = s.sbuf("tsb", [128, 4, 1024], BF16)
            tD_ = [[s.sbuf("tD%d_%d" % (k, i), [128, 1024], BF16) for i in range(2)] for k in range(1)]
            tP_ = [[s.sbuf("tP%d_%d" % (k, i), [128, 1024], BF16) for i in range(2)] for k in range(1)]
            wb_ = [[s.sbuf("wb%d_%d" % (k, i), [128, 1024], BF16) for i in range(2)] for k in range(1)]
            wri_ = [[s.sbuf("wri%d_%d" % (k, i), [128, 1024], F32) for i in range(2)] for k in range(1)]
            zb_ = [[s.sbuf("zb%d_%d" % (k, i), [128, 1024], BF16) for i in range(2)] for k in range(1)]
            bsb_ = [[s.sbuf("bsb%d_%d" % (k, i), [128, 1024], BF16) for i in range(2)] for k in range(1)]
            ucnt = [0]
            fq = s.sbuf("fq", [128, 4], F32)
            Sb = [[s.sbuf("Sb%d_%d" % (j4, ri), [128, 1024], BF16) for ri in range(2)] for j4 in range(4)]
            if os.environ.get("MK_VERBOSE"):
                print("ssm free sbuf bytes/partition:", nc.sbuf_bytes_remaining)
            units = [("p", sq, sq * 256, 256) for sq in range(4)]
            for kk in range(4):
                knat = kk if d == 0 else 3 - kk
                units.append(("s", kk, NP_TOK + knat * 1024, 1024))
            for a in range(2):
                s.dma("sp", uS[:], _ap(us_d)[a * 128:(a + 1) * 128, :], reads=[us_d], writes=[uS])
                for j4 in range(4):
                    j = a * 4 + j4
                    dv(lambda e, j=j: e.tensor_scalar(out=angt[:], in0=iota[:], scalar1=thr[:, j:j + 1], scalar2=None, op0=ALU.mult),
                       [iota, thr], [angt])
                    sincos(angt, angt[:], ang2, ang2[:], tabs, tabs[:], tabc, tabc[:], iiT, nfT, 1024)
                    s.op("act", lambda e, j=j, j4=j4: e.activation(out=rhoT[:, j4, :], in_=iota[:], func=AF.Identity, scale=0.0,
                                                                 bias=rho[:, j:j + 1]), reads=[iota, rho], writes=[rhoT])
                    s.op("act", lambda e, j4=j4: e.copy(out=tcb[:, j4, :], in_=tabc[:]), reads=[tabc], writes=[tcb])
                    s.op("act", lambda e, j4=j4: e.copy(out=tsb[:, j4, :], in_=tabs[:]), reads=[tabs], writes=[tsb])
                    for ci, col in enumerate((255, 1023)):
                        dv(lambda e, j4=j4, ci=ci, col=col: e.tensor_copy(out=ccol[:, j4, ci:ci + 1], in_=tabc[:, col:col + 1]), [tabc], [ccol])
                        dv(lambda e, j4=j4, ci=ci, col=col: e.tensor_copy(out=scol[:, j4, ci:ci + 1], in_=tabs[:, col:col + 1]), [tabs], [scol])
                    dv(lambda e, j=j, j4=j4: e.tensor_scalar(out=tq[:, 0:1], in0=scol[:, j4, 1:2], scalar1=sinv[:, j:j + 1], scalar2=None,
                                                             op0=ALU.mult), [scol, sinv], [tq])
                    dv(lambda e, j=j, j4=j4: e.scalar_tensor_tensor(out=cN[:, j:j + 1], in0=ccol[:, j4, 1:2], scalar=cosv[:, j:j + 1],
                                                                    in1=tq[:, 0:1], op0=ALU.mult, op1=ALU.subtract), [ccol, cosv, tq], [cN])
                    dv(lambda e, j=j, j4=j4: e.tensor_scalar(out=tq[:, 1:2], in0=ccol[:, j4, 1:2], scalar1=sinv[:, j:j + 1], scalar2=None,
                                                             op0=ALU.mult), [ccol, sinv], [tq])
                    dv(lambda e, j=j, j4=j4: e.scalar_tensor_tensor(out=sN[:, j:j + 1], in0=scol[:, j4, 1:2], scalar=cosv[:, j:j + 1],
                                                                    in1=tq[:, 1:2], op0=ALU.mult, op1=ALU.add), [scol, cosv, tq], [sN])
                for (kind, idx, tok0, n) in units:
                    for j4 in range(4):
                        j = a * 4 + j4
                        c_f, sn_f = tcb[:, j4, 0:n], tsb[:, j4, 0:n]
                        kb = 0
                        ucnt[0] += 1
                        tD, tP, wb, wri, zb, bsb = tD_[kb], tP_[kb], wb_[kb], wri_[kb], zb_[kb], bsb_[kb]
                        for p0 in range(0, n, 512):
                            pn = min(512, n - p0)
                            sl = slice(p0, p0 + pn) if d == 0 else slice(n - p0 - pn, n - p0)
                            for ri in range(2):
                                pb = ps()
                                mm_group(pb, pb[:, 0:pn], [(BT[:, j, ri, :], uS[:, tok0 + p0:tok0 + p0 + pn])], [BT, uS])
                                src_ = pb[:, 0:pn] if d == 0 else pb[:, 0:pn][:, ::-1]
                                s.op("act", lambda e, ri=ri, sl=sl, src_=src_: e.copy(out=bsb[ri][:, sl], in_=src_), reads=[pb], writes=[bsb[ri]])
                        br_, bi_ = bsb[0][:, 0:n], bsb[1][:, 0:n]
                        dv(lambda e: e.tensor_tensor(out=tD[0][:, 0:n], in0=c_f, in1=br_, op=ALU.mult), [tcb, bsb[0]], [tD[0]])
                        dv(lambda e: e.tensor_tensor(out=tD[1][:, 0:n], in0=sn_f, in1=bi_, op=ALU.mult), [tsb, bsb[1]], [tD[1]])
                        dv(lambda e: e.tensor_tensor(out=wb[0][:, 0:n], in0=tD[0][:, 0:n], in1=tD[1][:, 0:n], op=ALU.add),
                           [tD[0], tD[1]], [wb[0]])
                        dv(lambda e: e.tensor_tensor(out=tP[0][:, 0:n], in0=c_f, in1=bi_, op=ALU.mult), [tcb, bsb[1]], [tP[0]])
                        dv(lambda e: e.tensor_tensor(out=tP[1][:, 0:n], in0=sn_f, in1=br_, op=ALU.mult), [tsb, bsb[0]], [tP[1]])
                        dv(lambda e: e.tensor_tensor(out=wb[1][:, 0:n], in0=tP[0][:, 0:n], in1=tP[1][:, 0:n], op=ALU.subtract),
                           [tP[0], tP[1]], [wb[1]])
                        for ri in range(2):
                            if kind == "p":
                                ini = 0.0
                                rds = [rhoT, wb[ri]]
                            else:
                                ini = init[:, j, ri:ri + 1]
                                rds = [rhoT, wb[ri], init]
                            s.op("dve", lambda e, ri=ri, ini=ini, j4=j4: e.tensor_tensor_scan(
                                out=wri[ri][:, 0:n], data0=rhoT[:, j4, 0:n], data1=wb[ri][:, 0:n], initial=ini,
                                op0=ALU.mult, op1=ALU.add), reads=rds, writes=[wri[ri]])
                            s.op("act", lambda e, ri=ri: e.copy(out=zb[ri][:, 0:n], in_=wri[ri][:, 0:n]), reads=[wri[ri]], writes=[zb[ri]])
                        if kind == "s" and idx < 3:
                            dv(lambda e, j=j: e.tensor_scalar(out=tq[:, 0:1], in0=wri[1][:, n - 1:n], scalar1=sN[:, j:j + 1], scalar2=None,
                                                              op0=ALU.mult), [wri[1], sN], [tq])
                            dv(lambda e, j=j: e.tensor_scalar(out=tq[:, 1:2], in0=wri[0][:, n - 1:n], scalar1=sN[:, j:j + 1], scalar2=None,
                                                              op0=ALU.mult), [wri[0], sN], [tq])
                            dv(lambda e, j=j: e.scalar_tensor_tensor(out=init[:, j, 0:1], in0=wri[0][:, n - 1:n], scalar=cN[:, j:j + 1],
                                                                     in1=tq[:, 0:1], op0=ALU.mult, op1=ALU.subtract), [wri[0], cN, tq], [init])
                            dv(lambda e, j=j: e.scalar_tensor_tensor(out=init[:, j, 1:2], in0=wri[1][:, n - 1:n], scalar=cN[:, j:j + 1],
                                                                     in1=tq[:, 1:2], op0=ALU.mult, op1=ALU.add), [wri[1], cN, tq], [init])
                        if kind == "p":
                            fb = FIN[idx]
                            cl, sl_ = ccol[:, j4, 0:1], scol[:, j4, 0:1]
                            zrl, zil = wri[0][:, n - 1:n], wri[1][:, n - 1:n]
                            dv(lambda e: e.tensor_tensor(out=fq[:, 0:1], in0=sl_, in1=zil, op=ALU.mult), [scol, wri[1]], [fq])
                            dv(lambda e: e.tensor_tensor(out=fq[:, 1:2], in0=cl, in1=zrl, op=ALU.mult), [ccol, wri[0]], [fq])
                            dv(lambda e, fb=fb, j=j: e.tensor_tensor(out=fb[:, j, 0:1], in0=fq[:, 1:2], in1=fq[:, 0:1], op=ALU.subtract), [fq], [fb])
                            dv(lambda e: e.tensor_tensor(out=fq[:, 2:3], in0=sl_, in1=zrl, op=ALU.mult), [scol, wri[0]], [fq])
                            dv(lambda e: e.tensor_tensor(out=fq[:, 3:4], in0=cl, in1=zil, op=ALU.mult), [ccol, wri[1]], [fq])
                            dv(lambda e, fb=fb, j=j: e.tensor_tensor(out=fb[:, j, 1:2], in0=fq[:, 2:3], in1=fq[:, 3:4], op=ALU.add), [fq], [fb])
                        zr_, zi_ = zb[0][:, 0:n], zb[1][:, 0:n]
                        sr_o = Sb[j4][0][:, 0:n] if d == 0 else Sb[j4][0][:, 0:n][:, ::-1]
                        si_o = Sb[j4][1][:, 0:n] if d == 0 else Sb[j4][1][:, 0:n][:, ::-1]
                        dv(lambda e: e.tensor_tensor(out=tD[0][:, 0:n], in0=c_f, in1=zr_, op=ALU.mult), [tcb, zb[0]], [tD[0]])
                        dv(lambda e: e.tensor_tensor(out=tD[1][:, 0:n], in0=sn_f, in1=zi_, op=ALU.mult), [tsb, zb[1]], [tD[1]])
                        dv(lambda e, sr_o=sr_o: e.tensor_tensor(out=sr_o, in0=tD[0][:, 0:n], in1=tD[1][:, 0:n], op=ALU.subtract),
                           [tD[0], tD[1]], [Sb[j4][0]])
                        dv(lambda e: e.tensor_tensor(out=tP[0][:, 0:n], in0=sn_f, in1=zr_, op=ALU.mult), [tsb, zb[0]], [tP[0]])
                        dv(lambda e: e.tensor_tensor(out=tP[1][:, 0:n], in0=c_f, in1=zi_, op=ALU.mult), [tcb, zb[1]], [tP[1]])
                        dv(lambda e, si_o=si_o: e.tensor_tensor(out=si_o, in0=tP[0][:, 0:n], in1=tP[1][:, 0:n], op=ALU.add),
                           [tP[0], tP[1]], [Sb[j4][1]])
                        if pump is not None:
                            pump()
                    for p0 in range(0, n, 512):
                        pn = min(512, n - p0)
                        pb = ps()
                        pairs = []
                        rd = [CT]
                        for j4 in range(4):
                            for ri in range(2):
                                pairs.append((CT[:, a * 4 + j4, ri, :], Sb[j4][ri][:, p0:p0 + pn]))
                                rd.append(Sb[j4][ri])
                        mm_group(pb, pb[:, 0:pn], pairs, rd)
                        ysl = Y[:, a, tok0 + p0:tok0 + p0 + pn]
                        if d == 0:
                            dv(lambda e, pb=pb, ysl=ysl, pn=pn, a=a, p0=p0, tok0=tok0: e.scalar_tensor_tensor(
                                out=ysl, in0=uS[:, tok0 + p0:tok0 + p0 + pn], scalar=dco[:, a:a + 1], in1=pb[:, 0:pn],
                                op0=ALU.mult, op1=ALU.add), [uS, dco, pb], [Y])
                        else:
                            dv(lambda e, pb=pb, ysl=ysl, pn=pn: e.tensor_tensor(out=ysl, in0=ysl, in1=pb[:, 0:pn], op=ALU.add),
                               [Y, pb], [Y])
            for sq in range(4):
                s.dma("sp", _ap(nst)[sq, l, d].rearrange("(j q) r -> q j r", q=128), FIN[sq][:], reads=[FIN[sq]], writes=[nst])
            s.barrier()
            s.release(m2)
        m3 = s.mark()
        tD = [s.sbuf("tDg%d" % i, [128, 512], F32) for i in range(2)]
        tP = [s.sbuf("tPg%d" % i, [128, 512], F32) for i in range(2)]
        Wg = s.sbuf("Wg", [128, 2, 256], BF16)
        s.dma("pool", Wg[:], _ap(glu_w)[l].rearrange("(a p) f -> p a f", p=128), writes=[Wg])
        gb = s.sbuf("gb", [128, 2], F32)
        s.dma("sp", gb[:], _ap(glu_b)[l, :].rearrange("(a p) -> p a", p=128), writes=[gb])
        gel = [s.sbuf("gel%d" % a, [128, 512], F32) for a in range(2)]
        gelb = [s.sbuf("gelb%d" % a, [128, 512], BF16) for a in range(2)]
        sig = s.sbuf("sig", [128, 512], F32)
        yo = [s.sbuf("yo%d" % a, [128, 512], BF16) for a in range(2)]
        for blk in range(NB):
            tok = slice(blk * 512, (blk + 1) * 512)
            for a in range(2):
                ysl = Y[:, a, tok]
                s.op("act", lambda e, ysl=ysl: e.activation(out=tD[0][:, 0:512], in_=ysl, func=AF.Square), reads=[Y], writes=[tD[0]])
                dv(lambda e: e.tensor_scalar(out=tD[0][:, 0:512], in0=tD[0][:, 0:512], scalar1=0.044715, scalar2=1.0,
                                             op0=ALU.mult, op1=ALU.add), [tD[0]], [tD[0]])
                dv(lambda e, ysl=ysl: e.tensor_tensor(out=tD[1][:, 0:512], in0=tD[0][:, 0:512], in1=ysl, op=ALU.mult), [tD[0], Y], [tD[1]])
                s.op("act", lambda e: e.activation(out=tP[0][:, 0:512], in_=tD[1][:, 0:512], func=AF.Sigmoid, scale=1.5957691216057308),
                     reads=[tD[1]], writes=[tP[0]])
                dv(lambda e, a=a, ysl=ysl: e.tensor_tensor(out=gel[a][:], in0=tP[0][:, 0:512], in1=ysl, op=ALU.mult), [tP[0], Y], [gel[a]])
                s.op("act", lambda e, a=a: e.copy(out=gelb[a][:], in_=gel[a][:]), reads=[gel[a]], writes=[gelb[a]])
            for a in range(2):
                pb = ps()
                mm_group(pb, pb[:], [(Wg[:, a2, a * 128:(a + 1) * 128], gelb[a2][:]) for a2 in range(2)], [Wg, gelb[0], gelb[1]])
                s.op("act", lambda e, pb=pb, a=a: e.activation(out=sig[:], in_=pb[:], func=AF.Sigmoid, bias=gb[:, a:a + 1]),
                     reads=[pb, gb], writes=[sig])
                dv(lambda e, a=a: e.tensor_tensor(out=yo[a][:], in0=sig[:], in1=gel[a][:], op=ALU.mult), [sig, gel[a]], [yo[a]])
                s.dma("sp", _ap(ymix_d)[a * 128:(a + 1) * 128, tok], yo[a][:], reads=[yo[a]], writes=[ymix_d])
        s.barrier()
        s.release(m0)

    def dense_gen(l):
        qPs = [s.sbuf("qP%d" % i, [128, 4, 256], BF16) for i in range(2)]
        kPs = [s.sbuf("kP%d" % i, [128, 4, 256], BF16) for i in range(2)]
        vPs = [s.sbuf("vP%d" % i, [128, 2, 512], BF16) for i in range(2)]
        Eb = [s.sbuf("Eb%d" % i, [128, 2, 256], BF16) for i in range(2)]
        rdn = [s.sbuf("rdn%d" % i, [128, 256], F32) for i in range(2)]
        yat = [s.sbuf("yat%d" % i, [128, 256], BF16) for i in range(2)]

        def fetch(sq):
            qP, kP, vP = qPs[sq % 2], kPs[sq % 2], vPs[sq % 2]
            tk = slice(sq * 256, (sq + 1) * 256)
            s.dma("sp", qP[:], _ap(q_d)[:, tk].rearrange("(t p) n -> p t n", p=128), reads=[q_d], writes=[qP])
            s.dma("sp", kP[:], _ap(k_d)[:, tk].rearrange("(t p) n -> p t n", p=128), reads=[k_d], writes=[kP])
            s.dma("sp", vP[:], _ap(v_d)[tk, :].rearrange("(t p) f -> p t f", p=128), reads=[v_d], writes=[vP])

        fetch(0)
        n_it = 0
        yield
        for sq in range(4):
            qP, kP, vP = qPs[sq % 2], kPs[sq % 2], vPs[sq % 2]
            if sq + 1 < 4:
                fetch(sq + 1)
            for hp in range(4):
                ya = yat[(sq * 4 + hp) % 2]
                for hh in range(2):
                    pr = slice(hh * 64, (hh + 1) * 64)
                    E = Eb[n_it % 2]
                    rd_ = rdn[n_it % 2]
                    n_it += 1
                    for kt2 in range(2):
                        pb = ps()
                        mm_group(pb, pb[:, 0:256], [(kP[pr, hp, kt2 * 128:(kt2 + 1) * 128], qP[pr, hp, :])], [kP, qP])
                        s.op("act", lambda e, E=E, pb=pb, kt2=kt2: e.activation(out=E[:, kt2, :], in_=pb[:, 0:256], func=AF.Exp, scale=0.125),
                             reads=[pb], writes=[E])
                    pn_ = ps()
                    mm_group(pn_, pn_[:, 0:256], [(vP[:, kt2, hp * 128:(hp + 1) * 128], E[:, kt2, :]) for kt2 in range(2)], [vP, E])
                    pd_ = ps()
                    mm_group(pd_, pd_[:, 0:256], [(ones_bf[:], E[:, kt2, :]) for kt2 in range(2)], [ones_bf, E])
                    s.op("dve", lambda e, rd_=rd_, pd_=pd_, pr=pr: e.reciprocal(out=rd_[pr, :], in_=pd_[pr, 0:256]), reads=[pd_], writes=[rd_])
                    s.op("dve", lambda e, ya=ya, pn_=pn_, rd_=rd_, pr=pr: e.tensor_tensor(out=ya[pr, :], in0=pn_[pr, 0:256], in1=rd_[pr, :],
                                                                                        op=ALU.mult), reads=[pn_, rd_], writes=[ya])
                s.dma("sp", _ap(ymix_d)[256 + hp * 128:256 + (hp + 1) * 128, sq * 256:(sq + 1) * 256], ya[:], reads=[ya], writes=[ymix_d])
                yield

    def attn_part(l):
        m0 = s.mark()
        BTt = s.sbuf("BTt", [128, 8, 15, 64], BF16)
        identb = s.sbuf("identb", [128, 128], BF16)
        s.op("dve", lambda e: e.tensor_copy(out=identb[:], in_=ident[:]), reads=[ident], writes=[identb])
        m1 = s.mark()
        oh = s.sbuf("oh", [31, 64, 128], F32)
        s.dma("sp", oh[:], _ap(c_oh).rearrange("d (k q) -> d k q", q=128), writes=[oh])
        msk = s.sbuf("msk", [128, 64], F32)
        s.dma("sp", msk[:], _ap(c_mask), writes=[msk])
        rpT = s.sbuf("rpT", [31, 128], F32)
        s.op("dve", lambda e: e.memset(rpT[:], 0.0), writes=[rpT])
        s.dma("sp", rpT[:, 0:120], _ap(rpb)[l].rearrange("x d -> d x"), reads=[rpT], writes=[rpT])
        BTf = BTt[:].rearrange("p h d k -> p (h d) k")
        for k4 in range(16):
            pb = ps()
            for ki in range(4):
                kk = k4 * 4 + ki
                mm_group(pb, pb[:, ki * 128:ki * 128 + 128], [(oh[:, kk, :], rpT[:])], [oh, rpT])
            for ki in range(4):
                kk = k4 * 4 + ki
                s.op("dve", lambda e, pb=pb, ki=ki, kk=kk: e.tensor_scalar(out=BTf[:, :, kk], in0=pb[:, ki * 128:ki * 128 + 120],
                                                                           scalar1=msk[:, kk:kk + 1], scalar2=8.0, op0=ALU.add, op1=ALU.mult),
                     reads=[pb, msk], writes=[BTt])
        s.barrier()
        s.release(m1)
        if int(os.environ.get("MK_ATT_STOP", "9")) <= 2:
            return
        qS = s.sbuf("qS", [128, 4, NS_TOK], BF16)
        kS = s.sbuf("kS", [128, 4, NS_TOK], BF16)
        for t4 in range(4):
            s.dma("sp", qS[:, t4, :], _ap(q_d)[t4 * 128:(t4 + 1) * 128, NP_TOK:NT], reads=[q_d], writes=[qS])
            s.dma("sp", kS[:, t4, :], _ap(k_d)[t4 * 128:(t4 + 1) * 128, NP_TOK:NT], reads=[k_d], writes=[kS])
        vS = s.sbuf("vS", [128, 64, 512], BF16)
        s.dma("sp", vS[0:64, :, :], _ap(v_d)[NP_TOK:NT, :].rearrange("(r c) f -> c r f", c=64), reads=[v_d], writes=[vS])
        s.dma("sp", vS[64:128, 0:63, :], _ap(v_d)[NP_TOK + 64:NT, :].rearrange("(r c) f -> c r f", c=64), reads=[v_d], writes=[vS])
        s.op("dve", lambda e: e.memset(vS[64:128, 63:64, :], 0.0), reads=[vS], writes=[vS])
        ck32 = s.sbuf("ck32", [128, 2, 512], F32)
        s.dma("sp", ck32[:], _ap(ck)[l].rearrange("(t p) f -> p t f", p=128), writes=[ck32])
        kC = s.sbuf("kC", [128, 4, 256], BF16)
        for hp in range(4):
            pb = ps()
            s.group("pe", [lambda e, pb=pb, t=t, hp=hp: e.transpose(pb[:, t * 128:(t + 1) * 128], ck32[:, t, hp * 128:(hp + 1) * 128], ident[:])
                           for t in range(2)], reads=[ck32, ident], writes=[pb])
            s.op("act", lambda e, pb=pb, hp=hp: e.copy(out=kC[:, hp, :], in_=pb[:, 0:256]), reads=[pb], writes=[kC])
        vC = s.sbuf("vC", [128, 2, 512], BF16)
        s.dma("pool", vC[:], _ap(cv)[l].rearrange("(t p) f -> p t f", p=128), writes=[vC])
        NBUF = 3
        EC = [s.sbuf("EC%d" % i, [128, 6, 2, 64], BF16) for i in range(NBUF)]
        rdw = [s.sbuf("rdw%d" % i, [128, 128], F32) for i in range(NBUF)]
        YN = [s.sbuf("YN%d" % i, [128, 4, 512], BF16) for i in range(2)]
        it = 0
        for r in range(int(os.environ.get("MK_NA_ROWS", "64"))):
            rs = min(max(r - 4, 0), 56)
            dr0 = rs - r + 7
            yn = YN[(r // 8) % 2]
            for hp in range(4):
                k2 = it % NBUF
                it += 1
                E_ = EC[k2]
                for hh in range(2):
                    pr = slice(hh * 64, (hh + 1) * 64)
                    h = 2 * hp + hh
                    qrow = qS[pr, hp, r * 64:(r + 1) * 64]
                    idb = identb[pr, hh * 64:(hh + 1) * 64]
                    pw = ps()
                    fw = []
                    for m in range(4):
                        fw.append(lambda e, pw=pw, m=m, pr=pr, qrow=qrow: e.matmul(
                            pw[:, m * 64:(m + 1) * 64], kS[pr, hp, (rs + 2 * m) * 64:(rs + 2 * m + 2) * 64], qrow, start=True, stop=False))
                        fw.append(lambda e, pw=pw, m=m, pr=pr, h=h, idb=idb: e.matmul(
                            pw[:, m * 64:(m + 1) * 64], BTt[pr, h, dr0 + 2 * m:dr0 + 2 * m + 2, :].rearrange("p d k -> p (d k)"), idb,
                            start=False, stop=True))
                    for t in range(2):
                        fw.append(lambda e, pw=pw, t=t, pr=pr, qrow=qrow: e.matmul(
                            pw[:, 256 + t * 64:256 + (t + 1) * 64], kC[pr, hp, t * 128:(t + 1) * 128], qrow, start=True, stop=True))
                    s.group("pe", fw, reads=[kS, kC, qS, BTt, identb], writes=[pw])
                    s.op("act", lambda e, E_=E_, pw=pw, hh=hh: e.activation(
                        out=E_[:, :, hh, :], in_=pw[:, 0:384].rearrange("p (m q) -> p m q", m=6), func=AF.Exp, scale=0.125),
                        reads=[pw], writes=[E_])
                pnd = ps()
                rhs6 = [E_[:, m, :, :].rearrange("p h q -> p (h q)") for m in range(6)]
                lv = [vS[:, rs + 2 * m, hp * 128:(hp + 1) * 128] for m in range(4)] + [vC[:, t, hp * 128:(hp + 1) * 128] for t in range(2)]
                mm_group(pnd, pnd[:, 0:128], [(lv[m], rhs6[m]) for m in range(6)], [vS, vC, E_])
                mm_group(pnd, pnd[:, 128:256], [(ones_bf[:], rhs6[m]) for m in range(6)], [ones_bf, E_])
                rd_ = rdw[k2]
                s.op("dve", lambda e, rd_=rd_, pnd=pnd: e.reciprocal(out=rd_[:], in_=pnd[:, 128:256]), reads=[pnd], writes=[rd_])
                for hh in range(2):
                    pr = slice(hh * 64, (hh + 1) * 64)
                    s.op("dve", lambda e, yn=yn, pnd=pnd, rd_=rd_, pr=pr, hp=hp, r=r, hh=hh: e.tensor_tensor(
                        out=yn[pr, hp, (r % 8) * 64:(r % 8 + 1) * 64], in0=pnd[pr, hh * 64:(hh + 1) * 64], in1=rd_[pr, hh * 64:(hh + 1) * 64],
                        op=ALU.mult), reads=[pnd, rd_], writes=[yn])
            if r % 8 == 7:
                tok0 = NP_TOK + (r // 8) * 512
                for hp in range(4):
                    s.dma("sp", _ap(ymix_d)[256 + hp * 128:256 + (hp + 1) * 128, tok0:tok0 + 512], yn[:, hp, :], reads=[yn], writes=[ymix_d])
        s.barrier()
        s.release(m0)

    def gate_gen(l):
        ws32 = s.sbuf("ws32", [128, 4, 128], F32)
        s.dma("sp", ws32[:], _ap(gm_ws)[l].rearrange("g i j -> i g j"), writes=[ws32])
        wsT = s.sbuf("wsT", [128, 4, 128], BF16)
        pb = ps()
        s.group("pe", [lambda e, g=g: e.transpose(pb[:, g * 128:(g + 1) * 128], ws32[:, g, :], ident[:]) for g in range(4)],
                reads=[ws32, ident], writes=[pb])
        s.op("act", lambda e: e.copy(out=wsT[:].rearrange("p g i -> p (g i)"), in_=pb[:]), reads=[pb], writes=[wsT])
        BS = s.sbuf("BS", [128, 2, 128], F32)
        for g in range(4):
            s.dma("sp", BS[(g % 2) * 64:(g % 2 + 1) * 64, g // 2, :], _ap(gm_bs)[l, g:g + 1, :].to_broadcast([64, 128]), writes=[BS])
        ug = [s.sbuf("ug%d" % i, [128, 2, 512], BF16) for i in range(2)]
        vAl = [s.sbuf("vAl%d" % i, [128, 4, 256], BF16) for i in range(2)]
        vBl = [s.sbuf("vBl%d" % i, [128, 4, 256], BF16) for i in range(2)]
        tg = [s.sbuf("tg%d" % i, [128, 128], F32) for i in range(2)]
        yg = [s.sbuf("yg%d" % i, [128, 2, 512], BF16) for i in range(2)]
        it = 0
        def fetch(blk):
            k2 = blk % 2
            tok = slice(blk * 512, (blk + 1) * 512)
            for a in range(2):
                s.dma("sp", ug[k2][:, a, :], _ap(ug_d)[a * 128:(a + 1) * 128, tok], reads=[ug_d], writes=[ug[k2]])
            s.dma("sp", vAl[k2][:], _ap(va_d)[tok, :].rearrange("(t p) f -> p t f", p=128), reads=[va_d], writes=[vAl[k2]])
            s.dma("sp", vBl[k2][:], _ap(vb_d)[tok, :].rearrange("(t p) f -> p t f", p=128), reads=[vb_d], writes=[vBl[k2]])

        fetch(0)
        yield
        for blk in range(NB):
            k2 = blk % 2
            tok = slice(blk * 512, (blk + 1) * 512)
            if blk + 1 < NB:
                fetch(blk + 1)
            for t in range(4):
                if t:
                    yield
                for a in range(2):
                    pq = ps()
                    mm_group(pq, pq[:, 0:128], [(vAl[k2][:, t, a * 128:(a + 1) * 128], wsT[:, 2 * a, :]),
                                                (vBl[k2][:, t, a * 128:(a + 1) * 128], wsT[:, 2 * a + 1, :])], [vAl[k2], vBl[k2], wsT])
                    tt_ = tg[it % 2]
                    it += 1
                    s.op("dve", lambda e, tt_=tt_, pq=pq, a=a: e.tensor_tensor(out=tt_[:], in0=pq[:, 0:128], in1=BS[:, a, :], op=ALU.add),
                         reads=[pq, BS], writes=[tt_])
                    s.op("dve", lambda e, tt_=tt_, a=a, t=t, k2=k2: e.tensor_tensor(
                        out=yg[k2][:, a, t * 128:(t + 1) * 128], in0=tt_[:], in1=ug[k2][:, a, t * 128:(t + 1) * 128], op=ALU.mult),
                        reads=[tt_, ug[k2]], writes=[yg[k2]])
            for a in range(2):
                s.dma("sp", _ap(ymix_d)[768 + a * 128:768 + (a + 1) * 128, tok], yg[k2][:, a, :], reads=[yg[k2]], writes=[ymix_d])
            yield

    PH = os.environ.get("MK_PHASES", "all")
    if PH != "all":
        for name in PH.split(","):
            {"adaln": setup_adaln, "ffn": lambda: ffn_phase(0, 0, f1_in, f1_out, True, False), "proj": lambda: proj_phase(0),
             "ssm": lambda: ssm_part(0), "attn": lambda: attn_part(0),
             "outproj": lambda: outproj_phase(0)}[name]()
        s.barrier()
        s.release(0)
        ctx.__exit__(None, None, None)
        return nc
    mW = s.mark()
    W = alloc_ffn_w()
    setup_adaln(after_dma=lambda: issue_ffn_w(W, 0, f1_in, f1_out))
    for l in range(DEPTH):
        if l == 0:
            ffn_phase(l, 0, f1_in, f1_out, first=True, last=False, W=W)
            s.release(mW)
        else:
            ffn_phase(l, 0, f1_in, f1_out, first=False, last=False)
        proj_phase(l)
        mB = s.mark()
        gens = [dense_gen(l), gate_gen(l)]
        for g in gens:
            next(g)

        def pump():
            for g in gens:
                try:
                    next(g)
                    return
                except StopIteration:
                    continue

        ssm_part(l, pump)
        for g in gens:
            for _ in g:
                pass
        s.barrier()
        s.release(mB)
        attn_part(l)
        mW = s.mark()
        W = alloc_ffn_w()
        mWO = s.mark()
        WO = s.sbuf("WO", [128, 8, 1024], BF16)
        s.dma("pool", WO[:], _ap(w_out)[l].rearrange("(kt p) f -> p kt f", p=128), writes=[WO])
        issue_ffn_w(W, l, f2_in, f2_out)
        outproj_phase(l, WO=WO)
        s.release(mWO)
        ffn_phase(l, 2, f2_in, f2_out, first=False, last=(l == DEPTH - 1), W=W)
        s.release(mW)
    outs = [yp, ys, nk, nv, nst]
    s.barrier()
    s.release(0)
    ctx.__exit__(None, None, None)
    return nc


_NC_CACHE = {}


def _consts():
    ident = np.eye(128, dtype=np.float32)
    kc = np.arange(64)[:, None]
    qc = np.arange(64)[None, :]
    dc = kc - qc + 15
    oh = np.zeros((31, 64, 128), np.float32)
    for q in range(64):
        for k in range(64):
            if 0 <= dc[k, q] <= 30:
                oh[dc[k, q], k, q] = 1.0
                oh[dc[k, q], k, 64 + q] = 1.0
    cs = np.clip(np.arange(64) - 8, 0, 48)
    win = (kc >= cs[None, :]) & (kc < cs[None, :] + 16)
    mask = np.where(win, 0.0, NEG).astype(np.float32)
    iota = np.tile(np.arange(1024, dtype=np.float32)[None, :], (128, 1))
    return {"c_ident": ident, "c_oh": oh.reshape(31, 8192), "c_mask": np.ascontiguousarray(np.concatenate([mask.T, mask.T], axis=0)), "c_iota": iota}


def kernel(**inputs):
    debug = bool(int(os.environ.get("MK_DEBUG", "0")))
    key = ("nc", debug)
    if key not in _NC_CACHE:
        _NC_CACHE[key] = build_program(debug=debug)
    nc = _NC_CACHE[key]
    f = lambda a: np.ascontiguousarray(np.asarray(a, dtype=np.float32))
    x_prompt, x_sample = f(inputs["x_prompt"]), f(inputs["x_sample"])
    c, c_ctx = f(inputs["c"]), f(inputs["c_ctx"])
    cache_k, cache_v, state_ssm = f(inputs["cache_k"]), f(inputs["cache_v"]), f(inputs["state_ssm"])
    shared = {}
    for name in ("w_ada", "b_ada", "norm_ffn1", "norm_mix", "norm_ffn2", "ffn1_w_in", "ffn1_w_out", "ffn2_w_in",
                 "ffn2_w_out", "w_in", "w_out", "ssm_d", "ssm_glu_w", "ssm_glu_b", "na_q_norm", "na_k_norm",
                 "gm_ws", "gm_bs"):
        shared[name] = f(inputs[name])
    shared["ssm_lambda_re"] = f(inputs["ssm_lambda_re"]).reshape(DEPTH, 2, 1024)
    shared["ssm_lambda_im"] = f(inputs["ssm_lambda_im"]).reshape(DEPTH, 2, 1024)
    shared["ssm_log_dt"] = f(inputs["ssm_log_dt"])
    shared["ssm_b_re"] = f(inputs["ssm_b_re"]).reshape(DEPTH, 2, 1024, 16)
    shared["ssm_b_im"] = f(inputs["ssm_b_im"]).reshape(DEPTH, 2, 1024, 16)
    shared["ssm_c_re"] = f(inputs["ssm_c_re"]).reshape(DEPTH, 2, 256, 64)
    shared["ssm_c_im"] = f(inputs["ssm_c_im"]).reshape(DEPTH, 2, 256, 64)
    shared["na_rpb"] = f(inputs["na_rpb"]).reshape(DEPTH, 120, 31)
    shared.update(_consts())
    in_maps = []
    for core in range(8):
        b = core // 2
        m = dict(shared)
        m["xp"] = x_prompt[4 * core:4 * core + 4].reshape(NP_TOK, D)
        m["xs"] = x_sample[b]
        m["cond"] = np.stack([c_ctx, c[b]], axis=0)
        m["ck"] = cache_k[b].reshape(DEPTH, 256, 512)
        m["cv"] = cache_v[b].reshape(DEPTH, 256, 512)
        m["sst"] = state_ssm[b].reshape(DEPTH, 2, 1024, 2)
        hs = np.zeros((128, 2), np.float32)
        hs[:, core % 2] = 1.0
        m["c_hsel"] = hs
        in_maps.append(m)
    res = run_bass_kernel_spmd(nc, in_maps, core_ids=list(range(8)))
    R = res.results
    if debug:
        kernel.last_results = R
    y_prompt = np.concatenate([R[i]["yp"].reshape(4, 256, D) for i in range(8)], axis=0)
    y_sample = np.stack([np.concatenate([R[2 * b]["ys"], R[2 * b + 1]["ys"]], axis=0) for b in range(4)], axis=0)
    new_k = np.concatenate([R[i]["nk"].reshape(4, DEPTH, 256, 8, 64) for i in range(8)], axis=0)
    new_v = np.concatenate([R[i]["nv"].reshape(4, DEPTH, 256, 8, 64) for i in range(8)], axis=0)
    new_s = np.concatenate([R[i]["nst"].reshape(4, DEPTH, 2, 16, 64, 2) for i in range(8)], axis=0)
    return (y_prompt.astype(np.float32), y_sample.astype(np.float32), new_k.astype(np.float32),
            new_v.astype(np.float32), new_s.astype(np.float32))
```

```python
import math
import os
import numpy as np
import concourse.bass as bass
import concourse.mybir as mybir
from concourse.bass_utils import run_bass_kernel_spmd

F32 = mybir.dt.float32
BF16 = mybir.dt.bfloat16
AF = mybir.ActivationFunctionType
ALU = mybir.AluOpType
AX = mybir.AxisListType

D = 1024
DFF = 2816
DEPTH = 2
NP_TOK = 1024
NS_TOK = 4096
NT = NP_TOK + NS_TOK
NB = NT // 512
INC = 2304
TWO_PI = 2.0 * math.pi
NEG = -30000.0


class Buf:
    __slots__ = ("t", "w", "r", "name")

    def __init__(self, t, name=""):
        self.t = t
        self.w = None
        self.r = {}
        self.name = name

    def __getitem__(self, idx):
        return self.t[idx]


class Sched:
    def __init__(self, nc, n_dma_sems=40):
        self.nc = nc
        self.stack = []
        self.eng = {"pe": nc.tensor, "act": nc.scalar, "dve": nc.vector, "pool": nc.gpsimd, "sp": nc.sync}
        self.sems = {}
        self.cnt = {}
        for k in ("pe", "act", "dve", "pool"):
            self.sems[k] = self._sem("s_" + k)
            self.cnt[k] = 0
        self.dma_keys = []
        for i in range(n_dma_sems):
            k = "d%d" % i
            self.sems[k] = self._sem("s_" + k)
            self.cnt[k] = 0
            self.dma_keys.append(k)
        self.dma_rr = 0
        self.seen = {e: {} for e in self.eng}
        self.ninst = 0

    def _sem(self, name):
        cm = self.nc.semaphore(name)
        h = cm.__enter__()
        self.stack.append(cm)
        return h

    def mark(self):
        return len(self.stack)

    def release(self, mark):
        while len(self.stack) > mark:
            self.stack.pop().__exit__(None, None, None)

    def sbuf(self, name, shape, dtype):
        self.uid = getattr(self, "uid", 0) + 1
        name = "%s_u%d" % (name, self.uid)
        cm = self.nc.sbuf_tensor(name, list(shape), dtype)
        t = cm.__enter__()
        self.stack.append(cm)
        return Buf(t, name)

    def psum(self, name, shape, dtype=F32):
        cm = self.nc.psum_tensor(name, list(shape), dtype)
        t = cm.__enter__()
        self.stack.append(cm)
        return Buf(t, name)

    def dram(self, name, shape, dtype, kind="Internal"):
        return Buf(self.nc.dram_tensor(name, list(shape), dtype, kind=kind), name)

    def _need(self, e, deps):
        seen = self.seen[e]
        todo = {}
        for (k, v) in deps:
            if seen.get(k, 0) >= v:
                continue
            if todo.get(k, 0) < v:
                todo[k] = v
        for k, v in todo.items():
            self.eng[e].wait_ge(self.sems[k], v)
            seen[k] = v
            self.ninst += 1

    @staticmethod
    def _deps(reads, writes):
        deps = []
        for b in reads:
            if b.w is not None:
                deps.append(b.w)
        for b in writes:
            if b.w is not None:
                deps.append(b.w)
            deps.extend(b.r.items())
        return deps

    def _commit(self, k, v, reads, writes):
        for b in reads:
            if b.r.get(k, 0) < v:
                b.r[k] = v
        for b in writes:
            b.w = (k, v)
            b.r = {}

    def op(self, e, fn, reads=(), writes=()):
        self._need(e, self._deps(reads, writes))
        inst = fn(self.eng[e])
        self.cnt[e] += 1
        inst.then_inc(self.sems[e], 1)
        self._commit(e, self.cnt[e], reads, writes)
        self.ninst += 1
        return inst

    def group(self, e, fns, reads=(), writes=()):
        self._need(e, self._deps(reads, writes))
        inst = None
        for fn in fns:
            inst = fn(self.eng[e])
            self.ninst += 1
        self.cnt[e] += 1
        inst.then_inc(self.sems[e], 1)
        self._commit(e, self.cnt[e], reads, writes)

    def dma(self, q, out_ap, in_ap, reads=(), writes=(), **kw):
        nsw = 8
        if q == "pool":
            self.sw_rr = (getattr(self, "sw_rr", -1) + 1) % nsw
            k = self.dma_keys[self.sw_rr]
        else:
            k = self.dma_keys[nsw + self.dma_rr]
            self.dma_rr = (self.dma_rr + 1) % (len(self.dma_keys) - nsw)
        deps = self._deps(reads, writes)
        if self.cnt[k] > 0:
            deps.append((k, self.cnt[k]))
        self._need(q, deps)
        inst = self.eng[q].dma_start(out=out_ap, in_=in_ap, **kw)
        self.cnt[k] += 16
        inst.then_inc(self.sems[k], 16)
        self._commit(k, self.cnt[k], reads, writes)
        self.ninst += 1
        return inst

    def barrier(self):
        allk = [(k, v) for k, v in self.cnt.items() if v > 0]
        for e in self.eng:
            self._need(e, allk)


def _ap(b):
    return b.t.ap()


def build_program(debug=False):
    nc = bass.Bass("TRN2", target_bir_lowering=False)
    s = Sched(nc)
    ctx = nc.allow_non_contiguous_dma(reason="small strided parameter loads")
    ctx.__enter__()

    def din(name, shape):
        return s.dram(name, shape, F32, kind="ExternalInput")

    def dout(name, shape):
        return s.dram(name, shape, F32, kind="ExternalOutput")

    xp = din("xp", [NP_TOK, D])
    xs = din("xs", [NS_TOK, D])
    cond = din("cond", [2, D])
    ck = din("ck", [DEPTH, 256, 512])
    cv = din("cv", [DEPTH, 256, 512])
    sst = din("sst", [DEPTH, 2, 1024, 2])
    w_ada = din("w_ada", [DEPTH, D, 9 * D])
    b_ada = din("b_ada", [DEPTH, 9 * D])
    norms = [din("norm_ffn1", [DEPTH, D]), din("norm_mix", [DEPTH, D]), din("norm_ffn2", [DEPTH, D])]
    f1_in = din("ffn1_w_in", [DEPTH, D, 2 * DFF])
    f1_out = din("ffn1_w_out", [DEPTH, DFF, D])
    f2_in = din("ffn2_w_in", [DEPTH, D, 2 * DFF])
    f2_out = din("ffn2_w_out", [DEPTH, DFF, D])
    w_in = din("w_in", [DEPTH, D, INC])
    w_out = din("w_out", [DEPTH, D, D])
    lam_re = din("ssm_lambda_re", [DEPTH, 2, 1024])
    lam_im = din("ssm_lambda_im", [DEPTH, 2, 1024])
    log_dt = din("ssm_log_dt", [DEPTH, 2, 16])
    b_re = din("ssm_b_re", [DEPTH, 2, 1024, 16])
    b_im = din("ssm_b_im", [DEPTH, 2, 1024, 16])
    c_re = din("ssm_c_re", [DEPTH, 2, 256, 64])
    c_im = din("ssm_c_im", [DEPTH, 2, 256, 64])
    ssm_d = din("ssm_d", [DEPTH, 256])
    glu_w = din("ssm_glu_w", [DEPTH, 256, 256])
    glu_b = din("ssm_glu_b", [DEPTH, 256])
    qn_g = din("na_q_norm", [DEPTH, 64])
    kn_g = din("na_k_norm", [DEPTH, 64])
    rpb = din("na_rpb", [DEPTH, 8 * 15, 31])
    gm_ws = din("gm_ws", [DEPTH, 4, 128, 128])
    gm_bs = din("gm_bs", [DEPTH, 4, 128])
    c_ident = din("c_ident", [128, 128])
    c_oh = din("c_oh", [31, 64 * 128])
    c_mask = din("c_mask", [128, 64])
    c_iota = din("c_iota", [128, 1024])
    c_hsel = din("c_hsel", [128, 2])
    yp = dout("yp", [NP_TOK, D])
    ys = dout("ys", [NS_TOK // 2, D])
    nk = dout("nk", [4, DEPTH, 256, 512])
    nv = dout("nv", [4, DEPTH, 256, 512])
    nst = dout("nst", [4, DEPTH, 2, 1024, 2])
    skind = "ExternalOutput" if debug else "Internal"
    xT_d = s.dram("xT_d", [D, NT], F32, kind=skind)
    ymix_d = s.dram("ymix_d", [D, NT], BF16, kind=skind)
    us_d = s.dram("us_d", [256, NT], BF16, kind=skind)
    q_d = s.dram("q_d", [512, NT], BF16, kind=skind)
    k_d = s.dram("k_d", [512, NT], BF16, kind=skind)
    v_d = s.dram("v_d", [NT, 512], BF16, kind=skind)
    ug_d = s.dram("ug_d", [256, NT], BF16, kind=skind)
    va_d = s.dram("va_d", [NT, 256], BF16, kind=skind)
    vb_d = s.dram("vb_d", [NT, 256], BF16, kind=skind)

    banks = [s.psum("bank%d" % i, [128, 512], F32) for i in range(8)]
    bank_rr = [0]

    def ps():
        b = banks[bank_rr[0]]
        bank_rr[0] = (bank_rr[0] + 1) % 8
        return b

    ident = s.sbuf("ident", [128, 128], F32)
    s.dma("sp", ident[:], _ap(c_ident), writes=[ident])
    ones_bf = s.sbuf("ones_bf", [128, 128], BF16)
    s.op("dve", lambda e: e.memset(ones_bf[:], 1.0), writes=[ones_bf])
    mean_bf = s.sbuf("mean_bf", [128, 128], BF16)
    s.op("dve", lambda e: e.memset(mean_bf[:], 1.0 / 1024.0), writes=[mean_bf])
    bd_bf = s.sbuf("bd_bf", [128, 128], BF16)
    s.op("dve", lambda e: e.memset(bd_bf[:], 0.0), writes=[bd_bf])
    s.op("dve", lambda e: e.memset(bd_bf[0:64, 0:64], 1.0 / 64.0), reads=[bd_bf], writes=[bd_bf])
    s.op("dve", lambda e: e.memset(bd_bf[64:128, 64:128], 1.0 / 64.0), reads=[bd_bf], writes=[bd_bf])
    pi_c = s.sbuf("pi_c", [128, 1], F32)
    s.op("dve", lambda e: e.memset(pi_c[:], math.pi), writes=[pi_c])
    hsel = s.sbuf("hsel", [128, 2], F32)
    s.dma("sp", hsel[:], _ap(c_hsel), writes=[hsel])
    eps6 = s.sbuf("eps6", [128, 1], F32)
    s.op("dve", lambda e: e.memset(eps6[:], 1e-6), writes=[eps6])
    eps5 = s.sbuf("eps5", [128, 1], F32)
    s.op("dve", lambda e: e.memset(eps5[:], 1e-5), writes=[eps5])

    def rsqrt(dst_b, dst_ap, src_b, src_ap, eps_b, scale=1.0):
        P = dst_ap.shape[0]
        s.op("act", lambda e: e.activation(out=dst_ap, in_=src_ap, func=AF.Sqrt, scale=scale, bias=eps_b[0:P, 0:1]),
             reads=[src_b, eps_b], writes=[dst_b])
        s.op("dve", lambda e: e.reciprocal(out=dst_ap, in_=dst_ap), reads=[dst_b], writes=[dst_b])
    Aco = s.sbuf("Aco", [128, DEPTH, 3, 2, 8], F32)
    Bco = s.sbuf("Bco", [128, DEPTH, 3, 2, 8], F32)
    Gco = s.sbuf("Gco", [128, DEPTH, 3, 2, 8], F32)

    def mm_group(out_buf, out_ap, pairs, reads):
        n = len(pairs)
        fns = []
        for i, (l, r) in enumerate(pairs):
            fns.append(lambda e, l=l, r=r, i=i: e.matmul(out_ap, l, r, start=(i == 0), stop=(i == n - 1)))
        s.group("pe", fns, reads=reads, writes=[out_buf])

    def setup_adaln(after_dma=None):
        m0 = s.mark()
        cs32 = s.sbuf("cs32", [128, 8, 2], F32)
        csb = s.sbuf("csb", [128, 8, 2], BF16)
        for c in range(2):
            s.dma("sp", cs32[:, :, c], _ap(cond)[c, :].rearrange("(kt p) -> p kt", p=128), writes=[cs32])
        s.op("act", lambda e: e.activation(out=csb[:], in_=cs32[:], func=AF.Silu), reads=[cs32], writes=[csb])
        gn = s.sbuf("gn", [128, DEPTH, 3, 8], F32)
        for i in range(3):
            for l in range(DEPTH):
                s.dma("sp", gn[:, l, i, :], _ap(norms[i])[l, :].rearrange("(kt p) -> p kt", p=128), writes=[gn])
        wa = [s.sbuf("wa%d" % i, [128, 8, 1024], BF16) for i in range(2)]
        badaT = s.sbuf("badaT", [128, 72], F32)
        modT = s.sbuf("modT", [128, 72, 2], F32)
        for l in range(DEPTH):
            s.dma("sp", badaT[:], _ap(b_ada)[l, :].rearrange("(ft p) -> p ft", p=128), writes=[badaT])
            pb = ps()
            for ch in range(9):
                w = wa[ch % 2]
                s.dma("pool", w[:], _ap(w_ada)[l, :, ch * 1024:(ch + 1) * 1024].rearrange("(kt p) f -> p kt f", p=128),
                      writes=[w])
                for f8 in range(8):
                    ft = ch * 8 + f8
                    mm_group(pb, pb[:, 2 * ft:2 * ft + 2],
                             [(w[:, kt, f8 * 128:(f8 + 1) * 128], csb[:, kt, :]) for kt in range(8)], [w, csb])
            for c in range(2):
                s.op("dve", lambda e, c=c: e.tensor_tensor(out=modT[:, :, c], in0=pb[:, c:144:2], in1=badaT[:], op=ALU.add),
                     reads=[pb, badaT], writes=[modT])
            for i in range(3):
                for c in range(2):
                    sh = modT[:, (3 * i) * 8:(3 * i) * 8 + 8, c]
                    sc = modT[:, (3 * i + 1) * 8:(3 * i + 1) * 8 + 8, c]
                    gt = modT[:, (3 * i + 2) * 8:(3 * i + 2) * 8 + 8, c]
                    s.op("dve", lambda e, sc=sc, l=l, i=i, c=c: e.scalar_tensor_tensor(
                        out=Aco[:, l, i, c, :], in0=sc, scalar=1.0, in1=gn[:, l, i, :], op0=ALU.add, op1=ALU.mult),
                        reads=[modT, gn], writes=[Aco])
                    s.op("dve", lambda e, sh=sh, l=l, i=i, c=c: e.tensor_copy(out=Bco[:, l, i, c, :], in_=sh),
                         reads=[modT], writes=[Bco])
                    s.op("dve", lambda e, gt=gt, l=l, i=i, c=c: e.tensor_scalar(
                        out=Gco[:, l, i, c, :], in0=gt, scalar1=(1.0 if i == 1 else 0.5), scalar2=None, op0=ALU.mult),
                        reads=[modT], writes=[Gco])
        if after_dma is not None:
            after_dma()
        s.barrier()
        s.release(m0)

    def load_xT(xT, blk):
        s.dma("sp", xT[:], _ap(xT_d)[:, blk * 512:(blk + 1) * 512].rearrange("(kt p) n -> p kt n", p=128),
              reads=[xT_d], writes=[xT])

    def store_xT(xT, blk):
        s.dma("sp", _ap(xT_d)[:, blk * 512:(blk + 1) * 512].rearrange("(kt p) n -> p kt n", p=128), xT[:],
              reads=[xT], writes=[xT_d])

    def load_x_tm(xT, xtm, blk):
        src = _ap(xp)[blk * 512:(blk + 1) * 512, :] if blk < 2 else _ap(xs)[(blk - 2) * 512:(blk - 1) * 512, :]
        for tt in range(4):
            s.dma("sp", xtm[:, 0, :], src[tt * 128:(tt + 1) * 128, :], writes=[xtm])
            for half in range(2):
                pb = ps()
                s.group("pe", [lambda e, k4=k4, half=half, pb=pb: e.transpose(pb[:, k4 * 128:(k4 + 1) * 128],
                                                                               xtm[:, 0, (half * 4 + k4) * 128:(half * 4 + k4 + 1) * 128], ident[:])
                               for k4 in range(4)], reads=[xtm, ident], writes=[pb])
                dst = xT[:, half * 4:(half + 1) * 4, tt * 128:(tt + 1) * 128]
                if half:
                    s.op("act", lambda e, dst=dst, pb=pb: e.copy(out=dst, in_=pb[:].rearrange("p (k n) -> p k n", k=4)),
                         reads=[pb], writes=[xT])
                else:
                    s.op("dve", lambda e, dst=dst, pb=pb: e.tensor_copy(out=dst, in_=pb[:].rearrange("p (k n) -> p k n", k=4)),
                         reads=[pb], writes=[xT])

    def store_y_tm(xT, ytm, blk):
        dst = _ap(yp)[blk * 512:(blk + 1) * 512, :] if blk < 2 else _ap(ys)[(blk - 2) * 512:(blk - 1) * 512, :]
        ydr = yp if blk < 2 else ys
        for tt in range(4):
            for half in range(2):
                pb = ps()
                s.group("pe", [lambda e, k4=k4, tt=tt, half=half, pb=pb: e.transpose(
                    pb[:, k4 * 128:(k4 + 1) * 128], xT[:, half * 4 + k4, tt * 128:(tt + 1) * 128], ident[:])
                    for k4 in range(4)], reads=[xT, ident], writes=[pb])
                if half:
                    s.op("act", lambda e, pb=pb: e.copy(out=ytm[:, 0, 512:1024], in_=pb[:]), reads=[pb], writes=[ytm])
                else:
                    s.op("dve", lambda e, pb=pb: e.tensor_copy(out=ytm[:, 0, 0:512], in_=pb[:]), reads=[pb], writes=[ytm])
            s.dma("sp", dst[tt * 128:(tt + 1) * 128, :], ytm[:, 0, :], reads=[ytm], writes=[ydr])

    def norm_mod(xT, hb, tmp2, rstd, l, i, c):
        for kt in range(8):
            s.op("act", lambda e, kt=kt: e.activation(out=hb[:, kt, :], in_=xT[:, kt, :], func=AF.Square),
                 reads=[xT], writes=[hb])
        pb = ps()
        mm_group(pb, pb[:], [(mean_bf[:], hb[:, kt, :]) for kt in range(8)], [mean_bf, hb])
        rsqrt(rstd, rstd[:], pb, pb[:], eps6)
        for kt in range(8):
            t = tmp2[kt % 2]
            s.op("dve", lambda e, kt=kt, t=t: e.tensor_tensor(out=t[:], in0=xT[:, kt, :], in1=rstd[:], op=ALU.mult),
                 reads=[xT, rstd], writes=[t])
            s.op("act", lambda e, kt=kt, t=t: e.activation(out=hb[:, kt, :], in_=t[:], func=AF.Identity,
                                                           scale=Aco[:, l, i, c, kt:kt + 1], bias=Bco[:, l, i, c, kt:kt + 1]),
                 reads=[t, Aco, Bco], writes=[hb])

    def alloc_ffn_w():
        W1 = [s.sbuf("W1_%d" % j, [128, 8, 512], BF16) for j in range(11)]
        W2 = [s.sbuf("W2_%d" % j, [128, 2, 1024], BF16) for j in range(11)]
        return (W1, W2)

    def issue_ffn_w(W, l, win_d, wout_d):
        W1, W2 = W
        for j in (0, 5, 1, 6, 2, 7, 3, 8, 4, 9, 10):
            s.dma("pool", W1[j][:], _ap(win_d)[l, :, j * 512:(j + 1) * 512].rearrange("(kt p) f -> p kt f", p=128),
                  writes=[W1[j]])
        for j in range(11):
            s.dma("pool", W2[j][:], _ap(wout_d)[l, j * 256:(j + 1) * 256, :].rearrange("(kt p) f -> p kt f", p=128),
                  writes=[W2[j]])

    def ffn_phase(l, i, win_d, wout_d, first, last, W=None):
        m0 = s.mark()
        if W is None:
            W = alloc_ffn_w()
            issue_ffn_w(W, l, win_d, wout_d)
        W1, W2 = W
        xTs = [s.sbuf("xT%d" % j, [128, 8, 512], F32) for j in range(2)]
        hb = s.sbuf("hb", [128, 8, 512], BF16)
        actb = s.sbuf("actb", [128, 22, 512], BF16)
        tmp2 = [s.sbuf("tmp%d" % j, [128, 512], F32) for j in range(2)]
        sg2 = tmp2
        rstd = s.sbuf("rstd", [128, 512], F32)
        if os.environ.get("MK_VERBOSE"):
            print("ffn phase free sbuf bytes/partition:", nc.sbuf_bytes_remaining, "first/last", first, last)
        xtm = s.sbuf("xtm", [128, 1, 1024], F32) if (first or last) else None

        def w1cols(col):
            return W1[col // 512], col % 512

        if last:
            items = [("std", 0), ("std", 1)] + [("own", i) for i in range(4)]
        else:
            items = [("std", blk) for blk in range(NB)]

        def fetch(it):
            kind, b_ = items[it]
            xT_ = xTs[it % 2]
            if kind == "own":
                load_xT(xT_, 2 + b_)
                for kt in range(8):
                    t = tmp2[kt % 2]
                    s.dma("sp", t[:], _ap(xT_d)[kt * 128:(kt + 1) * 128, (6 + b_) * 512:(7 + b_) * 512], reads=[xT_d], writes=[t])
                    s.op("dve", lambda e, kt=kt, xT_=xT_: e.tensor_scalar(out=xT_[:, kt, :], in0=xT_[:, kt, :], scalar1=hsel[:, 0:1],
                                                                          scalar2=None, op0=ALU.mult), reads=[xT_, hsel], writes=[xT_])
                    s.op("dve", lambda e, kt=kt, xT_=xT_, t=t: e.scalar_tensor_tensor(
                        out=xT_[:, kt, :], in0=t[:], scalar=hsel[:, 1:2], in1=xT_[:, kt, :], op0=ALU.mult, op1=ALU.add),
                        reads=[t, hsel, xT_], writes=[xT_])
            elif first:
                load_x_tm(xT_, xtm, b_)
            else:
                load_xT(xT_, b_)

        fetch(0)
        for it, (kind, blk) in enumerate(items):
            c = 0 if (kind == "std" and blk < 2) else 1
            xT = xTs[it % 2]
            nxt_own = it + 1 < len(items) and items[it + 1][0] == "own"
            if it + 1 < len(items) and not first and not nxt_own:
                fetch(it + 1)
            norm_mod(xT, hb, tmp2, rstd, l, i, c)
            for j in range(22):
                wg, og = w1cols(j * 128)
                wu, ou = w1cols(DFF + j * 128)
                pg = ps()
                mm_group(pg, pg[:], [(wg[:, kt, og:og + 128], hb[:, kt, :]) for kt in range(8)], [wg, hb])
                pu = ps()
                mm_group(pu, pu[:], [(wu[:, kt, ou:ou + 128], hb[:, kt, :]) for kt in range(8)], [wu, hb])
                sg = sg2[j % 2]
                s.op("act", lambda e, sg=sg, pg=pg: e.activation(out=sg[:], in_=pg[:], func=AF.Silu), reads=[pg], writes=[sg])
                s.op("dve", lambda e, sg=sg, pu=pu, j=j: e.tensor_tensor(out=actb[:, j, :], in0=sg[:], in1=pu[:], op=ALU.mult),
                     reads=[sg, pu], writes=[actb])
            if it + 1 < len(items) and first:
                fetch(it + 1)
            for ft in range(8):
                po = ps()
                mm_group(po, po[:], [(W2[j // 2][:, j % 2, ft * 128:(ft + 1) * 128], actb[:, j, :]) for j in range(22)],
                         W2 + [actb])
                s.op("dve", lambda e, po=po, ft=ft, xT=xT: e.scalar_tensor_tensor(
                    out=xT[:, ft, :], in0=po[:], scalar=Gco[:, l, i, c, ft:ft + 1], in1=xT[:, ft, :], op0=ALU.mult, op1=ALU.add),
                    reads=[po, Gco, xT], writes=[xT])
            if last:
                store_y_tm(xT, xtm, blk if kind == "std" else 2 + blk)
            else:
                store_xT(xT, blk)
            if nxt_own:
                fetch(it + 1)
        s.barrier()
        s.release(m0)

    def proj_phase(l):
        m0 = s.mark()
        WI = [s.sbuf("WI_%d" % j, [128, 8, 256], BF16) for j in range(9)]
        for j in range(9):
            s.dma("pool", WI[j][:], _ap(w_in)[l, :, j * 256:(j + 1) * 256].rearrange("(kt p) f -> p kt f", p=128),
                  writes=[WI[j]])
        xTs = [s.sbuf("xT%d" % j, [128, 8, 512], F32) for j in range(2)]
        hbs = [s.sbuf("hb%d" % j, [128, 8, 512], BF16) for j in range(2)]
        tmp2 = [s.sbuf("tmp%d" % j, [128, 512], F32) for j in range(2)]
        rstd = s.sbuf("rstd", [128, 512], F32)
        gq = s.sbuf("gq", [128, 2], F32)
        for h in range(2):
            s.dma("sp", gq[h * 64:(h + 1) * 64, 0:1], _ap(qn_g)[l, :].rearrange("(p o) -> p o", o=1), writes=[gq])
            s.dma("sp", gq[h * 64:(h + 1) * 64, 1:2], _ap(kn_g)[l, :].rearrange("(p o) -> p o", o=1), writes=[gq])
        gk_b = s.sbuf("gk_b", [128, 64], F32)
        s.dma("sp", gk_b[:], _ap(kn_g)[l:l + 1, :].to_broadcast([128, 64]), writes=[gk_b])
        NSLOT = 3

        class Slot:
            pass

        slots = []
        for i in range(NSLOT):
            sl = Slot()
            sl.sq = s.sbuf("sq%d" % i, [128, 512], BF16)
            sl.f32 = s.sbuf("f32_%d" % i, [128, 512], F32)
            sl.r = s.sbuf("r%d" % i, [128, 512], F32)
            sl.sb = s.sbuf("sb%d" % i, [128, 512], BF16)
            sl.ga = s.sbuf("ga%d" % i, [128, 512], F32)
            sl.gb = s.sbuf("gb%d" % i, [128, 512], F32)
            sl.gc = s.sbuf("gc%d" % i, [128, 512], F32)
            sl.v32 = s.sbuf("v32_%d" % i, [128, 512], F32)
            sl.k32 = s.sbuf("k32_%d" % i, [128, 512], F32)
            sl.gv = s.sbuf("gv%d" % i, [128, 256], F32)
            sl.vA = s.sbuf("vA%d" % i, [128, 256], BF16)
            sl.vB = s.sbuf("vB%d" % i, [128, 256], BF16)
            s.op("dve", lambda e, sl=sl: e.memset(sl.vA[:], 0.0), writes=[sl.vA])
            s.op("dve", lambda e, sl=sl: e.memset(sl.vB[:], 0.0), writes=[sl.vB])
            sl.stat = s.sbuf("stat%d" % i, [128, 6], F32)
            sl.mv = s.sbuf("mv%d" % i, [128, 2], F32)
            sl.rs1 = s.sbuf("rs1_%d" % i, [128, 1], F32)
            sl.s8 = s.sbuf("s8_%d" % i, [128, 8], F32)
            slots.append(sl)

        def run_chains(makers):
            pending = list(makers)
            active = [None] * NSLOT
            while pending or any(g is not None for g in active):
                for i in range(NSLOT):
                    if active[i] is None and pending:
                        active[i] = pending.pop(0)(slots[i])
                    if active[i] is not None:
                        try:
                            next(active[i])
                        except StopIteration:
                            active[i] = None

        def fm_tile(hb, col):
            w, o = WI[col // 256], col % 256
            pb = ps()
            mm_group(pb, pb[:], [(w[:, kt, o:o + 128], hb[:, kt, :]) for kt in range(8)], [w, hb])
            return pb

        def rsqrt_gen(dst_b, dst_ap, src_b, src_ap, eps_b, scale=1.0):
            P = dst_ap.shape[0]
            s.op("act", lambda e: e.activation(out=dst_ap, in_=src_ap, func=AF.Sqrt, scale=scale, bias=eps_b[0:P, 0:1]),
                 reads=[src_b, eps_b], writes=[dst_b])
            yield
            s.op("dve", lambda e: e.reciprocal(out=dst_ap, in_=dst_ap), reads=[dst_b], writes=[dst_b])

        def gelu_gen(sl, src_b, src_ap, dst_b, dst_ap, n):
            a, b2, c2 = sl.ga, sl.gb, sl.gc
            s.op("act", lambda e: e.activation(out=a[:, 0:n], in_=src_ap, func=AF.Square), reads=[src_b], writes=[a])
            yield
            s.op("dve", lambda e: e.tensor_scalar(out=a[:, 0:n], in0=a[:, 0:n], scalar1=0.044715, scalar2=1.0,
                                                  op0=ALU.mult, op1=ALU.add), reads=[a], writes=[a])
            s.op("dve", lambda e: e.tensor_tensor(out=b2[:, 0:n], in0=a[:, 0:n], in1=src_ap, op=ALU.mult),
                 reads=[a, src_b], writes=[b2])
            yield
            s.op("act", lambda e: e.activation(out=c2[:, 0:n], in_=b2[:, 0:n], func=AF.Sigmoid, scale=1.5957691216057308),
                 reads=[b2], writes=[c2])
            yield
            s.op("dve", lambda e: e.tensor_tensor(out=dst_ap, in0=c2[:, 0:n], in1=src_ap, op=ALU.mult),
                 reads=[c2, src_b], writes=[dst_b])

        for blk in range(NB):
            c = 0 if blk < 2 else 1
            tok = slice(blk * 512, (blk + 1) * 512)
            xT = xTs[blk % 2]
            hb = hbs[blk % 2]
            if blk == 0:
                load_xT(xT, 0)
                norm_mod(xT, hb, tmp2, rstd, l, 1, c)
            if blk + 1 < NB:
                load_xT(xTs[(blk + 1) % 2], blk + 1)

            def ch_norm(nb):
                def gen(sl):
                    xn, hn, cn = xTs[nb % 2], hbs[nb % 2], (0 if nb < 2 else 1)
                    for kt in range(8):
                        s.op("act", lambda e, kt=kt: e.activation(out=hn[:, kt, :], in_=xn[:, kt, :], func=AF.Square),
                             reads=[xn], writes=[hn])
                        if kt % 4 == 3:
                            yield
                    pb = ps()
                    mm_group(pb, pb[:], [(mean_bf[:], hn[:, kt, :]) for kt in range(8)], [mean_bf, hn])
                    yield
                    yield from rsqrt_gen(rstd, rstd[:], pb, pb[:], eps6)
                    for kt in range(8):
                        t = tmp2[kt % 2]
                        s.op("dve", lambda e, kt=kt, t=t: e.tensor_tensor(out=t[:], in0=xn[:, kt, :], in1=rstd[:], op=ALU.mult),
                             reads=[xn, rstd], writes=[t])
                        if kt % 2 == 0:
                            yield
                        s.op("act", lambda e, kt=kt, t=t: e.activation(out=hn[:, kt, :], in_=t[:], func=AF.Identity,
                                                                       scale=Aco[:, l, 1, cn, kt:kt + 1], bias=Bco[:, l, 1, cn, kt:kt + 1]),
                             reads=[t, Aco, Bco], writes=[hn])
                return gen

            def ch_xssm(a):
                def gen(sl):
                    pb = fm_tile(hb, a * 128)
                    yield
                    s.op("act", lambda e: e.copy(out=sl.sb[:], in_=pb[:]), reads=[pb], writes=[sl.sb])
                    yield
                    s.dma("sp", _ap(us_d)[a * 128:(a + 1) * 128, tok], sl.sb[:], reads=[sl.sb], writes=[us_d])
                return gen

            def ch_qk(qk, t4):
                def gen(sl):
                    pb = fm_tile(hb, 256 + qk * 512 + t4 * 128)
                    yield
                    s.op("act", lambda e: e.activation(out=sl.sq[:], in_=pb[:], func=AF.Square), reads=[pb], writes=[sl.sq])
                    s.op("act", lambda e: e.copy(out=sl.f32[:], in_=pb[:]), reads=[pb], writes=[sl.f32])
                    yield
                    pm = ps()
                    mm_group(pm, pm[:], [(bd_bf[:], sl.sq[:])], [bd_bf, sl.sq])
                    yield
                    yield from rsqrt_gen(sl.r, sl.r[:], pm, pm[:], eps6)
                    s.op("dve", lambda e: e.scalar_tensor_tensor(
                        out=sl.sb[:], in0=sl.f32[:], scalar=gq[:, qk:qk + 1], in1=sl.r[:], op0=ALU.mult, op1=ALU.mult),
                        reads=[sl.f32, gq, sl.r], writes=[sl.sb])
                    yield
                    dd = q_d if qk == 0 else k_d
                    s.dma("sp", _ap(dd)[t4 * 128:(t4 + 1) * 128, tok], sl.sb[:], reads=[sl.sb], writes=[dd])
                return gen

            def ch_ug(a):
                def gen(sl):
                    pb = fm_tile(hb, 1792 + a * 128)
                    yield
                    s.op("act", lambda e: e.copy(out=sl.f32[:], in_=pb[:]), reads=[pb], writes=[sl.f32])
                    yield
                    yield from gelu_gen(sl, sl.f32, sl.f32[:], sl.sb, sl.sb[:], 512)
                    yield
                    s.dma("sp", _ap(ug_d)[a * 128:(a + 1) * 128, tok], sl.sb[:], reads=[sl.sb], writes=[ug_d])
                return gen

            def ch_v(tt):
                def gen(sl):
                    trow = slice(blk * 512 + tt * 128, blk * 512 + (tt + 1) * 128)
                    hT = [hb[:, kt, tt * 128:(tt + 1) * 128] for kt in range(8)]
                    pv = ps()
                    for half in range(2):
                        w = WI[5 + half]
                        mm_group(pv, pv[:, half * 256:(half + 1) * 256], [(hT[kt], w[:, kt, :]) for kt in range(8)], [w, hb])
                    yield
                    s.op("act", lambda e: e.copy(out=sl.sb[:], in_=pv[:]), reads=[pv], writes=[sl.sb])
                    if blk < 2:
                        s.op("act", lambda e: e.copy(out=sl.v32[:], in_=pv[:]), reads=[pv], writes=[sl.v32])
                    yield
                    s.dma("sp", _ap(v_d)[trow, :], sl.sb[:], reads=[sl.sb], writes=[v_d])
                    if blk < 2:
                        seq = (blk * 512 + tt * 128) // 256
                        pos = (tt % 2) * 128
                        s.dma("sp", _ap(nv)[seq, l, pos:pos + 128, :], sl.v32[:], reads=[sl.v32], writes=[nv])
                return gen

            def ch_k(tt):
                def gen(sl):
                    hT = [hb[:, kt, tt * 128:(tt + 1) * 128] for kt in range(8)]
                    seq = (blk * 512 + tt * 128) // 256
                    pos = (tt % 2) * 128
                    pk = ps()
                    for half in range(2):
                        w = WI[3 + half]
                        mm_group(pk, pk[:, half * 256:(half + 1) * 256], [(hT[kt], w[:, kt, :]) for kt in range(8)], [w, hb])
                    yield
                    s.op("act", lambda e: e.activation(out=sl.ga[:], in_=pk[:], func=AF.Square), reads=[pk], writes=[sl.ga])
                    s.op("act", lambda e: e.copy(out=sl.f32[:], in_=pk[:]), reads=[pk], writes=[sl.f32])
                    yield
                    s.op("dve", lambda e: e.tensor_reduce(out=sl.s8[:], in_=sl.ga[:].rearrange("p (h d) -> p h d", d=64),
                                                          axis=AX.X, op=ALU.add), reads=[sl.ga], writes=[sl.s8])
                    yield
                    yield from rsqrt_gen(sl.s8, sl.s8[:], sl.s8, sl.s8[:], eps6, scale=1.0 / 64.0)
                    for h in range(8):
                        s.op("dve", lambda e, h=h: e.scalar_tensor_tensor(
                            out=sl.k32[:, h * 64:(h + 1) * 64], in0=sl.f32[:, h * 64:(h + 1) * 64], scalar=sl.s8[:, h:h + 1],
                            in1=gk_b[:], op0=ALU.mult, op1=ALU.mult), reads=[sl.f32, sl.s8, gk_b], writes=[sl.k32])
                    yield
                    s.dma("sp", _ap(nk)[seq, l, pos:pos + 128, :], sl.k32[:], reads=[sl.k32], writes=[nk])
                return gen

            def ch_vg(tt):
                def gen(sl):
                    trow = slice(blk * 512 + tt * 128, blk * 512 + (tt + 1) * 128)
                    hT = [hb[:, kt, tt * 128:(tt + 1) * 128] for kt in range(8)]
                    pg = ps()
                    mm_group(pg, pg[:, 0:256], [(hT[kt], WI[8][:, kt, :]) for kt in range(8)], [WI[8], hb])
                    yield
                    s.op("act", lambda e: e.copy(out=sl.f32[:, 0:256], in_=pg[:, 0:256]), reads=[pg], writes=[sl.f32])
                    yield
                    yield from gelu_gen(sl, sl.f32, sl.f32[:, 0:256], sl.gv, sl.gv[:, 0:256], 256)
                    s.op("dve", lambda e: e.bn_stats(out=sl.stat[:], in_=sl.gv[:, 0:256]), reads=[sl.gv], writes=[sl.stat])
                    s.op("dve", lambda e: e.bn_aggr(out=sl.mv[:], in_=sl.stat[:]), reads=[sl.stat], writes=[sl.mv])
                    yield
                    yield from rsqrt_gen(sl.rs1, sl.rs1[:], sl.mv, sl.mv[:, 1:2], eps5)
                    for (dst, off) in ((sl.vA, 0), (sl.vB, 64)):
                        for a in range(2):
                            cs_ = slice(a * 128 + off, a * 128 + off + 64)
                            s.op("dve", lambda e, dst=dst, cs_=cs_: e.tensor_scalar(
                                out=dst[:, cs_], in0=sl.gv[:, cs_], scalar1=sl.mv[:, 0:1], scalar2=sl.rs1[:, 0:1],
                                op0=ALU.subtract, op1=ALU.mult), reads=[sl.gv, sl.mv, sl.rs1], writes=[dst])
                    yield
                    s.dma("sp", _ap(va_d)[trow, :], sl.vA[:], reads=[sl.vA], writes=[va_d])
                    s.dma("sp", _ap(vb_d)[trow, :], sl.vB[:], reads=[sl.vB], writes=[vb_d])
                return gen

            chains = [ch_xssm(0), ch_xssm(1)]
            chains += [ch_qk(qk, t4) for qk in range(2) for t4 in range(4)]
            chains += [ch_ug(0), ch_ug(1)]
            for tt in range(4):
                chains.append(ch_v(tt))
                if blk < 2:
                    chains.append(ch_k(tt))
                chains.append(ch_vg(tt))
            if blk + 1 < NB:
                chains.insert(10, ch_norm(blk + 1))
            run_chains(chains)
        s.barrier()
        s.release(m0)

    def outproj_phase(l, WO=None):
        m0 = s.mark()
        if WO is None:
            WO = s.sbuf("WO", [128, 8, 1024], BF16)
            s.dma("pool", WO[:], _ap(w_out)[l].rearrange("(kt p) f -> p kt f", p=128), writes=[WO])
        xTs = [s.sbuf("xT%d" % j, [128, 8, 512], F32) for j in range(2)]
        yms = [s.sbuf("ym%d" % j, [128, 8, 512], BF16) for j in range(2)]
        def fetch(blk):
            load_xT(xTs[blk % 2], blk)
            s.dma("sp", yms[blk % 2][:], _ap(ymix_d)[:, blk * 512:(blk + 1) * 512].rearrange("(kt p) n -> p kt n", p=128),
                  reads=[ymix_d], writes=[yms[blk % 2]])

        fetch(0)
        for blk in range(NB):
            c = 0 if blk < 2 else 1
            xT, ym = xTs[blk % 2], yms[blk % 2]
            if blk + 1 < NB:
                fetch(blk + 1)
            for ft in range(8):
                po = ps()
                mm_group(po, po[:], [(WO[:, kt, ft * 128:(ft + 1) * 128], ym[:, kt, :]) for kt in range(8)], [WO, ym])
                s.op("dve", lambda e, po=po, ft=ft, xT=xT: e.scalar_tensor_tensor(
                    out=xT[:, ft, :], in0=po[:], scalar=Gco[:, l, 1, c, ft:ft + 1], in1=xT[:, ft, :], op0=ALU.mult, op1=ALU.add),
                    reads=[po, Gco, xT], writes=[xT])
            store_xT(xT, blk)
        s.barrier()
        s.release(m0)

    def ssm_part(l, pump=None):
        m0 = s.mark()
        uS = s.sbuf("uS", [128, NT], BF16)
        Y = s.sbuf("Y", [128, 2, NT], F32)
        dco = s.sbuf("dco", [128, 2], F32)
        s.dma("sp", dco[:], _ap(ssm_d)[l, :].rearrange("(a p) -> p a", p=128), writes=[dco])
        iota = s.sbuf("iota", [128, 1024], F32)
        s.dma("sp", iota[:], _ap(c_iota), writes=[iota])
        P8 = lambda n: s.sbuf(n, [128, 8], F32)
        lr, li, ldt, dtv, ar, ai, rho, ang, sinv, cosv, lbr, lbi = [P8("p8_%d" % i) for i in range(12)]
        nr, den, kr, ki, t8a, t8b, thr, nf8, cN, sN = [P8("q8_%d" % i) for i in range(10)]
        ii8 = s.sbuf("ii8", [128, 8], mybir.dt.int32)
        BT = s.sbuf("BT", [128, 8, 2, 128], BF16)
        CT = s.sbuf("CT", [128, 8, 2, 128], BF16)
        s0 = s.sbuf("s0", [128, 8, 2], F32)
        init = s.sbuf("init", [128, 8, 2], F32)
        FIN = [s.sbuf("FIN%d" % i, [128, 8, 2], F32) for i in range(4)]
        zl = s.sbuf("zl", [128, 2], F32)
        tq = s.sbuf("tq", [128, 2], F32)

        def dv(fn, reads, writes, e="dve"):
            s.op(e, fn, reads=reads, writes=writes)

        C1 = 6.28125
        C2 = TWO_PI - C1
        PI_S = 3.1415925
        hpi = s.sbuf("hpi", [128, 1], F32)
        dv(lambda e: e.memset(hpi[:], 0.5 * math.pi), [], [hpi])

        def sincos(ang_b, ang_ap, r_b, r_ap, sin_b, sin_ap, cos_b, cos_ap, ii_b, nf_b, n):
            dv(lambda e: e.tensor_scalar(out=ii_b[:, 0:n], in0=ang_ap, scalar1=1.0 / TWO_PI, scalar2=None, op0=ALU.mult), [ang_b], [ii_b])
            dv(lambda e: e.tensor_copy(out=nf_b[:, 0:n], in_=ii_b[:, 0:n]), [ii_b], [nf_b])
            dv(lambda e: e.scalar_tensor_tensor(out=r_ap, in0=nf_b[:, 0:n], scalar=-C1, in1=ang_ap, op0=ALU.mult, op1=ALU.add),
               [nf_b, ang_b], [r_b])
            dv(lambda e: e.scalar_tensor_tensor(out=r_ap, in0=nf_b[:, 0:n], scalar=-C2, in1=r_ap, op0=ALU.mult, op1=ALU.add),
               [nf_b, r_b], [r_b])
            dv(lambda e: e.tensor_scalar(out=r_ap, in0=r_ap, scalar1=-PI_S, scalar2=None, op0=ALU.max), [r_b], [r_b])
            dv(lambda e: e.tensor_scalar(out=r_ap, in0=r_ap, scalar1=PI_S, scalar2=None, op0=ALU.min), [r_b], [r_b])
            s.op("act", lambda e: e.activation(out=sin_ap, in_=r_ap, func=AF.Sin), reads=[r_b], writes=[sin_b])
            s.op("act", lambda e: e.activation(out=nf_b[:, 0:n], in_=r_ap, func=AF.Abs), reads=[r_b], writes=[nf_b])
            s.op("act", lambda e: e.activation(out=cos_ap, in_=nf_b[:, 0:n], func=AF.Sin, scale=-1.0, bias=hpi[:, 0:1]),
                 reads=[nf_b, hpi], writes=[cos_b])

        for d in range(2):
            s.dma("sp", lr[:], _ap(lam_re)[l, d, :].rearrange("(j q) -> q j", q=128), writes=[lr])
            s.dma("sp", li[:], _ap(lam_im)[l, d, :].rearrange("(j q) -> q j", q=128), writes=[li])
            for h in range(2):
                s.dma("sp", ldt[h * 64:(h + 1) * 64, :],
                      _ap(log_dt)[l, d, :].rearrange("(j h) -> h j", h=2)[h:h + 1, :].to_broadcast([64, 8]), writes=[ldt])
            s.op("act", lambda e: e.activation(out=dtv[:], in_=ldt[:], func=AF.Exp), reads=[ldt], writes=[dtv])
            dv(lambda e: e.tensor_tensor(out=ar[:], in0=lr[:], in1=dtv[:], op=ALU.mult), [lr, dtv], [ar])
            dv(lambda e: e.tensor_tensor(out=ai[:], in0=li[:], in1=dtv[:], op=ALU.mult), [li, dtv], [ai])
            s.op("act", lambda e: e.activation(out=rho[:], in_=ar[:], func=AF.Exp), reads=[ar], writes=[rho])
            sincos(ai, ai[:], thr, thr[:], sinv, sinv[:], cosv, cosv[:], ii8, nf8, 8)
            dv(lambda e: e.tensor_tensor(out=lbr[:], in0=rho[:], in1=cosv[:], op=ALU.mult), [rho, cosv], [lbr])
            dv(lambda e: e.tensor_tensor(out=lbi[:], in0=rho[:], in1=sinv[:], op=ALU.mult), [rho, sinv], [lbi])
            dv(lambda e: e.tensor_scalar(out=nr[:], in0=lbr[:], scalar1=-1.0, scalar2=None, op0=ALU.add), [lbr], [nr])
            dv(lambda e: e.tensor_tensor(out=den[:], in0=lr[:], in1=lr[:], op=ALU.mult), [lr], [den])
            dv(lambda e: e.tensor_tensor(out=t8a[:], in0=li[:], in1=li[:], op=ALU.mult), [li], [t8a])
            dv(lambda e: e.tensor_tensor(out=den[:], in0=den[:], in1=t8a[:], op=ALU.add), [den, t8a], [den])
            dv(lambda e: e.reciprocal(out=den[:], in_=den[:]), [den], [den])
            dv(lambda e: e.tensor_tensor(out=t8a[:], in0=nr[:], in1=lr[:], op=ALU.mult), [nr, lr], [t8a])
            dv(lambda e: e.tensor_tensor(out=t8b[:], in0=lbi[:], in1=li[:], op=ALU.mult), [lbi, li], [t8b])
            dv(lambda e: e.tensor_tensor(out=t8a[:], in0=t8a[:], in1=t8b[:], op=ALU.add), [t8a, t8b], [t8a])
            dv(lambda e: e.tensor_tensor(out=kr[:], in0=t8a[:], in1=den[:], op=ALU.mult), [t8a, den], [kr])
            dv(lambda e: e.tensor_tensor(out=t8a[:], in0=lbi[:], in1=lr[:], op=ALU.mult), [lbi, lr], [t8a])
            dv(lambda e: e.tensor_tensor(out=t8b[:], in0=nr[:], in1=li[:], op=ALU.mult), [nr, li], [t8b])
            dv(lambda e: e.tensor_tensor(out=t8a[:], in0=t8a[:], in1=t8b[:], op=ALU.subtract), [t8a, t8b], [t8a])
            dv(lambda e: e.tensor_tensor(out=ki[:], in0=t8a[:], in1=den[:], op=ALU.mult), [t8a, den], [ki])
            m1 = s.mark()
            Bn = [s.sbuf("Bn%d" % i, [128, 8, 16], F32) for i in range(2)]
            Zp = [s.sbuf("Zp%d" % i, [128, 8, 128], F32) for i in range(2)]
            tz = s.sbuf("tz", [128, 16], F32)
            INc = [s.sbuf("INc%d" % i, [128, 8, 128], F32) for i in range(2)]
            s.dma("sp", Bn[0][:], _ap(b_re)[l, d].rearrange("(j q) c -> q j c", q=128), writes=[Bn[0]])
            s.dma("sp", Bn[1][:], _ap(b_im)[l, d].rearrange("(j q) c -> q j c", q=128), writes=[Bn[1]])
            for ri in range(2):
                dv(lambda e, ri=ri: e.memset(Zp[ri][:], 0.0), [], [Zp[ri]])
            for j in range(8):
                for h in range(2):
                    pr = slice(h * 64, (h + 1) * 64)
                    cs_ = slice(32 * (j % 4) + 16 * h, 32 * (j % 4) + 16 * h + 16)
                    dv(lambda e, j=j, pr=pr: e.tensor_scalar(out=tz[pr, :], in0=Bn[1][pr, j, :], scalar1=ki[pr, j:j + 1],
                                                             scalar2=None, op0=ALU.mult), [Bn[1], ki], [tz])
                    dv(lambda e, j=j, pr=pr, cs_=cs_: e.scalar_tensor_tensor(
                        out=Zp[0][pr, j, cs_], in0=Bn[0][pr, j, :], scalar=kr[pr, j:j + 1], in1=tz[pr, :],
                        op0=ALU.mult, op1=ALU.subtract), [Bn[0], kr, tz], [Zp[0]])
                    dv(lambda e, j=j, pr=pr: e.tensor_scalar(out=tz[pr, :], in0=Bn[0][pr, j, :], scalar1=ki[pr, j:j + 1],
                                                             scalar2=None, op0=ALU.mult), [Bn[0], ki], [tz])
                    dv(lambda e, j=j, pr=pr, cs_=cs_: e.scalar_tensor_tensor(
                        out=Zp[1][pr, j, cs_], in0=Bn[1][pr, j, :], scalar=kr[pr, j:j + 1], in1=tz[pr, :],
                        op0=ALU.mult, op1=ALU.add), [Bn[1], kr, tz], [Zp[1]])
            for ri, cd in enumerate((c_re, c_im)):
                dv(lambda e, ri=ri: e.memset(INc[ri][:], 0.0), [], [INc[ri]])
                for j in range(8):
                    for h in range(2):
                        g = 2 * j + h
                        r0 = 32 * (j % 4) + 16 * h
                        s.dma("sp", INc[ri][r0:r0 + 16, j, h * 64:(h + 1) * 64], _ap(cd)[l, d, g * 16:(g + 1) * 16, :],
                              reads=[INc[ri]], writes=[INc[ri]])
            for j in range(8):
                for ri in range(2):
                    pb = ps()
                    s.group("pe", [lambda e, pb=pb, j=j, ri=ri: e.transpose(pb[:, 0:128], Zp[ri][:, j, :], ident[:])],
                            reads=[Zp[ri], ident], writes=[pb])
                    s.op("act", lambda e, pb=pb, j=j, ri=ri: e.copy(out=BT[:, j, ri, :], in_=pb[:, 0:128]), reads=[pb], writes=[BT])
                    pc = ps()
                    s.group("pe", [lambda e, pc=pc, j=j, ri=ri: e.transpose(pc[:, 0:128], INc[ri][:, j, :], ident[:])],
                            reads=[INc[ri], ident], writes=[pc])
                    s.op("act", lambda e, pc=pc, j=j, ri=ri: e.activation(out=CT[:, j, ri, :], in_=pc[:, 0:128], func=AF.Copy,
                                                                        scale=(1.0 if ri == 0 else -1.0)),
                         reads=[pc], writes=[CT])
            s.dma("sp", s0[:], _ap(sst)[l, d].rearrange("(j q) r -> q j r", q=128), writes=[s0])
            dv(lambda e: e.tensor_tensor(out=t8a[:], in0=cosv[:], in1=s0[:, :, 0], op=ALU.mult), [cosv, s0], [t8a])
            dv(lambda e: e.tensor_tensor(out=t8b[:], in0=sinv[:], in1=s0[:, :, 1], op=ALU.mult), [sinv, s0], [t8b])
            dv(lambda e: e.tensor_tensor(out=init[:, :, 0], in0=t8a[:], in1=t8b[:], op=ALU.subtract), [t8a, t8b], [init])
            dv(lambda e: e.tensor_tensor(out=t8a[:], in0=sinv[:], in1=s0[:, :, 0], op=ALU.mult), [sinv, s0], [t8a])
            dv(lambda e: e.tensor_tensor(out=t8b[:], in0=cosv[:], in1=s0[:, :, 1], op=ALU.mult), [cosv, s0], [t8b])
            dv(lambda e: e.tensor_tensor(out=init[:, :, 1], in0=t8a[:], in1=t8b[:], op=ALU.add), [t8a, t8b], [init])
            s.barrier()
            s.release(m1)
            m2 = s.mark()
            rhoT = s.sbuf("rhoT", [128, 4, 1024], F32)
            tabc = s.sbuf("tabc", [128, 1024], F32)
            tabs = s.sbuf("tabs", [128, 1024], F32)
            ccol = s.sbuf("ccol", [128, 4, 2], F32)
            scol = s.sbuf("scol", [128, 4, 2], F32)
            angt = s.sbuf("angt", [128, 1024], F32)
            ang2 = s.sbuf("ang2", [128, 1024], F32)
            iiT = s.sbuf("iiT", [128, 1024], mybir.dt.int32)
            nfT = s.sbuf("nfT", [128, 1024], F32)
            tcb = s.sbuf("tcb", [128, 4, 1024], BF16)
            tsb = s.sbuf("tsb", [128, 4, 1024], BF16)
            tD_ = [[s.sbuf("tD%d_%d" % (k, i), [128, 1024], BF16) for i in range(2)] for k in range(1)]
            tP_ = [[s.sbuf("tP%d_%d" % (k, i), [128, 1024], BF16) for i in range(2)] for k in range(1)]
            wb_ = [[s.sbuf("wb%d_%d" % (k, i), [128, 1024], BF16) for i in range(2)] for k in range(1)]
            wri_ = [[s.sbuf("wri%d_%d" % (k, i), [128, 1024], F32) for i in range(2)] for k in range(1)]
            zb_ = [[s.sbuf("zb%d_%d" % (k, i), [128, 1024], BF16) for i in range(2)] for k in range(1)]
            bsb_ = [[s.sbuf("bsb%d_%d" % (k, i), [128, 1024], BF16) for i in range(2)] for k in range(1)]
            ucnt = [0]
            fq = s.sbuf("fq", [128, 4], F32)
            Sb = [[s.sbuf("Sb%d_%d" % (j4, ri), [128, 1024], BF16) for ri in range(2)] for j4 in range(4)]
            if os.environ.get("MK_VERBOSE"):
                print("ssm free sbuf bytes/partition:", nc.sbuf_bytes_remaining)
            units = [("p", sq, sq * 256, 256) for sq in range(4)]
            for kk in range(4):
                knat = kk if d == 0 else 3 - kk
                units.append(("s", kk, NP_TOK + knat * 1024, 1024))
            for a in range(2):
                s.dma("sp", uS[:], _ap(us_d)[a * 128:(a + 1) * 128, :], reads=[us_d], writes=[uS])
                for j4 in range(4):
                    j = a * 4 + j4
                    dv(lambda e, j=j: e.tensor_scalar(out=angt[:], in0=iota[:], scalar1=thr[:, j:j + 1], scalar2=None, op0=ALU.mult),
                       [iota, thr], [angt])
                    sincos(angt, angt[:], ang2, ang2[:], tabs, tabs[:], tabc, tabc[:], iiT, nfT, 1024)
                    s.op("act", lambda e, j=j, j4=j4: e.activation(out=rhoT[:, j4, :], in_=iota[:], func=AF.Identity, scale=0.0,
                                                                 bias=rho[:, j:j + 1]), reads=[iota, rho], writes=[rhoT])
                    s.op("act", lambda e, j4=j4: e.copy(out=tcb[:, j4, :], in_=tabc[:]), reads=[tabc], writes=[tcb])
                    s.op("act", lambda e, j4=j4: e.copy(out=tsb[:, j4, :], in_=tabs[:]), reads=[tabs], writes=[tsb])
                    for ci, col in enumerate((255, 1023)):
                        dv(lambda e, j4=j4, ci=ci, col=col: e.tensor_copy(out=ccol[:, j4, ci:ci + 1], in_=tabc[:, col:col + 1]), [tabc], [ccol])
                        dv(lambda e, j4=j4, ci=ci, col=col: e.tensor_copy(out=scol[:, j4, ci:ci + 1], in_=tabs[:, col:col + 1]), [tabs], [scol])
                    dv(lambda e, j=j, j4=j4: e.tensor_scalar(out=tq[:, 0:1], in0=scol[:, j4, 1:2], scalar1=sinv[:, j:j + 1], scalar2=None,
                                                             op0=ALU.mult), [scol, sinv], [tq])
                    dv(lambda e, j=j, j4=j4: e.scalar_tensor_tensor(out=cN[:, j:j + 1], in0=ccol[:, j4, 1:2], scalar=cosv[:, j:j + 1],
                                                                    in1=tq[:, 0:1], op0=ALU.mult, op1=ALU.subtract), [ccol, cosv, tq], [cN])
                    dv(lambda e, j=j, j4=j4: e.tensor_scalar(out=tq[:, 1:2], in0=ccol[:, j4, 1:2], scalar1=sinv[:, j:j + 1], scalar2=None,
                                                             op0=ALU.mult), [ccol, sinv], [tq])
                    dv(lambda e, j=j, j4=j4: e.scalar_tensor_tensor(out=sN[:, j:j + 1], in0=scol[:, j4, 1:2], scalar=cosv[:, j:j + 1],
                                                                    in1=tq[:, 1:2], op0=ALU.mult, op1=ALU.add), [scol, cosv, tq], [sN])
                for (kind, idx, tok0, n) in units:
                    for j4 in range(4):
                        j = a * 4 + j4
                        c_f, sn_f = tcb[:, j4, 0:n], tsb[:, j4, 0:n]
                        kb = 0
                        ucnt[0] += 1
                        tD, tP, wb, wri, zb, bsb = tD_[kb], tP_[kb], wb_[kb], wri_[kb], zb_[kb], bsb_[kb]
                        for p0 in range(0, n, 512):
                            pn = min(512, n - p0)
                            sl = slice(p0, p0 + pn) if d == 0 else slice(n - p0 - pn, n - p0)
                            for ri in range(2):
                                pb = ps()
                                mm_group(pb, pb[:, 0:pn], [(BT[:, j, ri, :], uS[:, tok0 + p0:tok0 + p0 + pn])], [BT, uS])
                                src_ = pb[:, 0:pn] if d == 0 else pb[:, 0:pn][:, ::-1]
                                s.op("act", lambda e, ri=ri, sl=sl, src_=src_: e.copy(out=bsb[ri][:, sl], in_=src_), reads=[pb], writes=[bsb[ri]])
                        br_, bi_ = bsb[0][:, 0:n], bsb[1][:, 0:n]
                        dv(lambda e: e.tensor_tensor(out=tD[0][:, 0:n], in0=c_f, in1=br_, op=ALU.mult), [tcb, bsb[0]], [tD[0]])
                        dv(lambda e: e.tensor_tensor(out=tD[1][:, 0:n], in0=sn_f, in1=bi_, op=ALU.mult), [tsb, bsb[1]], [tD[1]])
                        dv(lambda e: e.tensor_tensor(out=wb[0][:, 0:n], in0=tD[0][:, 0:n], in1=tD[1][:, 0:n], op=ALU.add),
                           [tD[0], tD[1]], [wb[0]])
                        dv(lambda e: e.tensor_tensor(out=tP[0][:, 0:n], in0=c_f, in1=bi_, op=ALU.mult), [tcb, bsb[1]], [tP[0]])
                        dv(lambda e: e.tensor_tensor(out=tP[1][:, 0:n], in0=sn_f, in1=br_, op=ALU.mult), [tsb, bsb[0]], [tP[1]])
                        dv(lambda e: e.tensor_tensor(out=wb[1][:, 0:n], in0=tP[0][:, 0:n], in1=tP[1][:, 0:n], op=ALU.subtract),
                           [tP[0], tP[1]], [wb[1]])
                        for ri in range(2):
                            if kind == "p":
                                ini = 0.0
                                rds = [rhoT, wb[ri]]
                            else:
                                ini = init[:, j, ri:ri + 1]
                                rds = [rhoT, wb[ri], init]
                            s.op("dve", lambda e, ri=ri, ini=ini, j4=j4: e.tensor_tensor_scan(
                                out=wri[ri][:, 0:n], data0=rhoT[:, j4, 0:n], data1=wb[ri][:, 0:n], initial=ini,
                                op0=ALU.mult, op1=ALU.add), reads=rds, writes=[wri[ri]])
                            s.op("act", lambda e, ri=ri: e.copy(out=zb[ri][:, 0:n], in_=wri[ri][:, 0:n]), reads=[wri[ri]], writes=[zb[ri]])
                        if kind == "s" and idx < 3:
                            dv(lambda e, j=j: e.tensor_scalar(out=tq[:, 0:1], in0=wri[1][:, n - 1:n], scalar1=sN[:, j:j + 1], scalar2=None,
                                                              op0=ALU.mult), [wri[1], sN], [tq])
                            dv(lambda e, j=j: e.tensor_scalar(out=tq[:, 1:2], in0=wri[0][:, n - 1:n], scalar1=sN[:, j:j + 1], scalar2=None,
                                                              op0=ALU.mult), [wri[0], sN], [tq])
                            dv(lambda e, j=j: e.scalar_tensor_tensor(out=init[:, j, 0:1], in0=wri[0][:, n - 1:n], scalar=cN[:, j:j + 1],
                                                                     in1=tq[:, 0:1], op0=ALU.mult, op1=ALU.subtract), [wri[0], cN, tq], [init])
                            dv(lambda e, j=j: e.scalar_tensor_tensor(out=init[:, j, 1:2], in0=wri[1][:, n - 1:n], scalar=cN[:, j:j + 1],
                                                                     in1=tq[:, 1:2], op0=ALU.mult, op1=ALU.add), [wri[1], cN, tq], [init])
                        if kind == "p":
                            fb = FIN[idx]
                            cl, sl_ = ccol[:, j4, 0:1], scol[:, j4, 0:1]
                            zrl, zil = wri[0][:, n - 1:n], wri[1][:, n - 1:n]
                            dv(lambda e: e.tensor_tensor(out=fq[:, 0:1], in0=sl_, in1=zil, op=ALU.mult), [scol, wri[1]], [fq])
                            dv(lambda e: e.tensor_tensor(out=fq[:, 1:2], in0=cl, in1=zrl, op=ALU.mult), [ccol, wri[0]], [fq])
                            dv(lambda e, fb=fb, j=j: e.tensor_tensor(out=fb[:, j, 0:1], in0=fq[:, 1:2], in1=fq[:, 0:1], op=ALU.subtract), [fq], [fb])
                            dv(lambda e: e.tensor_tensor(out=fq[:, 2:3], in0=sl_, in1=zrl, op=ALU.mult), [scol, wri[0]], [fq])
                            dv(lambda e: e.tensor_tensor(out=fq[:, 3:4], in0=cl, in1=zil, op=ALU.mult), [ccol, wri[1]], [fq])
                            dv(lambda e, fb=fb, j=j: e.tensor_tensor(out=fb[:, j, 1:2], in0=fq[:, 2:3], in1=fq[:, 3:4], op=ALU.add), [fq], [fb])
                        zr_, zi_ = zb[0][:, 0:n], zb[1][:, 0:n]
                        sr_o = Sb[j4][0][:, 0:n] if d == 0 else Sb[j4][0][:, 0:n][:, ::-1]
                        si_o = Sb[j4][1][:, 0:n] if d == 0 else Sb[j4][1][:, 0:n][:, ::-1]
                        dv(lambda e: e.tensor_tensor(out=tD[0][:, 0:n], in0=c_f, in1=zr_, op=ALU.mult), [tcb, zb[0]], [tD[0]])
                        dv(lambda e: e.tensor_tensor(out=tD[1][:, 0:n], in0=sn_f, in1=zi_, op=ALU.mult), [tsb, zb[1]], [tD[1]])
                        dv(lambda e, sr_o=sr_o: e.tensor_tensor(out=sr_o, in0=tD[0][:, 0:n], in1=tD[1][:, 0:n], op=ALU.subtract),
                           [tD[0], tD[1]], [Sb[j4][0]])
                        dv(lambda e: e.tensor_tensor(out=tP[0][:, 0:n], in0=sn_f, in1=zr_, op=ALU.mult), [tsb, zb[0]], [tP[0]])
                        dv(lambda e: e.tensor_tensor(out=tP[1][:, 0:n], in0=c_f, in1=zi_, op=ALU.mult), [tcb, zb[1]], [tP[1]])
                        dv(lambda e, si_o=si_o: e.tensor_tensor(out=si_o, in0=tP[0][:, 0:n], in1=tP[1][:, 0:n], op=ALU.add),
                           [tP[0], tP[1]], [Sb[j4][1]])
                        if pump is not None:
                            pump()
                    for p0 in range(0, n, 512):
                        pn = min(512, n - p0)
                        pb = ps()
                        pairs = []
                        rd = [CT]
                        for j4 in range(4):
                            for ri in range(2):
                                pairs.append((CT[:, a * 4 + j4, ri, :], Sb[j4][ri][:, p0:p0 + pn]))
                                rd.append(Sb[j4][ri])
                        mm_group(pb, pb[:, 0:pn], pairs, rd)
                        ysl = Y[:, a, tok0 + p0:tok0 + p0 + pn]
                        if d == 0:
                            dv(lambda e, pb=pb, ysl=ysl, pn=pn, a=a, p0=p0, tok0=tok0: e.scalar_tensor_tensor(
                                out=ysl, in0=uS[:, tok0 + p0:tok0 + p0 + pn], scalar=dco[:, a:a + 1], in1=pb[:, 0:pn],
                                op0=ALU.mult, op1=ALU.add), [uS, dco, pb], [Y])
                        else:
                            dv(lambda e, pb=pb, ysl=ysl, pn=pn: e.tensor_tensor(out=ysl, in0=ysl, in1=pb[:, 0:pn], op=ALU.add),
                               [Y, pb], [Y])
            for sq in range(4):
                s.dma("sp", _ap(nst)[sq, l, d].rearrange("(j q) r -> q j r", q=128), FIN[sq][:], reads=[FIN[sq]], writes=[nst])
            s.barrier()
            s.release(m2)
        m3 = s.mark()
        tD = [s.sbuf("tDg%d" % i, [128, 512], F32) for i in range(2)]
        tP = [s.sbuf("tPg%d" % i, [128, 512], F32) for i in range(2)]
        Wg = s.sbuf("Wg", [128, 2, 256], BF16)
        s.dma("pool", Wg[:], _ap(glu_w)[l].rearrange("(a p) f -> p a f", p=128), writes=[Wg])
        gb = s.sbuf("gb", [128, 2], F32)
        s.dma("sp", gb[:], _ap(glu_b)[l, :].rearrange("(a p) -> p a", p=128), writes=[gb])
        gel = [s.sbuf("gel%d" % a, [128, 512], F32) for a in range(2)]
        gelb = [s.sbuf("gelb%d" % a, [128, 512], BF16) for a in range(2)]
        sig = s.sbuf("sig", [128, 512], F32)
        yo = [s.sbuf("yo%d" % a, [128, 512], BF16) for a in range(2)]
        for blk in range(NB):
            tok = slice(blk * 512, (blk + 1) * 512)
            for a in range(2):
                ysl = Y[:, a, tok]
                s.op("act", lambda e, ysl=ysl: e.activation(out=tD[0][:, 0:512], in_=ysl, func=AF.Square), reads=[Y], writes=[tD[0]])
                dv(lambda e: e.tensor_scalar(out=tD[0][:, 0:512], in0=tD[0][:, 0:512], scalar1=0.044715, scalar2=1.0,
                                             op0=ALU.mult, op1=ALU.add), [tD[0]], [tD[0]])
                dv(lambda e, ysl=ysl: e.tensor_tensor(out=tD[1][:, 0:512], in0=tD[0][:, 0:512], in1=ysl, op=ALU.mult), [tD[0], Y], [tD[1]])
                s.op("act", lambda e: e.activation(out=tP[0][:, 0:512], in_=tD[1][:, 0:512], func=AF.Sigmoid, scale=1.5957691216057308),
                     reads=[tD[1]], writes=[tP[0]])
                dv(lambda e, a=a, ysl=ysl: e.tensor_tensor(out=gel[a][:], in0=tP[0][:, 0:512], in1=ysl, op=ALU.mult), [tP[0], Y], [gel[a]])
                s.op("act", lambda e, a=a: e.copy(out=gelb[a][:], in_=gel[a][:]), reads=[gel[a]], writes=[gelb[a]])
            for a in range(2):
                pb = ps()
                mm_group(pb, pb[:], [(Wg[:, a2, a * 128:(a + 1) * 128], gelb[a2][:]) for a2 in range(2)], [Wg, gelb[0], gelb[1]])
                s.op("act", lambda e, pb=pb, a=a: e.activation(out=sig[:], in_=pb[:], func=AF.Sigmoid, bias=gb[:, a:a + 1]),
                     reads=[pb, gb], writes=[sig])
                dv(lambda e, a=a: e.tensor_tensor(out=yo[a][:], in0=sig[:], in1=gel[a][:], op=ALU.mult), [sig, gel[a]], [yo[a]])
                s.dma("sp", _ap(ymix_d)[a * 128:(a + 1) * 128, tok], yo[a][:], reads=[yo[a]], writes=[ymix_d])
        s.barrier()
        s.release(m0)

    def dense_gen(l):
        qPs = [s.sbuf("qP%d" % i, [128, 4, 256], BF16) for i in range(2)]
        kPs = [s.sbuf("kP%d" % i, [128, 4, 256], BF16) for i in range(2)]
        vPs = [s.sbuf("vP%d" % i, [128, 2, 512], BF16) for i in range(2)]
        Eb = [s.sbuf("Eb%d" % i, [128, 2, 256], BF16) for i in range(2)]
        rdn = [s.sbuf("rdn%d" % i, [128, 256], F32) for i in range(2)]
        yat = [s.sbuf("yat%d" % i, [128, 256], BF16) for i in range(2)]

        def fetch(sq):
            qP, kP, vP = qPs[sq % 2], kPs[sq % 2], vPs[sq % 2]
            tk = slice(sq * 256, (sq + 1) * 256)
            s.dma("sp", qP[:], _ap(q_d)[:, tk].rearrange("(t p) n -> p t n", p=128), reads=[q_d], writes=[qP])
            s.dma("sp", kP[:], _ap(k_d)[:, tk].rearrange("(t p) n -> p t n", p=128), reads=[k_d], writes=[kP])
            s.dma("sp", vP[:], _ap(v_d)[tk, :].rearrange("(t p) f -> p t f", p=128), reads=[v_d], writes=[vP])

        fetch(0)
        n_it = 0
        yield
        for sq in range(4):
            qP, kP, vP = qPs[sq % 2], kPs[sq % 2], vPs[sq % 2]
            if sq + 1 < 4:
                fetch(sq + 1)
            for hp in range(4):
                ya = yat[(sq * 4 + hp) % 2]
                for hh in range(2):
                    pr = slice(hh * 64, (hh + 1) * 64)
                    E = Eb[n_it % 2]
                    rd_ = rdn[n_it % 2]
                    n_it += 1
                    for kt2 in range(2):
                        pb = ps()
                        mm_group(pb, pb[:, 0:256], [(kP[pr, hp, kt2 * 128:(kt2 + 1) * 128], qP[pr, hp, :])], [kP, qP])
                        s.op("act", lambda e, E=E, pb=pb, kt2=kt2: e.activation(out=E[:, kt2, :], in_=pb[:, 0:256], func=AF.Exp, scale=0.125),
                             reads=[pb], writes=[E])
                    pn_ = ps()
                    mm_group(pn_, pn_[:, 0:256], [(vP[:, kt2, hp * 128:(hp + 1) * 128], E[:, kt2, :]) for kt2 in range(2)], [vP, E])
                    pd_ = ps()
                    mm_group(pd_, pd_[:, 0:256], [(ones_bf[:], E[:, kt2, :]) for kt2 in range(2)], [ones_bf, E])
                    s.op("dve", lambda e, rd_=rd_, pd_=pd_, pr=pr: e.reciprocal(out=rd_[pr, :], in_=pd_[pr, 0:256]), reads=[pd_], writes=[rd_])
                    s.op("dve", lambda e, ya=ya, pn_=pn_, rd_=rd_, pr=pr: e.tensor_tensor(out=ya[pr, :], in0=pn_[pr, 0:256], in1=rd_[pr, :],
                                                                                        op=ALU.mult), reads=[pn_, rd_], writes=[ya])
                s.dma("sp", _ap(ymix_d)[256 + hp * 128:256 + (hp + 1) * 128, sq * 256:(sq + 1) * 256], ya[:], reads=[ya], writes=[ymix_d])
                yield

    def attn_part(l):
        m0 = s.mark()
        BTt = s.sbuf("BTt", [128, 8, 15, 64], BF16)
        identb = s.sbuf("identb", [128, 128], BF16)
        s.op("dve", lambda e: e.tensor_copy(out=identb[:], in_=ident[:]), reads=[ident], writes=[identb])
        m1 = s.mark()
        oh = s.sbuf("oh", [31, 64, 128], F32)
        s.dma("sp", oh[:], _ap(c_oh).rearrange("d (k q) -> d k q", q=128), writes=[oh])
        msk = s.sbuf("msk", [128, 64], F32)
        s.dma("sp", msk[:], _ap(c_mask), writes=[msk])
        rpT = s.sbuf("rpT", [31, 128], F32)
        s.op("dve", lambda e: e.memset(rpT[:], 0.0), writes=[rpT])
        s.dma("sp", rpT[:, 0:120], _ap(rpb)[l].rearrange("x d -> d x"), reads=[rpT], writes=[rpT])
        BTf = BTt[:].rearrange("p h d k -> p (h d) k")
        for k4 in range(16):
            pb = ps()
            for ki in range(4):
                kk = k4 * 4 + ki
                mm_group(pb, pb[:, ki * 128:ki * 128 + 128], [(oh[:, kk, :], rpT[:])], [oh, rpT])
            for ki in range(4):
                kk = k4 * 4 + ki
                s.op("dve", lambda e, pb=pb, ki=ki, kk=kk: e.tensor_scalar(out=BTf[:, :, kk], in0=pb[:, ki * 128:ki * 128 + 120],
                                                                           scalar1=msk[:, kk:kk + 1], scalar2=8.0, op0=ALU.add, op1=ALU.mult),
                     reads=[pb, msk], writes=[BTt])
        s.barrier()
        s.release(m1)
        if int(os.environ.get("MK_ATT_STOP", "9")) <= 2:
            return
        qS = s.sbuf("qS", [128, 4, NS_TOK], BF16)
        kS = s.sbuf("kS", [128, 4, NS_TOK], BF16)
        for t4 in range(4):
            s.dma("sp", qS[:, t4, :], _ap(q_d)[t4 * 128:(t4 + 1) * 128, NP_TOK:NT], reads=[q_d], writes=[qS])
            s.dma("sp", kS[:, t4, :], _ap(k_d)[t4 * 128:(t4 + 1) * 128, NP_TOK:NT], reads=[k_d], writes=[kS])
        vS = s.sbuf("vS", [128, 64, 512], BF16)
        s.dma("sp", vS[0:64, :, :], _ap(v_d)[NP_TOK:NT, :].rearrange("(r c) f -> c r f", c=64), reads=[v_d], writes=[vS])
        s.dma("sp", vS[64:128, 0:63, :], _ap(v_d)[NP_TOK + 64:NT, :].rearrange("(r c) f -> c r f", c=64), reads=[v_d], writes=[vS])
        s.op("dve", lambda e: e.memset(vS[64:128, 63:64, :], 0.0), reads=[vS], writes=[vS])
        ck32 = s.sbuf("ck32", [128, 2, 512], F32)
        s.dma("sp", ck32[:], _ap(ck)[l].rearrange("(t p) f -> p t f", p=128), writes=[ck32])
        kC = s.sbuf("kC", [128, 4, 256], BF16)
        for hp in range(4):
            pb = ps()
            s.group("pe", [lambda e, pb=pb, t=t, hp=hp: e.transpose(pb[:, t * 128:(t + 1) * 128], ck32[:, t, hp * 128:(hp + 1) * 128], ident[:])
                           for t in range(2)], reads=[ck32, ident], writes=[pb])
            s.op("act", lambda e, pb=pb, hp=hp: e.copy(out=kC[:, hp, :], in_=pb[:, 0:256]), reads=[pb], writes=[kC])
        vC = s.sbuf("vC", [128, 2, 512], BF16)
        s.dma("pool", vC[:], _ap(cv)[l].rearrange("(t p) f -> p t f", p=128), writes=[vC])
        NBUF = 3
        EC = [s.sbuf("EC%d" % i, [128, 6, 2, 64], BF16) for i in range(NBUF)]
        rdw = [s.sbuf("rdw%d" % i, [128, 128], F32) for i in range(NBUF)]
        YN = [s.sbuf("YN%d" % i, [128, 4, 512], BF16) for i in range(2)]
        it = 0
        for r in range(int(os.environ.get("MK_NA_ROWS", "64"))):
            rs = min(max(r - 4, 0), 56)
            dr0 = rs - r + 7
            yn = YN[(r // 8) % 2]
            for hp in range(4):
                k2 = it % NBUF
                it += 1
                E_ = EC[k2]
                for hh in range(2):
                    pr = slice(hh * 64, (hh + 1) * 64)
                    h = 2 * hp + hh
                    qrow = qS[pr, hp, r * 64:(r + 1) * 64]
                    idb = identb[pr, hh * 64:(hh + 1) * 64]
                    pw = ps()
                    fw = []
                    for m in range(4):
                        fw.append(lambda e, pw=pw, m=m, pr=pr, qrow=qrow: e.matmul(
                            pw[:, m * 64:(m + 1) * 64], kS[pr, hp, (rs + 2 * m) * 64:(rs + 2 * m + 2) * 64], qrow, start=True, stop=False))
                        fw.append(lambda e, pw=pw, m=m, pr=pr, h=h, idb=idb: e.matmul(
                            pw[:, m * 64:(m + 1) * 64], BTt[pr, h, dr0 + 2 * m:dr0 + 2 * m + 2, :].rearrange("p d k -> p (d k)"), idb,
                            start=False, stop=True))
                    for t in range(2):
                        fw.append(lambda e, pw=pw, t=t, pr=pr, qrow=qrow: e.matmul(
                            pw[:, 256 + t * 64:256 + (t + 1) * 64], kC[pr, hp, t * 128:(t + 1) * 128], qrow, start=True, stop=True))
                    s.group("pe", fw, reads=[kS, kC, qS, BTt, identb], writes=[pw])
                    s.op("act", lambda e, E_=E_, pw=pw, hh=hh: e.activation(
                        out=E_[:, :, hh, :], in_=pw[:, 0:384].rearrange("p (m q) -> p m q", m=6), func=AF.Exp, scale=0.125),
                        reads=[pw], writes=[E_])
                pnd = ps()
                rhs6 = [E_[:, m, :, :].rearrange("p h q -> p (h q)") for m in range(6)]
                lv = [vS[:, rs + 2 * m, hp * 128:(hp + 1) * 128] for m in range(4)] + [vC[:, t, hp * 128:(hp + 1) * 128] for t in range(2)]
                mm_group(pnd, pnd[:, 0:128], [(lv[m], rhs6[m]) for m in range(6)], [vS, vC, E_])
                mm_group(pnd, pnd[:, 128:256], [(ones_bf[:], rhs6[m]) for m in range(6)], [ones_bf, E_])
                rd_ = rdw[k2]
                s.op("dve", lambda e, rd_=rd_, pnd=pnd: e.reciprocal(out=rd_[:], in_=pnd[:, 128:256]), reads=[pnd], writes=[rd_])
                for hh in range(2):
                    pr = slice(hh * 64, (hh + 1) * 64)
                    s.op("dve", lambda e, yn=yn, pnd=pnd, rd_=rd_, pr=pr, hp=hp, r=r, hh=hh: e.tensor_tensor(
                        out=yn[pr, hp, (r % 8) * 64:(r % 8 + 1) * 64], in0=pnd[pr, hh * 64:(hh + 1) * 64], in1=rd_[pr, hh * 64:(hh + 1) * 64],
                        op=ALU.mult), reads=[pnd, rd_], writes=[yn])
            if r % 8 == 7:
                tok0 = NP_TOK + (r // 8) * 512
                for hp in range(4):
                    s.dma("sp", _ap(ymix_d)[256 + hp * 128:256 + (hp + 1) * 128, tok0:tok0 + 512], yn[:, hp, :], reads=[yn], writes=[ymix_d])
        s.barrier()
        s.release(m0)

    def gate_gen(l):
        ws32 = s.sbuf("ws32", [128, 4, 128], F32)
        s.dma("sp", ws32[:], _ap(gm_ws)[l].rearrange("g i j -> i g j"), writes=[ws32])
        wsT = s.sbuf("wsT", [128, 4, 128], BF16)
        pb = ps()
        s.group("pe", [lambda e, g=g: e.transpose(pb[:, g * 128:(g + 1) * 128], ws32[:, g, :], ident[:]) for g in range(4)],
                reads=[ws32, ident], writes=[pb])
        s.op("act", lambda e: e.copy(out=wsT[:].rearrange("p g i -> p (g i)"), in_=pb[:]), reads=[pb], writes=[wsT])
        BS = s.sbuf("BS", [128, 2, 128], F32)
        for g in range(4):
            s.dma("sp", BS[(g % 2) * 64:(g % 2 + 1) * 64, g // 2, :], _ap(gm_bs)[l, g:g + 1, :].to_broadcast([64, 128]), writes=[BS])
        ug = [s.sbuf("ug%d" % i, [128, 2, 512], BF16) for i in range(2)]
        vAl = [s.sbuf("vAl%d" % i, [128, 4, 256], BF16) for i in range(2)]
        vBl = [s.sbuf("vBl%d" % i, [128, 4, 256], BF16) for i in range(2)]
        tg = [s.sbuf("tg%d" % i, [128, 128], F32) for i in range(2)]
        yg = [s.sbuf("yg%d" % i, [128, 2, 512], BF16) for i in range(2)]
        it = 0
        def fetch(blk):
            k2 = blk % 2
            tok = slice(blk * 512, (blk + 1) * 512)
            for a in range(2):
                s.dma("sp", ug[k2][:, a, :], _ap(ug_d)[a * 128:(a + 1) * 128, tok], reads=[ug_d], writes=[ug[k2]])
            s.dma("sp", vAl[k2][:], _ap(va_d)[tok, :].rearrange("(t p) f -> p t f", p=128), reads=[va_d], writes=[vAl[k2]])
            s.dma("sp", vBl[k2][:], _ap(vb_d)[tok, :].rearrange("(t p) f -> p t f", p=128), reads=[vb_d], writes=[vBl[k2]])

        fetch(0)
        yield
        for blk in range(NB):
            k2 = blk % 2
            tok = slice(blk * 512, (blk + 1) * 512)
            if blk + 1 < NB:
                fetch(blk + 1)
            for t in range(4):
                if t:
                    yield
                for a in range(2):
                    pq = ps()
                    mm_group(pq, pq[:, 0:128], [(vAl[k2][:, t, a * 128:(a + 1) * 128], wsT[:, 2 * a, :]),
                                                (vBl[k2][:, t, a * 128:(a + 1) * 128], wsT[:, 2 * a + 1, :])], [vAl[k2], vBl[k2], wsT])
                    tt_ = tg[it % 2]
                    it += 1
                    s.op("dve", lambda e, tt_=tt_, pq=pq, a=a: e.tensor_tensor(out=tt_[:], in0=pq[:, 0:128], in1=BS[:, a, :], op=ALU.add),
                         reads=[pq, BS], writes=[tt_])
                    s.op("dve", lambda e, tt_=tt_, a=a, t=t, k2=k2: e.tensor_tensor(
                        out=yg[k2][:, a, t * 128:(t + 1) * 128], in0=tt_[:], in1=ug[k2][:, a, t * 128:(t + 1) * 128], op=ALU.mult),
                        reads=[tt_, ug[k2]], writes=[yg[k2]])
            for a in range(2):
                s.dma("sp", _ap(ymix_d)[768 + a * 128:768 + (a + 1) * 128, tok], yg[k2][:, a, :], reads=[yg[k2]], writes=[ymix_d])
            yield

    PH = os.environ.get("MK_PHASES", "all")
    if PH != "all":
        for name in PH.split(","):
            {"adaln": setup_adaln, "ffn": lambda: ffn_phase(0, 0, f1_in, f1_out, True, False), "proj": lambda: proj_phase(0),
             "ssm": lambda: ssm_part(0), "attn": lambda: attn_part(0),
             "outproj": lambda: outproj_phase(0)}[name]()
        s.barrier()
        s.release(0)
        ctx.__exit__(None, None, None)
        return nc
    mW = s.mark()
    W = alloc_ffn_w()
    setup_adaln(after_dma=lambda: issue_ffn_w(W, 0, f1_in, f1_out))
    for l in range(DEPTH):
        if l == 0:
            ffn_phase(l, 0, f1_in, f1_out, first=True, last=False, W=W)
            s.release(mW)
        else:
            ffn_phase(l, 0, f1_in, f1_out, first=False, last=False)
        proj_phase(l)
        mB = s.mark()
        gens = [dense_gen(l), gate_gen(l)]
        for g in gens:
            next(g)

        def pump():
            for g in gens:
                try:
                    next(g)
                    return
                except StopIteration:
                    continue

        ssm_part(l, pump)
        for g in gens:
            for _ in g:
                pass
        s.barrier()
        s.release(mB)
        attn_part(l)
        mW = s.mark()
        W = alloc_ffn_w()
        mWO = s.mark()
        WO = s.sbuf("WO", [128, 8, 1024], BF16)
        s.dma("pool", WO[:], _ap(w_out)[l].rearrange("(kt p) f -> p kt f", p=128), writes=[WO])
        issue_ffn_w(W, l, f2_in, f2_out)
        outproj_phase(l, WO=WO)
        s.release(mWO)
        ffn_phase(l, 2, f2_in, f2_out, first=False, last=(l == DEPTH - 1), W=W)
        s.release(mW)
    outs = [yp, ys, nk, nv, nst]
    s.barrier()
    s.release(0)
    ctx.__exit__(None, None, None)
    return nc


_NC_CACHE = {}


def _consts():
    ident = np.eye(128, dtype=np.float32)
    kc = np.arange(64)[:, None]
    qc = np.arange(64)[None, :]
    dc = kc - qc + 15
    oh = np.zeros((31, 64, 128), np.float32)
    for q in range(64):
        for k in range(64):
            if 0 <= dc[k, q] <= 30:
                oh[dc[k, q], k, q] = 1.0
                oh[dc[k, q], k, 64 + q] = 1.0
    cs = np.clip(np.arange(64) - 8, 0, 48)
    win = (kc >= cs[None, :]) & (kc < cs[None, :] + 16)
    mask = np.where(win, 0.0, NEG).astype(np.float32)
    iota = np.tile(np.arange(1024, dtype=np.float32)[None, :], (128, 1))
    return {"c_ident": ident, "c_oh": oh.reshape(31, 8192), "c_mask": np.ascontiguousarray(np.concatenate([mask.T, mask.T], axis=0)), "c_iota": iota}


def kernel(**inputs):
    debug = bool(int(os.environ.get("MK_DEBUG", "0")))
    key = ("nc", debug)
    if key not in _NC_CACHE:
        _NC_CACHE[key] = build_program(debug=debug)
    nc = _NC_CACHE[key]
    f = lambda a: np.ascontiguousarray(np.asarray(a, dtype=np.float32))
    x_prompt, x_sample = f(inputs["x_prompt"]), f(inputs["x_sample"])
    c, c_ctx = f(inputs["c"]), f(inputs["c_ctx"])
    cache_k, cache_v, state_ssm = f(inputs["cache_k"]), f(inputs["cache_v"]), f(inputs["state_ssm"])
    shared = {}
    for name in ("w_ada", "b_ada", "norm_ffn1", "norm_mix", "norm_ffn2", "ffn1_w_in", "ffn1_w_out", "ffn2_w_in",
                 "ffn2_w_out", "w_in", "w_out", "ssm_d", "ssm_glu_w", "ssm_glu_b", "na_q_norm", "na_k_norm",
                 "gm_ws", "gm_bs"):
        shared[name] = f(inputs[name])
    shared["ssm_lambda_re"] = f(inputs["ssm_lambda_re"]).reshape(DEPTH, 2, 1024)
    shared["ssm_lambda_im"] = f(inputs["ssm_lambda_im"]).reshape(DEPTH, 2, 1024)
    shared["ssm_log_dt"] = f(inputs["ssm_log_dt"])
    shared["ssm_b_re"] = f(inputs["ssm_b_re"]).reshape(DEPTH, 2, 1024, 16)
    shared["ssm_b_im"] = f(inputs["ssm_b_im"]).reshape(DEPTH, 2, 1024, 16)
    shared["ssm_c_re"] = f(inputs["ssm_c_re"]).reshape(DEPTH, 2, 256, 64)
    shared["ssm_c_im"] = f(inputs["ssm_c_im"]).reshape(DEPTH, 2, 256, 64)
    shared["na_rpb"] = f(inputs["na_rpb"]).reshape(DEPTH, 120, 31)
    shared.update(_consts())
    in_maps = []
    for core in range(8):
        b = core // 2
        m = dict(shared)
        m["xp"] = x_prompt[4 * core:4 * core + 4].reshape(NP_TOK, D)
        m["xs"] = x_sample[b]
        m["cond"] = np.stack([c_ctx, c[b]], axis=0)
        m["ck"] = cache_k[b].reshape(DEPTH, 256, 512)
        m["cv"] = cache_v[b].reshape(DEPTH, 256, 512)
        m["sst"] = state_ssm[b].reshape(DEPTH, 2, 1024, 2)
        hs = np.zeros((128, 2), np.float32)
        hs[:, core % 2] = 1.0
        m["c_hsel"] = hs
        in_maps.append(m)
    res = run_bass_kernel_spmd(nc, in_maps, core_ids=list(range(8)))
    R = res.results
    if debug:
        kernel.last_results = R
    y_prompt = np.concatenate([R[i]["yp"].reshape(4, 256, D) for i in range(8)], axis=0)
    y_sample = np.stack([np.concatenate([R[2 * b]["ys"], R[2 * b + 1]["ys"]], axis=0) for b in range(4)], axis=0)
    new_k = np.concatenate([R[i]["nk"].reshape(4, DEPTH, 256, 8, 64) for i in range(8)], axis=0)
    new_v = np.concatenate([R[i]["nv"].reshape(4, DEPTH, 256, 8, 64) for i in range(8)], axis=0)
    new_s = np.concatenate([R[i]["nst"].reshape(4, DEPTH, 2, 16, 64, 2) for i in range(8)], axis=0)
    return (y_prompt.astype(np.float32), y_sample.astype(np.float32), new_k.astype(np.float32),
            new_v.astype(np.float32), new_s.astype(np.float32))
```

```python
import math
import os
import numpy as np
import concourse.bass as bass
import concourse.mybir as mybir
from concourse.bass_utils import run_bass_kernel_spmd

F32 = mybir.dt.float32
BF16 = mybir.dt.bfloat16
AF = mybir.ActivationFunctionType
ALU = mybir.AluOpType
AX = mybir.AxisListType

D = 1024
DFF = 2816
DEPTH = 2
NP_TOK = 1024
NS_TOK = 4096
NT = NP_TOK + NS_TOK
NB = NT // 512
INC = 2304
TWO_PI = 2.0 * math.pi
NEG = -30000.0


class Buf:
    __slots__ = ("t", "w", "r", "name")

    def __init__(self, t, name=""):
        self.t = t
        self.w = None
        self.r = {}
        self.name = name

    def __getitem__(self, idx):
        return self.t[idx]


class Sched:
    def __init__(self, nc, n_dma_sems=40):
        self.nc = nc
        self.stack = []
        self.eng = {"pe": nc.tensor, "act": nc.scalar, "dve": nc.vector, "pool": nc.gpsimd, "sp": nc.sync}
        self.sems = {}
        self.cnt = {}
        for k in ("pe", "act", "dve", "pool"):
            self.sems[k] = self._sem("s_" + k)
            self.cnt[k] = 0
        self.dma_keys = []
        for i in range(n_dma_sems):
            k = "d%d" % i
            self.sems[k] = self._sem("s_" + k)
            self.cnt[k] = 0
            self.dma_keys.append(k)
        self.dma_rr = 0
        self.seen = {e: {} for e in self.eng}
        self.ninst = 0

    def _sem(self, name):
        cm = self.nc.semaphore(name)
        h = cm.__enter__()
        self.stack.append(cm)
        return h

    def mark(self):
        return len(self.stack)

    def release(self, mark):
        while len(self.stack) > mark:
            self.stack.pop().__exit__(None, None, None)

    def sbuf(self, name, shape, dtype):
        self.uid = getattr(self, "uid", 0) + 1
        name = "%s_u%d" % (name, self.uid)
        cm = self.nc.sbuf_tensor(name, list(shape), dtype)
        t = cm.__enter__()
        self.stack.append(cm)
        return Buf(t, name)

    def psum(self, name, shape, dtype=F32):
        cm = self.nc.psum_tensor(name, list(shape), dtype)
        t = cm.__enter__()
        self.stack.append(cm)
        return Buf(t, name)

    def dram(self, name, shape, dtype, kind="Internal"):
        return Buf(self.nc.dram_tensor(name, list(shape), dtype, kind=kind), name)

    def _need(self, e, deps):
        seen = self.seen[e]
        todo = {}
        for (k, v) in deps:
            if seen.get(k, 0) >= v:
                continue
            if todo.get(k, 0) < v:
                todo[k] = v
        for k, v in todo.items():
            self.eng[e].wait_ge(self.sems[k], v)
            seen[k] = v
            self.ninst += 1

    @staticmethod
    def _deps(reads, writes):
        deps = []
        for b in reads:
            if b.w is not None:
                deps.append(b.w)
        for b in writes:
            if b.w is not None:
                deps.append(b.w)
            deps.extend(b.r.items())
        return deps

    def _commit(self, k, v, reads, writes):
        for b in reads:
            if b.r.get(k, 0) < v:
                b.r[k] = v
        for b in writes:
            b.w = (k, v)
            b.r = {}

    def op(self, e, fn, reads=(), writes=()):
        self._need(e, self._deps(reads, writes))
        inst = fn(self.eng[e])
        self.cnt[e] += 1
        inst.then_inc(self.sems[e], 1)
        self._commit(e, self.cnt[e], reads, writes)
        self.ninst += 1
        return inst

    def group(self, e, fns, reads=(), writes=()):
        self._need(e, self._deps(reads, writes))
        inst = None
        for fn in fns:
            inst = fn(self.eng[e])
            self.ninst += 1
        self.cnt[e] += 1
        inst.then_inc(self.sems[e], 1)
        self._commit(e, self.cnt[e], reads, writes)

    def dma(self, q, out_ap, in_ap, reads=(), writes=(), **kw):
        nsw = 8
        if q == "pool":
            self.sw_rr = (getattr(self, "sw_rr", -1) + 1) % nsw
            k = self.dma_keys[self.sw_rr]
        else:
            k = self.dma_keys[nsw + self.dma_rr]
            self.dma_rr = (self.dma_rr + 1) % (len(self.dma_keys) - nsw)
        deps = self._deps(reads, writes)
        if self.cnt[k] > 0:
            deps.append((k, self.cnt[k]))
        self._need(q, deps)
        inst = self.eng[q].dma_start(out=out_ap, in_=in_ap, **kw)
        self.cnt[k] += 16
        inst.then_inc(self.sems[k], 16)
        self._commit(k, self.cnt[k], reads, writes)
        self.ninst += 1
        return inst

    def barrier(self):
        allk = [(k, v) for k, v in self.cnt.items() if v > 0]
        for e in self.eng:
            self._need(e, allk)


def _ap(b):
    return b.t.ap()


def build_program(debug=False):
    nc = bass.Bass("TRN2", target_bir_lowering=False)
    s = Sched(nc)
    ctx = nc.allow_non_contiguous_dma(reason="small strided parameter loads")
    ctx.__enter__()

    def din(name, shape):
        return s.dram(name, shape, F32, kind="ExternalInput")

    def dout(name, shape):
        return s.dram(name, shape, F32, kind="ExternalOutput")

    xp = din("xp", [NP_TOK, D])
    xs = din("xs", [NS_TOK, D])
    cond = din("cond", [2, D])
    ck = din("ck", [DEPTH, 256, 512])
    cv = din("cv", [DEPTH, 256, 512])
    sst = din("sst", [DEPTH, 2, 1024, 2])
    w_ada = din("w_ada", [DEPTH, D, 9 * D])
    b_ada = din("b_ada", [DEPTH, 9 * D])
    norms = [din("norm_ffn1", [DEPTH, D]), din("norm_mix", [DEPTH, D]), din("norm_ffn2", [DEPTH, D])]
    f1_in = din("ffn1_w_in", [DEPTH, D, 2 * DFF])
    f1_out = din("ffn1_w_out", [DEPTH, DFF, D])
    f2_in = din("ffn2_w_in", [DEPTH, D, 2 * DFF])
    f2_out = din("ffn2_w_out", [DEPTH, DFF, D])
    w_in = din("w_in", [DEPTH, D, INC])
    w_out = din("w_out", [DEPTH, D, D])
    lam_re = din("ssm_lambda_re", [DEPTH, 2, 1024])
    lam_im = din("ssm_lambda_im", [DEPTH, 2, 1024])
    log_dt = din("ssm_log_dt", [DEPTH, 2, 16])
    b_re = din("ssm_b_re", [DEPTH, 2, 1024, 16])
    b_im = din("ssm_b_im", [DEPTH, 2, 1024, 16])
    c_re = din("ssm_c_re", [DEPTH, 2, 256, 64])
    c_im = din("ssm_c_im", [DEPTH, 2, 256, 64])
    ssm_d = din("ssm_d", [DEPTH, 256])
    glu_w = din("ssm_glu_w", [DEPTH, 256, 256])
    glu_b = din("ssm_glu_b", [DEPTH, 256])
    qn_g = din("na_q_norm", [DEPTH, 64])
    kn_g = din("na_k_norm", [DEPTH, 64])
    rpb = din("na_rpb", [DEPTH, 8 * 15, 31])
    gm_ws = din("gm_ws", [DEPTH, 4, 128, 128])
    gm_bs = din("gm_bs", [DEPTH, 4, 128])
    c_ident = din("c_ident", [128, 128])
    c_oh = din("c_oh", [31, 64 * 128])
    c_mask = din("c_mask", [128, 64])
    c_iota = din("c_iota", [128, 1024])
    c_hsel = din("c_hsel", [128, 2])
    yp = dout("yp", [NP_TOK, D])
    ys = dout("ys", [NS_TOK // 2, D])
    nk = dout("nk", [4, DEPTH, 256, 512])
    nv = dout("nv", [4, DEPTH, 256, 512])
    nst = dout("nst", [4, DEPTH, 2, 1024, 2])
    skind = "ExternalOutput" if debug else "Internal"
    xT_d = s.dram("xT_d", [D, NT], F32, kind=skind)
    ymix_d = s.dram("ymix_d", [D, NT], BF16, kind=skind)
    us_d = s.dram("us_d", [256, NT], BF16, kind=skind)
    q_d = s.dram("q_d", [512, NT], BF16, kind=skind)
    k_d = s.dram("k_d", [512, NT], BF16, kind=skind)
    v_d = s.dram("v_d", [NT, 512], BF16, kind=skind)
    ug_d = s.dram("ug_d", [256, NT], BF16, kind=skind)
    va_d = s.dram("va_d", [NT, 256], BF16, kind=skind)
    vb_d = s.dram("vb_d", [NT, 256], BF16, kind=skind)

    banks = [s.psum("bank%d" % i, [128, 512], F32) for i in range(8)]
    bank_rr = [0]

    def ps():
        b = banks[bank_rr[0]]
        bank_rr[0] = (bank_rr[0] + 1) % 8
        return b

    ident = s.sbuf("ident", [128, 128], F32)
    s.dma("sp", ident[:], _ap(c_ident), writes=[ident])
    ones_bf = s.sbuf("ones_bf", [128, 128], BF16)
    s.op("dve", lambda e: e.memset(ones_bf[:], 1.0), writes=[ones_bf])
    mean_bf = s.sbuf("mean_bf", [128, 128], BF16)
    s.op("dve", lambda e: e.memset(mean_bf[:], 1.0 / 1024.0), writes=[mean_bf])
    bd_bf = s.sbuf("bd_bf", [128, 128], BF16)
    s.op("dve", lambda e: e.memset(bd_bf[:], 0.0), writes=[bd_bf])
    s.op("dve", lambda e: e.memset(bd_bf[0:64, 0:64], 1.0 / 64.0), reads=[bd_bf], writes=[bd_bf])
    s.op("dve", lambda e: e.memset(bd_bf[64:128, 64:128], 1.0 / 64.0), reads=[bd_bf], writes=[bd_bf])
    pi_c = s.sbuf("pi_c", [128, 1], F32)
    s.op("dve", lambda e: e.memset(pi_c[:], math.pi), writes=[pi_c])
    hsel = s.sbuf("hsel", [128, 2], F32)
    s.dma("sp", hsel[:], _ap(c_hsel), writes=[hsel])
    eps6 = s.sbuf("eps6", [128, 1], F32)
    s.op("dve", lambda e: e.memset(eps6[:], 1e-6), writes=[eps6])
    eps5 = s.sbuf("eps5", [128, 1], F32)
    s.op("dve", lambda e: e.memset(eps5[:], 1e-5), writes=[eps5])

    def rsqrt(dst_b, dst_ap, src_b, src_ap, eps_b, scale=1.0):
        P = dst_ap.shape[0]
        s.op("act", lambda e: e.activation(out=dst_ap, in_=src_ap, func=AF.Sqrt, scale=scale, bias=eps_b[0:P, 0:1]),
             reads=[src_b, eps_b], writes=[dst_b])
        s.op("dve", lambda e: e.reciprocal(out=dst_ap, in_=dst_ap), reads=[dst_b], writes=[dst_b])
    Aco = s.sbuf("Aco", [128, DEPTH, 3, 2, 8], F32)
    Bco = s.sbuf("Bco", [128, DEPTH, 3, 2, 8], F32)
    Gco = s.sbuf("Gco", [128, DEPTH, 3, 2, 8], F32)

    def mm_group(out_buf, out_ap, pairs, reads):
        n = len(pairs)
        fns = []
        for i, (l, r) in enumerate(pairs):
            fns.append(lambda e, l=l, r=r, i=i: e.matmul(out_ap, l, r, start=(i == 0), stop=(i == n - 1)))
        s.group("pe", fns, reads=reads, writes=[out_buf])

    def setup_adaln(after_dma=None):
        m0 = s.mark()
        cs32 = s.sbuf("cs32", [128, 8, 2], F32)
        csb = s.sbuf("csb", [128, 8, 2], BF16)
        for c in range(2):
            s.dma("sp", cs32[:, :, c], _ap(cond)[c, :].rearrange("(kt p) -> p kt", p=128), writes=[cs32])
        s.op("act", lambda e: e.activation(out=csb[:], in_=cs32[:], func=AF.Silu), reads=[cs32], writes=[csb])
        gn = s.sbuf("gn", [128, DEPTH, 3, 8], F32)
        for i in range(3):
            for l in range(DEPTH):
                s.dma("sp", gn[:, l, i, :], _ap(norms[i])[l, :].rearrange("(kt p) -> p kt", p=128), writes=[gn])
        wa = [s.sbuf("wa%d" % i, [128, 8, 1024], BF16) for i in range(2)]
        badaT = s.sbuf("badaT", [128, 72], F32)
        modT = s.sbuf("modT", [128, 72, 2], F32)
        for l in range(DEPTH):
            s.dma("sp", badaT[:], _ap(b_ada)[l, :].rearrange("(ft p) -> p ft", p=128), writes=[badaT])
            pb = ps()
            for ch in range(9):
                w = wa[ch % 2]
                s.dma("pool", w[:], _ap(w_ada)[l, :, ch * 1024:(ch + 1) * 1024].rearrange("(kt p) f -> p kt f", p=128),
                      writes=[w])
                for f8 in range(8):
                    ft = ch * 8 + f8
                    mm_group(pb, pb[:, 2 * ft:2 * ft + 2],
                             [(w[:, kt, f8 * 128:(f8 + 1) * 128], csb[:, kt, :]) for kt in range(8)], [w, csb])
            for c in range(2):
                s.op("dve", lambda e, c=c: e.tensor_tensor(out=modT[:, :, c], in0=pb[:, c:144:2], in1=badaT[:], op=ALU.add),
                     reads=[pb, badaT], writes=[modT])
            for i in range(3):
                for c in range(2):
                    sh = modT[:, (3 * i) * 8:(3 * i) * 8 + 8, c]
                    sc = modT[:, (3 * i + 1) * 8:(3 * i + 1) * 8 + 8, c]
                    gt = modT[:, (3 * i + 2) * 8:(3 * i + 2) * 8 + 8, c]
                    s.op("dve", lambda e, sc=sc, l=l, i=i, c=c: e.scalar_tensor_tensor(
                        out=Aco[:, l, i, c, :], in0=sc, scalar=1.0, in1=gn[:, l, i, :], op0=ALU.add, op1=ALU.mult),
                        reads=[modT, gn], writes=[Aco])
                    s.op("dve", lambda e, sh=sh, l=l, i=i, c=c: e.tensor_copy(out=Bco[:, l, i, c, :], in_=sh),
                         reads=[modT], writes=[Bco])
                    s.op("dve", lambda e, gt=gt, l=l, i=i, c=c: e.tensor_scalar(
                        out=Gco[:, l, i, c, :], in0=gt, scalar1=(1.0 if i == 1 else 0.5), scalar2=None, op0=ALU.mult),
                        reads=[modT], writes=[Gco])
        if after_dma is not None:
            after_dma()
        s.barrier()
        s.release(m0)

    def load_xT(xT, blk):
        s.dma("sp", xT[:], _ap(xT_d)[:, blk * 512:(blk + 1) * 512].rearrange("(kt p) n -> p kt n", p=128),
              reads=[xT_d], writes=[xT])

    def store_xT(xT, blk):
        s.dma("sp", _ap(xT_d)[:, blk * 512:(blk + 1) * 512].rearrange("(kt p) n -> p kt n", p=128), xT[:],
              reads=[xT], writes=[xT_d])

    def load_x_tm(xT, xtm, blk):
        src = _ap(xp)[blk * 512:(blk + 1) * 512, :] if blk < 2 else _ap(xs)[(blk - 2) * 512:(blk - 1) * 512, :]
        for tt in range(4):
            s.dma("sp", xtm[:, 0, :], src[tt * 128:(tt + 1) * 128, :], writes=[xtm])
            for half in range(2):
                pb = ps()
                s.group("pe", [lambda e, k4=k4, half=half, pb=pb: e.transpose(pb[:, k4 * 128:(k4 + 1) * 128],
                                                                               xtm[:, 0, (half * 4 + k4) * 128:(half * 4 + k4 + 1) * 128], ident[:])
                               for k4 in range(4)], reads=[xtm, ident], writes=[pb])
                dst = xT[:, half * 4:(half + 1) * 4, tt * 128:(tt + 1) * 128]
                if half:
                    s.op("act", lambda e, dst=dst, pb=pb: e.copy(out=dst, in_=pb[:].rearrange("p (k n) -> p k n", k=4)),
                         reads=[pb], writes=[xT])
                else:
                    s.op("dve", lambda e, dst=dst, pb=pb: e.tensor_copy(out=dst, in_=pb[:].rearrange("p (k n) -> p k n", k=4)),
                         reads=[pb], writes=[xT])

    def store_y_tm(xT, ytm, blk):
        dst = _ap(yp)[blk * 512:(blk + 1) * 512, :] if blk < 2 else _ap(ys)[(blk - 2) * 512:(blk - 1) * 512, :]
        ydr = yp if blk < 2 else ys
        for tt in range(4):
            for half in range(2):
                pb = ps()
                s.group("pe", [lambda e, k4=k4, tt=tt, half=half, pb=pb: e.transpose(
                    pb[:, k4 * 128:(k4 + 1) * 128], xT[:, half * 4 + k4, tt * 128:(tt + 1) * 128], ident[:])
                    for k4 in range(4)], reads=[xT, ident], writes=[pb])
                if half:
                    s.op("act", lambda e, pb=pb: e.copy(out=ytm[:, 0, 512:1024], in_=pb[:]), reads=[pb], writes=[ytm])
                else:
                    s.op("dve", lambda e, pb=pb: e.tensor_copy(out=ytm[:, 0, 0:512], in_=pb[:]), reads=[pb], writes=[ytm])
            s.dma("sp", dst[tt * 128:(tt + 1) * 128, :], ytm[:, 0, :], reads=[ytm], writes=[ydr])

    def norm_sq(xT, hb):
        for kt in range(8):
            s.op("act", lambda e, kt=kt: e.activation(out=hb[:, kt, :], in_=xT[:, kt, :], func=AF.Square),
                 reads=[xT], writes=[hb])

    def norm_mod(xT, hb, tmp2, rstd, l, i, c, squares_done=False):
        if not squares_done:
            norm_sq(xT, hb)
        pb = ps()
        mm_group(pb, pb[:], [(mean_bf[:], hb[:, kt, :]) for kt in range(8)], [mean_bf, hb])
        rsqrt(rstd, rstd[:], pb, pb[:], eps6)
        for kt in range(8):
            t = tmp2[kt % 2]
            s.op("dve", lambda e, kt=kt, t=t: e.tensor_tensor(out=t[:], in0=xT[:, kt, :], in1=rstd[:], op=ALU.mult),
                 reads=[xT, rstd], writes=[t])
            s.op("act", lambda e, kt=kt, t=t: e.activation(out=hb[:, kt, :], in_=t[:], func=AF.Identity,
                                                           scale=Aco[:, l, i, c, kt:kt + 1], bias=Bco[:, l, i, c, kt:kt + 1]),
                 reads=[t, Aco, Bco], writes=[hb])

    def alloc_ffn_w():
        W1 = [s.sbuf("W1_%d" % j, [128, 8, 512], BF16) for j in range(11)]
        W2 = [s.sbuf("W2_%d" % j, [128, 2, 1024], BF16) for j in range(11)]
        return (W1, W2)

    def issue_ffn_w(W, l, win_d, wout_d):
        W1, W2 = W
        for j in (0, 5, 1, 6, 2, 7, 3, 8, 4, 9, 10):
            s.dma("pool", W1[j][:], _ap(win_d)[l, :, j * 512:(j + 1) * 512].rearrange("(kt p) f -> p kt f", p=128),
                  writes=[W1[j]])
        for j in range(11):
            s.dma("pool", W2[j][:], _ap(wout_d)[l, j * 256:(j + 1) * 256, :].rearrange("(kt p) f -> p kt f", p=128),
                  writes=[W2[j]])

    def ffn_phase(l, i, win_d, wout_d, first, last, W=None):
        m0 = s.mark()
        if W is None:
            W = alloc_ffn_w()
            issue_ffn_w(W, l, win_d, wout_d)
        W1, W2 = W
        xTs = [s.sbuf("xT%d" % j, [128, 8, 512], F32) for j in range(2)]
        hb = s.sbuf("hb", [128, 8, 512], BF16)
        actb = s.sbuf("actb", [128, 22, 512], BF16)
        tmp2 = [s.sbuf("tmp%d" % j, [128, 512], F32) for j in range(2)]
        sg2 = tmp2
        rstd = s.sbuf("rstd", [128, 512], F32)
        if os.environ.get("MK_VERBOSE"):
            print("ffn phase free sbuf bytes/partition:", nc.sbuf_bytes_remaining, "first/last", first, last)
        xtm = s.sbuf("xtm", [128, 1, 1024], F32) if (first or last) else None

        def w1cols(col):
            return W1[col // 512], col % 512

        if last:
            items = [("std", 0), ("std", 1)] + [("own", i) for i in range(4)]
        else:
            items = [("std", blk) for blk in range(NB)]

        def fetch(it):
            kind, b_ = items[it]
            xT_ = xTs[it % 2]
            if kind == "own":
                load_xT(xT_, 2 + b_)
                for kt in range(8):
                    t = tmp2[kt % 2]
                    s.dma("sp", t[:], _ap(xT_d)[kt * 128:(kt + 1) * 128, (6 + b_) * 512:(7 + b_) * 512], reads=[xT_d], writes=[t])
                    s.op("dve", lambda e, kt=kt, xT_=xT_: e.tensor_scalar(out=xT_[:, kt, :], in0=xT_[:, kt, :], scalar1=hsel[:, 0:1],
                                                                          scalar2=None, op0=ALU.mult), reads=[xT_, hsel], writes=[xT_])
                    s.op("dve", lambda e, kt=kt, xT_=xT_, t=t: e.scalar_tensor_tensor(
                        out=xT_[:, kt, :], in0=t[:], scalar=hsel[:, 1:2], in1=xT_[:, kt, :], op0=ALU.mult, op1=ALU.add),
                        reads=[t, hsel, xT_], writes=[xT_])
            elif first:
                load_x_tm(xT_, xtm, b_)
            else:
                load_xT(xT_, b_)

        def cond_of(it):
            kind, blk = items[it]
            return 0 if (kind == "std" and blk < 2) else 1

        fetch(0)
        normed = [False] * len(items)
        for it, (kind, blk) in enumerate(items):
            c = cond_of(it)
            xT = xTs[it % 2]
            nxt_own = it + 1 < len(items) and items[it + 1][0] == "own"
            prenorm = it + 1 < len(items) and not nxt_own
            if it + 1 < len(items) and not first and not nxt_own:
                fetch(it + 1)
            if not normed[it]:
                norm_mod(xT, hb, tmp2, rstd, l, i, c)
            for j in range(22):
                wg, og = w1cols(j * 128)
                wu, ou = w1cols(DFF + j * 128)
                pg = ps()
                mm_group(pg, pg[:], [(wg[:, kt, og:og + 128], hb[:, kt, :]) for kt in range(8)], [wg, hb])
                pu = ps()
                mm_group(pu, pu[:], [(wu[:, kt, ou:ou + 128], hb[:, kt, :]) for kt in range(8)], [wu, hb])
                sg = sg2[j % 2]
                s.op("act", lambda e, sg=sg, pg=pg: e.activation(out=sg[:], in_=pg[:], func=AF.Silu), reads=[pg], writes=[sg])
                s.op("dve", lambda e, sg=sg, pu=pu, j=j: e.tensor_tensor(out=actb[:, j, :], in0=sg[:], in1=pu[:], op=ALU.mult),
                     reads=[sg, pu], writes=[actb])
            if it + 1 < len(items) and first:
                fetch(it + 1)
            if prenorm:
                norm_sq(xTs[(it + 1) % 2], hb)
            for ft in range(8):
                if ft == 4 and prenorm:
                    norm_mod(xTs[(it + 1) % 2], hb, tmp2, rstd, l, i, cond_of(it + 1), squares_done=True)
                    normed[it + 1] = True
                po = ps()
                mm_group(po, po[:], [(W2[j // 2][:, j % 2, ft * 128:(ft + 1) * 128], actb[:, j, :]) for j in range(22)],
                         W2 + [actb])
                s.op("dve", lambda e, po=po, ft=ft, xT=xT: e.scalar_tensor_tensor(
                    out=xT[:, ft, :], in0=po[:], scalar=Gco[:, l, i, c, ft:ft + 1], in1=xT[:, ft, :], op0=ALU.mult, op1=ALU.add),
                    reads=[po, Gco, xT], writes=[xT])
            if last:
                store_y_tm(xT, xtm, blk if kind == "std" else 2 + blk)
            else:
                store_xT(xT, blk)
            if nxt_own:
                fetch(it + 1)
        s.barrier()
        s.release(m0)

    def proj_phase(l):
        m0 = s.mark()
        WI = [s.sbuf("WI_%d" % j, [128, 8, 256], BF16) for j in range(9)]
        for j in range(9):
            s.dma("pool", WI[j][:], _ap(w_in)[l, :, j * 256:(j + 1) * 256].rearrange("(kt p) f -> p kt f", p=128),
                  writes=[WI[j]])
        xTs = [s.sbuf("xT%d" % j, [128, 8, 512], F32) for j in range(2)]
        hbs = [s.sbuf("hb%d" % j, [128, 8, 512], BF16) for j in range(2)]
        tmp2 = [s.sbuf("tmp%d" % j, [128, 512], F32) for j in range(2)]
        rstd = s.sbuf("rstd", [128, 512], F32)
        gq = s.sbuf("gq", [128, 2], F32)
        for h in range(2):
            s.dma("sp", gq[h * 64:(h + 1) * 64, 0:1], _ap(qn_g)[l, :].rearrange("(p o) -> p o", o=1), writes=[gq])
            s.dma("sp", gq[h * 64:(h + 1) * 64, 1:2], _ap(kn_g)[l, :].rearrange("(p o) -> p o", o=1), writes=[gq])
        gk_b = s.sbuf("gk_b", [128, 64], F32)
        s.dma("sp", gk_b[:], _ap(kn_g)[l:l + 1, :].to_broadcast([128, 64]), writes=[gk_b])
        NSLOT = 3

        class Slot:
            pass

        slots = []
        for i in range(NSLOT):
            sl = Slot()
            sl.sq = s.sbuf("sq%d" % i, [128, 512], BF16)
            sl.f32 = s.sbuf("f32_%d" % i, [128, 512], F32)
            sl.r = s.sbuf("r%d" % i, [128, 512], F32)
            sl.sb = s.sbuf("sb%d" % i, [128, 512], BF16)
            sl.ga = s.sbuf("ga%d" % i, [128, 512], F32)
            sl.gb = s.sbuf("gb%d" % i, [128, 512], F32)
            sl.gc = s.sbuf("gc%d" % i, [128, 512], F32)
            sl.v32 = s.sbuf("v32_%d" % i, [128, 512], F32)
            sl.k32 = s.sbuf("k32_%d" % i, [128, 512], F32)
            sl.gv = s.sbuf("gv%d" % i, [128, 256], F32)
            sl.vA = s.sbuf("vA%d" % i, [128, 256], BF16)
            sl.vB = s.sbuf("vB%d" % i, [128, 256], BF16)
            s.op("dve", lambda e, sl=sl: e.memset(sl.vA[:], 0.0), writes=[sl.vA])
            s.op("dve", lambda e, sl=sl: e.memset(sl.vB[:], 0.0), writes=[sl.vB])
            sl.stat = s.sbuf("stat%d" % i, [128, 6], F32)
            sl.mv = s.sbuf("mv%d" % i, [128, 2], F32)
            sl.rs1 = s.sbuf("rs1_%d" % i, [128, 1], F32)
            sl.s8 = s.sbuf("s8_%d" % i, [128, 8], F32)
            slots.append(sl)

        def run_chains(makers):
            pending = list(makers)
            active = [None] * NSLOT
            while pending or any(g is not None for g in active):
                for i in range(NSLOT):
                    if active[i] is None and pending:
                        active[i] = pending.pop(0)(slots[i])
                    if active[i] is not None:
                        try:
                            next(active[i])
                        except StopIteration:
                            active[i] = None

        def fm_tile(hb, col):
            w, o = WI[col // 256], col % 256
            pb = ps()
            mm_group(pb, pb[:], [(w[:, kt, o:o + 128], hb[:, kt, :]) for kt in range(8)], [w, hb])
            return pb

        def rsqrt_gen(dst_b, dst_ap, src_b, src_ap, eps_b, scale=1.0):
            P = dst_ap.shape[0]
            s.op("act", lambda e: e.activation(out=dst_ap, in_=src_ap, func=AF.Sqrt, scale=scale, bias=eps_b[0:P, 0:1]),
                 reads=[src_b, eps_b], writes=[dst_b])
            yield
            s.op("dve", lambda e: e.reciprocal(out=dst_ap, in_=dst_ap), reads=[dst_b], writes=[dst_b])

        def gelu_gen(sl, src_b, src_ap, dst_b, dst_ap, n):
            a, b2, c2 = sl.ga, sl.gb, sl.gc
            s.op("act", lambda e: e.activation(out=a[:, 0:n], in_=src_ap, func=AF.Square), reads=[src_b], writes=[a])
            yield
            s.op("dve", lambda e: e.tensor_scalar(out=a[:, 0:n], in0=a[:, 0:n], scalar1=0.044715, scalar2=1.0,
                                                  op0=ALU.mult, op1=ALU.add), reads=[a], writes=[a])
            s.op("dve", lambda e: e.tensor_tensor(out=b2[:, 0:n], in0=a[:, 0:n], in1=src_ap, op=ALU.mult),
                 reads=[a, src_b], writes=[b2])
            yield
            s.op("act", lambda e: e.activation(out=c2[:, 0:n], in_=b2[:, 0:n], func=AF.Sigmoid, scale=1.5957691216057308),
                 reads=[b2], writes=[c2])
            yield
            s.op("dve", lambda e: e.tensor_tensor(out=dst_ap, in0=c2[:, 0:n], in1=src_ap, op=ALU.mult),
                 reads=[c2, src_b], writes=[dst_b])

        for blk in range(NB):
            c = 0 if blk < 2 else 1
            tok = slice(blk * 512, (blk + 1) * 512)
            xT = xTs[blk % 2]
            hb = hbs[blk % 2]
            if blk == 0:
                load_xT(xT, 0)
                norm_mod(xT, hb, tmp2, rstd, l, 1, c)
            if blk + 1 < NB:
                load_xT(xTs[(blk + 1) % 2], blk + 1)

            def ch_norm(nb):
                def gen(sl):
                    xn, hn, cn = xTs[nb % 2], hbs[nb % 2], (0 if nb < 2 else 1)
                    for kt in range(8):
                        s.op("act", lambda e, kt=kt: e.activation(out=hn[:, kt, :], in_=xn[:, kt, :], func=AF.Square),
                             reads=[xn], writes=[hn])
                        if kt % 4 == 3:
                            yield
                    pb = ps()
                    mm_group(pb, pb[:], [(mean_bf[:], hn[:, kt, :]) for kt in range(8)], [mean_bf, hn])
                    yield
                    yield from rsqrt_gen(rstd, rstd[:], pb, pb[:], eps6)
                    for kt in range(8):
                        t = tmp2[kt % 2]
                        s.op("dve", lambda e, kt=kt, t=t: e.tensor_tensor(out=t[:], in0=xn[:, kt, :], in1=rstd[:], op=ALU.mult),
                             reads=[xn, rstd], writes=[t])
                        if kt % 2 == 0:
                            yield
                        s.op("act", lambda e, kt=kt, t=t: e.activation(out=hn[:, kt, :], in_=t[:], func=AF.Identity,
                                                                       scale=Aco[:, l, 1, cn, kt:kt + 1], bias=Bco[:, l, 1, cn, kt:kt + 1]),
                             reads=[t, Aco, Bco], writes=[hn])
                return gen

            def ch_xssm(a):
                def gen(sl):
                    pb = fm_tile(hb, a * 128)
                    yield
                    s.op("act", lambda e: e.copy(out=sl.sb[:], in_=pb[:]), reads=[pb], writes=[sl.sb])
                    yield
                    s.dma("sp", _ap(us_d)[a * 128:(a + 1) * 128, tok], sl.sb[:], reads=[sl.sb], writes=[us_d])
                return gen

            def ch_qk(qk, t4):
                def gen(sl):
                    pb = fm_tile(hb, 256 + qk * 512 + t4 * 128)
                    yield
                    s.op("act", lambda e: e.activation(out=sl.sq[:], in_=pb[:], func=AF.Square), reads=[pb], writes=[sl.sq])
                    s.op("act", lambda e: e.copy(out=sl.f32[:], in_=pb[:]), reads=[pb], writes=[sl.f32])
                    yield
                    pm = ps()
                    mm_group(pm, pm[:], [(bd_bf[:], sl.sq[:])], [bd_bf, sl.sq])
                    yield
                    yield from rsqrt_gen(sl.r, sl.r[:], pm, pm[:], eps6)
                    s.op("dve", lambda e: e.scalar_tensor_tensor(
                        out=sl.sb[:], in0=sl.f32[:], scalar=gq[:, qk:qk + 1], in1=sl.r[:], op0=ALU.mult, op1=ALU.mult),
                        reads=[sl.f32, gq, sl.r], writes=[sl.sb])
                    yield
                    dd = q_d if qk == 0 else k_d
                    s.dma("sp", _ap(dd)[t4 * 128:(t4 + 1) * 128, tok], sl.sb[:], reads=[sl.sb], writes=[dd])
                return gen

            def ch_ug(a):
                def gen(sl):
                    pb = fm_tile(hb, 1792 + a * 128)
                    yield
                    s.op("act", lambda e: e.copy(out=sl.f32[:], in_=pb[:]), reads=[pb], writes=[sl.f32])
                    yield
                    yield from gelu_gen(sl, sl.f32, sl.f32[:], sl.sb, sl.sb[:], 512)
                    yield
                    s.dma("sp", _ap(ug_d)[a * 128:(a + 1) * 128, tok], sl.sb[:], reads=[sl.sb], writes=[ug_d])
                return gen

            def ch_v(tt):
                def gen(sl):
                    trow = slice(blk * 512 + tt * 128, blk * 512 + (tt + 1) * 128)
                    hT = [hb[:, kt, tt * 128:(tt + 1) * 128] for kt in range(8)]
                    pv = ps()
                    for half in range(2):
                        w = WI[5 + half]
                        mm_group(pv, pv[:, half * 256:(half + 1) * 256], [(hT[kt], w[:, kt, :]) for kt in range(8)], [w, hb])
                    yield
                    s.op("act", lambda e: e.copy(out=sl.sb[:], in_=pv[:]), reads=[pv], writes=[sl.sb])
                    if blk < 2:
                        s.op("act", lambda e: e.copy(out=sl.v32[:], in_=pv[:]), reads=[pv], writes=[sl.v32])
                    yield
                    s.dma("sp", _ap(v_d)[trow, :], sl.sb[:], reads=[sl.sb], writes=[v_d])
                    if blk < 2:
                        seq = (blk * 512 + tt * 128) // 256
                        pos = (tt % 2) * 128
                        s.dma("sp", _ap(nv)[seq, l, pos:pos + 128, :], sl.v32[:], reads=[sl.v32], writes=[nv])
                return gen

            def ch_k(tt):
                def gen(sl):
                    hT = [hb[:, kt, tt * 128:(tt + 1) * 128] for kt in range(8)]
                    seq = (blk * 512 + tt * 128) // 256
                    pos = (tt % 2) * 128
                    pk = ps()
                    for half in range(2):
                        w = WI[3 + half]
                        mm_group(pk, pk[:, half * 256:(half + 1) * 256], [(hT[kt], w[:, kt, :]) for kt in range(8)], [w, hb])
                    yield
                    s.op("act", lambda e: e.activation(out=sl.ga[:], in_=pk[:], func=AF.Square), reads=[pk], writes=[sl.ga])
                    s.op("act", lambda e: e.copy(out=sl.f32[:], in_=pk[:]), reads=[pk], writes=[sl.f32])
                    yield
                    s.op("dve", lambda e: e.tensor_reduce(out=sl.s8[:], in_=sl.ga[:].rearrange("p (h d) -> p h d", d=64),
                                                          axis=AX.X, op=ALU.add), reads=[sl.ga], writes=[sl.s8])
                    yield
                    yield from rsqrt_gen(sl.s8, sl.s8[:], sl.s8, sl.s8[:], eps6, scale=1.0 / 64.0)
                    for h in range(8):
                        s.op("dve", lambda e, h=h: e.scalar_tensor_tensor(
                            out=sl.k32[:, h * 64:(h + 1) * 64], in0=sl.f32[:, h * 64:(h + 1) * 64], scalar=sl.s8[:, h:h + 1],
                            in1=gk_b[:], op0=ALU.mult, op1=ALU.mult), reads=[sl.f32, sl.s8, gk_b], writes=[sl.k32])
                    yield
                    s.dma("sp", _ap(nk)[seq, l, pos:pos + 128, :], sl.k32[:], reads=[sl.k32], writes=[nk])
                return gen

            def ch_vg(tt):
                def gen(sl):
                    trow = slice(blk * 512 + tt * 128, blk * 512 + (tt + 1) * 128)
                    hT = [hb[:, kt, tt * 128:(tt + 1) * 128] for kt in range(8)]
                    pg = ps()
                    mm_group(pg, pg[:, 0:256], [(hT[kt], WI[8][:, kt, :]) for kt in range(8)], [WI[8], hb])
                    yield
                    s.op("act", lambda e: e.copy(out=sl.f32[:, 0:256], in_=pg[:, 0:256]), reads=[pg], writes=[sl.f32])
                    yield
                    yield from gelu_gen(sl, sl.f32, sl.f32[:, 0:256], sl.gv, sl.gv[:, 0:256], 256)
                    s.op("dve", lambda e: e.bn_stats(out=sl.stat[:], in_=sl.gv[:, 0:256]), reads=[sl.gv], writes=[sl.stat])
                    s.op("dve", lambda e: e.bn_aggr(out=sl.mv[:], in_=sl.stat[:]), reads=[sl.stat], writes=[sl.mv])
                    yield
                    yield from rsqrt_gen(sl.rs1, sl.rs1[:], sl.mv, sl.mv[:, 1:2], eps5)
                    for (dst, off) in ((sl.vA, 0), (sl.vB, 64)):
                        for a in range(2):
                            cs_ = slice(a * 128 + off, a * 128 + off + 64)
                            s.op("dve", lambda e, dst=dst, cs_=cs_: e.tensor_scalar(
                                out=dst[:, cs_], in0=sl.gv[:, cs_], scalar1=sl.mv[:, 0:1], scalar2=sl.rs1[:, 0:1],
                                op0=ALU.subtract, op1=ALU.mult), reads=[sl.gv, sl.mv, sl.rs1], writes=[dst])
                    yield
                    s.dma("sp", _ap(va_d)[trow, :], sl.vA[:], reads=[sl.vA], writes=[va_d])
                    s.dma("sp", _ap(vb_d)[trow, :], sl.vB[:], reads=[sl.vB], writes=[vb_d])
                return gen

            chains = [ch_xssm(0), ch_xssm(1)]
            chains += [ch_qk(qk, t4) for qk in range(2) for t4 in range(4)]
            chains += [ch_ug(0), ch_ug(1)]
            for tt in range(4):
                chains.append(ch_v(tt))
                if blk < 2:
                    chains.append(ch_k(tt))
                chains.append(ch_vg(tt))
            if blk + 1 < NB:
                chains.insert(10, ch_norm(blk + 1))
            run_chains(chains)
        s.barrier()
        s.release(m0)

    def outproj_phase(l, WO=None):
        m0 = s.mark()
        if WO is None:
            WO = s.sbuf("WO", [128, 8, 1024], BF16)
            s.dma("pool", WO[:], _ap(w_out)[l].rearrange("(kt p) f -> p kt f", p=128), writes=[WO])
        xTs = [s.sbuf("xT%d" % j, [128, 8, 512], F32) for j in range(2)]
        yms = [s.sbuf("ym%d" % j, [128, 8, 512], BF16) for j in range(2)]
        def fetch(blk):
            load_xT(xTs[blk % 2], blk)
            s.dma("sp", yms[blk % 2][:], _ap(ymix_d)[:, blk * 512:(blk + 1) * 512].rearrange("(kt p) n -> p kt n", p=128),
                  reads=[ymix_d], writes=[yms[blk % 2]])

        fetch(0)
        for blk in range(NB):
            c = 0 if blk < 2 else 1
            xT, ym = xTs[blk % 2], yms[blk % 2]
            if blk + 1 < NB:
                fetch(blk + 1)
            for ft in range(8):
                po = ps()
                mm_group(po, po[:], [(WO[:, kt, ft * 128:(ft + 1) * 128], ym[:, kt, :]) for kt in range(8)], [WO, ym])
                s.op("dve", lambda e, po=po, ft=ft, xT=xT: e.scalar_tensor_tensor(
                    out=xT[:, ft, :], in0=po[:], scalar=Gco[:, l, 1, c, ft:ft + 1], in1=xT[:, ft, :], op0=ALU.mult, op1=ALU.add),
                    reads=[po, Gco, xT], writes=[xT])
            store_xT(xT, blk)
        s.barrier()
        s.release(m0)

    def ssm_part(l, pump=None):
        m0 = s.mark()
        uS = s.sbuf("uS", [128, NT], BF16)
        Y = s.sbuf("Y", [128, 2, NT], F32)
        dco = s.sbuf("dco", [128, 2], F32)
        s.dma("sp", dco[:], _ap(ssm_d)[l, :].rearrange("(a p) -> p a", p=128), writes=[dco])
        iota = s.sbuf("iota", [128, 1024], F32)
        s.dma("sp", iota[:], _ap(c_iota), writes=[iota])
        P8 = lambda n: s.sbuf(n, [128, 8], F32)
        lr, li, ldt, dtv, ar, ai, rho, ang, sinv, cosv, lbr, lbi = [P8("p8_%d" % i) for i in range(12)]
        nr, den, kr, ki, t8a, t8b, thr, nf8, cN, sN = [P8("q8_%d" % i) for i in range(10)]
        ii8 = s.sbuf("ii8", [128, 8], mybir.dt.int32)
        BT = s.sbuf("BT", [128, 8, 2, 128], BF16)
        CT = s.sbuf("CT", [128, 8, 2, 128], BF16)
        s0 = s.sbuf("s0", [128, 8, 2], F32)
        init = s.sbuf("init", [128, 8, 2], F32)
        FIN = [s.sbuf("FIN%d" % i, [128, 8, 2], F32) for i in range(4)]
        zl = s.sbuf("zl", [128, 2], F32)
        tq = s.sbuf("tq", [128, 2], F32)

        def dv(fn, reads, writes, e="dve"):
            s.op(e, fn, reads=reads, writes=writes)

        C1 = 6.28125
        C2 = TWO_PI - C1
        PI_S = 3.1415925
        hpi = s.sbuf("hpi", [128, 1], F32)
        dv(lambda e: e.memset(hpi[:], 0.5 * math.pi), [], [hpi])

        def sincos(ang_b, ang_ap, r_b, r_ap, sin_b, sin_ap, cos_b, cos_ap, ii_b, nf_b, n):
            dv(lambda e: e.tensor_scalar(out=ii_b[:, 0:n], in0=ang_ap, scalar1=1.0 / TWO_PI, scalar2=None, op0=ALU.mult), [ang_b], [ii_b])
            dv(lambda e: e.tensor_copy(out=nf_b[:, 0:n], in_=ii_b[:, 0:n]), [ii_b], [nf_b])
            dv(lambda e: e.scalar_tensor_tensor(out=r_ap, in0=nf_b[:, 0:n], scalar=-C1, in1=ang_ap, op0=ALU.mult, op1=ALU.add),
               [nf_b, ang_b], [r_b])
            dv(lambda e: e.scalar_tensor_tensor(out=r_ap, in0=nf_b[:, 0:n], scalar=-C2, in1=r_ap, op0=ALU.mult, op1=ALU.add),
               [nf_b, r_b], [r_b])
            dv(lambda e: e.tensor_scalar(out=r_ap, in0=r_ap, scalar1=-PI_S, scalar2=None, op0=ALU.max), [r_b], [r_b])
            dv(lambda e: e.tensor_scalar(out=r_ap, in0=r_ap, scalar1=PI_S, scalar2=None, op0=ALU.min), [r_b], [r_b])
            s.op("act", lambda e: e.activation(out=sin_ap, in_=r_ap, func=AF.Sin), reads=[r_b], writes=[sin_b])
            s.op("act", lambda e: e.activation(out=nf_b[:, 0:n], in_=r_ap, func=AF.Abs), reads=[r_b], writes=[nf_b])
            s.op("act", lambda e: e.activation(out=cos_ap, in_=nf_b[:, 0:n], func=AF.Sin, scale=-1.0, bias=hpi[:, 0:1]),
                 reads=[nf_b, hpi], writes=[cos_b])

        for d in range(2):
            s.dma("sp", lr[:], _ap(lam_re)[l, d, :].rearrange("(j q) -> q j", q=128), writes=[lr])
            s.dma("sp", li[:], _ap(lam_im)[l, d, :].rearrange("(j q) -> q j", q=128), writes=[li])
            for h in range(2):
                s.dma("sp", ldt[h * 64:(h + 1) * 64, :],
                      _ap(log_dt)[l, d, :].rearrange("(j h) -> h j", h=2)[h:h + 1, :].to_broadcast([64, 8]), writes=[ldt])
            s.op("act", lambda e: e.activation(out=dtv[:], in_=ldt[:], func=AF.Exp), reads=[ldt], writes=[dtv])
            dv(lambda e: e.tensor_tensor(out=ar[:], in0=lr[:], in1=dtv[:], op=ALU.mult), [lr, dtv], [ar])
            dv(lambda e: e.tensor_tensor(out=ai[:], in0=li[:], in1=dtv[:], op=ALU.mult), [li, dtv], [ai])
            s.op("act", lambda e: e.activation(out=rho[:], in_=ar[:], func=AF.Exp), reads=[ar], writes=[rho])
            sincos(ai, ai[:], thr, thr[:], sinv, sinv[:], cosv, cosv[:], ii8, nf8, 8)
            dv(lambda e: e.tensor_tensor(out=lbr[:], in0=rho[:], in1=cosv[:], op=ALU.mult), [rho, cosv], [lbr])
            dv(lambda e: e.tensor_tensor(out=lbi[:], in0=rho[:], in1=sinv[:], op=ALU.mult), [rho, sinv], [lbi])
            dv(lambda e: e.tensor_scalar(out=nr[:], in0=lbr[:], scalar1=-1.0, scalar2=None, op0=ALU.add), [lbr], [nr])
            dv(lambda e: e.tensor_tensor(out=den[:], in0=lr[:], in1=lr[:], op=ALU.mult), [lr], [den])
            dv(lambda e: e.tensor_tensor(out=t8a[:], in0=li[:], in1=li[:], op=ALU.mult), [li], [t8a])
            dv(lambda e: e.tensor_tensor(out=den[:], in0=den[:], in1=t8a[:], op=ALU.add), [den, t8a], [den])
            dv(lambda e: e.reciprocal(out=den[:], in_=den[:]), [den], [den])
            dv(lambda e: e.tensor_tensor(out=t8a[:], in0=nr[:], in1=lr[:], op=ALU.mult), [nr, lr], [t8a])
            dv(lambda e: e.tensor_tensor(out=t8b[:], in0=lbi[:], in1=li[:], op=ALU.mult), [lbi, li], [t8b])
            dv(lambda e: e.tensor_tensor(out=t8a[:], in0=t8a[:], in1=t8b[:], op=ALU.add), [t8a, t8b], [t8a])
            dv(lambda e: e.tensor_tensor(out=kr[:], in0=t8a[:], in1=den[:], op=ALU.mult), [t8a, den], [kr])
            dv(lambda e: e.tensor_tensor(out=t8a[:], in0=lbi[:], in1=lr[:], op=ALU.mult), [lbi, lr], [t8a])
            dv(lambda e: e.tensor_tensor(out=t8b[:], in0=nr[:], in1=li[:], op=ALU.mult), [nr, li], [t8b])
            dv(lambda e: e.tensor_tensor(out=t8a[:], in0=t8a[:], in1=t8b[:], op=ALU.subtract), [t8a, t8b], [t8a])
            dv(lambda e: e.tensor_tensor(out=ki[:], in0=t8a[:], in1=den[:], op=ALU.mult), [t8a, den], [ki])
            m1 = s.mark()
            Bn = [s.sbuf("Bn%d" % i, [128, 8, 16], F32) for i in range(2)]
            Zp = [s.sbuf("Zp%d" % i, [128, 8, 128], F32) for i in range(2)]
            tz = s.sbuf("tz", [128, 16], F32)
            INc = [s.sbuf("INc%d" % i, [128, 8, 128], F32) for i in range(2)]
            s.dma("sp", Bn[0][:], _ap(b_re)[l, d].rearrange("(j q) c -> q j c", q=128), writes=[Bn[0]])
            s.dma("sp", Bn[1][:], _ap(b_im)[l, d].rearrange("(j q) c -> q j c", q=128), writes=[Bn[1]])
            for ri in range(2):
                dv(lambda e, ri=ri: e.memset(Zp[ri][:], 0.0), [], [Zp[ri]])
            for j in range(8):
                for h in range(2):
                    pr = slice(h * 64, (h + 1) * 64)
                    cs_ = slice(32 * (j % 4) + 16 * h, 32 * (j % 4) + 16 * h + 16)
                    dv(lambda e, j=j, pr=pr: e.tensor_scalar(out=tz[pr, :], in0=Bn[1][pr, j, :], scalar1=ki[pr, j:j + 1],
                                                             scalar2=None, op0=ALU.mult), [Bn[1], ki], [tz])
                    dv(lambda e, j=j, pr=pr, cs_=cs_: e.scalar_tensor_tensor(
                        out=Zp[0][pr, j, cs_], in0=Bn[0][pr, j, :], scalar=kr[pr, j:j + 1], in1=tz[pr, :],
                        op0=ALU.mult, op1=ALU.subtract), [Bn[0], kr, tz], [Zp[0]])
                    dv(lambda e, j=j, pr=pr: e.tensor_scalar(out=tz[pr, :], in0=Bn[0][pr, j, :], scalar1=ki[pr, j:j + 1],
                                                             scalar2=None, op0=ALU.mult), [Bn[0], ki], [tz])
                    dv(lambda e, j=j, pr=pr, cs_=cs_: e.scalar_tensor_tensor(
                        out=Zp[1][pr, j, cs_], in0=Bn[1][pr, j, :], scalar=kr[pr, j:j + 1], in1=tz[pr, :],
                        op0=ALU.mult, op1=ALU.add), [Bn[1], kr, tz], [Zp[1]])
            for ri, cd in enumerate((c_re, c_im)):
                dv(lambda e, ri=ri: e.memset(INc[ri][:], 0.0), [], [INc[ri]])
                for j in range(8):
                    for h in range(2):
                        g = 2 * j + h
                        r0 = 32 * (j % 4) + 16 * h
                        s.dma("sp", INc[ri][r0:r0 + 16, j, h * 64:(h + 1) * 64], _ap(cd)[l, d, g * 16:(g + 1) * 16, :],
                              reads=[INc[ri]], writes=[INc[ri]])
            for j in range(8):
                for ri in range(2):
                    pb = ps()
                    s.group("pe", [lambda e, pb=pb, j=j, ri=ri: e.transpose(pb[:, 0:128], Zp[ri][:, j, :], ident[:])],
                            reads=[Zp[ri], ident], writes=[pb])
                    s.op("act", lambda e, pb=pb, j=j, ri=ri: e.copy(out=BT[:, j, ri, :], in_=pb[:, 0:128]), reads=[pb], writes=[BT])
                    pc = ps()
                    s.group("pe", [lambda e, pc=pc, j=j, ri=ri: e.transpose(pc[:, 0:128], INc[ri][:, j, :], ident[:])],
                            reads=[INc[ri], ident], writes=[pc])
                    s.op("act", lambda e, pc=pc, j=j, ri=ri: e.activation(out=CT[:, j, ri, :], in_=pc[:, 0:128], func=AF.Copy,
                                                                        scale=(1.0 if ri == 0 else -1.0)),
                         reads=[pc], writes=[CT])
            s.dma("sp", s0[:], _ap(sst)[l, d].rearrange("(j q) r -> q j r", q=128), writes=[s0])
            dv(lambda e: e.tensor_tensor(out=t8a[:], in0=cosv[:], in1=s0[:, :, 0], op=ALU.mult), [cosv, s0], [t8a])
            dv(lambda e: e.tensor_tensor(out=t8b[:], in0=sinv[:], in1=s0[:, :, 1], op=ALU.mult), [sinv, s0], [t8b])
            dv(lambda e: e.tensor_tensor(out=init[:, :, 0], in0=t8a[:], in1=t8b[:], op=ALU.subtract), [t8a, t8b], [init])
            dv(lambda e: e.tensor_tensor(out=t8a[:], in0=sinv[:], in1=s0[:, :, 0], op=ALU.mult), [sinv, s0], [t8a])
            dv(lambda e: e.tensor_tensor(out=t8b[:], in0=cosv[:], in1=s0[:, :, 1], op=ALU.mult), [cosv, s0], [t8b])
            dv(lambda e: e.tensor_tensor(out=init[:, :, 1], in0=t8a[:], in1=t8b[:], op=ALU.add), [t8a, t8b], [init])
            s.barrier()
            s.release(m1)
            m2 = s.mark()
            rhoT = s.sbuf("rhoT", [128, 4, 1024], F32)
            tabc = s.sbuf("tabc", [128, 1024], F32)
            tabs = s.sbuf("tabs", [128, 1024], F32)
            ccol = s.sbuf("ccol", [128, 4, 2], F32)
            scol = s.sbuf("scol", [128, 4, 2], F32)
            angt = s.sbuf("angt", [128, 1024], F32)
            ang2 = s.sbuf("ang2", [128, 1024], F32)
            iiT = s.sbuf("iiT", [128, 1024], mybir.dt.int32)
            nfT = s.sbuf("nfT", [128, 1024], F32)
            tcb = s.sbuf("tcb", [128, 4, 1024], BF16)
            tsb = s.sbuf("tsb", [128, 4, 1024], BF16)
            tD_ = [[s.sbuf("tD%d_%d" % (k, i), [128, 1024], BF16) for i in range(2)] for k in range(1)]
            tP_ = [[s.sbuf("tP%d_%d" % (k, i), [128, 1024], BF16) for i in range(2)] for k in range(1)]
            wb_ = [[s.sbuf("wb%d_%d" % (k, i), [128, 1024], BF16) for i in range(2)] for k in range(1)]
            wri_ = [[s.sbuf("wri%d_%d" % (k, i), [128, 1024], F32) for i in range(2)] for k in range(1)]
            zb_ = [[s.sbuf("zb%d_%d" % (k, i), [128, 1024], BF16) for i in range(2)] for k in range(1)]
            bsb_ = [[s.sbuf("bsb%d_%d" % (k, i), [128, 1024], BF16) for i in range(2)] for k in range(1)]
            ucnt = [0]
            fq = s.sbuf("fq", [128, 4], F32)
            Sb = [[s.sbuf("Sb%d_%d" % (j4, ri), [128, 1024], BF16) for ri in range(2)] for j4 in range(4)]
            if os.environ.get("MK_VERBOSE"):
                print("ssm free sbuf bytes/partition:", nc.sbuf_bytes_remaining)
            units = [("p", sq, sq * 256, 256) for sq in range(4)]
            for kk in range(4):
                knat = kk if d == 0 else 3 - kk
                units.append(("s", kk, NP_TOK + knat * 1024, 1024))
            for a in range(2):
                s.dma("sp", uS[:], _ap(us_d)[a * 128:(a + 1) * 128, :], reads=[us_d], writes=[uS])
                for j4 in range(4):
                    j = a * 4 + j4
                    dv(lambda e, j=j: e.tensor_scalar(out=angt[:], in0=iota[:], scalar1=thr[:, j:j + 1], scalar2=None, op0=ALU.mult),
                       [iota, thr], [angt])
                    sincos(angt, angt[:], ang2, ang2[:], tabs, tabs[:], tabc, tabc[:], iiT, nfT, 1024)
                    s.op("act", lambda e, j=j, j4=j4: e.activation(out=rhoT[:, j4, :], in_=iota[:], func=AF.Identity, scale=0.0,
                                                                 bias=rho[:, j:j + 1]), reads=[iota, rho], writes=[rhoT])
                    s.op("act", lambda e, j4=j4: e.copy(out=tcb[:, j4, :], in_=tabc[:]), reads=[tabc], writes=[tcb])
                    s.op("act", lambda e, j4=j4: e.copy(out=tsb[:, j4, :], in_=tabs[:]), reads=[tabs], writes=[tsb])
                    for ci, col in enumerate((255, 1023)):
                        dv(lambda e, j4=j4, ci=ci, col=col: e.tensor_copy(out=ccol[:, j4, ci:ci + 1], in_=tabc[:, col:col + 1]), [tabc], [ccol])
                        dv(lambda e, j4=j4, ci=ci, col=col: e.tensor_copy(out=scol[:, j4, ci:ci + 1], in_=tabs[:, col:col + 1]), [tabs], [scol])
                    dv(lambda e, j=j, j4=j4: e.tensor_scalar(out=tq[:, 0:1], in0=scol[:, j4, 1:2], scalar1=sinv[:, j:j + 1], scalar2=None,
                                                             op0=ALU.mult), [scol, sinv], [tq])
                    dv(lambda e, j=j, j4=j4: e.scalar_tensor_tensor(out=cN[:, j:j + 1], in0=ccol[:, j4, 1:2], scalar=cosv[:, j:j + 1],
                                                                    in1=tq[:, 0:1], op0=ALU.mult, op1=ALU.subtract), [ccol, cosv, tq], [cN])
                    dv(lambda e, j=j, j4=j4: e.tensor_scalar(out=tq[:, 1:2], in0=ccol[:, j4, 1:2], scalar1=sinv[:, j:j + 1], scalar2=None,
                                                             op0=ALU.mult), [ccol, sinv], [tq])
                    dv(lambda e, j=j, j4=j4: e.scalar_tensor_tensor(out=sN[:, j:j + 1], in0=scol[:, j4, 1:2], scalar=cosv[:, j:j + 1],
                                                                    in1=tq[:, 1:2], op0=ALU.mult, op1=ALU.add), [scol, cosv, tq], [sN])
                for (kind, idx, tok0, n) in units:
                    for j4 in range(4):
                        j = a * 4 + j4
                        c_f, sn_f = tcb[:, j4, 0:n], tsb[:, j4, 0:n]
                        kb = 0
                        ucnt[0] += 1
                        tD, tP, wb, wri, zb, bsb = tD_[kb], tP_[kb], wb_[kb], wri_[kb], zb_[kb], bsb_[kb]
                        for p0 in range(0, n, 512):
                            pn = min(512, n - p0)
                            sl = slice(p0, p0 + pn) if d == 0 else slice(n - p0 - pn, n - p0)
                            for ri in range(2):
                                pb = ps()
                                mm_group(pb, pb[:, 0:pn], [(BT[:, j, ri, :], uS[:, tok0 + p0:tok0 + p0 + pn])], [BT, uS])
                                src_ = pb[:, 0:pn] if d == 0 else pb[:, 0:pn][:, ::-1]
                                s.op("act", lambda e, ri=ri, sl=sl, src_=src_: e.copy(out=bsb[ri][:, sl], in_=src_), reads=[pb], writes=[bsb[ri]])
                        br_, bi_ = bsb[0][:, 0:n], bsb[1][:, 0:n]
                        dv(lambda e: e.tensor_tensor(out=tD[0][:, 0:n], in0=c_f, in1=br_, op=ALU.mult), [tcb, bsb[0]], [tD[0]])
                        dv(lambda e: e.tensor_tensor(out=tD[1][:, 0:n], in0=sn_f, in1=bi_, op=ALU.mult), [tsb, bsb[1]], [tD[1]])
                        dv(lambda e: e.tensor_tensor(out=wb[0][:, 0:n], in0=tD[0][:, 0:n], in1=tD[1][:, 0:n], op=ALU.add),
                           [tD[0], tD[1]], [wb[0]])
                        dv(lambda e: e.tensor_tensor(out=tP[0][:, 0:n], in0=c_f, in1=bi_, op=ALU.mult), [tcb, bsb[1]], [tP[0]])
                        dv(lambda e: e.tensor_tensor(out=tP[1][:, 0:n], in0=sn_f, in1=br_, op=ALU.mult), [tsb, bsb[0]], [tP[1]])
                        dv(lambda e: e.tensor_tensor(out=wb[1][:, 0:n], in0=tP[0][:, 0:n], in1=tP[1][:, 0:n], op=ALU.subtract),
                           [tP[0], tP[1]], [wb[1]])
                        for ri in range(2):
                            if kind == "p":
                                ini = 0.0
                                rds = [rhoT, wb[ri]]
                            else:
                                ini = init[:, j, ri:ri + 1]
                                rds = [rhoT, wb[ri], init]
                            s.op("dve", lambda e, ri=ri, ini=ini, j4=j4: e.tensor_tensor_scan(
                                out=wri[ri][:, 0:n], data0=rhoT[:, j4, 0:n], data1=wb[ri][:, 0:n], initial=ini,
                                op0=ALU.mult, op1=ALU.add), reads=rds, writes=[wri[ri]])
                            s.op("act", lambda e, ri=ri: e.copy(out=zb[ri][:, 0:n], in_=wri[ri][:, 0:n]), reads=[wri[ri]], writes=[zb[ri]])
                        if kind == "s" and idx < 3:
                            dv(lambda e, j=j: e.tensor_scalar(out=tq[:, 0:1], in0=wri[1][:, n - 1:n], scalar1=sN[:, j:j + 1], scalar2=None,
                                                              op0=ALU.mult), [wri[1], sN], [tq])
                            dv(lambda e, j=j: e.tensor_scalar(out=tq[:, 1:2], in0=wri[0][:, n - 1:n], scalar1=sN[:, j:j + 1], scalar2=None,
                                                              op0=ALU.mult), [wri[0], sN], [tq])
                            dv(lambda e, j=j: e.scalar_tensor_tensor(out=init[:, j, 0:1], in0=wri[0][:, n - 1:n], scalar=cN[:, j:j + 1],
                                                                     in1=tq[:, 0:1], op0=ALU.mult, op1=ALU.subtract), [wri[0], cN, tq], [init])
                            dv(lambda e, j=j: e.scalar_tensor_tensor(out=init[:, j, 1:2], in0=wri[1][:, n - 1:n], scalar=cN[:, j:j + 1],
                                                                     in1=tq[:, 1:2], op0=ALU.mult, op1=ALU.add), [wri[1], cN, tq], [init])
                        if kind == "p":
                            fb = FIN[idx]
                            cl, sl_ = ccol[:, j4, 0:1], scol[:, j4, 0:1]
                            zrl, zil = wri[0][:, n - 1:n], wri[1][:, n - 1:n]
                            dv(lambda e: e.tensor_tensor(out=fq[:, 0:1], in0=sl_, in1=zil, op=ALU.mult), [scol, wri[1]], [fq])
                            dv(lambda e: e.tensor_tensor(out=fq[:, 1:2], in0=cl, in1=zrl, op=ALU.mult), [ccol, wri[0]], [fq])
                            dv(lambda e, fb=fb, j=j: e.tensor_tensor(out=fb[:, j, 0:1], in0=fq[:, 1:2], in1=fq[:, 0:1], op=ALU.subtract), [fq], [fb])
                            dv(lambda e: e.tensor_tensor(out=fq[:, 2:3], in0=sl_, in1=zrl, op=ALU.mult), [scol, wri[0]], [fq])
                            dv(lambda e: e.tensor_tensor(out=fq[:, 3:4], in0=cl, in1=zil, op=ALU.mult), [ccol, wri[1]], [fq])
                            dv(lambda e, fb=fb, j=j: e.tensor_tensor(out=fb[:, j, 1:2], in0=fq[:, 2:3], in1=fq[:, 3:4], op=ALU.add), [fq], [fb])
                        zr_, zi_ = zb[0][:, 0:n], zb[1][:, 0:n]
                        sr_o = Sb[j4][0][:, 0:n] if d == 0 else Sb[j4][0][:, 0:n][:, ::-1]
                        si_o = Sb[j4][1][:, 0:n] if d == 0 else Sb[j4][1][:, 0:n][:, ::-1]
                        dv(lambda e: e.tensor_tensor(out=tD[0][:, 0:n], in0=c_f, in1=zr_, op=ALU.mult), [tcb, zb[0]], [tD[0]])
                        dv(lambda e: e.tensor_tensor(out=tD[1][:, 0:n], in0=sn_f, in1=zi_, op=ALU.mult), [tsb, zb[1]], [tD[1]])
                        dv(lambda e, sr_o=sr_o: e.tensor_tensor(out=sr_o, in0=tD[0][:, 0:n], in1=tD[1][:, 0:n], op=ALU.subtract),
                           [tD[0], tD[1]], [Sb[j4][0]])
                        dv(lambda e: e.tensor_tensor(out=tP[0][:, 0:n], in0=sn_f, in1=zr_, op=ALU.mult), [tsb, zb[0]], [tP[0]])
                        dv(lambda e: e.tensor_tensor(out=tP[1][:, 0:n], in0=c_f, in1=zi_, op=ALU.mult), [tcb, zb[1]], [tP[1]])
                        dv(lambda e, si_o=si_o: e.tensor_tensor(out=si_o, in0=tP[0][:, 0:n], in1=tP[1][:, 0:n], op=ALU.add),
                           [tP[0], tP[1]], [Sb[j4][1]])
                        if pump is not None:
                            pump()
                    for p0 in range(0, n, 512):
                        pn = min(512, n - p0)
                        pb = ps()
                        pairs = []
                        rd = [CT]
                        for j4 in range(4):
                            for ri in range(2):
                                pairs.append((CT[:, a * 4 + j4, ri, :], Sb[j4][ri][:, p0:p0 + pn]))
                                rd.append(Sb[j4][ri])
                        mm_group(pb, pb[:, 0:pn], pairs, rd)
                        ysl = Y[:, a, tok0 + p0:tok0 + p0 + pn]
                        if d == 0:
                            dv(lambda e, pb=pb, ysl=ysl, pn=pn, a=a, p0=p0, tok0=tok0: e.scalar_tensor_tensor(
                                out=ysl, in0=uS[:, tok0 + p0:tok0 + p0 + pn], scalar=dco[:, a:a + 1], in1=pb[:, 0:pn],
                                op0=ALU.mult, op1=ALU.add), [uS, dco, pb], [Y])
                        else:
                            dv(lambda e, pb=pb, ysl=ysl, pn=pn: e.tensor_tensor(out=ysl, in0=ysl, in1=pb[:, 0:pn], op=ALU.add),
                               [Y, pb], [Y])
            for sq in range(4):
                s.dma("sp", _ap(nst)[sq, l, d].rearrange("(j q) r -> q j r", q=128), FIN[sq][:], reads=[FIN[sq]], writes=[nst])
            s.barrier()
            s.release(m2)
        m3 = s.mark()
        tD = [s.sbuf("tDg%d" % i, [128, 512], F32) for i in range(2)]
        tP = [s.sbuf("tPg%d" % i, [128, 512], F32) for i in range(2)]
        Wg = s.sbuf("Wg", [128, 2, 256], BF16)
        s.dma("pool", Wg[:], _ap(glu_w)[l].rearrange("(a p) f -> p a f", p=128), writes=[Wg])
        gb = s.sbuf("gb", [128, 2], F32)
        s.dma("sp", gb[:], _ap(glu_b)[l, :].rearrange("(a p) -> p a", p=128), writes=[gb])
        gel = [s.sbuf("gel%d" % a, [128, 512], F32) for a in range(2)]
        gelb = [s.sbuf("gelb%d" % a, [128, 512], BF16) for a in range(2)]
        sig = s.sbuf("sig", [128, 512], F32)
        yo = [s.sbuf("yo%d" % a, [128, 512], BF16) for a in range(2)]
        for blk in range(NB):
            tok = slice(blk * 512, (blk + 1) * 512)
            for a in range(2):
                ysl = Y[:, a, tok]
                s.op("act", lambda e, ysl=ysl: e.activation(out=tD[0][:, 0:512], in_=ysl, func=AF.Square), reads=[Y], writes=[tD[0]])
                dv(lambda e: e.tensor_scalar(out=tD[0][:, 0:512], in0=tD[0][:, 0:512], scalar1=0.044715, scalar2=1.0,
                                             op0=ALU.mult, op1=ALU.add), [tD[0]], [tD[0]])
                dv(lambda e, ysl=ysl: e.tensor_tensor(out=tD[1][:, 0:512], in0=tD[0][:, 0:512], in1=ysl, op=ALU.mult), [tD[0], Y], [tD[1]])
                s.op("act", lambda e: e.activation(out=tP[0][:, 0:512], in_=tD[1][:, 0:512], func=AF.Sigmoid, scale=1.5957691216057308),
                     reads=[tD[1]], writes=[tP[0]])
                dv(lambda e, a=a, ysl=ysl: e.tensor_tensor(out=gel[a][:], in0=tP[0][:, 0:512], in1=ysl, op=ALU.mult), [tP[0], Y], [gel[a]])
                s.op("act", lambda e, a=a: e.copy(out=gelb[a][:], in_=gel[a][:]), reads=[gel[a]], writes=[gelb[a]])
            for a in range(2):
                pb = ps()
                mm_group(pb, pb[:], [(Wg[:, a2, a * 128:(a + 1) * 128], gelb[a2][:]) for a2 in range(2)], [Wg, gelb[0], gelb[1]])
                s.op("act", lambda e, pb=pb, a=a: e.activation(out=sig[:], in_=pb[:], func=AF.Sigmoid, bias=gb[:, a:a + 1]),
                     reads=[pb, gb], writes=[sig])
                dv(lambda e, a=a: e.tensor_tensor(out=yo[a][:], in0=sig[:], in1=gel[a][:], op=ALU.mult), [sig, gel[a]], [yo[a]])
                s.dma("sp", _ap(ymix_d)[a * 128:(a + 1) * 128, tok], yo[a][:], reads=[yo[a]], writes=[ymix_d])
        s.barrier()
        s.release(m0)

    def dense_gen(l):
        qPs = [s.sbuf("qP%d" % i, [128, 4, 256], BF16) for i in range(2)]
        kPs = [s.sbuf("kP%d" % i, [128, 4, 256], BF16) for i in range(2)]
        vPs = [s.sbuf("vP%d" % i, [128, 2, 512], BF16) for i in range(2)]
        Eb = [s.sbuf("Eb%d" % i, [128, 2, 256], BF16) for i in range(2)]
        rdn = [s.sbuf("rdn%d" % i, [128, 256], F32) for i in range(2)]
        yat = [s.sbuf("yat%d" % i, [128, 256], BF16) for i in range(2)]

        def fetch(sq):
            qP, kP, vP = qPs[sq % 2], kPs[sq % 2], vPs[sq % 2]
            tk = slice(sq * 256, (sq + 1) * 256)
            s.dma("sp", qP[:], _ap(q_d)[:, tk].rearrange("(t p) n -> p t n", p=128), reads=[q_d], writes=[qP])
            s.dma("sp", kP[:], _ap(k_d)[:, tk].rearrange("(t p) n -> p t n", p=128), reads=[k_d], writes=[kP])
            s.dma("sp", vP[:], _ap(v_d)[tk, :].rearrange("(t p) f -> p t f", p=128), reads=[v_d], writes=[vP])

        fetch(0)
        n_it = 0
        yield
        for sq in range(4):
            qP, kP, vP = qPs[sq % 2], kPs[sq % 2], vPs[sq % 2]
            if sq + 1 < 4:
                fetch(sq + 1)
            for hp in range(4):
                ya = yat[(sq * 4 + hp) % 2]
                for hh in range(2):
                    pr = slice(hh * 64, (hh + 1) * 64)
                    E = Eb[n_it % 2]
                    rd_ = rdn[n_it % 2]
                    n_it += 1
                    for kt2 in range(2):
                        pb = ps()
                        mm_group(pb, pb[:, 0:256], [(kP[pr, hp, kt2 * 128:(kt2 + 1) * 128], qP[pr, hp, :])], [kP, qP])
                        s.op("act", lambda e, E=E, pb=pb, kt2=kt2: e.activation(out=E[:, kt2, :], in_=pb[:, 0:256], func=AF.Exp, scale=0.125),
                             reads=[pb], writes=[E])
                    pn_ = ps()
                    mm_group(pn_, pn_[:, 0:256], [(vP[:, kt2, hp * 128:(hp + 1) * 128], E[:, kt2, :]) for kt2 in range(2)], [vP, E])
                    pd_ = ps()
                    mm_group(pd_, pd_[:, 0:256], [(ones_bf[:], E[:, kt2, :]) for kt2 in range(2)], [ones_bf, E])
                    s.op("dve", lambda e, rd_=rd_, pd_=pd_, pr=pr: e.reciprocal(out=rd_[pr, :], in_=pd_[pr, 0:256]), reads=[pd_], writes=[rd_])
                    s.op("dve", lambda e, ya=ya, pn_=pn_, rd_=rd_, pr=pr: e.tensor_tensor(out=ya[pr, :], in0=pn_[pr, 0:256], in1=rd_[pr, :],
                                                                                        op=ALU.mult), reads=[pn_, rd_], writes=[ya])
                s.dma("sp", _ap(ymix_d)[256 + hp * 128:256 + (hp + 1) * 128, sq * 256:(sq + 1) * 256], ya[:], reads=[ya], writes=[ymix_d])
                yield

    def attn_part(l):
        m0 = s.mark()
        BTt = s.sbuf("BTt", [128, 8, 15, 64], BF16)
        identb = s.sbuf("identb", [128, 128], BF16)
        s.op("dve", lambda e: e.tensor_copy(out=identb[:], in_=ident[:]), reads=[ident], writes=[identb])
        m1 = s.mark()
        oh = s.sbuf("oh", [31, 64, 128], F32)
        s.dma("sp", oh[:], _ap(c_oh).rearrange("d (k q) -> d k q", q=128), writes=[oh])
        msk = s.sbuf("msk", [128, 64], F32)
        s.dma("sp", msk[:], _ap(c_mask), writes=[msk])
        rpT = s.sbuf("rpT", [31, 128], F32)
        s.op("dve", lambda e: e.memset(rpT[:], 0.0), writes=[rpT])
        s.dma("sp", rpT[:, 0:120], _ap(rpb)[l].rearrange("x d -> d x"), reads=[rpT], writes=[rpT])
        BTf = BTt[:].rearrange("p h d k -> p (h d) k")
        for k4 in range(16):
            pb = ps()
            for ki in range(4):
                kk = k4 * 4 + ki
                mm_group(pb, pb[:, ki * 128:ki * 128 + 128], [(oh[:, kk, :], rpT[:])], [oh, rpT])
            for ki in range(4):
                kk = k4 * 4 + ki
                s.op("dve", lambda e, pb=pb, ki=ki, kk=kk: e.tensor_scalar(out=BTf[:, :, kk], in0=pb[:, ki * 128:ki * 128 + 120],
                                                                           scalar1=msk[:, kk:kk + 1], scalar2=8.0, op0=ALU.add, op1=ALU.mult),
                     reads=[pb, msk], writes=[BTt])
        s.barrier()
        s.release(m1)
        if int(os.environ.get("MK_ATT_STOP", "9")) <= 2:
            return
        qS = s.sbuf("qS", [128, 4, NS_TOK], BF16)
        kS = s.sbuf("kS", [128, 4, NS_TOK], BF16)
        for t4 in range(4):
            s.dma("sp", qS[:, t4, :], _ap(q_d)[t4 * 128:(t4 + 1) * 128, NP_TOK:NT], reads=[q_d], writes=[qS])
            s.dma("sp", kS[:, t4, :], _ap(k_d)[t4 * 128:(t4 + 1) * 128, NP_TOK:NT], reads=[k_d], writes=[kS])
        vS = s.sbuf("vS", [128, 64, 512], BF16)
        s.dma("sp", vS[0:64, :, :], _ap(v_d)[NP_TOK:NT, :].rearrange("(r c) f -> c r f", c=64), reads=[v_d], writes=[vS])
        s.dma("sp", vS[64:128, 0:63, :], _ap(v_d)[NP_TOK + 64:NT, :].rearrange("(r c) f -> c r f", c=64), reads=[v_d], writes=[vS])
        s.op("dve", lambda e: e.memset(vS[64:128, 63:64, :], 0.0), reads=[vS], writes=[vS])
        ck32 = s.sbuf("ck32", [128, 2, 512], F32)
        s.dma("sp", ck32[:], _ap(ck)[l].rearrange("(t p) f -> p t f", p=128), writes=[ck32])
        kC = s.sbuf("kC", [128, 4, 256], BF16)
        for hp in range(4):
            pb = ps()
            s.group("pe", [lambda e, pb=pb, t=t, hp=hp: e.transpose(pb[:, t * 128:(t + 1) * 128], ck32[:, t, hp * 128:(hp + 1) * 128], ident[:])
                           for t in range(2)], reads=[ck32, ident], writes=[pb])
            s.op("act", lambda e, pb=pb, hp=hp: e.copy(out=kC[:, hp, :], in_=pb[:, 0:256]), reads=[pb], writes=[kC])
        vC = s.sbuf("vC", [128, 2, 512], BF16)
        s.dma("pool", vC[:], _ap(cv)[l].rearrange("(t p) f -> p t f", p=128), writes=[vC])
        NBUF = 3
        EC = [s.sbuf("EC%d" % i, [128, 6, 2, 64], BF16) for i in range(NBUF)]
        rdw = [s.sbuf("rdw%d" % i, [128, 128], F32) for i in range(NBUF)]
        YN = [s.sbuf("YN%d" % i, [128, 4, 512], BF16) for i in range(2)]
        it = 0
        for r in range(int(os.environ.get("MK_NA_ROWS", "64"))):
            rs = min(max(r - 4, 0), 56)
            dr0 = rs - r + 7
            yn = YN[(r // 8) % 2]
            for hp in range(4):
                k2 = it % NBUF
                it += 1
                E_ = EC[k2]
                for hh in range(2):
                    pr = slice(hh * 64, (hh + 1) * 64)
                    h = 2 * hp + hh
                    qrow = qS[pr, hp, r * 64:(r + 1) * 64]
                    idb = identb[pr, hh * 64:(hh + 1) * 64]
                    pw = ps()
                    fw = []
                    for m in range(4):
                        fw.append(lambda e, pw=pw, m=m, pr=pr, qrow=qrow: e.matmul(
                            pw[:, m * 64:(m + 1) * 64], kS[pr, hp, (rs + 2 * m) * 64:(rs + 2 * m + 2) * 64], qrow, start=True, stop=False))
                        fw.append(lambda e, pw=pw, m=m, pr=pr, h=h, idb=idb: e.matmul(
                            pw[:, m * 64:(m + 1) * 64], BTt[pr, h, dr0 + 2 * m:dr0 + 2 * m + 2, :].rearrange("p d k -> p (d k)"), idb,
                            start=False, stop=True))
                    for t in range(2):
                        fw.append(lambda e, pw=pw, t=t, pr=pr, qrow=qrow: e.matmul(
                            pw[:, 256 + t * 64:256 + (t + 1) * 64], kC[pr, hp, t * 128:(t + 1) * 128], qrow, start=True, stop=True))
                    s.group("pe", fw, reads=[kS, kC, qS, BTt, identb], writes=[pw])
                    s.op("act", lambda e, E_=E_, pw=pw, hh=hh: e.activation(
                        out=E_[:, :, hh, :], in_=pw[:, 0:384].rearrange("p (m q) -> p m q", m=6), func=AF.Exp, scale=0.125),
                        reads=[pw], writes=[E_])
                pnd = ps()
                rhs6 = [E_[:, m, :, :].rearrange("p h q -> p (h q)") for m in range(6)]
                lv = [vS[:, rs + 2 * m, hp * 128:(hp + 1) * 128] for m in range(4)] + [vC[:, t, hp * 128:(hp + 1) * 128] for t in range(2)]
                mm_group(pnd, pnd[:, 0:128], [(lv[m], rhs6[m]) for m in range(6)], [vS, vC, E_])
                mm_group(pnd, pnd[:, 128:256], [(ones_bf[:], rhs6[m]) for m in range(6)], [ones_bf, E_])
                rd_ = rdw[k2]
                s.op("dve", lambda e, rd_=rd_, pnd=pnd: e.reciprocal(out=rd_[:], in_=pnd[:, 128:256]), reads=[pnd], writes=[rd_])
                for hh in range(2):
                    pr = slice(hh * 64, (hh + 1) * 64)
                    s.op("dve", lambda e, yn=yn, pnd=pnd, rd_=rd_, pr=pr, hp=hp, r=r, hh=hh: e.tensor_tensor(
                        out=yn[pr, hp, (r % 8) * 64:(r % 8 + 1) * 64], in0=pnd[pr, hh * 64:(hh + 1) * 64], in1=rd_[pr, hh * 64:(hh + 1) * 64],
                        op=ALU.mult), reads=[pnd, rd_], writes=[yn])
            if r % 8 == 7:
                tok0 = NP_TOK + (r // 8) * 512
                for hp in range(4):
                    s.dma("sp", _ap(ymix_d)[256 + hp * 128:256 + (hp + 1) * 128, tok0:tok0 + 512], yn[:, hp, :], reads=[yn], writes=[ymix_d])
        s.barrier()
        s.release(m0)

    def gate_gen(l):
        ws32 = s.sbuf("ws32", [128, 4, 128], F32)
        s.dma("sp", ws32[:], _ap(gm_ws)[l].rearrange("g i j -> i g j"), writes=[ws32])
        wsT = s.sbuf("wsT", [128, 4, 128], BF16)
        pb = ps()
        s.group("pe", [lambda e, g=g: e.transpose(pb[:, g * 128:(g + 1) * 128], ws32[:, g, :], ident[:]) for g in range(4)],
                reads=[ws32, ident], writes=[pb])
        s.op("act", lambda e: e.copy(out=wsT[:].rearrange("p g i -> p (g i)"), in_=pb[:]), reads=[pb], writes=[wsT])
        BS = s.sbuf("BS", [128, 2, 128], F32)
        for g in range(4):
            s.dma("sp", BS[(g % 2) * 64:(g % 2 + 1) * 64, g // 2, :], _ap(gm_bs)[l, g:g + 1, :].to_broadcast([64, 128]), writes=[BS])
        ug = [s.sbuf("ug%d" % i, [128, 2, 512], BF16) for i in range(2)]
        vAl = [s.sbuf("vAl%d" % i, [128, 4, 256], BF16) for i in range(2)]
        vBl = [s.sbuf("vBl%d" % i, [128, 4, 256], BF16) for i in range(2)]
        tg = [s.sbuf("tg%d" % i, [128, 128], F32) for i in range(2)]
        yg = [s.sbuf("yg%d" % i, [128, 2, 512], BF16) for i in range(2)]
        it = 0
        def fetch(blk):
            k2 = blk % 2
            tok = slice(blk * 512, (blk + 1) * 512)
            for a in range(2):
                s.dma("sp", ug[k2][:, a, :], _ap(ug_d)[a * 128:(a + 1) * 128, tok], reads=[ug_d], writes=[ug[k2]])
            s.dma("sp", vAl[k2][:], _ap(va_d)[tok, :].rearrange("(t p) f -> p t f", p=128), reads=[va_d], writes=[vAl[k2]])
            s.dma("sp", vBl[k2][:], _ap(vb_d)[tok, :].rearrange("(t p) f -> p t f", p=128), reads=[vb_d], writes=[vBl[k2]])

        fetch(0)
        yield
        for blk in range(NB):
            k2 = blk % 2
            tok = slice(blk * 512, (blk + 1) * 512)
            if blk + 1 < NB:
                fetch(blk + 1)
            for t in range(4):
                if t:
                    yield
                for a in range(2):
                    pq = ps()
                    mm_group(pq, pq[:, 0:128], [(vAl[k2][:, t, a * 128:(a + 1) * 128], wsT[:, 2 * a, :]),
                                                (vBl[k2][:, t, a * 128:(a + 1) * 128], wsT[:, 2 * a + 1, :])], [vAl[k2], vBl[k2], wsT])
                    tt_ = tg[it % 2]
                    it += 1
                    s.op("dve", lambda e, tt_=tt_, pq=pq, a=a: e.tensor_tensor(out=tt_[:], in0=pq[:, 0:128], in1=BS[:, a, :], op=ALU.add),
                         reads=[pq, BS], writes=[tt_])
                    s.op("dve", lambda e, tt_=tt_, a=a, t=t, k2=k2: e.tensor_tensor(
                        out=yg[k2][:, a, t * 128:(t + 1) * 128], in0=tt_[:], in1=ug[k2][:, a, t * 128:(t + 1) * 128], op=ALU.mult),
                        reads=[tt_, ug[k2]], writes=[yg[k2]])
            for a in range(2):
                s.dma("sp", _ap(ymix_d)[768 + a * 128:768 + (a + 1) * 128, tok], yg[k2][:, a, :], reads=[yg[k2]], writes=[ymix_d])
            yield

    PH = os.environ.get("MK_PHASES", "all")
    if PH != "all":
        for name in PH.split(","):
            {"adaln": setup_adaln, "ffn": lambda: ffn_phase(0, 0, f1_in, f1_out, True, False), "proj": lambda: proj_phase(0),
             "ssm": lambda: ssm_part(0), "attn": lambda: attn_part(0),
             "outproj": lambda: outproj_phase(0)}[name]()
        s.barrier()
        s.release(0)
        ctx.__exit__(None, None, None)
        return nc
    mW = s.mark()
    W = alloc_ffn_w()
    setup_adaln(after_dma=lambda: issue_ffn_w(W, 0, f1_in, f1_out))
    for l in range(DEPTH):
        if l == 0:
            ffn_phase(l, 0, f1_in, f1_out, first=True, last=False, W=W)
            s.release(mW)
        else:
            ffn_phase(l, 0, f1_in, f1_out, first=False, last=False)
        proj_phase(l)
        mB = s.mark()
        gens = [dense_gen(l), gate_gen(l)]
        for g in gens:
            next(g)

        def pump():
            for g in gens:
                try:
                    next(g)
                    return
                except StopIteration:
                    continue

        ssm_part(l, pump)
        for g in gens:
            for _ in g:
                pass
        s.barrier()
        s.release(mB)
        attn_part(l)
        mW = s.mark()
        W = alloc_ffn_w()
        mWO = s.mark()
        WO = s.sbuf("WO", [128, 8, 1024], BF16)
        s.dma("pool", WO[:], _ap(w_out)[l].rearrange("(kt p) f -> p kt f", p=128), writes=[WO])
        issue_ffn_w(W, l, f2_in, f2_out)
        outproj_phase(l, WO=WO)
        s.release(mWO)
        ffn_phase(l, 2, f2_in, f2_out, first=False, last=(l == DEPTH - 1), W=W)
        s.release(mW)
    outs = [yp, ys, nk, nv, nst]
    s.barrier()
    s.release(0)
    ctx.__exit__(None, None, None)
    return nc


_NC_CACHE = {}


def _consts():
    ident = np.eye(128, dtype=np.float32)
    kc = np.arange(64)[:, None]
    qc = np.arange(64)[None, :]
    dc = kc - qc + 15
    oh = np.zeros((31, 64, 128), np.float32)
    for q in range(64):
        for k in range(64):
            if 0 <= dc[k, q] <= 30:
                oh[dc[k, q], k, q] = 1.0
                oh[dc[k, q], k, 64 + q] = 1.0
    cs = np.clip(np.arange(64) - 8, 0, 48)
    win = (kc >= cs[None, :]) & (kc < cs[None, :] + 16)
    mask = np.where(win, 0.0, NEG).astype(np.float32)
    iota = np.tile(np.arange(1024, dtype=np.float32)[None, :], (128, 1))
    return {"c_ident": ident, "c_oh": oh.reshape(31, 8192), "c_mask": np.ascontiguousarray(np.concatenate([mask.T, mask.T], axis=0)), "c_iota": iota}


def kernel(**inputs):
    debug = bool(int(os.environ.get("MK_DEBUG", "0")))
    key = ("nc", debug)
    if key not in _NC_CACHE:
        _NC_CACHE[key] = build_program(debug=debug)
    nc = _NC_CACHE[key]
    f = lambda a: np.ascontiguousarray(np.asarray(a, dtype=np.float32))
    x_prompt, x_sample = f(inputs["x_prompt"]), f(inputs["x_sample"])
    c, c_ctx = f(inputs["c"]), f(inputs["c_ctx"])
    cache_k, cache_v, state_ssm = f(inputs["cache_k"]), f(inputs["cache_v"]), f(inputs["state_ssm"])
    shared = {}
    for name in ("w_ada", "b_ada", "norm_ffn1", "norm_mix", "norm_ffn2", "ffn1_w_in", "ffn1_w_out", "ffn2_w_in",
                 "ffn2_w_out", "w_in", "w_out", "ssm_d", "ssm_glu_w", "ssm_glu_b", "na_q_norm", "na_k_norm",
                 "gm_ws", "gm_bs"):
        shared[name] = f(inputs[name])
    shared["ssm_lambda_re"] = f(inputs["ssm_lambda_re"]).reshape(DEPTH, 2, 1024)
    shared["ssm_lambda_im"] = f(inputs["ssm_lambda_im"]).reshape(DEPTH, 2, 1024)
    shared["ssm_log_dt"] = f(inputs["ssm_log_dt"])
    shared["ssm_b_re"] = f(inputs["ssm_b_re"]).reshape(DEPTH, 2, 1024, 16)
    shared["ssm_b_im"] = f(inputs["ssm_b_im"]).reshape(DEPTH, 2, 1024, 16)
    shared["ssm_c_re"] = f(inputs["ssm_c_re"]).reshape(DEPTH, 2, 256, 64)
    shared["ssm_c_im"] = f(inputs["ssm_c_im"]).reshape(DEPTH, 2, 256, 64)
    shared["na_rpb"] = f(inputs["na_rpb"]).reshape(DEPTH, 120, 31)
    shared.update(_consts())
    in_maps = []
    for core in range(8):
        b = core // 2
        m = dict(shared)
        m["xp"] = x_prompt[4 * core:4 * core + 4].reshape(NP_TOK, D)
        m["xs"] = x_sample[b]
        m["cond"] = np.stack([c_ctx, c[b]], axis=0)
        m["ck"] = cache_k[b].reshape(DEPTH, 256, 512)
        m["cv"] = cache_v[b].reshape(DEPTH, 256, 512)
        m["sst"] = state_ssm[b].reshape(DEPTH, 2, 1024, 2)
        hs = np.zeros((128, 2), np.float32)
        hs[:, core % 2] = 1.0
        m["c_hsel"] = hs
        in_maps.append(m)
    res = run_bass_kernel_spmd(nc, in_maps, core_ids=list(range(8)))
    R = res.results
    if debug:
        kernel.last_results = R
    y_prompt = np.concatenate([R[i]["yp"].reshape(4, 256, D) for i in range(8)], axis=0)
    y_sample = np.stack([np.concatenate([R[2 * b]["ys"], R[2 * b + 1]["ys"]], axis=0) for b in range(4)], axis=0)
    new_k = np.concatenate([R[i]["nk"].reshape(4, DEPTH, 256, 8, 64) for i in range(8)], axis=0)
    new_v = np.concatenate([R[i]["nv"].reshape(4, DEPTH, 256, 8, 64) for i in range(8)], axis=0)
    new_s = np.concatenate([R[i]["nst"].reshape(4, DEPTH, 2, 16, 64, 2) for i in range(8)], axis=0)
    return (y_prompt.astype(np.float32), y_sample.astype(np.float32), new_k.astype(np.float32),
            new_v.astype(np.float32), new_s.astype(np.float32))
```

```python
import math
import os
import numpy as np
import concourse.bass as bass
import concourse.mybir as mybir
from concourse.bass_utils import run_bass_kernel_spmd

F32 = mybir.dt.float32
BF16 = mybir.dt.bfloat16
AF = mybir.ActivationFunctionType
ALU = mybir.AluOpType
AX = mybir.AxisListType

D = 1024
DFF = 2816
DEPTH = 2
NP_TOK = 1024
NS_TOK = 4096
NT = NP_TOK + NS_TOK
NB = NT // 512
INC = 2304
TWO_PI = 2.0 * math.pi
NEG = -30000.0


class Buf:
    __slots__ = ("t", "w", "r", "name")

    def __init__(self, t, name=""):
        self.t = t
        self.w = None
        self.r = {}
        self.name = name

    def __getitem__(self, idx):
        return self.t[idx]


class Sched:
    def __init__(self, nc, n_dma_sems=40):
        self.nc = nc
        self.stack = []
        self.eng = {"pe": nc.tensor, "act": nc.scalar, "dve": nc.vector, "pool": nc.gpsimd, "sp": nc.sync}
        self.sems = {}
        self.cnt = {}
        for k in ("pe", "act", "dve", "pool"):
            self.sems[k] = self._sem("s_" + k)
            self.cnt[k] = 0
        self.dma_keys = []
        for i in range(n_dma_sems):
            k = "d%d" % i
            self.sems[k] = self._sem("s_" + k)
            self.cnt[k] = 0
            self.dma_keys.append(k)
        self.dma_rr = 0
        self.seen = {e: {} for e in self.eng}
        self.ninst = 0

    def _sem(self, name):
        cm = self.nc.semaphore(name)
        h = cm.__enter__()
        self.stack.append(cm)
        return h

    def mark(self):
        return len(self.stack)

    def release(self, mark):
        while len(self.stack) > mark:
            self.stack.pop().__exit__(None, None, None)

    def sbuf(self, name, shape, dtype):
        self.uid = getattr(self, "uid", 0) + 1
        name = "%s_u%d" % (name, self.uid)
        cm = self.nc.sbuf_tensor(name, list(shape), dtype)
        t = cm.__enter__()
        self.stack.append(cm)
        return Buf(t, name)

    def psum(self, name, shape, dtype=F32):
        cm = self.nc.psum_tensor(name, list(shape), dtype)
        t = cm.__enter__()
        self.stack.append(cm)
        return Buf(t, name)

    def dram(self, name, shape, dtype, kind="Internal"):
        return Buf(self.nc.dram_tensor(name, list(shape), dtype, kind=kind), name)

    def _need(self, e, deps):
        seen = self.seen[e]
        todo = {}
        for (k, v) in deps:
            if seen.get(k, 0) >= v:
                continue
            if todo.get(k, 0) < v:
                todo[k] = v
        for k, v in todo.items():
            self.eng[e].wait_ge(self.sems[k], v)
            seen[k] = v
            self.ninst += 1

    @staticmethod
    def _deps(reads, writes):
        deps = []
        for b in reads:
            if b.w is not None:
                deps.append(b.w)
        for b in writes:
            if b.w is not None:
                deps.append(b.w)
            deps.extend(b.r.items())
        return deps

    def _commit(self, k, v, reads, writes):
        for b in reads:
            if b.r.get(k, 0) < v:
                b.r[k] = v
        for b in writes:
            b.w = (k, v)
            b.r = {}

    def op(self, e, fn, reads=(), writes=()):
        self._need(e, self._deps(reads, writes))
        inst = fn(self.eng[e])
        self.cnt[e] += 1
        inst.then_inc(self.sems[e], 1)
        self._commit(e, self.cnt[e], reads, writes)
        self.ninst += 1
        return inst

    def group(self, e, fns, reads=(), writes=()):
        self._need(e, self._deps(reads, writes))
        inst = None
        for fn in fns:
            inst = fn(self.eng[e])
            self.ninst += 1
        self.cnt[e] += 1
        inst.then_inc(self.sems[e], 1)
        self._commit(e, self.cnt[e], reads, writes)

    def dma(self, q, out_ap, in_ap, reads=(), writes=(), **kw):
        nsw = 8
        if q == "pool":
            self.sw_rr = (getattr(self, "sw_rr", -1) + 1) % nsw
            k = self.dma_keys[self.sw_rr]
        else:
            k = self.dma_keys[nsw + self.dma_rr]
            self.dma_rr = (self.dma_rr + 1) % (len(self.dma_keys) - nsw)
        deps = self._deps(reads, writes)
        if self.cnt[k] > 0:
            deps.append((k, self.cnt[k]))
        self._need(q, deps)
        inst = self.eng[q].dma_start(out=out_ap, in_=in_ap, **kw)
        self.cnt[k] += 16
        inst.then_inc(self.sems[k], 16)
        self._commit(k, self.cnt[k], reads, writes)
        self.ninst += 1
        return inst

    def barrier(self):
        allk = [(k, v) for k, v in self.cnt.items() if v > 0]
        for e in self.eng:
            self._need(e, allk)


def _ap(b):
    return b.t.ap()


def build_program(debug=False):
    nc = bass.Bass("TRN2", target_bir_lowering=False)
    s = Sched(nc)
    ctx = nc.allow_non_contiguous_dma(reason="small strided parameter loads")
    ctx.__enter__()

    def din(name, shape):
        return s.dram(name, shape, F32, kind="ExternalInput")

    def dout(name, shape):
        return s.dram(name, shape, F32, kind="ExternalOutput")

    xp = din("xp", [NP_TOK, D])
    xs = din("xs", [NS_TOK, D])
    cond = din("cond", [2, D])
    ck = din("ck", [DEPTH, 256, 512])
    cv = din("cv", [DEPTH, 256, 512])
    sst = din("sst", [DEPTH, 2, 1024, 2])
    w_ada = din("w_ada", [DEPTH, D, 9 * D])
    b_ada = din("b_ada", [DEPTH, 9 * D])
    norms = [din("norm_ffn1", [DEPTH, D]), din("norm_mix", [DEPTH, D]), din("norm_ffn2", [DEPTH, D])]
    f1_in = din("ffn1_w_in", [DEPTH, D, 2 * DFF])
    f1_out = din("ffn1_w_out", [DEPTH, DFF, D])
    f2_in = din("ffn2_w_in", [DEPTH, D, 2 * DFF])
    f2_out = din("ffn2_w_out", [DEPTH, DFF, D])
    w_in = din("w_in", [DEPTH, D, INC])
    w_out = din("w_out", [DEPTH, D, D])
    lam_re = din("ssm_lambda_re", [DEPTH, 2, 1024])
    lam_im = din("ssm_lambda_im", [DEPTH, 2, 1024])
    log_dt = din("ssm_log_dt", [DEPTH, 2, 16])
    b_re = din("ssm_b_re", [DEPTH, 2, 1024, 16])
    b_im = din("ssm_b_im", [DEPTH, 2, 1024, 16])
    c_re = din("ssm_c_re", [DEPTH, 2, 256, 64])
    c_im = din("ssm_c_im", [DEPTH, 2, 256, 64])
    ssm_d = din("ssm_d", [DEPTH, 256])
    glu_w = din("ssm_glu_w", [DEPTH, 256, 256])
    glu_b = din("ssm_glu_b", [DEPTH, 256])
    qn_g = din("na_q_norm", [DEPTH, 64])
    kn_g = din("na_k_norm", [DEPTH, 64])
    rpb = din("na_rpb", [DEPTH, 8 * 15, 31])
    gm_ws = din("gm_ws", [DEPTH, 4, 128, 128])
    gm_bs = din("gm_bs", [DEPTH, 4, 128])
    c_ident = din("c_ident", [128, 128])
    c_oh = din("c_oh", [31, 64 * 128])
    c_mask = din("c_mask", [128, 64])
    c_iota = din("c_iota", [128, 1024])
    c_hsel = din("c_hsel", [128, 2])
    yp = dout("yp", [NP_TOK, D])
    ys = dout("ys", [NS_TOK // 2, D])
    nk = dout("nk", [4, DEPTH, 256, 512])
    nv = dout("nv", [4, DEPTH, 256, 512])
    nst = dout("nst", [4, DEPTH, 2, 1024, 2])
    skind = "ExternalOutput" if debug else "Internal"
    xT_d = s.dram("xT_d", [D, NT], F32, kind=skind)
    ymix_d = s.dram("ymix_d", [D, NT], BF16, kind=skind)
    us_d = s.dram("us_d", [256, NT], BF16, kind=skind)
    q_d = s.dram("q_d", [512, NT], BF16, kind=skind)
    k_d = s.dram("k_d", [512, NT], BF16, kind=skind)
    v_d = s.dram("v_d", [NT, 512], BF16, kind=skind)
    ug_d = s.dram("ug_d", [256, NT], BF16, kind=skind)
    va_d = s.dram("va_d", [NT, 256], BF16, kind=skind)
    vb_d = s.dram("vb_d", [NT, 256], BF16, kind=skind)

    banks = [s.psum("bank%d" % i, [128, 512], F32) for i in range(8)]
    bank_rr = [0]

    def ps():
        b = banks[bank_rr[0]]
        bank_rr[0] = (bank_rr[0] + 1) % 8
        return b

    ident = s.sbuf("ident", [128, 128], F32)
    s.dma("sp", ident[:], _ap(c_ident), writes=[ident])
    ones_bf = s.sbuf("ones_bf", [128, 128], BF16)
    s.op("dve", lambda e: e.memset(ones_bf[:], 1.0), writes=[ones_bf])
    mean_bf = s.sbuf("mean_bf", [128, 128], BF16)
    s.op("dve", lambda e: e.memset(mean_bf[:], 1.0 / 1024.0), writes=[mean_bf])
    bd_bf = s.sbuf("bd_bf", [128, 128], BF16)
    s.op("dve", lambda e: e.memset(bd_bf[:], 0.0), writes=[bd_bf])
    s.op("dve", lambda e: e.memset(bd_bf[0:64, 0:64], 1.0 / 64.0), reads=[bd_bf], writes=[bd_bf])
    s.op("dve", lambda e: e.memset(bd_bf[64:128, 64:128], 1.0 / 64.0), reads=[bd_bf], writes=[bd_bf])
    pi_c = s.sbuf("pi_c", [128, 1], F32)
    s.op("dve", lambda e: e.memset(pi_c[:], math.pi), writes=[pi_c])
    hsel = s.sbuf("hsel", [128, 2], F32)
    s.dma("sp", hsel[:], _ap(c_hsel), writes=[hsel])
    eps6 = s.sbuf("eps6", [128, 1], F32)
    s.op("dve", lambda e: e.memset(eps6[:], 1e-6), writes=[eps6])
    eps5 = s.sbuf("eps5", [128, 1], F32)
    s.op("dve", lambda e: e.memset(eps5[:], 1e-5), writes=[eps5])

    def rsqrt(dst_b, dst_ap, src_b, src_ap, eps_b, scale=1.0):
        P = dst_ap.shape[0]
        s.op("act", lambda e: e.activation(out=dst_ap, in_=src_ap, func=AF.Sqrt, scale=scale, bias=eps_b[0:P, 0:1]),
             reads=[src_b, eps_b], writes=[dst_b])
        s.op("dve", lambda e: e.reciprocal(out=dst_ap, in_=dst_ap), reads=[dst_b], writes=[dst_b])
    Aco = s.sbuf("Aco", [128, DEPTH, 3, 2, 8], F32)
    Bco = s.sbuf("Bco", [128, DEPTH, 3, 2, 8], F32)
    Gco = s.sbuf("Gco", [128, DEPTH, 3, 2, 8], F32)

    def mm_group(out_buf, out_ap, pairs, reads):
        n = len(pairs)
        fns = []
        for i, (l, r) in enumerate(pairs):
            fns.append(lambda e, l=l, r=r, i=i: e.matmul(out_ap, l, r, start=(i == 0), stop=(i == n - 1)))
        s.group("pe", fns, reads=reads, writes=[out_buf])

    def setup_adaln(after_dma=None):
        m0 = s.mark()
        cs32 = s.sbuf("cs32", [128, 8, 2], F32)
        csb = s.sbuf("csb", [128, 8, 2], BF16)
        for c in range(2):
            s.dma("sp", cs32[:, :, c], _ap(cond)[c, :].rearrange("(kt p) -> p kt", p=128), writes=[cs32])
        s.op("act", lambda e: e.activation(out=csb[:], in_=cs32[:], func=AF.Silu), reads=[cs32], writes=[csb])
        gn = s.sbuf("gn", [128, DEPTH, 3, 8], F32)
        for i in range(3):
            for l in range(DEPTH):
                s.dma("sp", gn[:, l, i, :], _ap(norms[i])[l, :].rearrange("(kt p) -> p kt", p=128), writes=[gn])
        wa = [s.sbuf("wa%d" % i, [128, 8, 1024], BF16) for i in range(2)]
        badaT = s.sbuf("badaT", [128, 72], F32)
        modT = s.sbuf("modT", [128, 72, 2], F32)
        for l in range(DEPTH):
            s.dma("sp", badaT[:], _ap(b_ada)[l, :].rearrange("(ft p) -> p ft", p=128), writes=[badaT])
            pb = ps()
            for ch in range(9):
                w = wa[ch % 2]
                s.dma("pool", w[:], _ap(w_ada)[l, :, ch * 1024:(ch + 1) * 1024].rearrange("(kt p) f -> p kt f", p=128),
                      writes=[w])
                for f8 in range(8):
                    ft = ch * 8 + f8
                    mm_group(pb, pb[:, 2 * ft:2 * ft + 2],
                             [(w[:, kt, f8 * 128:(f8 + 1) * 128], csb[:, kt, :]) for kt in range(8)], [w, csb])
            for c in range(2):
                s.op("dve", lambda e, c=c: e.tensor_tensor(out=modT[:, :, c], in0=pb[:, c:144:2], in1=badaT[:], op=ALU.add),
                     reads=[pb, badaT], writes=[modT])
            for i in range(3):
                for c in range(2):
                    sh = modT[:, (3 * i) * 8:(3 * i) * 8 + 8, c]
                    sc = modT[:, (3 * i + 1) * 8:(3 * i + 1) * 8 + 8, c]
                    gt = modT[:, (3 * i + 2) * 8:(3 * i + 2) * 8 + 8, c]
                    s.op("dve", lambda e, sc=sc, l=l, i=i, c=c: e.scalar_tensor_tensor(
                        out=Aco[:, l, i, c, :], in0=sc, scalar=1.0, in1=gn[:, l, i, :], op0=ALU.add, op1=ALU.mult),
                        reads=[modT, gn], writes=[Aco])
                    s.op("dve", lambda e, sh=sh, l=l, i=i, c=c: e.tensor_copy(out=Bco[:, l, i, c, :], in_=sh),
                         reads=[modT], writes=[Bco])
                    s.op("dve", lambda e, gt=gt, l=l, i=i, c=c: e.tensor_scalar(
                        out=Gco[:, l, i, c, :], in0=gt, scalar1=(1.0 if i == 1 else 0.5), scalar2=None, op0=ALU.mult),
                        reads=[modT], writes=[Gco])
        if after_dma is not None:
            after_dma()
        s.barrier()
        s.release(m0)

    def load_xT(xT, blk):
        s.dma("sp", xT[:], _ap(xT_d)[:, blk * 512:(blk + 1) * 512].rearrange("(kt p) n -> p kt n", p=128),
              reads=[xT_d], writes=[xT])

    def store_xT(xT, blk):
        s.dma("sp", _ap(xT_d)[:, blk * 512:(blk + 1) * 512].rearrange("(kt p) n -> p kt n", p=128), xT[:],
              reads=[xT], writes=[xT_d])

    def load_x_tm(xT, xtm, blk):
        src = _ap(xp)[blk * 512:(blk + 1) * 512, :] if blk < 2 else _ap(xs)[(blk - 2) * 512:(blk - 1) * 512, :]
        for tt in range(4):
            s.dma("sp", xtm[:, 0, :], src[tt * 128:(tt + 1) * 128, :], writes=[xtm])
            for half in range(2):
                pb = ps()
                s.group("pe", [lambda e, k4=k4, half=half, pb=pb: e.transpose(pb[:, k4 * 128:(k4 + 1) * 128],
                                                                               xtm[:, 0, (half * 4 + k4) * 128:(half * 4 + k4 + 1) * 128], ident[:])
                               for k4 in range(4)], reads=[xtm, ident], writes=[pb])
                dst = xT[:, half * 4:(half + 1) * 4, tt * 128:(tt + 1) * 128]
                if half:
                    s.op("act", lambda e, dst=dst, pb=pb: e.copy(out=dst, in_=pb[:].rearrange("p (k n) -> p k n", k=4)),
                         reads=[pb], writes=[xT])
                else:
                    s.op("dve", lambda e, dst=dst, pb=pb: e.tensor_copy(out=dst, in_=pb[:].rearrange("p (k n) -> p k n", k=4)),
                         reads=[pb], writes=[xT])

    def store_y_tm(xT, ytm, blk):
        dst = _ap(yp)[blk * 512:(blk + 1) * 512, :] if blk < 2 else _ap(ys)[(blk - 2) * 512:(blk - 1) * 512, :]
        ydr = yp if blk < 2 else ys
        for tt in range(4):
            for half in range(2):
                pb = ps()
                s.group("pe", [lambda e, k4=k4, tt=tt, half=half, pb=pb: e.transpose(
                    pb[:, k4 * 128:(k4 + 1) * 128], xT[:, half * 4 + k4, tt * 128:(tt + 1) * 128], ident[:])
                    for k4 in range(4)], reads=[xT, ident], writes=[pb])
                if half:
                    s.op("act", lambda e, pb=pb: e.copy(out=ytm[:, 0, 512:1024], in_=pb[:]), reads=[pb], writes=[ytm])
                else:
                    s.op("dve", lambda e, pb=pb: e.tensor_copy(out=ytm[:, 0, 0:512], in_=pb[:]), reads=[pb], writes=[ytm])
            s.dma("sp", dst[tt * 128:(tt + 1) * 128, :], ytm[:, 0, :], reads=[ytm], writes=[ydr])

    def norm_sq(xT, hb):
        for kt in range(8):
            s.op("act", lambda e, kt=kt: e.activation(out=hb[:, kt, :], in_=xT[:, kt, :], func=AF.Square),
                 reads=[xT], writes=[hb])

    def norm_mod(xT, hb, tmp2, rstd, l, i, c, squares_done=False):
        if not squares_done:
            norm_sq(xT, hb)
        pb = ps()
        mm_group(pb, pb[:], [(mean_bf[:], hb[:, kt, :]) for kt in range(8)], [mean_bf, hb])
        rsqrt(rstd, rstd[:], pb, pb[:], eps6)
        for kt in range(8):
            t = tmp2[kt % 2]
            s.op("dve", lambda e, kt=kt, t=t: e.tensor_tensor(out=t[:], in0=xT[:, kt, :], in1=rstd[:], op=ALU.mult),
                 reads=[xT, rstd], writes=[t])
            s.op("act", lambda e, kt=kt, t=t: e.activation(out=hb[:, kt, :], in_=t[:], func=AF.Identity,
                                                           scale=Aco[:, l, i, c, kt:kt + 1], bias=Bco[:, l, i, c, kt:kt + 1]),
                 reads=[t, Aco, Bco], writes=[hb])

    def alloc_ffn_w():
        W1 = [s.sbuf("W1_%d" % j, [128, 8, 512], BF16) for j in range(11)]
        W2 = [s.sbuf("W2_%d" % j, [128, 2, 1024], BF16) for j in range(11)]
        return (W1, W2)

    def issue_ffn_w(W, l, win_d, wout_d):
        W1, W2 = W
        for j in (0, 5, 1, 6, 2, 7, 3, 8, 4, 9, 10):
            s.dma("pool", W1[j][:], _ap(win_d)[l, :, j * 512:(j + 1) * 512].rearrange("(kt p) f -> p kt f", p=128),
                  writes=[W1[j]])
        for j in range(11):
            s.dma("pool", W2[j][:], _ap(wout_d)[l, j * 256:(j + 1) * 256, :].rearrange("(kt p) f -> p kt f", p=128),
                  writes=[W2[j]])

    def ffn_phase(l, i, win_d, wout_d, first, last, W=None):
        m0 = s.mark()
        if W is None:
            W = alloc_ffn_w()
            issue_ffn_w(W, l, win_d, wout_d)
        W1, W2 = W
        xTs = [s.sbuf("xT%d" % j, [128, 8, 512], F32) for j in range(2)]
        hb = s.sbuf("hb", [128, 8, 512], BF16)
        actb = s.sbuf("actb", [128, 22, 512], BF16)
        tmp2 = [s.sbuf("tmp%d" % j, [128, 512], F32) for j in range(2)]
        sg2 = tmp2
        rstd = s.sbuf("rstd", [128, 512], F32)
        if os.environ.get("MK_VERBOSE"):
            print("ffn phase free sbuf bytes/partition:", nc.sbuf_bytes_remaining, "first/last", first, last)
        xtm = s.sbuf("xtm", [128, 1, 1024], F32) if (first or last) else None

        def w1cols(col):
            return W1[col // 512], col % 512

        if last:
            items = [("std", 0), ("std", 1)] + [("own", i) for i in range(4)]
        else:
            items = [("std", blk) for blk in range(NB)]

        def fetch(it):
            kind, b_ = items[it]
            xT_ = xTs[it % 2]
            if kind == "own":
                load_xT(xT_, 2 + b_)
                for kt in range(8):
                    t = tmp2[kt % 2]
                    s.dma("sp", t[:], _ap(xT_d)[kt * 128:(kt + 1) * 128, (6 + b_) * 512:(7 + b_) * 512], reads=[xT_d], writes=[t])
                    s.op("dve", lambda e, kt=kt, xT_=xT_: e.tensor_scalar(out=xT_[:, kt, :], in0=xT_[:, kt, :], scalar1=hsel[:, 0:1],
                                                                          scalar2=None, op0=ALU.mult), reads=[xT_, hsel], writes=[xT_])
                    s.op("dve", lambda e, kt=kt, xT_=xT_, t=t: e.scalar_tensor_tensor(
                        out=xT_[:, kt, :], in0=t[:], scalar=hsel[:, 1:2], in1=xT_[:, kt, :], op0=ALU.mult, op1=ALU.add),
                        reads=[t, hsel, xT_], writes=[xT_])
            elif first:
                load_x_tm(xT_, xtm, b_)
            else:
                load_xT(xT_, b_)

        def cond_of(it):
            kind, blk = items[it]
            return 0 if (kind == "std" and blk < 2) else 1

        fetch(0)
        normed = [False] * len(items)
        for it, (kind, blk) in enumerate(items):
            c = cond_of(it)
            xT = xTs[it % 2]
            nxt_own = it + 1 < len(items) and items[it + 1][0] == "own"
            prenorm = it + 1 < len(items) and not nxt_own
            if it + 1 < len(items) and not first and not nxt_own:
                fetch(it + 1)
            if not normed[it]:
                norm_mod(xT, hb, tmp2, rstd, l, i, c)
            for j in range(22):
                wg, og = w1cols(j * 128)
                wu, ou = w1cols(DFF + j * 128)
                pg = ps()
                mm_group(pg, pg[:], [(wg[:, kt, og:og + 128], hb[:, kt, :]) for kt in range(8)], [wg, hb])
                pu = ps()
                mm_group(pu, pu[:], [(wu[:, kt, ou:ou + 128], hb[:, kt, :]) for kt in range(8)], [wu, hb])
                sg = sg2[j % 2]
                s.op("act", lambda e, sg=sg, pg=pg: e.activation(out=sg[:], in_=pg[:], func=AF.Silu), reads=[pg], writes=[sg])
                s.op("dve", lambda e, sg=sg, pu=pu, j=j: e.tensor_tensor(out=actb[:, j, :], in0=sg[:], in1=pu[:], op=ALU.mult),
                     reads=[sg, pu], writes=[actb])
            if it + 1 < len(items) and first:
                fetch(it + 1)
            if prenorm:
                norm_sq(xTs[(it + 1) % 2], hb)
            for ft in range(8):
                if ft == 4 and prenorm:
                    norm_mod(xTs[(it + 1) % 2], hb, tmp2, rstd, l, i, cond_of(it + 1), squares_done=True)
                    normed[it + 1] = True
                po = ps()
                mm_group(po, po[:], [(W2[j // 2][:, j % 2, ft * 128:(ft + 1) * 128], actb[:, j, :]) for j in range(22)],
                         W2 + [actb])
                s.op("dve", lambda e, po=po, ft=ft, xT=xT: e.scalar_tensor_tensor(
                    out=xT[:, ft, :], in0=po[:], scalar=Gco[:, l, i, c, ft:ft + 1], in1=xT[:, ft, :], op0=ALU.mult, op1=ALU.add),
                    reads=[po, Gco, xT], writes=[xT])
            if last:
                store_y_tm(xT, xtm, blk if kind == "std" else 2 + blk)
            else:
                store_xT(xT, blk)
            if nxt_own:
                fetch(it + 1)
        s.barrier()
        s.release(m0)

    def proj_phase(l):
        m0 = s.mark()
        WI = [s.sbuf("WI_%d" % j, [128, 8, 256], BF16) for j in range(9)]
        for j in range(9):
            s.dma("pool", WI[j][:], _ap(w_in)[l, :, j * 256:(j + 1) * 256].rearrange("(kt p) f -> p kt f", p=128),
                  writes=[WI[j]])
        xTs = [s.sbuf("xT%d" % j, [128, 8, 512], F32) for j in range(2)]
        hbs = [s.sbuf("hb%d" % j, [128, 8, 512], BF16) for j in range(2)]
        tmp2 = [s.sbuf("tmp%d" % j, [128, 512], F32) for j in range(2)]
        rstd = s.sbuf("rstd", [128, 512], F32)
        gq = s.sbuf("gq", [128, 2], F32)
        for h in range(2):
            s.dma("sp", gq[h * 64:(h + 1) * 64, 0:1], _ap(qn_g)[l, :].rearrange("(p o) -> p o", o=1), writes=[gq])
            s.dma("sp", gq[h * 64:(h + 1) * 64, 1:2], _ap(kn_g)[l, :].rearrange("(p o) -> p o", o=1), writes=[gq])
        gk_b = s.sbuf("gk_b", [128, 64], F32)
        s.dma("sp", gk_b[:], _ap(kn_g)[l:l + 1, :].to_broadcast([128, 64]), writes=[gk_b])
        NSLOT = 3

        class Slot:
            pass

        slots = []
        for i in range(NSLOT):
            sl = Slot()
            sl.sq = s.sbuf("sq%d" % i, [128, 512], BF16)
            sl.f32 = s.sbuf("f32_%d" % i, [128, 512], F32)
            sl.r = s.sbuf("r%d" % i, [128, 512], F32)
            sl.sb = s.sbuf("sb%d" % i, [128, 512], BF16)
            sl.ga = s.sbuf("ga%d" % i, [128, 512], F32)
            sl.gb = s.sbuf("gb%d" % i, [128, 512], F32)
            sl.gc = s.sbuf("gc%d" % i, [128, 512], F32)
            sl.v32 = s.sbuf("v32_%d" % i, [128, 512], F32)
            sl.k32 = s.sbuf("k32_%d" % i, [128, 512], F32)
            sl.gv = s.sbuf("gv%d" % i, [128, 256], F32)
            sl.vA = s.sbuf("vA%d" % i, [128, 256], BF16)
            sl.vB = s.sbuf("vB%d" % i, [128, 256], BF16)
            s.op("dve", lambda e, sl=sl: e.memset(sl.vA[:], 0.0), writes=[sl.vA])
            s.op("dve", lambda e, sl=sl: e.memset(sl.vB[:], 0.0), writes=[sl.vB])
            sl.stat = s.sbuf("stat%d" % i, [128, 6], F32)
            sl.mv = s.sbuf("mv%d" % i, [128, 2], F32)
            sl.rs1 = s.sbuf("rs1_%d" % i, [128, 1], F32)
            sl.s8 = s.sbuf("s8_%d" % i, [128, 8], F32)
            slots.append(sl)

        def run_chains(makers):
            pending = list(makers)
            active = [None] * NSLOT
            while pending or any(g is not None for g in active):
                for i in range(NSLOT):
                    if active[i] is None and pending:
                        active[i] = pending.pop(0)(slots[i])
                    if active[i] is not None:
                        try:
                            next(active[i])
                        except StopIteration:
                            active[i] = None

        def fm_tile(hb, col):
            w, o = WI[col // 256], col % 256
            pb = ps()
            mm_group(pb, pb[:], [(w[:, kt, o:o + 128], hb[:, kt, :]) for kt in range(8)], [w, hb])
            return pb

        def rsqrt_gen(dst_b, dst_ap, src_b, src_ap, eps_b, scale=1.0):
            P = dst_ap.shape[0]
            s.op("act", lambda e: e.activation(out=dst_ap, in_=src_ap, func=AF.Sqrt, scale=scale, bias=eps_b[0:P, 0:1]),
                 reads=[src_b, eps_b], writes=[dst_b])
            yield
            s.op("dve", lambda e: e.reciprocal(out=dst_ap, in_=dst_ap), reads=[dst_b], writes=[dst_b])

        def gelu_gen(sl, src_b, src_ap, dst_b, dst_ap, n):
            a, b2, c2 = sl.ga, sl.gb, sl.gc
            s.op("act", lambda e: e.activation(out=a[:, 0:n], in_=src_ap, func=AF.Square), reads=[src_b], writes=[a])
            yield
            s.op("dve", lambda e: e.tensor_scalar(out=a[:, 0:n], in0=a[:, 0:n], scalar1=0.044715, scalar2=1.0,
                                                  op0=ALU.mult, op1=ALU.add), reads=[a], writes=[a])
            s.op("dve", lambda e: e.tensor_tensor(out=b2[:, 0:n], in0=a[:, 0:n], in1=src_ap, op=ALU.mult),
                 reads=[a, src_b], writes=[b2])
            yield
            s.op("act", lambda e: e.activation(out=c2[:, 0:n], in_=b2[:, 0:n], func=AF.Sigmoid, scale=1.5957691216057308),
                 reads=[b2], writes=[c2])
            yield
            s.op("dve", lambda e: e.tensor_tensor(out=dst_ap, in0=c2[:, 0:n], in1=src_ap, op=ALU.mult),
                 reads=[c2, src_b], writes=[dst_b])

        for blk in range(NB):
            c = 0 if blk < 2 else 1
            tok = slice(blk * 512, (blk + 1) * 512)
            xT = xTs[blk % 2]
            hb = hbs[blk % 2]
            if blk == 0:
                load_xT(xT, 0)
                norm_mod(xT, hb, tmp2, rstd, l, 1, c)
            if blk + 1 < NB:
                load_xT(xTs[(blk + 1) % 2], blk + 1)

            def ch_norm(nb):
                def gen(sl):
                    xn, hn, cn = xTs[nb % 2], hbs[nb % 2], (0 if nb < 2 else 1)
                    for kt in range(8):
                        s.op("act", lambda e, kt=kt: e.activation(out=hn[:, kt, :], in_=xn[:, kt, :], func=AF.Square),
                             reads=[xn], writes=[hn])
                        if kt % 4 == 3:
                            yield
                    pb = ps()
                    mm_group(pb, pb[:], [(mean_bf[:], hn[:, kt, :]) for kt in range(8)], [mean_bf, hn])
                    yield
                    yield from rsqrt_gen(rstd, rstd[:], pb, pb[:], eps6)
                    for kt in range(8):
                        t = tmp2[kt % 2]
                        s.op("dve", lambda e, kt=kt, t=t: e.tensor_tensor(out=t[:], in0=xn[:, kt, :], in1=rstd[:], op=ALU.mult),
                             reads=[xn, rstd], writes=[t])
                        if kt % 2 == 0:
                            yield
                        s.op("act", lambda e, kt=kt, t=t: e.activation(out=hn[:, kt, :], in_=t[:], func=AF.Identity,
                                                                       scale=Aco[:, l, 1, cn, kt:kt + 1], bias=Bco[:, l, 1, cn, kt:kt + 1]),
                             reads=[t, Aco, Bco], writes=[hn])
                return gen

            def ch_xssm(a):
                def gen(sl):
                    pb = fm_tile(hb, a * 128)
                    yield
                    s.op("act", lambda e: e.copy(out=sl.sb[:], in_=pb[:]), reads=[pb], writes=[sl.sb])
                    yield
                    s.dma("sp", _ap(us_d)[a * 128:(a + 1) * 128, tok], sl.sb[:], reads=[sl.sb], writes=[us_d])
                return gen

            def ch_qk(qk, t4):
                def gen(sl):
                    pb = fm_tile(hb, 256 + qk * 512 + t4 * 128)
                    yield
                    s.op("act", lambda e: e.activation(out=sl.sq[:], in_=pb[:], func=AF.Square), reads=[pb], writes=[sl.sq])
                    s.op("act", lambda e: e.copy(out=sl.f32[:], in_=pb[:]), reads=[pb], writes=[sl.f32])
                    yield
                    pm = ps()
                    mm_group(pm, pm[:], [(bd_bf[:], sl.sq[:])], [bd_bf, sl.sq])
                    yield
                    yield from rsqrt_gen(sl.r, sl.r[:], pm, pm[:], eps6)
                    s.op("dve", lambda e: e.scalar_tensor_tensor(
                        out=sl.sb[:], in0=sl.f32[:], scalar=gq[:, qk:qk + 1], in1=sl.r[:], op0=ALU.mult, op1=ALU.mult),
                        reads=[sl.f32, gq, sl.r], writes=[sl.sb])
                    yield
                    dd = q_d if qk == 0 else k_d
                    s.dma("sp", _ap(dd)[t4 * 128:(t4 + 1) * 128, tok], sl.sb[:], reads=[sl.sb], writes=[dd])
                return gen

            def ch_ug(a):
                def gen(sl):
                    pb = fm_tile(hb, 1792 + a * 128)
                    yield
                    s.op("act", lambda e: e.copy(out=sl.f32[:], in_=pb[:]), reads=[pb], writes=[sl.f32])
                    yield
                    yield from gelu_gen(sl, sl.f32, sl.f32[:], sl.sb, sl.sb[:], 512)
                    yield
                    s.dma("sp", _ap(ug_d)[a * 128:(a + 1) * 128, tok], sl.sb[:], reads=[sl.sb], writes=[ug_d])
                return gen

            def ch_v(tt):
                def gen(sl):
                    trow = slice(blk * 512 + tt * 128, blk * 512 + (tt + 1) * 128)
                    hT = [hb[:, kt, tt * 128:(tt + 1) * 128] for kt in range(8)]
                    pv = ps()
                    for half in range(2):
                        w = WI[5 + half]
                        mm_group(pv, pv[:, half * 256:(half + 1) * 256], [(hT[kt], w[:, kt, :]) for kt in range(8)], [w, hb])
                    yield
                    s.op("act", lambda e: e.copy(out=sl.sb[:], in_=pv[:]), reads=[pv], writes=[sl.sb])
                    if blk < 2:
                        s.op("act", lambda e: e.copy(out=sl.v32[:], in_=pv[:]), reads=[pv], writes=[sl.v32])
                    yield
                    s.dma("sp", _ap(v_d)[trow, :], sl.sb[:], reads=[sl.sb], writes=[v_d])
                    if blk < 2:
                        seq = (blk * 512 + tt * 128) // 256
                        pos = (tt % 2) * 128
                        s.dma("sp", _ap(nv)[seq, l, pos:pos + 128, :], sl.v32[:], reads=[sl.v32], writes=[nv])
                return gen

            def ch_k(tt):
                def gen(sl):
                    hT = [hb[:, kt, tt * 128:(tt + 1) * 128] for kt in range(8)]
                    seq = (blk * 512 + tt * 128) // 256
                    pos = (tt % 2) * 128
                    pk = ps()
                    for half in range(2):
                        w = WI[3 + half]
                        mm_group(pk, pk[:, half * 256:(half + 1) * 256], [(hT[kt], w[:, kt, :]) for kt in range(8)], [w, hb])
                    yield
                    s.op("act", lambda e: e.activation(out=sl.ga[:], in_=pk[:], func=AF.Square), reads=[pk], writes=[sl.ga])
                    s.op("act", lambda e: e.copy(out=sl.f32[:], in_=pk[:]), reads=[pk], writes=[sl.f32])
                    yield
                    s.op("dve", lambda e: e.tensor_reduce(out=sl.s8[:], in_=sl.ga[:].rearrange("p (h d) -> p h d", d=64),
                                                          axis=AX.X, op=ALU.add), reads=[sl.ga], writes=[sl.s8])
                    yield
                    yield from rsqrt_gen(sl.s8, sl.s8[:], sl.s8, sl.s8[:], eps6, scale=1.0 / 64.0)
                    for h in range(8):
                        s.op("dve", lambda e, h=h: e.scalar_tensor_tensor(
                            out=sl.k32[:, h * 64:(h + 1) * 64], in0=sl.f32[:, h * 64:(h + 1) * 64], scalar=sl.s8[:, h:h + 1],
                            in1=gk_b[:], op0=ALU.mult, op1=ALU.mult), reads=[sl.f32, sl.s8, gk_b], writes=[sl.k32])
                    yield
                    s.dma("sp", _ap(nk)[seq, l, pos:pos + 128, :], sl.k32[:], reads=[sl.k32], writes=[nk])
                return gen

            def ch_vg(tt):
                def gen(sl):
                    trow = slice(blk * 512 + tt * 128, blk * 512 + (tt + 1) * 128)
                    hT = [hb[:, kt, tt * 128:(tt + 1) * 128] for kt in range(8)]
                    pg = ps()
                    mm_group(pg, pg[:, 0:256], [(hT[kt], WI[8][:, kt, :]) for kt in range(8)], [WI[8], hb])
                    yield
                    s.op("act", lambda e: e.copy(out=sl.f32[:, 0:256], in_=pg[:, 0:256]), reads=[pg], writes=[sl.f32])
                    yield
                    yield from gelu_gen(sl, sl.f32, sl.f32[:, 0:256], sl.gv, sl.gv[:, 0:256], 256)
                    s.op("dve", lambda e: e.bn_stats(out=sl.stat[:], in_=sl.gv[:, 0:256]), reads=[sl.gv], writes=[sl.stat])
                    s.op("dve", lambda e: e.bn_aggr(out=sl.mv[:], in_=sl.stat[:]), reads=[sl.stat], writes=[sl.mv])
                    yield
                    yield from rsqrt_gen(sl.rs1, sl.rs1[:], sl.mv, sl.mv[:, 1:2], eps5)
                    for (dst, off) in ((sl.vA, 0), (sl.vB, 64)):
                        for a in range(2):
                            cs_ = slice(a * 128 + off, a * 128 + off + 64)
                            s.op("dve", lambda e, dst=dst, cs_=cs_: e.tensor_scalar(
                                out=dst[:, cs_], in0=sl.gv[:, cs_], scalar1=sl.mv[:, 0:1], scalar2=sl.rs1[:, 0:1],
                                op0=ALU.subtract, op1=ALU.mult), reads=[sl.gv, sl.mv, sl.rs1], writes=[dst])
                    yield
                    s.dma("sp", _ap(va_d)[trow, :], sl.vA[:], reads=[sl.vA], writes=[va_d])
                    s.dma("sp", _ap(vb_d)[trow, :], sl.vB[:], reads=[sl.vB], writes=[vb_d])
                return gen

            chains = [ch_xssm(0), ch_xssm(1)]
            chains += [ch_qk(qk, t4) for qk in range(2) for t4 in range(4)]
            chains += [ch_ug(0), ch_ug(1)]
            for tt in range(4):
                chains.append(ch_v(tt))
                if blk < 2:
                    chains.append(ch_k(tt))
                chains.append(ch_vg(tt))
            if blk + 1 < NB:
                chains.insert(10, ch_norm(blk + 1))
            run_chains(chains)
        s.barrier()
        s.release(m0)

    def outproj_phase(l, WO=None):
        m0 = s.mark()
        if WO is None:
            WO = s.sbuf("WO", [128, 8, 1024], BF16)
            s.dma("pool", WO[:], _ap(w_out)[l].rearrange("(kt p) f -> p kt f", p=128), writes=[WO])
        xTs = [s.sbuf("xT%d" % j, [128, 8, 512], F32) for j in range(2)]
        yms = [s.sbuf("ym%d" % j, [128, 8, 512], BF16) for j in range(2)]
        def fetch(blk):
            load_xT(xTs[blk % 2], blk)
            s.dma("sp", yms[blk % 2][:], _ap(ymix_d)[:, blk * 512:(blk + 1) * 512].rearrange("(kt p) n -> p kt n", p=128),
                  reads=[ymix_d], writes=[yms[blk % 2]])

        fetch(0)
        for blk in range(NB):
            c = 0 if blk < 2 else 1
            xT, ym = xTs[blk % 2], yms[blk % 2]
            if blk + 1 < NB:
                fetch(blk + 1)
            for ft in range(8):
                po = ps()
                mm_group(po, po[:], [(WO[:, kt, ft * 128:(ft + 1) * 128], ym[:, kt, :]) for kt in range(8)], [WO, ym])
                s.op("dve", lambda e, po=po, ft=ft, xT=xT: e.scalar_tensor_tensor(
                    out=xT[:, ft, :], in0=po[:], scalar=Gco[:, l, 1, c, ft:ft + 1], in1=xT[:, ft, :], op0=ALU.mult, op1=ALU.add),
                    reads=[po, Gco, xT], writes=[xT])
            store_xT(xT, blk)
        s.barrier()
        s.release(m0)

    def ssm_part(l, pump=None):
        m0 = s.mark()
        uS = s.sbuf("uS", [128, NT], BF16)
        Y = s.sbuf("Y", [128, 2, NT], F32)
        dco = s.sbuf("dco", [128, 2], F32)
        s.dma("sp", dco[:], _ap(ssm_d)[l, :].rearrange("(a p) -> p a", p=128), writes=[dco])
        iota = s.sbuf("iota", [128, 1024], F32)
        s.dma("sp", iota[:], _ap(c_iota), writes=[iota])
        P8 = lambda n: s.sbuf(n, [128, 8], F32)
        lr, li, ldt, dtv, ar, ai, rho, ang, sinv, cosv, lbr, lbi = [P8("p8_%d" % i) for i in range(12)]
        nr, den, kr, ki, t8a, t8b, thr, nf8, cN, sN = [P8("q8_%d" % i) for i in range(10)]
        ii8 = s.sbuf("ii8", [128, 8], mybir.dt.int32)
        BT = s.sbuf("BT", [128, 8, 2, 128], BF16)
        CT = s.sbuf("CT", [128, 8, 2, 128], BF16)
        s0 = s.sbuf("s0", [128, 8, 2], F32)
        init = s.sbuf("init", [128, 8, 2], F32)
        FIN = [s.sbuf("FIN%d" % i, [128, 8, 2], F32) for i in range(4)]
        zl = s.sbuf("zl", [128, 2], F32)
        tq = s.sbuf("tq", [128, 2], F32)

        def dv(fn, reads, writes, e="dve"):
            s.op(e, fn, reads=reads, writes=writes)

        C1 = 6.28125
        C2 = TWO_PI - C1
        PI_S = 3.1415925
        hpi = s.sbuf("hpi", [128, 1], F32)
        dv(lambda e: e.memset(hpi[:], 0.5 * math.pi), [], [hpi])

        def sincos(ang_b, ang_ap, r_b, r_ap, sin_b, sin_ap, cos_b, cos_ap, ii_b, nf_b, n):
            dv(lambda e: e.tensor_scalar(out=ii_b[:, 0:n], in0=ang_ap, scalar1=1.0 / TWO_PI, scalar2=None, op0=ALU.mult), [ang_b], [ii_b])
            dv(lambda e: e.tensor_copy(out=nf_b[:, 0:n], in_=ii_b[:, 0:n]), [ii_b], [nf_b])
            dv(lambda e: e.scalar_tensor_tensor(out=r_ap, in0=nf_b[:, 0:n], scalar=-C1, in1=ang_ap, op0=ALU.mult, op1=ALU.add),
               [nf_b, ang_b], [r_b])
            dv(lambda e: e.scalar_tensor_tensor(out=r_ap, in0=nf_b[:, 0:n], scalar=-C2, in1=r_ap, op0=ALU.mult, op1=ALU.add),
               [nf_b, r_b], [r_b])
            dv(lambda e: e.tensor_scalar(out=r_ap, in0=r_ap, scalar1=-PI_S, scalar2=None, op0=ALU.max), [r_b], [r_b])
            dv(lambda e: e.tensor_scalar(out=r_ap, in0=r_ap, scalar1=PI_S, scalar2=None, op0=ALU.min), [r_b], [r_b])
            s.op("act", lambda e: e.activation(out=sin_ap, in_=r_ap, func=AF.Sin), reads=[r_b], writes=[sin_b])
            s.op("act", lambda e: e.activation(out=nf_b[:, 0:n], in_=r_ap, func=AF.Abs), reads=[r_b], writes=[nf_b])
            s.op("act", lambda e: e.activation(out=cos_ap, in_=nf_b[:, 0:n], func=AF.Sin, scale=-1.0, bias=hpi[:, 0:1]),
                 reads=[nf_b, hpi], writes=[cos_b])

        for d in range(2):
            s.dma("sp", lr[:], _ap(lam_re)[l, d, :].rearrange("(j q) -> q j", q=128), writes=[lr])
            s.dma("sp", li[:], _ap(lam_im)[l, d, :].rearrange("(j q) -> q j", q=128), writes=[li])
            for h in range(2):
                s.dma("sp", ldt[h * 64:(h + 1) * 64, :],
                      _ap(log_dt)[l, d, :].rearrange("(j h) -> h j", h=2)[h:h + 1, :].to_broadcast([64, 8]), writes=[ldt])
            s.op("act", lambda e: e.activation(out=dtv[:], in_=ldt[:], func=AF.Exp), reads=[ldt], writes=[dtv])
            dv(lambda e: e.tensor_tensor(out=ar[:], in0=lr[:], in1=dtv[:], op=ALU.mult), [lr, dtv], [ar])
            dv(lambda e: e.tensor_tensor(out=ai[:], in0=li[:], in1=dtv[:], op=ALU.mult), [li, dtv], [ai])
            s.op("act", lambda e: e.activation(out=rho[:], in_=ar[:], func=AF.Exp), reads=[ar], writes=[rho])
            sincos(ai, ai[:], thr, thr[:], sinv, sinv[:], cosv, cosv[:], ii8, nf8, 8)
            dv(lambda e: e.tensor_tensor(out=lbr[:], in0=rho[:], in1=cosv[:], op=ALU.mult), [rho, cosv], [lbr])
            dv(lambda e: e.tensor_tensor(out=lbi[:], in0=rho[:], in1=sinv[:], op=ALU.mult), [rho, sinv], [lbi])
            dv(lambda e: e.tensor_scalar(out=nr[:], in0=lbr[:], scalar1=-1.0, scalar2=None, op0=ALU.add), [lbr], [nr])
            dv(lambda e: e.tensor_tensor(out=den[:], in0=lr[:], in1=lr[:], op=ALU.mult), [lr], [den])
            dv(lambda e: e.tensor_tensor(out=t8a[:], in0=li[:], in1=li[:], op=ALU.mult), [li], [t8a])
            dv(lambda e: e.tensor_tensor(out=den[:], in0=den[:], in1=t8a[:], op=ALU.add), [den, t8a], [den])
            dv(lambda e: e.reciprocal(out=den[:], in_=den[:]), [den], [den])
            dv(lambda e: e.tensor_tensor(out=t8a[:], in0=nr[:], in1=lr[:], op=ALU.mult), [nr, lr], [t8a])
            dv(lambda e: e.tensor_tensor(out=t8b[:], in0=lbi[:], in1=li[:], op=ALU.mult), [lbi, li], [t8b])
            dv(lambda e: e.tensor_tensor(out=t8a[:], in0=t8a[:], in1=t8b[:], op=ALU.add), [t8a, t8b], [t8a])
            dv(lambda e: e.tensor_tensor(out=kr[:], in0=t8a[:], in1=den[:], op=ALU.mult), [t8a, den], [kr])
            dv(lambda e: e.tensor_tensor(out=t8a[:], in0=lbi[:], in1=lr[:], op=ALU.mult), [lbi, lr], [t8a])
            dv(lambda e: e.tensor_tensor(out=t8b[:], in0=nr[:], in1=li[:], op=ALU.mult), [nr, li], [t8b])
            dv(lambda e: e.tensor_tensor(out=t8a[:], in0=t8a[:], in1=t8b[:], op=ALU.subtract), [t8a, t8b], [t8a])
            dv(lambda e: e.tensor_tensor(out=ki[:], in0=t8a[:], in1=den[:], op=ALU.mult), [t8a, den], [ki])
            m1 = s.mark()
            Bn = [s.sbuf("Bn%d" % i, [128, 8, 16], F32) for i in range(2)]
            Zp = [s.sbuf("Zp%d" % i, [128, 8, 128], F32) for i in range(2)]
            tz = s.sbuf("tz", [128, 16], F32)
            INc = [s.sbuf("INc%d" % i, [128, 8, 128], F32) for i in range(2)]
            s.dma("sp", Bn[0][:], _ap(b_re)[l, d].rearrange("(j q) c -> q j c", q=128), writes=[Bn[0]])
            s.dma("sp", Bn[1][:], _ap(b_im)[l, d].rearrange("(j q) c -> q j c", q=128), writes=[Bn[1]])
            for ri in range(2):
                dv(lambda e, ri=ri: e.memset(Zp[ri][:], 0.0), [], [Zp[ri]])
            for j in range(8):
                for h in range(2):
                    pr = slice(h * 64, (h + 1) * 64)
                    cs_ = slice(32 * (j % 4) + 16 * h, 32 * (j % 4) + 16 * h + 16)
                    dv(lambda e, j=j, pr=pr: e.tensor_scalar(out=tz[pr, :], in0=Bn[1][pr, j, :], scalar1=ki[pr, j:j + 1],
                                                             scalar2=None, op0=ALU.mult), [Bn[1], ki], [tz])
                    dv(lambda e, j=j, pr=pr, cs_=cs_: e.scalar_tensor_tensor(
                        out=Zp[0][pr, j, cs_], in0=Bn[0][pr, j, :], scalar=kr[pr, j:j + 1], in1=tz[pr, :],
                        op0=ALU.mult, op1=ALU.subtract), [Bn[0], kr, tz], [Zp[0]])
                    dv(lambda e, j=j, pr=pr: e.tensor_scalar(out=tz[pr, :], in0=Bn[0][pr, j, :], scalar1=ki[pr, j:j + 1],
                                                             scalar2=None, op0=ALU.mult), [Bn[0], ki], [tz])
                    dv(lambda e, j=j, pr=pr, cs_=cs_: e.scalar_tensor_tensor(
                        out=Zp[1][pr, j, cs_], in0=Bn[1][pr, j, :], scalar=kr[pr, j:j + 1], in1=tz[pr, :],
                        op0=ALU.mult, op1=ALU.add), [Bn[1], kr, tz], [Zp[1]])
            for ri, cd in enumerate((c_re, c_im)):
                dv(lambda e, ri=ri: e.memset(INc[ri][:], 0.0), [], [INc[ri]])
                for j in range(8):
                    for h in range(2):
                        g = 2 * j + h
                        r0 = 32 * (j % 4) + 16 * h
                        s.dma("sp", INc[ri][r0:r0 + 16, j, h * 64:(h + 1) * 64], _ap(cd)[l, d, g * 16:(g + 1) * 16, :],
                              reads=[INc[ri]], writes=[INc[ri]])
            for j in range(8):
                for ri in range(2):
                    pb = ps()
                    s.group("pe", [lambda e, pb=pb, j=j, ri=ri: e.transpose(pb[:, 0:128], Zp[ri][:, j, :], ident[:])],
                            reads=[Zp[ri], ident], writes=[pb])
                    s.op("act", lambda e, pb=pb, j=j, ri=ri: e.copy(out=BT[:, j, ri, :], in_=pb[:, 0:128]), reads=[pb], writes=[BT])
                    pc = ps()
                    s.group("pe", [lambda e, pc=pc, j=j, ri=ri: e.transpose(pc[:, 0:128], INc[ri][:, j, :], ident[:])],
                            reads=[INc[ri], ident], writes=[pc])
                    s.op("act", lambda e, pc=pc, j=j, ri=ri: e.activation(out=CT[:, j, ri, :], in_=pc[:, 0:128], func=AF.Copy,
                                                                        scale=(1.0 if ri == 0 else -1.0)),
                         reads=[pc], writes=[CT])
            s.dma("sp", s0[:], _ap(sst)[l, d].rearrange("(j q) r -> q j r", q=128), writes=[s0])
            dv(lambda e: e.tensor_tensor(out=t8a[:], in0=cosv[:], in1=s0[:, :, 0], op=ALU.mult), [cosv, s0], [t8a])
            dv(lambda e: e.tensor_tensor(out=t8b[:], in0=sinv[:], in1=s0[:, :, 1], op=ALU.mult), [sinv, s0], [t8b])
            dv(lambda e: e.tensor_tensor(out=init[:, :, 0], in0=t8a[:], in1=t8b[:], op=ALU.subtract), [t8a, t8b], [init])
            dv(lambda e: e.tensor_tensor(out=t8a[:], in0=sinv[:], in1=s0[:, :, 0], op=ALU.mult), [sinv, s0], [t8a])
            dv(lambda e: e.tensor_tensor(out=t8b[:], in0=cosv[:], in1=s0[:, :, 1], op=ALU.mult), [cosv, s0], [t8b])
            dv(lambda e: e.tensor_tensor(out=init[:, :, 1], in0=t8a[:], in1=t8b[:], op=ALU.add), [t8a, t8b], [init])
            s.barrier()
            s.release(m1)
            m2 = s.mark()
            rhoT = s.sbuf("rhoT", [128, 4, 1024], F32)
            tabc = s.sbuf("tabc", [128, 1024], F32)
            tabs = s.sbuf("tabs", [128, 1024], F32)
            ccol = s.sbuf("ccol", [128, 4, 2], F32)
            scol = s.sbuf("scol", [128, 4, 2], F32)
            angt = s.sbuf("angt", [128, 1024], F32)
            ang2 = s.sbuf("ang2", [128, 1024], F32)
            iiT = s.sbuf("iiT", [128, 1024], mybir.dt.int32)
            nfT = s.sbuf("nfT", [128, 1024], F32)
            tcb = s.sbuf("tcb", [128, 4, 1024], BF16)
            tsb = s.sbuf("tsb", [128, 4, 1024], BF16)
            tD_ = [[s.sbuf("tD%d_%d" % (k, i), [128, 1024], BF16) for i in range(2)] for k in range(1)]
            tP_ = [[s.sbuf("tP%d_%d" % (k, i), [128, 1024], BF16) for i in range(2)] for k in range(1)]
            wb_ = [[s.sbuf("wb%d_%d" % (k, i), [128, 1024], BF16) for i in range(2)] for k in range(1)]
            wri_ = [[s.sbuf("wri%d_%d" % (k, i), [128, 1024], F32) for i in range(2)] for k in range(1)]
            zb_ = [[s.sbuf("zb%d_%d" % (k, i), [128, 1024], BF16) for i in range(2)] for k in range(1)]
            bsb_ = [[s.sbuf("bsb%d_%d" % (k, i), [128, 1024], BF16) for i in range(2)] for k in range(1)]
            ucnt = [0]
            fq = s.sbuf("fq", [128, 4], F32)
            Sb = [[s.sbuf("Sb%d_%d" % (j4, ri), [128, 1024], BF16) for ri in range(2)] for j4 in range(4)]
            if os.environ.get("MK_VERBOSE"):
                print("ssm free sbuf bytes/partition:", nc.sbuf_bytes_remaining)
            units = [("p", sq, sq * 256, 256) for sq in range(4)]
            for kk in range(4):
                knat = kk if d == 0 else 3 - kk
                units.append(("s", kk, NP_TOK + knat * 1024, 1024))
            for a in range(2):
                s.dma("sp", uS[:], _ap(us_d)[a * 128:(a + 1) * 128, :], reads=[us_d], writes=[uS])
                for j4 in range(4):
                    j = a * 4 + j4
                    dv(lambda e, j=j: e.tensor_scalar(out=angt[:], in0=iota[:], scalar1=thr[:, j:j + 1], scalar2=None, op0=ALU.mult),
                       [iota, thr], [angt])
                    sincos(angt, angt[:], ang2, ang2[:], tabs, tabs[:], tabc, tabc[:], iiT, nfT, 1024)
                    s.op("act", lambda e, j=j, j4=j4: e.activation(out=rhoT[:, j4, :], in_=iota[:], func=AF.Identity, scale=0.0,
                                                                 bias=rho[:, j:j + 1]), reads=[iota, rho], writes=[rhoT])
                    s.op("act", lambda e, j4=j4: e.copy(out=tcb[:, j4, :], in_=tabc[:]), reads=[tabc], writes=[tcb])
                    s.op("act", lambda e, j4=j4: e.copy(out=tsb[:, j4, :], in_=tabs[:]), reads=[tabs], writes=[tsb])
                    for ci, col in enumerate((255, 1023)):
                        dv(lambda e, j4=j4, ci=ci, col=col: e.tensor_copy(out=ccol[:, j4, ci:ci + 1], in_=tabc[:, col:col + 1]), [tabc], [ccol])
                        dv(lambda e, j4=j4, ci=ci, col=col: e.tensor_copy(out=scol[:, j4, ci:ci + 1], in_=tabs[:, col:col + 1]), [tabs], [scol])
                    dv(lambda e, j=j, j4=j4: e.tensor_scalar(out=tq[:, 0:1], in0=scol[:, j4, 1:2], scalar1=sinv[:, j:j + 1], scalar2=None,
                                                             op0=ALU.mult), [scol, sinv], [tq])
                    dv(lambda e, j=j, j4=j4: e.scalar_tensor_tensor(out=cN[:, j:j + 1], in0=ccol[:, j4, 1:2], scalar=cosv[:, j:j + 1],
                                                                    in1=tq[:, 0:1], op0=ALU.mult, op1=ALU.subtract), [ccol, cosv, tq], [cN])
                    dv(lambda e, j=j, j4=j4: e.tensor_scalar(out=tq[:, 1:2], in0=ccol[:, j4, 1:2], scalar1=sinv[:, j:j + 1], scalar2=None,
                                                             op0=ALU.mult), [ccol, sinv], [tq])
                    dv(lambda e, j=j, j4=j4: e.scalar_tensor_tensor(out=sN[:, j:j + 1], in0=scol[:, j4, 1:2], scalar=cosv[:, j:j + 1],
                                                                    in1=tq[:, 1:2], op0=ALU.mult, op1=ALU.add), [scol, cosv, tq], [sN])
                for (kind, idx, tok0, n) in units:
                    for j4 in range(4):
                        j = a * 4 + j4
                        c_f, sn_f = tcb[:, j4, 0:n], tsb[:, j4, 0:n]
                        kb = 0
                        ucnt[0] += 1
                        tD, tP, wb, wri, zb, bsb = tD_[kb], tP_[kb], wb_[kb], wri_[kb], zb_[kb], bsb_[kb]
                        for p0 in range(0, n, 512):
                            pn = min(512, n - p0)
                            sl = slice(p0, p0 + pn) if d == 0 else slice(n - p0 - pn, n - p0)
                            for ri in range(2):
                                pb = ps()
                                mm_group(pb, pb[:, 0:pn], [(BT[:, j, ri, :], uS[:, tok0 + p0:tok0 + p0 + pn])], [BT, uS])
                                src_ = pb[:, 0:pn] if d == 0 else pb[:, 0:pn][:, ::-1]
                                s.op("act", lambda e, ri=ri, sl=sl, src_=src_: e.copy(out=bsb[ri][:, sl], in_=src_), reads=[pb], writes=[bsb[ri]])
                        br_, bi_ = bsb[0][:, 0:n], bsb[1][:, 0:n]
                        dv(lambda e: e.tensor_tensor(out=tD[0][:, 0:n], in0=c_f, in1=br_, op=ALU.mult), [tcb, bsb[0]], [tD[0]])
                        dv(lambda e: e.tensor_tensor(out=tD[1][:, 0:n], in0=sn_f, in1=bi_, op=ALU.mult), [tsb, bsb[1]], [tD[1]])
                        dv(lambda e: e.tensor_tensor(out=tP[0][:, 0:n], in0=c_f, in1=bi_, op=ALU.mult), [tcb, bsb[1]], [tP[0]])
                        dv(lambda e: e.tensor_tensor(out=tP[1][:, 0:n], in0=sn_f, in1=br_, op=ALU.mult), [tsb, bsb[0]], [tP[1]])
                        dv(lambda e: e.tensor_tensor(out=wb[0][:, 0:n], in0=tD[0][:, 0:n], in1=tD[1][:, 0:n], op=ALU.add),
                           [tD[0], tD[1]], [wb[0]])
                        dv(lambda e: e.tensor_tensor(out=wb[1][:, 0:n], in0=tP[0][:, 0:n], in1=tP[1][:, 0:n], op=ALU.subtract),
                           [tP[0], tP[1]], [wb[1]])
                        for ri in range(2):
                            if kind == "p":
                                ini = 0.0
                                rds = [rhoT, wb[ri]]
                            else:
                                ini = init[:, j, ri:ri + 1]
                                rds = [rhoT, wb[ri], init]
                            s.op("dve", lambda e, ri=ri, ini=ini, j4=j4: e.tensor_tensor_scan(
                                out=wri[ri][:, 0:n], data0=rhoT[:, j4, 0:n], data1=wb[ri][:, 0:n], initial=ini,
                                op0=ALU.mult, op1=ALU.add), reads=rds, writes=[wri[ri]])
                            s.op("act", lambda e, ri=ri: e.copy(out=zb[ri][:, 0:n], in_=wri[ri][:, 0:n]), reads=[wri[ri]], writes=[zb[ri]])
                        if kind == "s" and idx < 3:
                            dv(lambda e, j=j: e.tensor_scalar(out=tq[:, 0:1], in0=wri[1][:, n - 1:n], scalar1=sN[:, j:j + 1], scalar2=None,
                                                              op0=ALU.mult), [wri[1], sN], [tq])
                            dv(lambda e, j=j: e.tensor_scalar(out=tq[:, 1:2], in0=wri[0][:, n - 1:n], scalar1=sN[:, j:j + 1], scalar2=None,
                                                              op0=ALU.mult), [wri[0], sN], [tq])
                            dv(lambda e, j=j: e.scalar_tensor_tensor(out=init[:, j, 0:1], in0=wri[0][:, n - 1:n], scalar=cN[:, j:j + 1],
                                                                     in1=tq[:, 0:1], op0=ALU.mult, op1=ALU.subtract), [wri[0], cN, tq], [init])
                            dv(lambda e, j=j: e.scalar_tensor_tensor(out=init[:, j, 1:2], in0=wri[1][:, n - 1:n], scalar=cN[:, j:j + 1],
                                                                     in1=tq[:, 1:2], op0=ALU.mult, op1=ALU.add), [wri[1], cN, tq], [init])
                        if kind == "p":
                            fb = FIN[idx]
                            cl, sl_ = ccol[:, j4, 0:1], scol[:, j4, 0:1]
                            zrl, zil = wri[0][:, n - 1:n], wri[1][:, n - 1:n]
                            dv(lambda e: e.tensor_tensor(out=fq[:, 0:1], in0=sl_, in1=zil, op=ALU.mult), [scol, wri[1]], [fq])
                            dv(lambda e: e.tensor_tensor(out=fq[:, 1:2], in0=cl, in1=zrl, op=ALU.mult), [ccol, wri[0]], [fq])
                            dv(lambda e, fb=fb, j=j: e.tensor_tensor(out=fb[:, j, 0:1], in0=fq[:, 1:2], in1=fq[:, 0:1], op=ALU.subtract), [fq], [fb])
                            dv(lambda e: e.tensor_tensor(out=fq[:, 2:3], in0=sl_, in1=zrl, op=ALU.mult), [scol, wri[0]], [fq])
                            dv(lambda e: e.tensor_tensor(out=fq[:, 3:4], in0=cl, in1=zil, op=ALU.mult), [ccol, wri[1]], [fq])
                            dv(lambda e, fb=fb, j=j: e.tensor_tensor(out=fb[:, j, 1:2], in0=fq[:, 2:3], in1=fq[:, 3:4], op=ALU.add), [fq], [fb])
                        zr_, zi_ = zb[0][:, 0:n], zb[1][:, 0:n]
                        sr_o = Sb[j4][0][:, 0:n] if d == 0 else Sb[j4][0][:, 0:n][:, ::-1]
                        si_o = Sb[j4][1][:, 0:n] if d == 0 else Sb[j4][1][:, 0:n][:, ::-1]
                        dv(lambda e: e.tensor_tensor(out=tD[0][:, 0:n], in0=c_f, in1=zr_, op=ALU.mult), [tcb, zb[0]], [tD[0]])
                        dv(lambda e: e.tensor_tensor(out=tP[0][:, 0:n], in0=sn_f, in1=zr_, op=ALU.mult), [tsb, zb[0]], [tP[0]])
                        dv(lambda e: e.tensor_tensor(out=tD[1][:, 0:n], in0=sn_f, in1=zi_, op=ALU.mult), [tsb, zb[1]], [tD[1]])
                        dv(lambda e: e.tensor_tensor(out=tP[1][:, 0:n], in0=c_f, in1=zi_, op=ALU.mult), [tcb, zb[1]], [tP[1]])
                        dv(lambda e, sr_o=sr_o: e.tensor_tensor(out=sr_o, in0=tD[0][:, 0:n], in1=tD[1][:, 0:n], op=ALU.subtract),
                           [tD[0], tD[1]], [Sb[j4][0]])
                        dv(lambda e, si_o=si_o: e.tensor_tensor(out=si_o, in0=tP[0][:, 0:n], in1=tP[1][:, 0:n], op=ALU.add),
                           [tP[0], tP[1]], [Sb[j4][1]])
                        if pump is not None:
                            pump()
                    for p0 in range(0, n, 512):
                        pn = min(512, n - p0)
                        pb = ps()
                        pairs = []
                        rd = [CT]
                        for j4 in range(4):
                            for ri in range(2):
                                pairs.append((CT[:, a * 4 + j4, ri, :], Sb[j4][ri][:, p0:p0 + pn]))
                                rd.append(Sb[j4][ri])
                        mm_group(pb, pb[:, 0:pn], pairs, rd)
                        ysl = Y[:, a, tok0 + p0:tok0 + p0 + pn]
                        if d == 0:
                            dv(lambda e, pb=pb, ysl=ysl, pn=pn, a=a, p0=p0, tok0=tok0: e.scalar_tensor_tensor(
                                out=ysl, in0=uS[:, tok0 + p0:tok0 + p0 + pn], scalar=dco[:, a:a + 1], in1=pb[:, 0:pn],
                                op0=ALU.mult, op1=ALU.add), [uS, dco, pb], [Y])
                        else:
                            dv(lambda e, pb=pb, ysl=ysl, pn=pn: e.tensor_tensor(out=ysl, in0=ysl, in1=pb[:, 0:pn], op=ALU.add),
                               [Y, pb], [Y])
            for sq in range(4):
                s.dma("sp", _ap(nst)[sq, l, d].rearrange("(j q) r -> q j r", q=128), FIN[sq][:], reads=[FIN[sq]], writes=[nst])
            s.barrier()
            s.release(m2)
        m3 = s.mark()
        tD = [s.sbuf("tDg%d" % i, [128, 512], F32) for i in range(2)]
        tP = [s.sbuf("tPg%d" % i, [128, 512], F32) for i in range(2)]
        Wg = s.sbuf("Wg", [128, 2, 256], BF16)
        s.dma("pool", Wg[:], _ap(glu_w)[l].rearrange("(a p) f -> p a f", p=128), writes=[Wg])
        gb = s.sbuf("gb", [128, 2], F32)
        s.dma("sp", gb[:], _ap(glu_b)[l, :].rearrange("(a p) -> p a", p=128), writes=[gb])
        gel = [s.sbuf("gel%d" % a, [128, 512], F32) for a in range(2)]
        gelb = [s.sbuf("gelb%d" % a, [128, 512], BF16) for a in range(2)]
        sig = s.sbuf("sig", [128, 512], F32)
        yo = [s.sbuf("yo%d" % a, [128, 512], BF16) for a in range(2)]
        for blk in range(NB):
            tok = slice(blk * 512, (blk + 1) * 512)
            for a in range(2):
                ysl = Y[:, a, tok]
                s.op("act", lambda e, ysl=ysl: e.activation(out=tD[0][:, 0:512], in_=ysl, func=AF.Square), reads=[Y], writes=[tD[0]])
                dv(lambda e: e.tensor_scalar(out=tD[0][:, 0:512], in0=tD[0][:, 0:512], scalar1=0.044715, scalar2=1.0,
                                             op0=ALU.mult, op1=ALU.add), [tD[0]], [tD[0]])
                dv(lambda e, ysl=ysl: e.tensor_tensor(out=tD[1][:, 0:512], in0=tD[0][:, 0:512], in1=ysl, op=ALU.mult), [tD[0], Y], [tD[1]])
                s.op("act", lambda e: e.activation(out=tP[0][:, 0:512], in_=tD[1][:, 0:512], func=AF.Sigmoid, scale=1.5957691216057308),
                     reads=[tD[1]], writes=[tP[0]])
                dv(lambda e, a=a, ysl=ysl: e.tensor_tensor(out=gel[a][:], in0=tP[0][:, 0:512], in1=ysl, op=ALU.mult), [tP[0], Y], [gel[a]])
                s.op("act", lambda e, a=a: e.copy(out=gelb[a][:], in_=gel[a][:]), reads=[gel[a]], writes=[gelb[a]])
            for a in range(2):
                pb = ps()
                mm_group(pb, pb[:], [(Wg[:, a2, a * 128:(a + 1) * 128], gelb[a2][:]) for a2 in range(2)], [Wg, gelb[0], gelb[1]])
                s.op("act", lambda e, pb=pb, a=a: e.activation(out=sig[:], in_=pb[:], func=AF.Sigmoid, bias=gb[:, a:a + 1]),
                     reads=[pb, gb], writes=[sig])
                dv(lambda e, a=a: e.tensor_tensor(out=yo[a][:], in0=sig[:], in1=gel[a][:], op=ALU.mult), [sig, gel[a]], [yo[a]])
                s.dma("sp", _ap(ymix_d)[a * 128:(a + 1) * 128, tok], yo[a][:], reads=[yo[a]], writes=[ymix_d])
        s.barrier()
        s.release(m0)

    def dense_gen(l):
        qPs = [s.sbuf("qP%d" % i, [128, 4, 256], BF16) for i in range(2)]
        kPs = [s.sbuf("kP%d" % i, [128, 4, 256], BF16) for i in range(2)]
        vPs = [s.sbuf("vP%d" % i, [128, 2, 512], BF16) for i in range(2)]
        Eb = [s.sbuf("Eb%d" % i, [128, 2, 256], BF16) for i in range(2)]
        rdn = [s.sbuf("rdn%d" % i, [128, 256], F32) for i in range(2)]
        yat = [s.sbuf("yat%d" % i, [128, 256], BF16) for i in range(2)]

        def fetch(sq):
            qP, kP, vP = qPs[sq % 2], kPs[sq % 2], vPs[sq % 2]
            tk = slice(sq * 256, (sq + 1) * 256)
            s.dma("sp", qP[:], _ap(q_d)[:, tk].rearrange("(t p) n -> p t n", p=128), reads=[q_d], writes=[qP])
            s.dma("sp", kP[:], _ap(k_d)[:, tk].rearrange("(t p) n -> p t n", p=128), reads=[k_d], writes=[kP])
            s.dma("sp", vP[:], _ap(v_d)[tk, :].rearrange("(t p) f -> p t f", p=128), reads=[v_d], writes=[vP])

        fetch(0)
        n_it = 0
        yield
        for sq in range(4):
            qP, kP, vP = qPs[sq % 2], kPs[sq % 2], vPs[sq % 2]
            if sq + 1 < 4:
                fetch(sq + 1)
            for hp in range(4):
                ya = yat[(sq * 4 + hp) % 2]
                for hh in range(2):
                    pr = slice(hh * 64, (hh + 1) * 64)
                    E = Eb[n_it % 2]
                    rd_ = rdn[n_it % 2]
                    n_it += 1
                    for kt2 in range(2):
                        pb = ps()
                        mm_group(pb, pb[:, 0:256], [(kP[pr, hp, kt2 * 128:(kt2 + 1) * 128], qP[pr, hp, :])], [kP, qP])
                        s.op("act", lambda e, E=E, pb=pb, kt2=kt2: e.activation(out=E[:, kt2, :], in_=pb[:, 0:256], func=AF.Exp, scale=0.125),
                             reads=[pb], writes=[E])
                    pn_ = ps()
                    mm_group(pn_, pn_[:, 0:256], [(vP[:, kt2, hp * 128:(hp + 1) * 128], E[:, kt2, :]) for kt2 in range(2)], [vP, E])
                    pd_ = ps()
                    mm_group(pd_, pd_[:, 0:256], [(ones_bf[:], E[:, kt2, :]) for kt2 in range(2)], [ones_bf, E])
                    s.op("dve", lambda e, rd_=rd_, pd_=pd_, pr=pr: e.reciprocal(out=rd_[pr, :], in_=pd_[pr, 0:256]), reads=[pd_], writes=[rd_])
                    s.op("dve", lambda e, ya=ya, pn_=pn_, rd_=rd_, pr=pr: e.tensor_tensor(out=ya[pr, :], in0=pn_[pr, 0:256], in1=rd_[pr, :],
                                                                                        op=ALU.mult), reads=[pn_, rd_], writes=[ya])
                s.dma("sp", _ap(ymix_d)[256 + hp * 128:256 + (hp + 1) * 128, sq * 256:(sq + 1) * 256], ya[:], reads=[ya], writes=[ymix_d])
                yield

    def attn_part(l):
        m0 = s.mark()
        BTt = s.sbuf("BTt", [128, 8, 15, 64], BF16)
        identb = s.sbuf("identb", [128, 128], BF16)
        s.op("dve", lambda e: e.tensor_copy(out=identb[:], in_=ident[:]), reads=[ident], writes=[identb])
        m1 = s.mark()
        oh = s.sbuf("oh", [31, 64, 128], F32)
        s.dma("sp", oh[:], _ap(c_oh).rearrange("d (k q) -> d k q", q=128), writes=[oh])
        msk = s.sbuf("msk", [128, 64], F32)
        s.dma("sp", msk[:], _ap(c_mask), writes=[msk])
        rpT = s.sbuf("rpT", [31, 128], F32)
        s.op("dve", lambda e: e.memset(rpT[:], 0.0), writes=[rpT])
        s.dma("sp", rpT[:, 0:120], _ap(rpb)[l].rearrange("x d -> d x"), reads=[rpT], writes=[rpT])
        BTf = BTt[:].rearrange("p h d k -> p (h d) k")
        for k4 in range(16):
            pb = ps()
            for ki in range(4):
                kk = k4 * 4 + ki
                mm_group(pb, pb[:, ki * 128:ki * 128 + 128], [(oh[:, kk, :], rpT[:])], [oh, rpT])
            for ki in range(4):
                kk = k4 * 4 + ki
                s.op("dve", lambda e, pb=pb, ki=ki, kk=kk: e.tensor_scalar(out=BTf[:, :, kk], in0=pb[:, ki * 128:ki * 128 + 120],
                                                                           scalar1=msk[:, kk:kk + 1], scalar2=8.0, op0=ALU.add, op1=ALU.mult),
                     reads=[pb, msk], writes=[BTt])
        s.barrier()
        s.release(m1)
        if int(os.environ.get("MK_ATT_STOP", "9")) <= 2:
            return
        qS = s.sbuf("qS", [128, 4, NS_TOK], BF16)
        kS = s.sbuf("kS", [128, 4, NS_TOK], BF16)
        for t4 in range(4):
            s.dma("sp", qS[:, t4, :], _ap(q_d)[t4 * 128:(t4 + 1) * 128, NP_TOK:NT], reads=[q_d], writes=[qS])
            s.dma("sp", kS[:, t4, :], _ap(k_d)[t4 * 128:(t4 + 1) * 128, NP_TOK:NT], reads=[k_d], writes=[kS])
        vS = s.sbuf("vS", [128, 64, 512], BF16)
        s.dma("sp", vS[0:64, :, :], _ap(v_d)[NP_TOK:NT, :].rearrange("(r c) f -> c r f", c=64), reads=[v_d], writes=[vS])
        s.dma("sp", vS[64:128, 0:63, :], _ap(v_d)[NP_TOK + 64:NT, :].rearrange("(r c) f -> c r f", c=64), reads=[v_d], writes=[vS])
        s.op("dve", lambda e: e.memset(vS[64:128, 63:64, :], 0.0), reads=[vS], writes=[vS])
        ck32 = s.sbuf("ck32", [128, 2, 512], F32)
        s.dma("sp", ck32[:], _ap(ck)[l].rearrange("(t p) f -> p t f", p=128), writes=[ck32])
        kC = s.sbuf("kC", [128, 4, 256], BF16)
        for hp in range(4):
            pb = ps()
            s.group("pe", [lambda e, pb=pb, t=t, hp=hp: e.transpose(pb[:, t * 128:(t + 1) * 128], ck32[:, t, hp * 128:(hp + 1) * 128], ident[:])
                           for t in range(2)], reads=[ck32, ident], writes=[pb])
            s.op("act", lambda e, pb=pb, hp=hp: e.copy(out=kC[:, hp, :], in_=pb[:, 0:256]), reads=[pb], writes=[kC])
        vC = s.sbuf("vC", [128, 2, 512], BF16)
        s.dma("pool", vC[:], _ap(cv)[l].rearrange("(t p) f -> p t f", p=128), writes=[vC])
        NBUF = 3
        EC = [s.sbuf("EC%d" % i, [128, 6, 2, 64], BF16) for i in range(NBUF)]
        rdw = [s.sbuf("rdw%d" % i, [128, 128], F32) for i in range(NBUF)]
        YN = [s.sbuf("YN%d" % i, [128, 4, 512], BF16) for i in range(2)]
        it = 0
        for r in range(int(os.environ.get("MK_NA_ROWS", "64"))):
            rs = min(max(r - 4, 0), 56)
            dr0 = rs - r + 7
            yn = YN[(r // 8) % 2]
            for hp in range(4):
                k2 = it % NBUF
                it += 1
                E_ = EC[k2]
                for hh in range(2):
                    pr = slice(hh * 64, (hh + 1) * 64)
                    h = 2 * hp + hh
                    qrow = qS[pr, hp, r * 64:(r + 1) * 64]
                    idb = identb[pr, hh * 64:(hh + 1) * 64]
                    pw = ps()
                    fw = []
                    for m in range(4):
                        fw.append(lambda e, pw=pw, m=m, pr=pr, qrow=qrow: e.matmul(
                            pw[:, m * 64:(m + 1) * 64], kS[pr, hp, (rs + 2 * m) * 64:(rs + 2 * m + 2) * 64], qrow, start=True, stop=False))
                        fw.append(lambda e, pw=pw, m=m, pr=pr, h=h, idb=idb: e.matmul(
                            pw[:, m * 64:(m + 1) * 64], BTt[pr, h, dr0 + 2 * m:dr0 + 2 * m + 2, :].rearrange("p d k -> p (d k)"), idb,
                            start=False, stop=True))
                    for t in range(2):
                        fw.append(lambda e, pw=pw, t=t, pr=pr, qrow=qrow: e.matmul(
                            pw[:, 256 + t * 64:256 + (t + 1) * 64], kC[pr, hp, t * 128:(t + 1) * 128], qrow, start=True, stop=True))
                    s.group("pe", fw, reads=[kS, kC, qS, BTt, identb], writes=[pw])
                    s.op("act", lambda e, E_=E_, pw=pw, hh=hh: e.activation(
                        out=E_[:, :, hh, :], in_=pw[:, 0:384].rearrange("p (m q) -> p m q", m=6), func=AF.Exp, scale=0.125),
                        reads=[pw], writes=[E_])
                pnd = ps()
                rhs6 = [E_[:, m, :, :].rearrange("p h q -> p (h q)") for m in range(6)]
                lv = [vS[:, rs + 2 * m, hp * 128:(hp + 1) * 128] for m in range(4)] + [vC[:, t, hp * 128:(hp + 1) * 128] for t in range(2)]
                mm_group(pnd, pnd[:, 0:128], [(lv[m], rhs6[m]) for m in range(6)], [vS, vC, E_])
                mm_group(pnd, pnd[:, 128:256], [(ones_bf[:], rhs6[m]) for m in range(6)], [ones_bf, E_])
                rd_ = rdw[k2]
                s.op("dve", lambda e, rd_=rd_, pnd=pnd: e.reciprocal(out=rd_[:], in_=pnd[:, 128:256]), reads=[pnd], writes=[rd_])
                for hh in range(2):
                    pr = slice(hh * 64, (hh + 1) * 64)
                    s.op("dve", lambda e, yn=yn, pnd=pnd, rd_=rd_, pr=pr, hp=hp, r=r, hh=hh: e.tensor_tensor(
                        out=yn[pr, hp, (r % 8) * 64:(r % 8 + 1) * 64], in0=pnd[pr, hh * 64:(hh + 1) * 64], in1=rd_[pr, hh * 64:(hh + 1) * 64],
                        op=ALU.mult), reads=[pnd, rd_], writes=[yn])
            if r % 8 == 7:
                tok0 = NP_TOK + (r // 8) * 512
                for hp in range(4):
                    s.dma("sp", _ap(ymix_d)[256 + hp * 128:256 + (hp + 1) * 128, tok0:tok0 + 512], yn[:, hp, :], reads=[yn], writes=[ymix_d])
        s.barrier()
        s.release(m0)

    def gate_gen(l):
        ws32 = s.sbuf("ws32", [128, 4, 128], F32)
        s.dma("sp", ws32[:], _ap(gm_ws)[l].rearrange("g i j -> i g j"), writes=[ws32])
        wsT = s.sbuf("wsT", [128, 4, 128], BF16)
        pb = ps()
        s.group("pe", [lambda e, g=g: e.transpose(pb[:, g * 128:(g + 1) * 128], ws32[:, g, :], ident[:]) for g in range(4)],
                reads=[ws32, ident], writes=[pb])
        s.op("act", lambda e: e.copy(out=wsT[:].rearrange("p g i -> p (g i)"), in_=pb[:]), reads=[pb], writes=[wsT])
        BS = s.sbuf("BS", [128, 2, 128], F32)
        for g in range(4):
            s.dma("sp", BS[(g % 2) * 64:(g % 2 + 1) * 64, g // 2, :], _ap(gm_bs)[l, g:g + 1, :].to_broadcast([64, 128]), writes=[BS])
        ug = [s.sbuf("ug%d" % i, [128, 2, 512], BF16) for i in range(2)]
        vAl = [s.sbuf("vAl%d" % i, [128, 4, 256], BF16) for i in range(2)]
        vBl = [s.sbuf("vBl%d" % i, [128, 4, 256], BF16) for i in range(2)]
        tg = [s.sbuf("tg%d" % i, [128, 128], F32) for i in range(2)]
        yg = [s.sbuf("yg%d" % i, [128, 2, 512], BF16) for i in range(2)]
        it = 0
        def fetch(blk):
            k2 = blk % 2
            tok = slice(blk * 512, (blk + 1) * 512)
            for a in range(2):
                s.dma("sp", ug[k2][:, a, :], _ap(ug_d)[a * 128:(a + 1) * 128, tok], reads=[ug_d], writes=[ug[k2]])
            s.dma("sp", vAl[k2][:], _ap(va_d)[tok, :].rearrange("(t p) f -> p t f", p=128), reads=[va_d], writes=[vAl[k2]])
            s.dma("sp", vBl[k2][:], _ap(vb_d)[tok, :].rearrange("(t p) f -> p t f", p=128), reads=[vb_d], writes=[vBl[k2]])

        fetch(0)
        yield
        for blk in range(NB):
            k2 = blk % 2
            tok = slice(blk * 512, (blk + 1) * 512)
            if blk + 1 < NB:
                fetch(blk + 1)
            for t in range(4):
                if t:
                    yield
                for a in range(2):
                    pq = ps()
                    mm_group(pq, pq[:, 0:128], [(vAl[k2][:, t, a * 128:(a + 1) * 128], wsT[:, 2 * a, :]),
                                                (vBl[k2][:, t, a * 128:(a + 1) * 128], wsT[:, 2 * a + 1, :])], [vAl[k2], vBl[k2], wsT])
                    tt_ = tg[it % 2]
                    it += 1
                    s.op("dve", lambda e, tt_=tt_, pq=pq, a=a: e.tensor_tensor(out=tt_[:], in0=pq[:, 0:128], in1=BS[:, a, :], op=ALU.add),
                         reads=[pq, BS], writes=[tt_])
                    s.op("dve", lambda e, tt_=tt_, a=a, t=t, k2=k2: e.tensor_tensor(
                        out=yg[k2][:, a, t * 128:(t + 1) * 128], in0=tt_[:], in1=ug[k2][:, a, t * 128:(t + 1) * 128], op=ALU.mult),
                        reads=[tt_, ug[k2]], writes=[yg[k2]])
            for a in range(2):
                s.dma("sp", _ap(ymix_d)[768 + a * 128:768 + (a + 1) * 128, tok], yg[k2][:, a, :], reads=[yg[k2]], writes=[ymix_d])
            yield

    PH = os.environ.get("MK_PHASES", "all")
    if PH != "all":
        for name in PH.split(","):
            {"adaln": setup_adaln, "ffn": lambda: ffn_phase(0, 0, f1_in, f1_out, True, False), "proj": lambda: proj_phase(0),
             "ssm": lambda: ssm_part(0), "attn": lambda: attn_part(0),
             "outproj": lambda: outproj_phase(0)}[name]()
        s.barrier()
        s.release(0)
        ctx.__exit__(None, None, None)
        return nc
    mW = s.mark()
    W = alloc_ffn_w()
    setup_adaln(after_dma=lambda: issue_ffn_w(W, 0, f1_in, f1_out))
    for l in range(DEPTH):
        if l == 0:
            ffn_phase(l, 0, f1_in, f1_out, first=True, last=False, W=W)
            s.release(mW)
        else:
            ffn_phase(l, 0, f1_in, f1_out, first=False, last=False)
        proj_phase(l)
        mB = s.mark()
        gens = [dense_gen(l), gate_gen(l)]
        for g in gens:
            next(g)

        def pump():
            for g in gens:
                try:
                    next(g)
                    return
                except StopIteration:
                    continue

        ssm_part(l, pump)
        for g in gens:
            for _ in g:
                pass
        s.barrier()
        s.release(mB)
        attn_part(l)
        mW = s.mark()
        W = alloc_ffn_w()
        mWO = s.mark()
        WO = s.sbuf("WO", [128, 8, 1024], BF16)
        s.dma("pool", WO[:], _ap(w_out)[l].rearrange("(kt p) f -> p kt f", p=128), writes=[WO])
        issue_ffn_w(W, l, f2_in, f2_out)
        outproj_phase(l, WO=WO)
        s.release(mWO)
        ffn_phase(l, 2, f2_in, f2_out, first=False, last=(l == DEPTH - 1), W=W)
        s.release(mW)
    outs = [yp, ys, nk, nv, nst]
    s.barrier()
    s.release(0)
    ctx.__exit__(None, None, None)
    return nc


_NC_CACHE = {}


def _consts():
    ident = np.eye(128, dtype=np.float32)
    kc = np.arange(64)[:, None]
    qc = np.arange(64)[None, :]
    dc = kc - qc + 15
    oh = np.zeros((31, 64, 128), np.float32)
    for q in range(64):
        for k in range(64):
            if 0 <= dc[k, q] <= 30:
                oh[dc[k, q], k, q] = 1.0
                oh[dc[k, q], k, 64 + q] = 1.0
    cs = np.clip(np.arange(64) - 8, 0, 48)
    win = (kc >= cs[None, :]) & (kc < cs[None, :] + 16)
    mask = np.where(win, 0.0, NEG).astype(np.float32)
    iota = np.tile(np.arange(1024, dtype=np.float32)[None, :], (128, 1))
    return {"c_ident": ident, "c_oh": oh.reshape(31, 8192), "c_mask": np.ascontiguousarray(np.concatenate([mask.T, mask.T], axis=0)), "c_iota": iota}


def kernel(**inputs):
    debug = bool(int(os.environ.get("MK_DEBUG", "0")))
    key = ("nc", debug)
    if key not in _NC_CACHE:
        _NC_CACHE[key] = build_program(debug=debug)
    nc = _NC_CACHE[key]
    f = lambda a: np.ascontiguousarray(np.asarray(a, dtype=np.float32))
    x_prompt, x_sample = f(inputs["x_prompt"]), f(inputs["x_sample"])
    c, c_ctx = f(inputs["c"]), f(inputs["c_ctx"])
    cache_k, cache_v, state_ssm = f(inputs["cache_k"]), f(inputs["cache_v"]), f(inputs["state_ssm"])
    shared = {}
    for name in ("w_ada", "b_ada", "norm_ffn1", "norm_mix", "norm_ffn2", "ffn1_w_in", "ffn1_w_out", "ffn2_w_in",
                 "ffn2_w_out", "w_in", "w_out", "ssm_d", "ssm_glu_w", "ssm_glu_b", "na_q_norm", "na_k_norm",
                 "gm_ws", "gm_bs"):
        shared[name] = f(inputs[name])
    shared["ssm_lambda_re"] = f(inputs["ssm_lambda_re"]).reshape(DEPTH, 2, 1024)
    shared["ssm_lambda_im"] = f(inputs["ssm_lambda_im"]).reshape(DEPTH, 2, 1024)
    shared["ssm_log_dt"] = f(inputs["ssm_log_dt"])
    shared["ssm_b_re"] = f(inputs["ssm_b_re"]).reshape(DEPTH, 2, 1024, 16)
    shared["ssm_b_im"] = f(inputs["ssm_b_im"]).reshape(DEPTH, 2, 1024, 16)
    shared["ssm_c_re"] = f(inputs["ssm_c_re"]).reshape(DEPTH, 2, 256, 64)
    shared["ssm_c_im"] = f(inputs["ssm_c_im"]).reshape(DEPTH, 2, 256, 64)
    shared["na_rpb"] = f(inputs["na_rpb"]).reshape(DEPTH, 120, 31)
    shared.update(_consts())
    in_maps = []
    for core in range(8):
        b = core // 2
        m = dict(shared)
        m["xp"] = x_prompt[4 * core:4 * core + 4].reshape(NP_TOK, D)
        m["xs"] = x_sample[b]
        m["cond"] = np.stack([c_ctx, c[b]], axis=0)
        m["ck"] = cache_k[b].reshape(DEPTH, 256, 512)
        m["cv"] = cache_v[b].reshape(DEPTH, 256, 512)
        m["sst"] = state_ssm[b].reshape(DEPTH, 2, 1024, 2)
        hs = np.zeros((128, 2), np.float32)
        hs[:, core % 2] = 1.0
        m["c_hsel"] = hs
        in_maps.append(m)
    res = run_bass_kernel_spmd(nc, in_maps, core_ids=list(range(8)))
    R = res.results
    if debug:
        kernel.last_results = R
    y_prompt = np.concatenate([R[i]["yp"].reshape(4, 256, D) for i in range(8)], axis=0)
    y_sample = np.stack([np.concatenate([R[2 * b]["ys"], R[2 * b + 1]["ys"]], axis=0) for b in range(4)], axis=0)
    new_k = np.concatenate([R[i]["nk"].reshape(4, DEPTH, 256, 8, 64) for i in range(8)], axis=0)
    new_v = np.concatenate([R[i]["nv"].reshape(4, DEPTH, 256, 8, 64) for i in range(8)], axis=0)
    new_s = np.concatenate([R[i]["nst"].reshape(4, DEPTH, 2, 16, 64, 2) for i in range(8)], axis=0)
    return (y_prompt.astype(np.float32), y_sample.astype(np.float32), new_k.astype(np.float32),
            new_v.astype(np.float32), new_s.astype(np.float32))
```
